# Optimizing a Trainium2 kernel written in Bass

```python
import math, functools
import jax, jax.numpy as jnp
from jax import lax
import numpy as np

D_MODEL = 1024
BATCH = 16
SEQ = 2048
DEPTH = 1
DEC_BATCH = 128
DEC_SEQ = 8
PAST_LEN = 16384
PAGE_SIZE = 128

HEAD_DIM = 64
N_HEADS = 8
N_KV = 2
GROUP = N_HEADS // N_KV
WINDOW = 128
BLOCK = WINDOW
Q_W = N_HEADS * HEAD_DIM
KV_W = N_KV * HEAD_DIM
N_BUCKETS = 32
MAX_EXACT = N_BUCKETS // 2
REL_MAX_DIST = 128
RW_N = 64
RW_HEADS = 8
RW = RW_HEADS * RW_N
LORA_W = 64
LORA_A = 64
LORA_G = 128
RW_SHIFT = 3 * RW + LORA_W + LORA_A + LORA_G
GATE_W = 2 * D_MODEL
IN_COLS = Q_W + 2 * KV_W + RW_SHIFT + GATE_W
D_FF = 2816
CONV_W = 3
NORM_EPS = 1e-6
GN_EPS = 64e-5
NEG = -1e30

kernel_name = 'swa_rwkv7_gated_hybrid_step'


def rmsnorm(x, g):
    xf = x.astype(jnp.float32)
    xf = xf * lax.rsqrt(jnp.mean(xf * xf, axis=-1, keepdims=True) + NORM_EPS)
    return (xf * g.astype(jnp.float32)).astype(x.dtype)


def t5_bucket(dist):
    n = jnp.maximum(dist, 0)
    nf = jnp.maximum(n, 1).astype(jnp.float32)
    large = MAX_EXACT + (jnp.log(nf / MAX_EXACT) / math.log(REL_MAX_DIST / MAX_EXACT)
                         * (N_BUCKETS - MAX_EXACT)).astype(jnp.int32)
    return jnp.where(n < MAX_EXACT, n, jnp.minimum(large, N_BUCKETS - 1))


def attend(q, k, v, dist, valid, rel_bias, sinks):
    s = jnp.einsum('...qhgd,...khd->...hgqk', q, k).astype(jnp.float32) * (HEAD_DIM ** -0.5)
    bias = rel_bias[t5_bucket(dist)].astype(jnp.float32)
    bias = jnp.transpose(bias, (2, 0, 1)).reshape(N_KV, GROUP, dist.shape[0], dist.shape[1])
    s = jnp.where(valid, s + bias, NEG)
    sink = jnp.broadcast_to(sinks.astype(jnp.float32).reshape(N_KV, GROUP, 1, 1), s.shape[:-1] + (1,))
    p = jax.nn.softmax(jnp.concatenate([s, sink], axis=-1), axis=-1)[..., :-1]
    return jnp.einsum('...hgqk,...khd->...qhgd', p.astype(v.dtype), v)


def banded_attention(q, k, v, rel_bias, sinks, win_buf):
    B, T = q.shape[:2]
    nb = T // BLOCK
    qb = q.reshape(B, nb, BLOCK, N_KV, GROUP, HEAD_DIM)
    kb = k.reshape(B, nb, BLOCK, N_KV, HEAD_DIM)
    vb = v.reshape(B, nb, BLOCK, N_KV, HEAD_DIM)

    def with_prev(t):
        prev = jnp.pad(t, ((0, 0), (1, 0), (0, 0), (0, 0), (0, 0)))[:, :-1]
        return jnp.concatenate([prev, t], axis=2)

    kc, vc = with_prev(kb), with_prev(vb)
    qi = jnp.arange(BLOCK) + BLOCK
    kj = jnp.arange(2 * BLOCK)
    dist = qi[:, None] - kj[None, :]
    key_abs = jnp.arange(nb)[:, None] * BLOCK - BLOCK + kj[None, :]
    valid = ((dist >= 0) & (dist < WINDOW))[None] & (key_abs >= 0)[:, None, :]
    o = attend(qb, kc, vc, dist, valid[:, None, None], rel_bias, sinks)
    return o.reshape(B, T, Q_W), k[:, T - win_buf:], v[:, T - win_buf:]


def window_cache_attention(q, k, v, cache_k, cache_v, rel_bias, sinks):
    B, T = q.shape[:2]
    wb = cache_k.shape[1]
    kc = jnp.concatenate([cache_k.astype(k.dtype), k], axis=1)
    vc = jnp.concatenate([cache_v.astype(v.dtype), v], axis=1)
    q_pos = PAST_LEN + jnp.arange(T)
    k_pos = jnp.concatenate([PAST_LEN - wb + jnp.arange(wb), PAST_LEN + jnp.arange(T)])
    dist = q_pos[:, None] - k_pos[None, :]
    valid = (dist >= 0) & (dist < WINDOW)
    o = attend(q, kc, vc, dist, valid, rel_bias, sinks)
    return o.reshape(B, T, Q_W), kc[:, T:], vc[:, T:]


def wkv_scan(r, w, k, v, av, bv, s0):
    def step(S, inp):
        r_t, w_t, k_t, v_t, a_t, b_t = inp
        sa = jnp.einsum('bhij,bhj->bhi', S, a_t)
        S = S * w_t[:, :, None, :] + sa[..., None] * b_t[:, :, None, :] + v_t[..., None] * k_t[:, :, None, :]
        return S, jnp.einsum('bhij,bhj->bhi', S, r_t)

    xs = tuple(jnp.moveaxis(t, 1, 0) for t in (r, w, k, v, av, bv))
    s_fin, ys = lax.scan(step, s0, xs)
    return jnp.moveaxis(ys, 0, 1), s_fin


def rwkv7_mixer(p, shift0, wkv0, mu_shift, w0, w2, a0, a2, g2, k_k, k_a, r_k, lnx_g, lnx_b):
    B, T, _ = p.shape
    f32 = jnp.float32
    prev = jnp.concatenate([shift0[:, None, :].astype(p.dtype), p[:, :-1]], axis=1)
    xs = p + mu_shift * (prev - p)
    r, k, v, lw, la, lg = jnp.split(xs, [RW, 2 * RW, 3 * RW, 3 * RW + LORA_W, 3 * RW + LORA_W + LORA_A], axis=-1)
    w_log = -jax.nn.softplus(-(w0 + jnp.tanh(lw) @ w2).astype(f32)) - 0.5
    decay = jnp.exp(-jnp.exp(w_log))
    a = jax.nn.sigmoid((a0 + la @ a2).astype(f32))
    g = jax.nn.sigmoid(lg) @ g2

    def hs(t):
        return t.astype(f32).reshape(B, T, RW_HEADS, RW_N)

    r, k, v, decay, a = hs(r), hs(k), hs(v), hs(decay), hs(a)
    kk = k * k_k.astype(f32).reshape(RW_HEADS, RW_N)
    kk = kk / jnp.maximum(jnp.sqrt(jnp.sum(kk * kk, axis=-1, keepdims=True)), 1e-12)
    k = k * (1.0 + (a - 1.0) * k_a.astype(f32).reshape(RW_HEADS, RW_N))
    y, s_new = wkv_scan(r, decay, k, v, -kk, kk * a, wkv0.astype(f32))
    mean = jnp.mean(y, axis=-1, keepdims=True)
    var = jnp.mean(jnp.square(y - mean), axis=-1, keepdims=True)
    y = (y - mean) * lax.rsqrt(var + GN_EPS)
    y = y * lnx_g.astype(f32).reshape(RW_HEADS, RW_N) + lnx_b.astype(f32).reshape(RW_HEADS, RW_N)
    y = y + jnp.sum(r * k * r_k.astype(f32), axis=-1, keepdims=True) * v
    out = y.reshape(B, T, RW).astype(p.dtype) * g
    return out, p[:, -1], s_new


def conv_ffn(h, conv0, w_up, conv_w, conv_b, w_down):
    T = h.shape[1]
    ug, uv = jnp.split(h @ w_up, 2, axis=-1)
    ucat = jnp.concatenate([conv0.astype(ug.dtype), ug], axis=1)
    c = conv_b
    for j in range(CONV_W):
        c = c + conv_w[j] * ucat[:, j:j + T]
    y = (jax.nn.gelu(c) * uv) @ w_down
    return y, ucat[:, T:]


def layer_forward(x, attn_fn, shift0, wkv0, conv0, norm1_g, w_in, mu_shift, w0, w2, a0, a2, g2,
                  k_k, k_a, r_k, lnx_g, lnx_b, w_pa, w_pb, w_o, norm2_g, w_up, conv_w, conv_b, w_down):
    B, T, _ = x.shape
    h = rmsnorm(x, norm1_g)
    proj = h @ w_in
    q, ka, va, rw, gates = jnp.split(proj, [Q_W, Q_W + KV_W, Q_W + 2 * KV_W, Q_W + 2 * KV_W + RW_SHIFT], axis=-1)
    q = q.reshape(B, T, N_KV, GROUP, HEAD_DIM)
    ka = ka.reshape(B, T, N_KV, HEAD_DIM)
    va = va.reshape(B, T, N_KV, HEAD_DIM)
    att, k_rows, v_rows = attn_fn(q, ka, va)
    rw_out, shift_new, wkv_new = rwkv7_mixer(rw, shift0, wkv0, mu_shift, w0, w2, a0, a2, g2,
                                             k_k, k_a, r_k, lnx_g, lnx_b)
    g_a, g_b = jnp.split(jax.nn.sigmoid(gates), 2, axis=-1)
    x = x + (g_a * (att @ w_pa) + g_b * (rw_out @ w_pb)) @ w_o
    y_ffn, conv_new = conv_ffn(rmsnorm(x, norm2_g), conv0, w_up, conv_w, conv_b, w_down)
    x = x + y_ffn
    return x, (k_rows, v_rows, shift_new, wkv_new, conv_new)


def setup_inputs(seed: int = 0) -> dict:
    key = jax.random.key(seed)
    ks = jax.random.split(key, 40)
    f32 = jnp.float32

    def nrm(k, shape, scale):
        return jax.random.normal(k, shape, f32) * scale

    wb = min(WINDOW, PAST_LEN)
    L = DEPTH
    return {
        'x_prompt': nrm(ks[0], (BATCH, SEQ, D_MODEL), 1.0),
        'x_sample': nrm(ks[1], (DEC_BATCH, DEC_SEQ, D_MODEL), 1.0),
        'cache_win_k': nrm(ks[2], (L, DEC_BATCH, wb, N_KV, HEAD_DIM), 1.0),
        'cache_win_v': nrm(ks[3], (L, DEC_BATCH, wb, N_KV, HEAD_DIM), 1.0),
        'state_shift': nrm(ks[4], (L, DEC_BATCH, RW_SHIFT), 1.0),
        'state_wkv': nrm(ks[5], (L, DEC_BATCH, RW_HEADS, RW_N, RW_N), 0.3),
        'state_conv': nrm(ks[6], (L, DEC_BATCH, CONV_W - 1, D_FF), 1.0),
        'rel_bias': nrm(ks[7], (N_BUCKETS, N_HEADS), 0.5),
        'norm1_g': 1.0 + nrm(ks[8], (L, D_MODEL), 0.05),
        'w_in': nrm(ks[9], (L, D_MODEL, IN_COLS), D_MODEL ** -0.5),
        'sinks': nrm(ks[10], (L, N_HEADS), 0.5),
        'mu_shift': jax.random.uniform(ks[11], (L, RW_SHIFT), f32),
        'w0': jax.random.uniform(ks[12], (L, RW), f32, -6.0, -1.0),
        'w2': nrm(ks[13], (L, LORA_W, RW), 0.5 * LORA_W ** -0.5),
        'a0': nrm(ks[14], (L, RW), 0.05),
        'a2': nrm(ks[15], (L, LORA_A, RW), LORA_A ** -0.5),
        'g2': nrm(ks[16], (L, LORA_G, RW), LORA_G ** -0.5),
        'k_k': 0.85 + nrm(ks[17], (L, RW), 0.05),
        'k_a': 1.0 + nrm(ks[18], (L, RW), 0.05),
        'r_k': nrm(ks[19], (L, RW_HEADS, RW_N), 0.1),
        'lnx_g': 1.0 + nrm(ks[20], (L, RW), 0.05),
        'lnx_b': nrm(ks[21], (L, RW), 0.02),
        'w_pa': nrm(ks[22], (L, Q_W, D_MODEL), Q_W ** -0.5),
        'w_pb': nrm(ks[23], (L, RW, D_MODEL), RW ** -0.5),
        'w_o': nrm(ks[24], (L, D_MODEL, D_MODEL), D_MODEL ** -0.5),
        'norm2_g': 1.0 + nrm(ks[25], (L, D_MODEL), 0.05),
        'w_up': nrm(ks[26], (L, D_MODEL, 2 * D_FF), D_MODEL ** -0.5),
        'conv_w': nrm(ks[27], (L, CONV_W, D_FF), CONV_W ** -0.5),
        'conv_b': nrm(ks[28], (L, D_FF), 0.02),
        'w_down': nrm(ks[29], (L, D_FF, D_MODEL), D_FF ** -0.5),
        'final_g': 1.0 + nrm(ks[30], (D_MODEL,), 0.05),
    }


def reference(x_prompt, x_sample, cache_win_k, cache_win_v, state_shift, state_wkv, state_conv,
              rel_bias, norm1_g, w_in, sinks, mu_shift, w0, w2, a0, a2, g2, k_k, k_a, r_k,
              lnx_g, lnx_b, w_pa, w_pb, w_o, norm2_g, w_up, conv_w, conv_b, w_down, final_g):
    xp, xs = x_prompt, x_sample
    bp = xp.shape[0]
    win_buf = cache_win_k.shape[2]
    pk, pv, psh, pwkv, pconv = [], [], [], [], []
    sk, sv, ssh, swkv, sconv = [], [], [], [], []
    for l in range(DEPTH):
        weights = (norm1_g[l], w_in[l], mu_shift[l], w0[l], w2[l], a0[l], a2[l], g2[l], k_k[l], k_a[l],
                   r_k[l], lnx_g[l], lnx_b[l], w_pa[l], w_pb[l], w_o[l], norm2_g[l], w_up[l], conv_w[l],
                   conv_b[l], w_down[l])
        prompt_attn = functools.partial(banded_attention, rel_bias=rel_bias, sinks=sinks[l], win_buf=win_buf)
        sample_attn = functools.partial(window_cache_attention, cache_k=cache_win_k[l], cache_v=cache_win_v[l],
                                        rel_bias=rel_bias, sinks=sinks[l])
        xp, st_p = layer_forward(xp, prompt_attn,
                                 jnp.zeros((bp, RW_SHIFT), xp.dtype),
                                 jnp.zeros((bp, RW_HEADS, RW_N, RW_N), jnp.float32),
                                 jnp.zeros((bp, CONV_W - 1, D_FF), xp.dtype),
                                 *weights)
        xs, st_s = layer_forward(xs, sample_attn, state_shift[l], state_wkv[l], state_conv[l], *weights)
        pk.append(st_p[0]); pv.append(st_p[1]); psh.append(st_p[2]); pwkv.append(st_p[3]); pconv.append(st_p[4])
        sk.append(st_s[0]); sv.append(st_s[1]); ssh.append(st_s[2]); swkv.append(st_s[3]); sconv.append(st_s[4])
    y_prompt = rmsnorm(xp, final_g)
    y_sample = rmsnorm(xs, final_g)
    p_win_k = jnp.stack(pk); p_win_v = jnp.stack(pv); p_shift = jnp.stack(psh)
    p_wkv = jnp.stack(pwkv); p_conv = jnp.stack(pconv)
    s_win_k = jnp.stack(sk); s_win_v = jnp.stack(sv); s_shift = jnp.stack(ssh)
    s_wkv = jnp.stack(swkv); s_conv = jnp.stack(sconv)
    return (y_prompt, y_sample, p_win_k, p_win_v, p_shift, p_wkv, p_conv, s_win_k, s_win_v, s_shift, s_wkv, s_conv)
```

```python
import numpy as np
from contextlib import ExitStack
import concourse.bass as bass
import concourse.mybir as mybir
from concourse.bass_utils import run_bass_kernel_spmd

F32 = mybir.dt.float32
BF16 = mybir.dt.bfloat16
AF = mybir.ActivationFunctionType
ALU = mybir.AluOpType
AX = mybir.AxisListType

import os as _os0
SAME_ENGINE_SYNC = _os0.environ.get("SES", "1") == "1"
EPOCH = 30000
RELAX_FSZ = int(_os0.environ.get('KRELAX', '0'))
N_DMA_SEMS = {"sp": 8, "pool": 4}


class _Op:
    __slots__ = ("eng", "fn", "deps", "isdma", "ms", "dsem", "dval", "needed", "desc", "fsz")


class _Rec:
    def __init__(self):
        self.call = None

    def __getattr__(self, name):
        def f(*a, **k):
            assert self.call is None
            self.call = (name, a, k)
            return self
        return f


class Prog:
    def __init__(self, nc):
        self.nc = nc
        self.ops = []
        self.lastw = {}
        self.rd_c = {}
        self.rd_d = {}
        self.stack = ExitStack()
        self.last_op = {}
        self.pending = {}
        self.bar_from = 0

    def sbuf(self, name, shape, dtype):
        return self.stack.enter_context(self.nc.sbuf_tensor(name, list(shape), dtype))

    def psum(self, name, shape, dtype):
        return self.stack.enter_context(self.nc.psum_tensor(name, list(shape), dtype))

    def op(self, eng, fn, reads=(), writes=(), isdma=False):
        idx = len(self.ops)
        deps = set()
        for r in reads:
            w = self.lastw.get(r)
            if w is not None:
                deps.add(w)
        for r in writes:
            w = self.lastw.get(r)
            if w is not None:
                deps.add(w)
            for i in self.rd_c.get(r, {}).values():
                deps.add(i)
            for i in self.rd_d.get(r, ()):
                deps.add(i)
        for r in writes:
            self.lastw[r] = idx
            self.rd_c[r] = {}
            self.rd_d[r] = []
        ws = set(writes)
        for r in reads:
            if r in ws:
                continue
            if isdma:
                self.rd_d.setdefault(r, []).append(idx)
            else:
                self.rd_c.setdefault(r, {})[eng] = idx
        if eng in self.pending:
            deps.update(self.pending.pop(eng))
        rec = _Rec()
        fn(rec)
        name_, a_, k_ = rec.call
        o = _Op()
        o.desc = name_ + " w=" + str(list(writes))[:60]
        o.fsz = 0
        try:
            out_ap = k_.get("out", a_[0] if a_ else None)
            shp = tuple(out_ap.shape)
            n = 1
            for d_ in shp[1:]:
                n *= int(d_)
            o.fsz = n
        except Exception:
            o.fsz = 0
        o.eng, o.fn, o.deps, o.isdma = eng, (lambda e: getattr(e, name_)(*a_, **k_)), deps, isdma
        o.ms = None
        o.dsem = None
        o.dval = None
        o.needed = False
        self.ops.append(o)
        self.last_op[eng] = idx
        return idx

    def barrier(self):
        prev = set(self.last_op.values())
        prev.update(i for i in range(self.bar_from, len(self.ops)) if self.ops[i].isdma)
        self.bar_from = len(self.ops)
        for eng in ("pe", "act", "dve", "pool", "sp"):
            self.pending.setdefault(eng, set()).update(prev)

    def dma(self, q, out, in_, reads=(), writes=(), **kw):
        return self.op(q, lambda e: e.dma_start(out=out, in_=in_, **kw), reads, writes, isdma=True)

    def finish(self):
        nc = self.nc
        ops = self.ops
        last_dma = [i for i, o in enumerate(ops) if o.isdma]
        for i, o in enumerate(ops):
            best = {}
            nd = set()
            for d in o.deps:
                od = ops[d]
                if od.isdma:
                    nd.add(d)
                else:
                    if od.eng == o.eng and not o.isdma:
                        if od.eng == "pe" or not SAME_ENGINE_SYNC:
                            continue
                        if RELAX_FSZ and od.fsz >= RELAX_FSZ and od.eng in ("dve", "act"):
                            continue
                    if best.get(od.eng, -1) < d:
                        best[od.eng] = d
            nd.update(best.values())
            o.deps = nd
            for d in nd:
                ops[d].needed = True
        for i in last_dma:
            ops[i].needed = True
        tail_ops = []
        for en in ("pe", "act", "dve", "pool"):
            idxs = [i for i, o in enumerate(ops) if o.eng == en and not o.isdma]
            if idxs:
                ops[idxs[-1]].needed = True
                tail_ops.append(idxs[-1])
        cnt = {e: 0 for e in ("pe", "act", "dve", "pool", "sp")}
        dcount = {}
        nd_used = {"sp": 0, "pool": 0}
        for o in ops:
            if o.isdma:
                k = nd_used[o.eng] % N_DMA_SEMS[o.eng]
                nd_used[o.eng] += 1
                key = (o.eng, k)
                dcount[key] = dcount.get(key, 0) + 1
                o.dsem = key
                o.dval = 16 * dcount[key]
            elif o.needed:
                o.ms = cnt[o.eng]
                cnt[o.eng] += 1
        sems = {}
        for e in cnt:
            for ep in range(cnt[e] // EPOCH + 1):
                sems[(e, ep)] = self.stack.enter_context(nc.semaphore("m_%s_%d" % (e, ep)))
        dsems = {}
        for key in dcount:
            dsems[key] = self.stack.enter_context(nc.semaphore("d_%s_%d" % key))
        final_dma = {}
        for i in last_dma:
            final_dma[ops[i].dsem] = max(final_dma.get(ops[i].dsem, 0), ops[i].dval)
        by_eng = {e: [] for e in cnt}
        for o in ops:
            by_eng[o.eng].append(o)
        import os as _os
        dump = _os.environ.get("DUMP")

        def emit(ename, e):
            known = {}
            for o in by_eng[ename]:
                if o.isdma and o.dval > 16:
                    k = ("d",) + o.dsem
                    if known.get(k, 0) < o.dval - 16:
                        e.wait_ge(dsems[o.dsem], o.dval - 16)
                        known[k] = o.dval - 16
                for d in sorted(o.deps):
                    od = ops[d]
                    if od.isdma:
                        k = ("d",) + od.dsem
                        if known.get(k, 0) < od.dval:
                            e.wait_ge(dsems[od.dsem], od.dval)
                            known[k] = od.dval
                            if dump: print("   ", ename, "WAITD", od.dsem, od.dval)
                    else:
                        k = ("m", od.eng)
                        if known.get(k, -1) < od.ms:
                            ep = od.ms // EPOCH
                            e.wait_ge(sems[(od.eng, ep)], od.ms % EPOCH + 1)
                            known[k] = od.ms
                            if dump: print("   ", ename, "WAITM", od.eng, od.ms + 1)
                ins = o.fn(e)
                if dump: print(ename, "OP", o.desc, "ms", o.ms)
                if o.isdma:
                    ins.then_inc(dsems[o.dsem], 16)
                elif o.ms is not None:
                    ins.then_inc(sems[(o.eng, o.ms // EPOCH)], 1)
            if ename == "sp":
                for key, v in final_dma.items():
                    e.wait_ge(dsems[key], v)
                for i in tail_ops:
                    od = ops[i]
                    e.wait_ge(sems[(od.eng, od.ms // EPOCH)], od.ms % EPOCH + 1)

        with nc.Block() as block:
            @block.tensor
            def _(e):
                emit("pe", e)

            @block.scalar
            def _(e):
                emit("act", e)

            @block.vector
            def _(e):
                emit("dve", e)

            @block.gpsimd
            def _(e):
                emit("pool", e)

            @block.sync
            def _(e):
                emit("sp", e)
        self.stack.close()

import os
STOP = os.environ.get('KSTOP', '')
SKIP = os.environ.get('KSKIP', '').split(',')
NTP = 256
NSEQ = int(os.environ.get('KNSEQ', '2'))
KAPPA = 0.6065306597126334
NEGB = -30000.0
RW_DT = F32
C_ID, C_BO, C_BO64, C_ONE, C_M64, C_L64, C_M8, C_L8, C_BSEL, C_END = 0, 128, 256, 384, 512, 768, 896, 1152, 1280, 1296


def host_consts():
    c = np.zeros((128, C_END), np.float32)
    c[:, C_ID:C_ID + 128] = np.eye(128)
    blk = (np.arange(128)[:, None] // 64 == np.arange(128)[None] // 64)
    c[:, C_BO:C_BO + 128] = blk
    c[:, C_BO64:C_BO64 + 128] = blk / 64.0
    c[:, C_ONE:C_ONE + 128] = 1.0
    s = np.arange(128)[:, None]
    t = np.arange(128)[None]
    for (C, cm, cl) in ((64, C_M64, C_L64), (8, C_M8, C_L8)):
        same = (s // C == t // C)
        c[:, cm:cm + 128] = same & (s < t)
        c[:, cm + 128:cm + 256] = same & (s <= t)
        c[:, cl:cl + 128] = same & (s > t)
    c[:, C_BSEL:C_BSEL + 16] = (np.arange(128)[:, None] // 8 == np.arange(16)[None])
    def bucket(d):
        d = np.asarray(d)
        n = np.maximum(d, 0)
        nf = np.maximum(n, 1).astype(np.float32)
        large = 16 + (np.log(nf / np.float32(16)) / np.float32(np.log(128 / 16)) * np.float32(16)).astype(np.int32)
        return np.where(n < 16, n, np.minimum(large, 31))
    oh = np.zeros((33, 2, 384), np.float32)
    m = np.arange(384)
    for x in range(2):
        dist = (m - 128) if x == 0 else m
        valid = (dist >= 0) & (dist < 128) if x == 0 else (m >= 1) & (m < 128)
        b = bucket(np.clip(dist, 0, 255))
        for mm_ in range(384):
            if valid[mm_]:
                oh[b[mm_], x, mm_] = 1.0
            else:
                oh[32, x, mm_] = NEGB
    return c, oh.reshape(33, 768)


class _Stop(Exception):
    pass


def stop_at(tag):
    if STOP == tag:
        raise _Stop()


def build(nc, n_ptiles=16, do_sample=True, dbg=(), n_stiles=None):
    if n_stiles is None:
        n_stiles = 1 if do_sample else 0
    do_sample = n_stiles > 0
    NST = max(n_stiles, 1)
    P = Prog(nc)
    D = {}

    def din(name, shape):
        D[name] = nc.dram_tensor(name, list(shape), F32, kind="ExternalInput").ap()
        return D[name]

    def dout(name, shape):
        D[name] = nc.dram_tensor(name, list(shape), F32, kind="ExternalOutput").ap()
        return D[name]

    xp = din("xp", [NSEQ, 2048, 1024]); xsm = din("xsm", [NST * 128, 1024])
    ck = din("ck", [NST * 16, 128, 128]); cv = din("cv", [NST * 16, 128, 128])
    sshift = din("sshift", [NST * 16, 1792]); swkv = din("swkv", [NST * 16, 8, 64, 64]); sconv = din("sconv", [NST * 32, 2816])
    rel_bias = din("rel_bias", [32, 8]); norm1_g = din("norm1_g", [1, 1024]); w_in = din("w_in", [1024, 4608])
    sinks = din("sinks", [8]); mu_shift = din("mu_shift", [1792]); w0 = din("w0", [512]); w2 = din("w2", [64, 512])
    a0 = din("a0", [512]); a2 = din("a2", [64, 512]); g2 = din("g2", [128, 512]); k_k = din("k_k", [512])
    k_a = din("k_a", [512]); r_k = din("r_k", [512]); lnx_g = din("lnx_g", [512]); lnx_b = din("lnx_b", [512])
    w_pa = din("w_pa", [512, 1024]); w_pb = din("w_pb", [512, 1024]); w_o = din("w_o", [1024, 1024])
    norm2_g = din("norm2_g", [1, 1024]); w_up = din("w_up", [1024, 5632]); conv_w = din("conv_w", [3, 2816])
    conv_b = din("conv_b", [2816]); w_down = din("w_down", [2816, 1024]); final_g = din("final_g", [1, 1024])
    consts_d = din("consts", [128, C_END]); oh_d = din("oh", [33, 768])

    yp = dout("yp", [NSEQ, 2048, 1024]); ys = dout("ys", [NST * 128, 1024])
    pk = dout("pk", [NSEQ, 128, 128]); pv = dout("pv", [NSEQ, 128, 128]); psh = dout("psh", [NSEQ, 1792])
    pwkv = dout("pwkv", [NSEQ, 8, 64, 64]); pconv = dout("pconv", [NSEQ, 2, 2816])
    sk = dout("sk", [NST * 16, 128, 128]); sv = dout("sv", [NST * 16, 128, 128]); ssh = dout("ssh", [NST * 16, 1792])
    swkvo = dout("swkvo", [NST * 16, 8, 64, 64]); sconvo = dout("sconvo", [NST * 32, 2816])
    dbg_out = {}

    def scratch(name, shape, dtype=BF16):
        return nc.dram_tensor(name, list(shape), dtype, kind="Internal").ap()

    wsc_in = scratch("wsc_in", [9, 128, 4096]); wsc_pa = scratch("wsc_pa", [2, 128, 2048])
    wsc_pb = scratch("wsc_pb", [2, 128, 2048]); wsc_o = scratch("wsc_o", [2, 128, 4096])
    wsc_up = scratch("wsc_up", [11, 128, 4096]); wsc_dn = scratch("wsc_dn", [6, 128, 4096])
    E_d = scratch("E_d", [16, 128, 384], F32)

    chunks_src = []
    for j in range(4):
        chunks_src.append([(j * 64, 64), ((4 + j) * 64, 64)])
    chunks_src.append([(512, 128)]); chunks_src.append([(640, 128)])
    rwb = 768
    chunks_src.append([(rwb + 1536, 128)]); chunks_src.append([(rwb + 1664, 128)])
    for p in range(4):
        chunks_src += [[(rwb + p * 128, 128)], [(rwb + 512 + p * 128, 128)], [(rwb + 1024 + p * 128, 128)]]
    for g in range(16):
        chunks_src.append([(2560 + g * 128, 128)])
    for c, srcs in enumerate(chunks_src):
        b, cc = divmod(c, 4)
        dst = wsc_in[b].rearrange("p (k n) -> p k n", n=512)
        off = cc * 128
        for (lo, n) in srcs:
            P.dma("pool", dst[:, :, off:off + n], w_in[:, lo:lo + n].rearrange("(k p) n -> p k n", p=128),
                  writes=[("wsc_in", c, lo)])
            off += n
    for ch in range(2):
        dpa = wsc_pa[ch].rearrange("p (k n) -> p k n", n=512)
        for j in range(4):
            for half in range(2):
                r0 = (half * 4 + j) * 64
                P.dma("pool", dpa[half * 64:(half + 1) * 64, j, :], w_pa[r0:r0 + 64, ch * 512:(ch + 1) * 512],
                      writes=[("wsc_pa", ch, j, half)])
        P.dma("pool", wsc_pb[ch].rearrange("p (k n) -> p k n", n=512),
              w_pb[:, ch * 512:(ch + 1) * 512].rearrange("(k p) n -> p k n", p=128), writes=[("wsc_pb", ch)])
        P.dma("pool", wsc_o[ch].rearrange("p (k n) -> p k n", n=512),
              w_o[:, ch * 512:(ch + 1) * 512].rearrange("(k p) n -> p k n", p=128), writes=[("wsc_o", ch)])
    for b in range(11):
        dst = wsc_up[b].rearrange("p (k n) -> p k n", n=512)
        P.dma("pool", dst[:, :, 0:256], w_up[:, b * 256:(b + 1) * 256].rearrange("(k p) n -> p k n", p=128),
              writes=[("wsc_up", b, 0)])
        P.dma("pool", dst[:, :, 256:512], w_up[:, 2816 + b * 256:2816 + (b + 1) * 256].rearrange("(k p) n -> p k n", p=128),
              writes=[("wsc_up", b, 1)])
    DN_NK = (8, 8, 6)
    for ch in range(2):
        for rg in range(3):
            nk = DN_NK[rg]
            dst = wsc_dn[ch * 3 + rg].rearrange("p (k n) -> p k n", n=512)
            P.dma("pool", dst[:, 0:nk, :],
                  w_down[rg * 1024:rg * 1024 + nk * 128, ch * 512:(ch + 1) * 512].rearrange("(k p) n -> p k n", p=128),
                  writes=[("wsc_dn", ch * 3 + rg)])

    if STOP == 'A':
        P.finish(); return D
    blk_sched = []
    for b in range(9):
        keys = []
        for c in range(4 * b, 4 * b + 4):
            keys += [("wsc_in", c, lo) for (lo, n) in chunks_src[c]]
        blk_sched.append((wsc_in[b], 4096, keys))
    for ch in range(2):
        blk_sched.append((wsc_pa[ch], 2048, [("wsc_pa", ch, j, h) for j in range(4) for h in range(2)]))
        blk_sched.append((wsc_pb[ch], 2048, [("wsc_pb", ch)]))
    for ch in range(2):
        blk_sched.append((wsc_o[ch], 4096, [("wsc_o", ch)]))
    for b in range(11):
        blk_sched.append((wsc_up[b], 4096, [("wsc_up", b, 0), ("wsc_up", b, 1)]))
    for i in range(6):
        blk_sched.append((wsc_dn[i], DN_NK[i % 3] * 512, [("wsc_dn", i)]))
    NBLK_T = len(blk_sched)
    n_tiles_total = n_ptiles + n_stiles
    NSLOT = 3
    ring_tiles = [P.sbuf("ring%d" % i, [128, 4096], BF16) for i in range(NSLOT)]
    ring_state = {"loaded": 0, "consumed": 0}
    total_blocks = NBLK_T * n_tiles_total

    def ring_get(hold=0):
        k = ring_state["consumed"]
        while ring_state["loaded"] < min(k + NSLOT - hold, total_blocks):
            j = ring_state["loaded"]
            src, nel, keys = blk_sched[j % NBLK_T]
            s = j % NSLOT
            P.dma("sp", ring_tiles[s][:, 0:nel], src[:, 0:nel], reads=keys, writes=[("ring", s)])
            ring_state["loaded"] += 1
        ring_state["consumed"] += 1
        s = k % NSLOT
        return ring_tiles[s], ("ring", s)

    cst = P.sbuf("cst", [128, C_END], F32)
    P.dma("sp", cst[:], consts_d, writes=["cst"])
    ident = cst[:, C_ID:C_ID + 128]
    bo = cst[:, C_BO:C_BO + 128]
    bo64 = cst[:, C_BO64:C_BO64 + 128]
    ones = cst[:, C_ONE:C_ONE + 128]
    ones_bf = P.sbuf("ones_bf", [128, 128], BF16)
    P.op("dve", lambda e: e.tensor_copy(ones_bf[:], ones), reads=["cst"], writes=["ones_bf"])
    zeros_t = P.sbuf("zeros_t", [128, 64], F32)
    P.op("pool", lambda e: e.memset(zeros_t[:], 0.0), writes=["zeros_t"])

    def load_cols(name, vec, ncol):
        t = P.sbuf(name, [128, ncol], F32)
        P.dma("sp", t[:], vec.rearrange("(c p) -> p c", p=128), writes=[name], allow_slow_non_contiguous=True)
        return t

    mu_c = load_cols("mu_c", mu_shift, 14)
    om_c = P.sbuf("om_c", [128, 14], F32)
    P.op("dve", lambda e: e.tensor_scalar(om_c[:], mu_c[:], -1.0, 1.0, ALU.mult, ALU.add), reads=["mu_c"], writes=["om_c"])
    w0_c = load_cols("w0_c", w0, 4); a0_c = load_cols("a0_c", a0, 4); kk_c = load_cols("kk_c", k_k, 4)
    ka_c = load_cols("ka_c", k_a, 4); rk_c = load_cols("rk_c", r_k, 4); lg_c = load_cols("lg_c", lnx_g, 4)
    lb_c = load_cols("lb_c", lnx_b, 4); cb_c = load_cols("cb_c", conv_b, 22)
    cw_c = P.sbuf("cw_c", [128, 3, 22], F32)
    for j in range(3):
        P.dma("sp", cw_c[:, j, :], conv_w[j].rearrange("(c p) -> p c", p=128), writes=[("cw_c", j)], allow_slow_non_contiguous=True)
    CW_KEYS = [("cw_c", j) for j in range(3)]
    gb = {}
    g1c = load_cols("g1b", norm1_g.rearrange("a n -> (a n)"), 8)
    g2c = load_cols("g2b", norm2_g.rearrange("a n -> (a n)"), 8)
    gcol = {"g1b": g1c, "g2b": g2c}
    for nm, src in (("gfb", final_g),):
        gb[nm] = P.sbuf(nm, [128, 1024], F32)
        P.dma("sp", gb[nm][:], src.partition_broadcast(128).rearrange("p a n -> p (a n)"), writes=[nm])
    w2b = P.sbuf("w2b", [128, 512], BF16); g2bf = P.sbuf("g2bf", [128, 512], BF16)
    P.dma("pool", w2b[0:64, :], w2, writes=["w2b"])
    P.dma("pool", w2b[64:128, :], a2, writes=["a2b"])
    P.dma("pool", g2bf[:], g2, writes=["g2bf"])
    sk_t = P.sbuf("sk_t", [128, 4], F32)
    P.dma("sp", sk_t[0:64, :], sinks[0:4].partition_broadcast(64), writes=[("sk_t", 0)])
    P.dma("sp", sk_t[64:128, :], sinks[4:8].partition_broadcast(64), writes=[("sk_t", 1)])
    esk = P.sbuf("esk", [128, 4], F32)
    P.op("act", lambda e: e.activation(esk[:], sk_t[:], AF.Exp), reads=[("sk_t", 0), ("sk_t", 1)], writes=["esk"])
    esink = P.sbuf("esink", [128, 4, 128], F32)
    for g in range(4):
        P.op("act", lambda e, g=g: e.activation(esink[:, g, :], ones, AF.Copy, scale=esk[:, g:g + 1]),
             reads=["cst", "esk"], writes=[("esink", g)])
    ESINK_KEYS = [("esink", g) for g in range(4)]

    xn = P.sbuf("xn", [128, 1024], F32)
    if STOP == 'B':
        P.finish(); return D
    for _i in range(int(os.environ.get('KDUMMY', '0'))):
        P.dma('sp', zeros_t[:, 0:32], consts_d[:, 0:32], writes=['zeros_dummy'])
    for _i in range(int(os.environ.get('KBIG', '0'))):
        P.dma('sp', yp[1], xp[0], writes=['yp1_dummy'])
    BK = [P.psum("bank%d" % i, [128, 512], F32) for i in range(8)]

    def bk(i):
        return ("ps", i)

    rb = P.sbuf("rb", [33, 8], F32)
    Yt = P.sbuf("Yt", [128, 4, NTP], F32)
    bon = P.sbuf("bon", [128, 4, NTP], F32)
    Lh = Yt[0:33, :, :].rearrange("p a t -> p (a t)").rearrange("p (h t) -> p h t", t=128)
    oh_t = bon[0:33, :, :].rearrange("p a t -> p (a t)")[:, 0:768]
    P.dma("sp", rb[0:32, :], rel_bias, writes=["rb"])
    P.dma("sp", oh_t, oh_d, writes=["oh_t"])
    P.op("pool", lambda e: e.memset(Lh[32:33, :, :], 1.0), writes=["Lh1"])
    for h in range(8):
        P.op("act", lambda e, h=h: e.activation(Lh[0:32, h, :], cst[0:32, C_ONE:C_ONE + 128], AF.Copy, scale=rb[0:32, h:h + 1]),
             reads=["cst", "rb"], writes=[("Lh", h)])
    biasT = [[P.sbuf("biasT%d%d" % (x, kv), [128, 4, 128], F32) for kv in range(2)] for x in range(2)]
    for x in range(2):
        for h in range(8):
            bnk = 4 + (x * 8 + h) % 4
            P.op("pe", lambda e, h=h, x=x, bnk=bnk: e.matmul(BK[bnk][:, 0:384], Lh[:, h, :], oh_t[:, x * 384:(x + 1) * 384], start=True, stop=True),
                 reads=["Lh1", ("Lh", h), "oh_t"], writes=[bk(bnk)])
            P.op("dve", lambda e, bnk=bnk: e.tensor_copy(xn[:, 0:384], BK[bnk][:, 0:384]), reads=[bk(bnk)], writes=["xn"])
            P.dma("sp", E_d[x * 8 + h], xn[:, 0:384], reads=["xn"], writes=[("E_d", x, h)])
            kv, g = divmod(h, 4)
            skew = bass.AP(E_d.tensor, (x * 8 + h) * 128 * 384 + 128, [[383, 128], [1, 128]])
            P.dma("sp", biasT[x][kv][:, g, :], skew, reads=[("E_d", x, h)], writes=[("biasT", x, kv, g)])
    if STOP == 'C':
        P.finish(); return D
    if os.environ.get('NOBAR') is None:
        P.barrier()
    BIAS_KEYS = {(x, kv): [("biasT", x, kv, g) for g in range(4)] for x in range(2) for kv in range(2)}

    wstack = [ExitStack()]

    def wsb(name, shape, dtype):
        return wstack[0].enter_context(nc.sbuf_tensor(name, list(shape), dtype))

    xn2 = None
    Pvg = None
    xt = None
    stat = None
    hT = None
    qT = None
    kT = None
    kTf = None
    vTf = None
    vtok = None
    kvrow = None
    gates = None
    attT = None
    rwoT = None
    PT = None
    rd_t = None
    pbuf = None
    tmpA = None
    xs3 = None
    TW = None
    SG = None
    car = None
    art = None
    kt_ = None
    bt_ = None
    gT = None
    cCt = None
    KPtok = None
    BPtok = None
    Vtok = None
    rw = None
    S = None
    AK = None
    AB = None
    Mm_ = None
    Mt_ = None
    Pv = None
    VA = None
    VK = None
    XT = None
    UT = None
    y1 = None
    ugx = None
    cc_t = None
    gl_t = None
    actT = None
    ccar = None

    def alloc_work(NTM, sfx):
        nonlocal xn2, Pvg, xt, stat, hT, qT, kT, kTf, vTf, vtok, kvrow, gates, attT, rwoT, PT, rd_t, pbuf, tmpA, xs3, TW, SG, car, art, kt_, bt_, gT, cCt, KPtok, BPtok, Vtok, rw, S, AK, AB, Mm_, Mt_, Pv, VA, VK, XT, UT, y1, ugx, cc_t, gl_t, actT, ccar
        NB_MAX = NTM // 128
        xt = [wsb(("xt%d" % i) + sfx, [128, 1024], F32) for i in range(NB_MAX)]
        stat = wsb("stat" + sfx, [128, 32], F32)
        xn2 = wsb("xn2" + sfx, [128, 1024], F32) if NTM > 128 else None
        hT = wsb("hT" + sfx, [128, 8, NTM], BF16)
        qT = wsb("qT" + sfx, [128, 4, NTM], BF16)
        kT = wsb("kT" + sfx, [128, 128 + NTM], BF16)
        kTf = wsb("kTf" + sfx, [128, NTM], F32)
        vTf = wsb("vTf" + sfx, [128, NTM], F32)
        vtok = wsb("vtok" + sfx, [128, NB_MAX + 1, 128], BF16)
        kvrow = wsb("kvrow" + sfx, [128, 2, 128], F32)
        gates = wsb("gates" + sfx, [128, 16, NTM], BF16)
        attT = wsb("attT" + sfx, [128, 4, NTM], BF16)
        rwoT = wsb("rwoT" + sfx, [128, 4, NTM], BF16)
        PT = [wsb(("PT%d" % i) + sfx, [128, 512], BF16) for i in range(2)]
        rd_t = wsb("rd_t" + sfx, [128, 512], F32)
        pbuf = wsb("pbuf" + sfx, [128, NTM + 16], F32)
        tmpA = wsb("tmpA" + sfx, [128, NTM], F32)
        xs3 = [wsb(("xs3_%d" % i) + sfx, [128, NTM], F32) for i in range(3)]
        TW = wsb("TW" + sfx, [128, NTM], BF16)
        SG = wsb("SG" + sfx, [128, NTM], BF16)
        car = wsb("car" + sfx, [128, 128], F32)
        art = wsb("art" + sfx, [128, 4, 2, NTM], RW_DT)
        kt_ = wsb("kt_" + sfx, [128, 4, NTM], RW_DT)
        bt_ = wsb("bt_" + sfx, [128, 4, NTM], RW_DT)
        gT = wsb("gT" + sfx, [128, 4, NTM], F32)
        cCt = wsb("cCt" + sfx, [128, 4, NTM // 8], F32)
        KPtok = wsb("KPtok" + sfx, [128, NB_MAX, 512], RW_DT)
        BPtok = wsb("BPtok" + sfx, [128, NB_MAX, 512], RW_DT)
        Vtok = wsb("Vtok" + sfx, [128, NB_MAX, 512], RW_DT)
        rw = {n: wsb("rw_" + sfx + n, [128, NTM], F32) for n in ("sg", "cs", "t1", "t2", "t3", "a", "kkn", "k2", "ein", "einv", "eend", "nk")}
        S = wsb("S" + sfx, [128, 4, 64], F32)
        AK = [wsb(("AK%d" % h) + sfx, [128, 256], RW_DT) for h in range(8)]
        AB = [wsb(("AB%d" % h) + sfx, [128, 256], RW_DT) for h in range(8)]
        Mm_ = [wsb(("Mmg%d" % g) + sfx, [128, 4, 128], RW_DT) for g in range(2)]
        Mt_ = [wsb(("Mtg%d" % g) + sfx, [128, 4, 128], RW_DT) for g in range(2)]
        Pvg = [wsb(("Pvg%d" % g) + sfx, [128, 4, 128], RW_DT) for g in range(2)]
        Pv = [Pvg[h // 4][:, h % 4, :] for h in range(8)]
        VA = wsb("VA" + sfx, [128, 512], F32)
        VK = wsb("VK" + sfx, [128, 4, 128], F32)
        XT = wsb("XT" + sfx, [128, 512], RW_DT)
        UT = wsb("UT" + sfx, [128, 512], RW_DT)
        y1 = wsb("y1" + sfx, [128, 4, 64], F32)
        ugx = [wsb(("ugx%d" % i) + sfx, [128, NTM + 40], F32) for i in range(2)]
        cc_t = [wsb(("cc_t%d" % i) + sfx, [128, NTM], F32) for i in range(2)]
        gl_t = [wsb(("gl_t%d" % i) + sfx, [128, NTM], F32) for i in range(2)]
        actT = wsb("actT" + sfx, [128, 22, NTM], BF16)
        ccar = wsb("ccar" + sfx, [128, 2, 128], F32)

    P.stack.callback(lambda: wstack[0].close())
    alloc_work(NTP, "")
    P.op("pool", lambda e: e.memset(car[:], 0.0), writes=["car_init"])
    P.op("pool", lambda e: e.memset(ccar[:].rearrange("p a f -> p (a f)"), 0.0), writes=["ccar_init"])

    def debug_tap(name, ap_sb, shape, keys):
        if name in dbg:
            d_ = nc.dram_tensor("dbg_" + name, list(shape), ap_sb.dtype, kind="ExternalOutput").ap()
            D["dbg_" + name] = d_
            P.dma("sp", d_, ap_sb, reads=keys, writes=[("dbg", name)])

    def do_tile(kind, seq, ti):
        prompt = kind == "p"
        NT = NTP if prompt else 128
        nb = NT // 128
        NBs, TB = (1, NT) if prompt else (16, 8)
        C = 64 if prompt else 8
        LV = 5 if prompt else 2
        cm, cl = (C_M64, C_L64) if prompt else (C_M8, C_L8)
        first = prompt and ti == 0
        last = prompt and ti == 2048 // NTP - 1
        xrows = xp[seq, ti * NT:(ti + 1) * NT, :] if prompt else xsm[seq * 128:(seq + 1) * 128, :]
        yrows = yp[seq, ti * NT:(ti + 1) * NT, :] if prompt else ys[seq * 128:(seq + 1) * 128, :]

        for b in range(nb):
            P.dma("sp", xt[b][:], xrows[b * 128:(b + 1) * 128, :], writes=[("xt", b)])

        stop_at("D1")

        def norm_T(gname):
            def steps(b):
                so = 16 * b
                st = stat[:, so:so + 16]
                xnb = xn if b == 0 else xn2
                xk_ = "xn" if b == 0 else "xn2"
                sk_ = "stat%d_" % b
                bks = (4, 5) if b == 0 else (6, 7)

                def tr(half):
                    for q4 in range(4):
                        kc = half * 4 + q4
                        P.op("pe", lambda e, kc=kc, q4=q4: e.transpose(BK[bks[half]][:, q4 * 128:(q4 + 1) * 128], xnb[:, kc * 128:(kc + 1) * 128], ident),
                             reads=[xk_, "cst"], writes=[bk(bks[half])])

                def ev(half):
                    for q4 in range(4):
                        kc = half * 4 + q4
                        if half == 0:
                            P.op("act", lambda e, kc=kc, q4=q4: e.activation(hT[:, kc, b * 128:(b + 1) * 128], BK[bks[0]][:, q4 * 128:(q4 + 1) * 128], AF.Copy, scale=gcol[gname][:, kc:kc + 1]),
                                 reads=[bk(bks[0]), gname], writes=[("hT", kc)])
                        else:
                            P.op("dve", lambda e, kc=kc, q4=q4: e.tensor_scalar(hT[:, kc, b * 128:(b + 1) * 128], BK[bks[1]][:, q4 * 128:(q4 + 1) * 128], gcol[gname][:, kc:kc + 1], None, ALU.mult),
                                 reads=[bk(bks[1]), gname], writes=[("hT", kc)])
                return [
                    lambda: P.op("dve", lambda e: e.bn_stats(st[:, 0:6], xt[b][:, 0:512]), reads=[("xt", b)], writes=[sk_ + "0"]),
                    lambda: P.op("dve", lambda e: e.bn_stats(st[:, 6:12], xt[b][:, 512:1024]), reads=[("xt", b)], writes=[sk_ + "1"]),
                    lambda: P.op("dve", lambda e: e.bn_aggr(st[:, 12:14], st[:, 0:12]), reads=[sk_ + "0", sk_ + "1"], writes=[sk_ + "2"]),
                    lambda: P.op("dve", lambda e: e.scalar_tensor_tensor(st[:, 14:15], st[:, 12:13], st[:, 12:13], st[:, 13:14], ALU.mult, ALU.add), reads=[sk_ + "2"], writes=[sk_ + "3"]),
                    lambda: P.op("act", lambda e: e.activation(st[:, 15:16], st[:, 14:15], AF.Sqrt, bias=1e-6), reads=[sk_ + "3"], writes=[sk_ + "4"]),
                    lambda: P.op("dve", lambda e: e.reciprocal(st[:, 15:16], st[:, 15:16]), reads=[sk_ + "4"], writes=[sk_ + "4"]),
                    lambda: P.op("dve", lambda e: e.tensor_scalar(xnb[:], xt[b][:], st[:, 15:16], None, ALU.mult), reads=[("xt", b), sk_ + "4"], writes=[xk_]),
                    lambda: tr(0), lambda: tr(1), lambda: ev(0), lambda: ev(1),
                ]
            for group in zip(*[steps(b) for b in range(nb)]):
                for step in group:
                    step()
        HT_KEYS = [("hT", kc) for kc in range(8)]

        norm_T("g1b")
        stop_at("D")

        def v3(ap2, k):
            return ap2.rearrange("p (b t) -> p b t", t=k)

        def token_shift(bnk, n, dst, dst_key):
            pb3 = v3(pbuf[:, 0:NBs * (TB + 1)], TB + 1)
            P.op("act", lambda e: e.copy(pb3[:, :, 1:TB + 1], v3(BK[bnk][:, 0:NT], TB)), reads=[bk(bnk)], writes=["pbuf"])
            if prompt:
                if first:
                    P.op("pool", lambda e: e.memset(pbuf[:, 0:1], 0.0), writes=["pbuf0"])
                else:
                    P.op("pool", lambda e: e.tensor_copy(pbuf[:, 0:1], car[:, n:n + 1]), reads=[("car", n)], writes=["pbuf0"])
                P.op("pool", lambda e: e.tensor_copy(car[:, n:n + 1], pbuf[:, NT:NT + 1]), reads=["pbuf"], writes=[("car", n)])
            else:
                P.op("pool", lambda e: e.tensor_copy(pb3[:, :, 0:1], SB["shcar"][:, n, :].unsqueeze(2)), reads=[("shcar", n)], writes=["pbuf0"])
                P.op("pool", lambda e: e.tensor_copy(SB["shout"][:, n, :].unsqueeze(2), pb3[:, :, TB:TB + 1]), reads=["pbuf"], writes=[("shout", n)])
            P.op("pool", lambda e: e.tensor_scalar(v3(tmpA[:, 0:NT], TB), pb3[:, :, 0:TB], mu_c[:, n:n + 1], None, ALU.mult),
                 reads=["pbuf", "pbuf0", "mu_c"], writes=["tmpA"])
            P.op("dve", lambda e: e.scalar_tensor_tensor(v3(dst, TB), pb3[:, :, 1:TB + 1], om_c[:, n:n + 1], v3(tmpA[:, 0:NT], TB), ALU.mult, ALU.add),
                 reads=["pbuf", "tmpA", "om_c"], writes=[dst_key])

        def pair_process(p):
            xr, xk, xv = xs3[0][:, 0:NT], xs3[1][:, 0:NT], xs3[2][:, 0:NT]
            R = {n: rw[n][:, 0:NT] for n in rw}
            nch = NT // C
            b6, b7 = 6, 7
            P.op("pool", lambda e: e.tensor_scalar(R["kkn"], xk, kk_c[:, p:p + 1], None, ALU.mult), reads=["xs1", "kk_c"], writes=["r_kkn"])
            P.op("pool", lambda e: e.tensor_tensor(R["t2"], R["kkn"], R["kkn"], ALU.mult), reads=["r_kkn"], writes=["r_t2"])
            P.op("pe", lambda e: e.matmul(BK[b6][:, 0:NT], w2b[0:64, p * 128:(p + 1) * 128], TW[0:64, 0:NT], start=True, stop=True),
                 reads=["w2b", "TW"], writes=[bk(b6)])
            P.op("pe", lambda e: e.matmul(BK[b7][:, 0:NT], w2b[64:128, p * 128:(p + 1) * 128], TW[64:128, 0:NT], start=True, stop=True),
                 reads=["a2b", "TW"], writes=[bk(b7)])
            P.op("act", lambda e: e.activation(R["sg"], BK[b6][:, 0:NT], AF.Sigmoid, bias=w0_c[:, p:p + 1]), reads=[bk(b6), "w0_c"], writes=["r_sg"])
            P.op("act", lambda e: e.activation(R["a"], BK[b7][:, 0:NT], AF.Sigmoid, bias=a0_c[:, p:p + 1]), reads=[bk(b7), "a0_c"], writes=["r_a"])
            P.op("pe", lambda e: e.matmul(BK[b6][:, 0:NT], bo, R["t2"], start=True, stop=True), reads=["cst", "r_t2"], writes=[bk(b6)])
            for b in range(nb):
                tb_ = 4 + b % 2
                P.op("pe", lambda e, b=b, tb_=tb_: e.transpose(BK[tb_][:, 0:128], xs3[2][:, b * 128:(b + 1) * 128], ident), reads=["xs2", "cst"], writes=[bk(tb_)])
                P.op("act", lambda e, b=b, tb_=tb_: e.copy(Vtok[:, b, p * 128:(p + 1) * 128], BK[tb_][:, 0:128]), reads=[bk(tb_)], writes=[("Vtok", b, p)])
            for c in range(nch):
                P.op("dve", lambda e, c=c: e.tensor_tensor_scan(R["cs"][:, c * C:(c + 1) * C], ones[:, 0:C], R["sg"][:, c * C:(c + 1) * C], 0.0, ALU.mult, ALU.add),
                     reads=["r_sg", "cst"], writes=["r_cs"])
            P.op("act", lambda e: e.activation(R["t2"], BK[b6][:, 0:NT], AF.Sqrt), reads=[bk(b6)], writes=["r_t2"])
            P.op("dve", lambda e: e.tensor_scalar(R["t3"], R["a"], -1.0, ka_c[:, p:p + 1], ALU.add, ALU.mult), reads=["r_a", "ka_c"], writes=["r_t3"])
            P.op("dve", lambda e: e.scalar_tensor_tensor(R["k2"], R["t3"], 1.0, xk, ALU.add, ALU.mult), reads=["r_t3", "xs1"], writes=["r_k2"])
            P.op("act", lambda e: e.activation(R["ein"], R["cs"], AF.Exp, scale=-KAPPA), reads=["r_cs"], writes=["r_ein"])
            P.op("act", lambda e: e.activation(R["einv"], R["cs"], AF.Exp, scale=KAPPA), reads=["r_cs"], writes=["r_einv"])
            P.op("pool", lambda e: e.tensor_tensor(R["t1"], R["cs"], R["sg"], ALU.subtract), reads=["r_cs", "r_sg"], writes=["r_t1"])
            P.op("pool", lambda e: e.tensor_scalar(R["nk"][:, 0:nch], R["cs"][:, C - 1:NT:C], -KAPPA, None, ALU.mult), reads=["r_cs"], writes=["r_nk"])
            P.op("act", lambda e: e.activation(R["t1"], R["t1"], AF.Exp, scale=-KAPPA), reads=["r_t1"], writes=["r_t1"])
            for c in range(nch):
                P.op("act", lambda e, c=c: e.activation(R["eend"][:, c * C:(c + 1) * C], R["cs"][:, c * C:(c + 1) * C], AF.Exp, scale=KAPPA, bias=R["nk"][:, c:c + 1]),
                     reads=["r_cs", "r_nk"], writes=["r_eend"])
            P.op("dve", lambda e: e.tensor_scalar(R["t2"], R["t2"], 1e-12, None, ALU.max), reads=["r_t2"], writes=["r_t2"])
            P.op("dve", lambda e: e.reciprocal(R["t2"], R["t2"]), reads=["r_t2"], writes=["r_t2"])
            P.op("dve", lambda e: e.tensor_tensor(R["kkn"], R["kkn"], R["t2"], ALU.mult), reads=["r_kkn", "r_t2"], writes=["r_kkn"])
            P.op("pool", lambda e: e.tensor_copy(cCt[:, p, 0:nch], R["ein"][:, C - 1:NT:C]), reads=["r_ein"], writes=[("cCt", p)])
            P.op("pool", lambda e: e.tensor_tensor(art[:, p, 1, 0:NT], xr, R["ein"], ALU.mult), reads=["xs0", "r_ein"], writes=[("art", p)])
            P.op("dve", lambda e: e.tensor_tensor(kt_[:, p, 0:NT], R["k2"], R["einv"], ALU.mult), reads=["r_k2", "r_einv"], writes=[("kt", p)])
            P.op("dve", lambda e: e.scalar_tensor_tensor(art[:, p, 0, 0:NT], R["kkn"], -1.0, R["t1"], ALU.mult, ALU.mult), reads=["r_kkn", "r_t1"], writes=[("art", p)])
            P.op("pool", lambda e: e.tensor_tensor(R["t3"], R["kkn"], R["a"], ALU.mult), reads=["r_kkn", "r_a", "r_k2"], writes=["r_t3"])
            P.op("dve", lambda e: e.tensor_tensor(bt_[:, p, 0:NT], R["t3"], R["einv"], ALU.mult), reads=["r_t3", "r_einv"], writes=[("bt", p)])
            P.op("pool", lambda e: e.tensor_tensor(R["t2"], R["k2"], R["eend"], ALU.mult), reads=["r_k2", "r_eend", "r_kkn"], writes=["r_t2"])
            P.op("dve", lambda e: e.scalar_tensor_tensor(R["t1"], xr, rk_c[:, p:p + 1], R["k2"], ALU.mult, ALU.mult), reads=["xs0", "rk_c", "r_k2", ("art", p)], writes=["r_t1"])
            P.op("pool", lambda e: e.tensor_tensor(R["t3"], R["t3"], R["eend"], ALU.mult), reads=["r_t3", "r_eend", ("bt", p)], writes=["r_t3"])
            for b in range(nb):
                tb_ = 4 + b % 2
                P.op("pe", lambda e, b=b, tb_=tb_: e.transpose(BK[tb_][:, 0:128], R["t2"][:, b * 128:(b + 1) * 128], ident), reads=["r_t2", "cst"], writes=[bk(tb_)])
                P.op("act", lambda e, b=b, tb_=tb_: e.copy(KPtok[:, b, p * 128:(p + 1) * 128], BK[tb_][:, 0:128]), reads=[bk(tb_)], writes=[("KPtok", b, p)])
            P.op("pe", lambda e: e.matmul(BK[b7][:, 0:NT], bo, R["t1"], start=True, stop=True), reads=["cst", "r_t1"], writes=[bk(b7)])
            for b in range(nb):
                tb_ = 4 + b % 2
                P.op("pe", lambda e, b=b, tb_=tb_: e.transpose(BK[tb_][:, 0:128], R["t3"][:, b * 128:(b + 1) * 128], ident), reads=["r_t3", "cst"], writes=[bk(tb_)])
                P.op("act", lambda e, b=b, tb_=tb_: e.copy(BPtok[:, b, p * 128:(p + 1) * 128], BK[tb_][:, 0:128]), reads=[bk(tb_)], writes=[("BPtok", b, p)])
            P.op("dve", lambda e: e.tensor_tensor(bon[:, p, 0:NT], BK[b7][:, 0:NT], xv, ALU.mult), reads=[bk(b7), "xs2"], writes=[("bon", p)])
            P.op("pe", lambda e: e.matmul(BK[b6][:, 0:NT], g2bf[:, p * 128:(p + 1) * 128], SG[:, 0:NT], start=True, stop=True), reads=["g2bf", "SG"], writes=[bk(b6)])
            P.op("act", lambda e: e.copy(gT[:, p, 0:NT], BK[b6][:, 0:NT]), reads=[bk(b6)], writes=[("gT", p)])

        for blkb in range(9):
            slot, skey = ring_get()
            sl3 = slot[:, 0:4096].rearrange("p (k n) -> p k n", n=512)
            for cc in range(4):
                c = blkb * 4 + cc
                bnk = c % 4
                for kc in range(8):
                    P.op("pe", lambda e, kc=kc, cc=cc, bnk=bnk, sl3=sl3: e.matmul(BK[bnk][:, 0:NT], sl3[:, kc, cc * 128:(cc + 1) * 128], hT[:, kc, 0:NT], start=(kc == 0), stop=(kc == 7)),
                         reads=[skey] + HT_KEYS, writes=[bk(bnk)])
                stop_at("E%d" % c)
                if c < 4:
                    P.op("act", lambda e, c=c, bnk=bnk: e.copy(qT[:, c, 0:NT], BK[bnk][:, 0:NT]), reads=[bk(bnk)], writes=[("qT", c)])
                elif c == 4:
                    P.op("act", lambda e, bnk=bnk: e.copy(kTf[:, 0:NT], BK[bnk][:, 0:NT]), reads=[bk(bnk)], writes=["kTf"])
                    P.op("pool", lambda e: e.tensor_copy(kT[:, 128:128 + NT], kTf[:, 0:NT]), reads=["kTf"], writes=["kT"])
                elif c == 5:
                    P.op("act", lambda e, bnk=bnk: e.copy(vTf[:, 0:NT], BK[bnk][:, 0:NT]), reads=[bk(bnk)], writes=["vTf"])
                    for b in range(nb):
                        tb_ = 4 + b % 2
                        P.op("pe", lambda e, b=b, tb_=tb_: e.transpose(BK[tb_][:, 0:128], vTf[:, b * 128:(b + 1) * 128], ident), reads=["vTf", "cst"], writes=[bk(tb_)])
                        P.op("dve", lambda e, b=b, tb_=tb_: e.tensor_copy(vtok[:, 1 + b, :], BK[tb_][:, 0:128]), reads=[bk(tb_)], writes=[("vtok", 1 + b)])
                        if (prompt and last and b == nb - 1) or not prompt:
                            P.op("dve", lambda e, tb_=tb_: e.tensor_copy(kvrow[:, 1, :], BK[tb_][:, 0:128]), reads=[bk(tb_)], writes=[("kvrow", 1)])
                elif c == 6:
                    token_shift(bnk, 12, xs3[0][:, 0:NT], "xs0")
                    P.op("act", lambda e: e.activation(TW[0:64, 0:NT], xs3[0][0:64, 0:NT], AF.Tanh), reads=["xs0"], writes=["TW"])
                    P.op("pool", lambda e: e.tensor_copy(TW[64:128, 0:NT], xs3[0][64:128, 0:NT]), reads=["xs0"], writes=["TW"])
                elif c == 7:
                    token_shift(bnk, 13, xs3[0][:, 0:NT], "xs0")
                    P.op("act", lambda e: e.activation(SG[:, 0:NT], xs3[0][:, 0:NT], AF.Sigmoid), reads=["xs0"], writes=["SG"])
                elif c < 20:
                    p, which = divmod(c - 8, 3)
                    token_shift(bnk, which * 4 + p, xs3[which][:, 0:NT], "xs%d" % which)
                    if which == 2:
                        pair_process(p)
                else:
                    gi = c - 20
                    P.op("act", lambda e, gi=gi, bnk=bnk: e.activation(gates[:, gi, 0:NT], BK[bnk][:, 0:NT], AF.Sigmoid), reads=[bk(bnk)], writes=[("gates", gi)])

        stop_at("E")
        debug_tap("qT", qT[:, :, 0:NT], [128, 4, NT], [("qT", c) for c in range(4)])

        if prompt:
            for b in range(nb):
                gbk = ti * nb + b
                for kv in range(2):
                    ph = slice(kv * 64, kv * 64 + 64)
                    qv = qT[ph, :, b * 128:(b + 1) * 128]
                    xs_ = [0] + ([1] if gbk > 0 else [])
                    for x in xs_:
                        kcols = slice(128 + b * 128, 256 + b * 128) if x == 0 else slice(b * 128, 128 + b * 128)
                        bnk = 4 + x
                        P.op("pe", lambda e, kcols=kcols, bnk=bnk, qv=qv: e.matmul(BK[bnk][:], kT[ph, kcols], qv, start=True, stop=True),
                             reads=["kT"] + [("qT", c) for c in range(4)], writes=[bk(bnk)])
                        P.op("dve", lambda e, x=x, bnk=bnk: e.scalar_tensor_tensor(BK[bnk][:], BK[bnk][:], 0.125, biasT[x][kv][:].rearrange("p g q -> p (g q)"), ALU.mult, ALU.add),
                             reads=[bk(bnk)] + BIAS_KEYS[(x, kv)], writes=[bk(bnk)])
                        P.op("act", lambda e, x=x: e.activation(PT[x][:], BK[4 + x][:], AF.Exp), reads=[bk(4 + x)], writes=[("PT", x)])
                    for i, x in enumerate(xs_):
                        vb = 1 + b if x == 0 else b
                        P.op("pe", lambda e, x=x, vb=vb, i=i: e.matmul(BK[6][:], vtok[:, vb, :], PT[x][:], start=(i == 0), stop=(i == len(xs_) - 1)),
                             reads=[("vtok", vb), ("PT", x)], writes=[bk(6)])
                    for i, x in enumerate(xs_):
                        P.op("pe", lambda e, x=x, i=i: e.matmul(BK[7][:], ones_bf[:], PT[x][:], start=(i == 0), stop=(i == len(xs_) - 1)),
                             reads=["ones_bf", ("PT", x)], writes=[bk(7)])
                    P.op("dve", lambda e: e.tensor_tensor(rd_t[ph, :], BK[7][ph, :], esink[ph, :, :].rearrange("p g q -> p (g q)"), ALU.add),
                         reads=[bk(7)] + ESINK_KEYS, writes=["rd_t"])
                    P.op("dve", lambda e: e.reciprocal(rd_t[ph, :], rd_t[ph, :]), reads=["rd_t"], writes=["rd_t"])
                    P.op("dve", lambda e, b=b: e.tensor_tensor(attT[ph, :, b * 128:(b + 1) * 128], BK[6][ph, :].rearrange("p (g q) -> p g q", q=128), rd_t[ph, :].rearrange("p (g q) -> p g q", q=128), ALU.mult),
                         reads=[bk(6), "rd_t"], writes=[("attT", kv, b)])
            P.op("pool", lambda e: e.tensor_copy(kT[:, 0:128], kT[:, NT:NT + 128]), reads=["kT"], writes=["kT"])
            P.op("pool", lambda e: e.tensor_copy(vtok[:, 0, :], vtok[:, nb, :]), reads=[("vtok", nb)], writes=[("vtok", 0)])
            if last and 'pk' not in SKIP:
                P.op("pe", lambda e: e.transpose(BK[4][:, 0:128], kTf[:, NT - 128:NT], ident), reads=["kTf", "cst"], writes=[bk(4)])
                P.op("act", lambda e: e.copy(kvrow[:, 0, :], BK[4][:, 0:128]), reads=[bk(4)], writes=[("kvrow", 0)])
                P.dma("sp", pk[seq], kvrow[:, 0, :], reads=[("kvrow", 0)], writes=["pk"])
                P.dma("sp", pv[seq], kvrow[:, 1, :], reads=[("kvrow", 1)], writes=["pv"])
        else:
            sample_attention()
        ATT_KEYS = [("attT", kv, b) for kv in range(2) for b in range(nb)]
        stop_at("F")
        debug_tap("attT", attT[:, :, 0:NT], [128, 4, NT], ATT_KEYS)

        def hcol(h):
            return (h % 2) * 256 + (h // 2) * 64

        def rw_pre(b):
            bc = slice(b * 128, (b + 1) * 128)
            for h in range(8):
                p, h2 = divmod(h, 2)
                ph = slice(h2 * 64, h2 * 64 + 64)
                b0, b1, b2 = (4, 5, 6) if h % 2 == 0 else (0, 1, 2)
                P.op("pe", lambda e, p=p, ph=ph: e.matmul(BK[b0][:, 0:256], kt_[ph, p, bc], art[ph, p, :, bc], start=True, stop=True),
                     reads=[("kt", p), ("art", p)], writes=[bk(b0)])
                P.op("dve", lambda e, h=h: e.tensor_tensor(AK[h][:], BK[b0][:, 0:256], cst[:, cm:cm + 256], ALU.mult), reads=[bk(b0), "cst"], writes=[("AK", h)])
                P.op("pe", lambda e, p=p, ph=ph: e.matmul(BK[b1][:, 0:256], bt_[ph, p, bc], art[ph, p, :, bc], start=True, stop=True),
                     reads=[("bt", p), ("art", p)], writes=[bk(b1)])
                P.op("dve", lambda e, h=h: e.tensor_tensor(AB[h][:], BK[b1][:, 0:256], cst[:, cm:cm + 256], ALU.mult), reads=[bk(b1), "cst"], writes=[("AB", h)])
                P.op("pe", lambda e, p=p, ph=ph: e.matmul(BK[b2][:, 0:128], art[ph, p, 0, bc], bt_[ph, p, bc], start=True, stop=True),
                     reads=[("bt", p), ("art", p)], writes=[bk(b2)])
                P.op("dve", lambda e, h=h: e.tensor_tensor(Mt_[h // 4][:, h % 4, :], BK[b2][:, 0:128], cst[:, cl:cl + 128], ALU.mult), reads=[bk(b2), "cst"], writes=[("Mt", h // 4)])
                P.op("pool", lambda e, h=h: e.tensor_tensor(Pv[h], AB[h][:, 0:128], ident, ALU.add), reads=[("AB", h), "cst"], writes=[("Pv", h)])
            for lv in range(1, LV + 1):
                lastlv = lv == LV
                for g in range(2):
                    bMT, bM, bP = (4, 5, 6) if g == 0 else (0, 1, 2)
                    for j in range(4):
                        h = 4 * g + j
                        Mcur = AB[h][:, 0:128] if lv == 1 else Mm_[g][:, j, :]
                        mk_c = ("AB", h) if lv == 1 else ("Mm", g)
                        Mtcur = Mt_[g][:, j, :]
                        P.op("pe", lambda e, Mcur=Mcur, Mtcur=Mtcur, j=j: e.matmul(BK[bMT][:, j * 128:(j + 1) * 128], Mcur, Mtcur, start=True, stop=True),
                             reads=[mk_c, ("Mt", g)], writes=[bk(bMT)])
                        if not lastlv:
                            P.op("pe", lambda e, Mcur=Mcur, Mtcur=Mtcur, j=j: e.matmul(BK[bM][:, j * 128:(j + 1) * 128], Mtcur, Mcur, start=True, stop=True),
                                 reads=[mk_c, ("Mt", g)], writes=[bk(bM)])
                    P.op("act", lambda e, g=g: e.copy(Mt_[g][:].rearrange("p a t -> p (a t)"), BK[bMT][:]), reads=[bk(bMT)], writes=[("Mt", g)])
                    if not lastlv:
                        P.op("act", lambda e, g=g: e.copy(Mm_[g][:].rearrange("p a t -> p (a t)"), BK[bM][:]), reads=[bk(bM)], writes=[("Mm", g)])
                    for j in range(4):
                        h = 4 * g + j
                        P.op("pe", lambda e, j=j, h=h, g=g: e.matmul(BK[bP][:, j * 128:(j + 1) * 128], Mt_[g][:, j, :], Pv[h], start=True, stop=True),
                             reads=[("Mt", g), ("Pv", h)], writes=[bk(bP)])
                    P.op("dve", lambda e, g=g: e.tensor_tensor(Pvg[g][:].rearrange("p a t -> p (a t)"), BK[bP][:], Pvg[g][:].rearrange("p a t -> p (a t)"), ALU.add),
                         reads=[bk(bP)] + [("Pv", 4 * g + j) for j in range(4)], writes=[("Pv", 4 * g + j) for j in range(4)])
            for h in range(8):
                P.op("pe", lambda e, h=h, b=b: e.matmul(BK[7][:, hcol(h):hcol(h) + 64], AK[h][:, 0:128], Vtok[:, b, h * 64:(h + 1) * 64], start=True, stop=True),
                     reads=[("AK", h), ("Vtok", b, h // 2)], writes=[bk(7)])
            P.op("act", lambda e: e.copy(VA[:], BK[7][:]), reads=[bk(7)], writes=["VA"])
            for h in range(8):
                p, h2 = divmod(h, 2)
                ph = slice(h2 * 64, h2 * 64 + 64)
                P.op("pe", lambda e, h=h, b=b, p=p, ph=ph: e.matmul(BK[4][ph, p * 128:(p + 1) * 128], Vtok[:, b, h * 64:(h + 1) * 64], AK[h][:, 128:256], start=True, stop=True),
                     reads=[("AK", h), ("Vtok", b, p)], writes=[bk(4)])
            P.op("act", lambda e: e.copy(VK[:].rearrange("p a t -> p (a t)"), BK[4][:]), reads=[bk(4)], writes=["VK"])
            stop_at('G2')

        if prompt:
            if first:
                P.op("pool", lambda e: e.memset(S[:], 0.0), writes=["S"])
            for b in range(nb):
                rw_pre(b)
                for c2 in range(2):
                    cr = slice(c2 * 64, c2 * 64 + 64)
                    tc_ = slice(b * 128 + c2 * 64, b * 128 + c2 * 64 + 64)
                    gci = (b * 128 + c2 * 64) // 64
                    for h in range(8):
                        p, h2 = divmod(h, 2)
                        ph = slice(h2 * 64, h2 * 64 + 64)
                        P.op("pe", lambda e, h=h, p=p, ph=ph, h2=h2: e.matmul(BK[5 + h2][cr, p * 64:(p + 1) * 64], art[ph, p, 0, tc_], S[ph, p, :], start=True, stop=True),
                             reads=[("art", p), "S"], writes=[bk(5 + h2)])
                    for h in range(8):
                        p, h2 = divmod(h, 2)
                        ph = slice(h2 * 64, h2 * 64 + 64)
                        P.op("pe", lambda e, p=p, ph=ph, h2=h2: e.matmul(BK[h2][ph, p * 64:(p + 1) * 64], S[ph, p, :], art[ph, p, 1, tc_], start=True, stop=True),
                             reads=[("art", p), "S"], writes=[bk(h2)])
                    for h2 in range(2):
                        ph = slice(h2 * 64, h2 * 64 + 64)
                        P.op("act", lambda e, h2=h2, ph=ph: e.copy(y1[ph, :, :].rearrange("p a t -> p (a t)"), BK[h2][ph, 0:256]), reads=[bk(h2)], writes=[("y1", h2)])
                    for h2 in range(2):
                        P.op("dve", lambda e, h2=h2: e.tensor_tensor(XT[cr, h2 * 256:(h2 + 1) * 256], BK[5 + h2][cr, 0:256], VA[cr, h2 * 256:(h2 + 1) * 256], ALU.add),
                             reads=[bk(5 + h2), "VA"], writes=[("XT", h2)])
                    for h in range(8):
                        P.op("pe", lambda e, h=h: e.matmul(BK[7][cr, hcol(h):hcol(h) + 64], Pv[h][cr, c2 * 64:(c2 + 1) * 64], XT[cr, hcol(h):hcol(h) + 64], start=True, stop=True),
                             reads=[("Pv", h), ("XT", h % 2)], writes=[bk(7)])
                    P.op("act", lambda e: e.copy(UT[cr, :], BK[7][cr, :]), reads=[bk(7)], writes=["UT"])
                    stop_at('G3')
                    for h in range(8):
                        p, h2 = divmod(h, 2)
                        ph = slice(h2 * 64, h2 * 64 + 64)
                        P.op("pe", lambda e, h=h, p=p, ph=ph: e.matmul(BK[5][ph, 256 + p * 64:256 + (p + 1) * 64], BPtok[cr, b, h * 64:(h + 1) * 64], UT[cr, hcol(h):hcol(h) + 64], start=True, stop=False),
                             reads=[("BPtok", b, p), "UT"], writes=[bk(5)])
                        P.op("pe", lambda e, h=h, p=p, ph=ph: e.matmul(BK[5][ph, 256 + p * 64:256 + (p + 1) * 64], KPtok[cr, b, h * 64:(h + 1) * 64], Vtok[cr, b, h * 64:(h + 1) * 64], start=False, stop=True),
                             reads=[("KPtok", b, p), ("Vtok", b, p)], writes=[bk(5)])
                    for p in range(4):
                        P.op("dve", lambda e, p=p: e.scalar_tensor_tensor(S[:, p, :], S[:, p, :], cCt[:, p, gci:gci + 1], BK[5][:, 256 + p * 64:256 + (p + 1) * 64], ALU.mult, ALU.add),
                             reads=["S", ("cCt", p), bk(5)], writes=["S"])
                    for h in range(8):
                        p, h2 = divmod(h, 2)
                        ph = slice(h2 * 64, h2 * 64 + 64)
                        P.op("pe", lambda e, h=h, p=p, ph=ph: e.matmul(BK[6][ph, 256 + p * 64:256 + (p + 1) * 64], UT[cr, hcol(h):hcol(h) + 64], AB[h][cr, 128 + c2 * 64:128 + (c2 + 1) * 64], start=True, stop=True),
                             reads=[("AB", h), "UT"], writes=[bk(6)])
                    P.op("dve", lambda e: e.tensor_tensor(y1[:], BK[6][:, 256:512].rearrange("p (a t) -> p a t", t=64), y1[:], ALU.add), reads=[bk(6), ("y1", 0), ("y1", 1)], writes=[("y1", 0), ("y1", 1)])
                    P.op("pool", lambda e: e.tensor_tensor(Yt[:, :, tc_], y1[:], VK[:, :, c2 * 64:(c2 + 1) * 64], ALU.add), reads=[("y1", 0), ("y1", 1), "VK"], writes=[("Yt", b)])
                    stop_at('G4')
            if last and 'pwkv' not in SKIP:
                for pp in range(2):
                    P.op("pe", lambda e, pp=pp: e.transpose(BK[4][:, pp * 128:(pp + 1) * 128], S[:, 2 * pp:2 * pp + 2, :].rearrange("p a i -> p (a i)"), ident), reads=["S", "cst"], writes=[bk(4)])
                P.op("act", lambda e: e.copy(xn[:, 0:256], BK[4][:, 0:256]), reads=[bk(4)], writes=["xn"])
                for pp in range(2):
                    for pl in range(2):
                        pidx = 2 * pp + pl
                        P.dma("sp", pwkv[seq, 2 * pidx:2 * pidx + 2].rearrange("h i j -> i h j"), xn[pl * 64:(pl + 1) * 64, pp * 128:(pp + 1) * 128].rearrange("p (h j) -> p h j", j=64), reads=["xn"], writes=[("pwkv", pidx)])
                P.op("pe", lambda e: e.transpose(BK[4][:, 0:128], car[:, :], ident), reads=[("car", n) for n in range(14)] + ["cst", "car_init"], writes=[bk(4)])
                P.op("act", lambda e: e.copy(xn[0:14, 0:128], BK[4][0:14, 0:128]), reads=[bk(4)], writes=["xn"])
                P.dma("sp", psh[seq].rearrange("(c p) -> c p", p=128), xn[0:14, 0:128], reads=["xn"], writes=["psh"])
        else:
            rw_pre(0)
            sample_rwkv(hcol)
        YT_KEYS = [("Yt", b) for b in range(nb)]
        stop_at("G")
        debug_tap("Yt", Yt[:, :, 0:NT], [128, 4, NT], YT_KEYS)

        def post_steps(p):
            tn = ("sg", "cs", "t1", "t2", "t3", "a", "kkn", "k2")
            n1, n2 = tn[2 * p], tn[2 * p + 1]
            T1, T2 = rw[n1][:, 0:NT], rw[n2][:, 0:NT]
            k1, k2_ = "r_" + n1, "r_" + n2
            bm, bv = 4 + p, p
            return [
                lambda: P.op("pe", lambda e: e.matmul(BK[bm][:, 0:NT], bo64, Yt[:, p, 0:NT], start=True, stop=True), reads=YT_KEYS + ["cst"], writes=[bk(bm)]),
                lambda: P.op("dve", lambda e: e.tensor_tensor(T1, Yt[:, p, 0:NT], BK[bm][:, 0:NT], ALU.subtract), reads=YT_KEYS + [bk(bm)], writes=[k1]),
                lambda: P.op("pool", lambda e: e.tensor_tensor(T2, T1, T1, ALU.mult), reads=[k1], writes=[k2_]),
                lambda: P.op("pe", lambda e: e.matmul(BK[bv][:, 0:NT], bo64, T2, start=True, stop=True), reads=[k2_, "cst"], writes=[bk(bv)]),
                lambda: P.op("act", lambda e: e.activation(T2, BK[bv][:, 0:NT], AF.Sqrt, bias=64e-5), reads=[bk(bv)], writes=[k2_]),
                lambda: P.op("dve", lambda e: e.reciprocal(T2, T2), reads=[k2_], writes=[k2_]),
                lambda: P.op("dve", lambda e: e.tensor_tensor(T1, T1, T2, ALU.mult), reads=[k1, k2_], writes=[k1]),
                lambda: P.op("dve", lambda e: e.tensor_scalar(T1, T1, lg_c[:, p:p + 1], lb_c[:, p:p + 1], ALU.mult, ALU.add), reads=[k1, "lg_c", "lb_c"], writes=[k1]),
                lambda: P.op("pool", lambda e: e.tensor_tensor(T1, T1, bon[:, p, 0:NT], ALU.add), reads=[k1, ("bon", p)], writes=[k1]),
                lambda: P.op("dve", lambda e: e.tensor_tensor(rwoT[:, p, 0:NT], T1, gT[:, p, 0:NT], ALU.mult), reads=[k1, ("gT", p)], writes=[("rwoT", p)]),
            ]
        for group in zip(*[post_steps(p) for p in range(4)]):
            for step in group:
                step()
        RWO_KEYS = [("rwoT", p) for p in range(4)]
        stop_at("H")
        debug_tap("rwoT", rwoT[:, :, 0:NT], [128, 4, NT], RWO_KEYS)

        for ch in range(2):
            sa, ka_ = ring_get()
            sb_, kb_ = ring_get(hold=1)
            sa3 = sa[:, 0:2048].rearrange("p (k n) -> p k n", n=512)
            sb3 = sb_[:, 0:2048].rearrange("p (k n) -> p k n", n=512)
            for cc in range(4):
                oc = ch * 4 + cc
                ba, bb = (0, 1) if cc % 2 == 0 else (2, 3)
                for kc in range(4):
                    P.op("pe", lambda e, kc=kc, cc=cc, ba=ba, sa3=sa3: e.matmul(BK[ba][:, 0:NT], sa3[:, kc, cc * 128:(cc + 1) * 128], attT[:, kc, 0:NT], start=(kc == 0), stop=(kc == 3)),
                         reads=[ka_] + ATT_KEYS, writes=[bk(ba)])
                for kc in range(4):
                    P.op("pe", lambda e, kc=kc, cc=cc, bb=bb, sb3=sb3: e.matmul(BK[bb][:, 0:NT], sb3[:, kc, cc * 128:(cc + 1) * 128], rwoT[:, kc, 0:NT], start=(kc == 0), stop=(kc == 3)),
                         reads=[kb_] + RWO_KEYS, writes=[bk(bb)])
                tA = cc_t[cc % 2][:, 0:NT]
                tB = gl_t[cc % 2][:, 0:NT]
                P.op("dve", lambda e, oc=oc, ba=ba, tA=tA: e.tensor_tensor(tA, BK[ba][:, 0:NT], gates[:, oc, 0:NT], ALU.mult), reads=[bk(ba), ("gates", oc)], writes=[("cc_t", cc % 2)])
                P.op("dve", lambda e, oc=oc, bb=bb, tB=tB: e.tensor_tensor(tB, BK[bb][:, 0:NT], gates[:, 8 + oc, 0:NT], ALU.mult), reads=[bk(bb), ("gates", 8 + oc)], writes=[("gl_t", cc % 2)])
                P.op("pool", lambda e, oc=oc, tA=tA, tB=tB: e.tensor_tensor(hT[:, oc, 0:NT], tA, tB, ALU.add), reads=[("cc_t", cc % 2), ("gl_t", cc % 2)], writes=[("hT", oc)])
        MIX_KEYS = [("hT", oc) for oc in range(8)]
        debug_tap("mixT", hT[:, :, 0:NT], [128, 8, NT], MIX_KEYS)

        stop_at("I")
        for ch in range(2):
            so, ko = ring_get()
            so3 = so[:, 0:4096].rearrange("p (k n) -> p k n", n=512)
            for b in range(nb):
                bnk = (ch * nb + b) % 4
                for kc in range(8):
                    P.op("pe", lambda e, kc=kc, b=b, bnk=bnk, so3=so3: e.matmul(BK[bnk][:], hT[:, kc, b * 128:(b + 1) * 128], so3[:, kc, :], start=(kc == 0), stop=(kc == 7)),
                         reads=[ko] + MIX_KEYS, writes=[bk(bnk)])
                P.op("dve", lambda e, b=b, bnk=bnk, ch=ch: e.tensor_tensor(xt[b][:, ch * 512:(ch + 1) * 512], xt[b][:, ch * 512:(ch + 1) * 512], BK[bnk][:], ALU.add),
                     reads=[bk(bnk), ("xt", b)], writes=[("xt", b)])
        stop_at("J")
        debug_tap("x1", xt[0][:], [128, 1024], [("xt", 0)])

        norm_T("g2b")
        def ffn_steps(i, f, bo_):
            ug3 = v3(ugx[i][:, 0:NBs * (TB + 2)], TB + 2)
            ugk = ("ugx", i)
            c3 = v3(cc_t[i][:, 0:NT], TB)

            def carry():
                if prompt:
                    if first:
                        P.op("pool", lambda e: e.memset(ugx[i][:, 0:2], 0.0), writes=[("ugx0", i)])
                    else:
                        P.op("pool", lambda e: e.tensor_copy(ugx[i][:, 0:2], ccar[:, :, f]), reads=[("ccar", f)], writes=[("ugx0", i)])
                    P.op("pool", lambda e: e.tensor_copy(ccar[:, :, f], ugx[i][:, NT:NT + 2]), reads=[ugk], writes=[("ccar", f)])
                else:
                    P.op("pool", lambda e: e.tensor_copy(ug3[:, :, 0:2], SB["ccar_s"][:, f, :, :]), reads=[("ccar_s", f)], writes=[("ugx0", i)])
                    P.op("pool", lambda e: e.tensor_copy(SB["cout_s"][:, f, :, :], ug3[:, :, TB:TB + 2]), reads=[ugk], writes=[("cout_s", f)])
            return [
                lambda: P.op("act", lambda e: e.copy(ug3[:, :, 2:TB + 2], v3(BK[bo_ + i][:, 0:NT], TB)), reads=[bk(bo_ + i)], writes=[ugk]),
                carry,
                lambda: P.op("pool", lambda e: e.tensor_scalar(c3, ug3[:, :, 0:TB], cw_c[:, 0, f:f + 1], cb_c[:, f:f + 1], ALU.mult, ALU.add),
                             reads=[ugk, ("ugx0", i), "cb_c"] + CW_KEYS, writes=[("cc_t", i)]),
                lambda: P.op("dve", lambda e: e.scalar_tensor_tensor(c3, ug3[:, :, 1:TB + 1], cw_c[:, 1, f:f + 1], c3, ALU.mult, ALU.add),
                             reads=[ugk, ("ugx0", i), ("cc_t", i)] + CW_KEYS, writes=[("cc_t", i)]),
                lambda: P.op("dve", lambda e: e.scalar_tensor_tensor(c3, ug3[:, :, 2:TB + 2], cw_c[:, 2, f:f + 1], c3, ALU.mult, ALU.add),
                             reads=[ugk, ("cc_t", i)] + CW_KEYS, writes=[("cc_t", i)]),
                lambda: P.op("act", lambda e: e.activation(gl_t[i][:, 0:NT], cc_t[i][:, 0:NT], AF.Gelu_apprx_tanh), reads=[("cc_t", i)], writes=[("gl_t", i)]),
                lambda: P.op("dve", lambda e: e.tensor_tensor(actT[:, f, 0:NT], gl_t[i][:, 0:NT], BK[bo_ + 2 + i][:, 0:NT], ALU.mult), reads=[("gl_t", i), bk(bo_ + 2 + i)], writes=[("actT", f)]),
            ]
        for blkb in range(11):
            slot, skey = ring_get()
            sl3 = slot[:, 0:4096].rearrange("p (k n) -> p k n", n=512)
            bo_ = 4 * (blkb % 2)
            for cc in range(4):
                for kc in range(8):
                    P.op("pe", lambda e, kc=kc, cc=cc, sl3=sl3, bo_=bo_: e.matmul(BK[bo_ + cc][:, 0:NT], sl3[:, kc, cc * 128:(cc + 1) * 128], hT[:, kc, 0:NT], start=(kc == 0), stop=(kc == 7)),
                         reads=[skey] + HT_KEYS, writes=[bk(bo_ + cc)])
            for sa, sb in zip(ffn_steps(0, blkb * 2, bo_), ffn_steps(1, blkb * 2 + 1, bo_)):
                sa()
                sb()
        ACT_KEYS = [("actT", f) for f in range(22)]
        if prompt and last and 'pconv' not in SKIP:
            for j in range(2):
                P.op("pe", lambda e, j=j: e.transpose(BK[4][:, j * 128:(j + 1) * 128], ccar[:, j, :], ident), reads=[("ccar", f) for f in range(22)] + ["cst", "ccar_init"], writes=[bk(4)])
            P.op("act", lambda e: e.copy(xn[0:22, 0:256], BK[4][0:22, 0:256]), reads=[bk(4)], writes=["xn"])
            for j in range(2):
                P.dma("sp", pconv[seq, j].rearrange("(c p) -> c p", p=128), xn[0:22, j * 128:(j + 1) * 128], reads=["xn"], writes=[("pconv", j)])

        stop_at("K")
        for ch in range(2):
            for rg in range(3):
                sd, kd = ring_get()
                nk = DN_NK[rg]
                sd3 = sd[:, 0:nk * 512].rearrange("p (k n) -> p k n", n=512)
                for b in range(nb):
                    bnk = b % 4
                    for kc in range(nk):
                        f = rg * 8 + kc
                        P.op("pe", lambda e, kc=kc, f=f, b=b, bnk=bnk, sd3=sd3: e.matmul(BK[bnk][:], actT[:, f, b * 128:(b + 1) * 128], sd3[:, kc, :], start=(f == 0), stop=(f == 21)),
                             reads=[kd] + ACT_KEYS, writes=[bk(bnk)])
            for b in range(nb):
                bnk = b % 4
                P.op("dve", lambda e, b=b, bnk=bnk, ch=ch: e.tensor_tensor(xt[b][:, ch * 512:(ch + 1) * 512], xt[b][:, ch * 512:(ch + 1) * 512], BK[bnk][:], ALU.add),
                     reads=[bk(bnk), ("xt", b)], writes=[("xt", b)])

        for b in range(nb):
            P.op("dve", lambda e, b=b: e.bn_stats(stat[:, 0:6], xt[b][:, 0:512]), reads=[("xt", b)], writes=["stat0_0"])
            P.op("dve", lambda e, b=b: e.bn_stats(stat[:, 6:12], xt[b][:, 512:1024]), reads=[("xt", b)], writes=["stat0_1"])
            P.op("dve", lambda e: e.bn_aggr(stat[:, 12:14], stat[:, 0:12]), reads=["stat0_0", "stat0_1"], writes=["stat0_2"])
            P.op("dve", lambda e: e.scalar_tensor_tensor(stat[:, 14:15], stat[:, 12:13], stat[:, 12:13], stat[:, 13:14], ALU.mult, ALU.add), reads=["stat0_2"], writes=["stat0_3"])
            P.op("act", lambda e: e.activation(stat[:, 15:16], stat[:, 14:15], AF.Sqrt, bias=1e-6), reads=["stat0_3"], writes=["stat0_4"])
            P.op("dve", lambda e: e.reciprocal(stat[:, 15:16], stat[:, 15:16]), reads=["stat0_4"], writes=["stat0_4"])
            P.op("dve", lambda e, b=b: e.scalar_tensor_tensor(xn[:], xt[b][:], stat[:, 15:16], gb["gfb"][:], ALU.mult, ALU.mult), reads=[("xt", b), "stat0_4", "gfb"], writes=["xn"])
            P.dma("sp", yrows[b * 128:(b + 1) * 128, :], xn[:], reads=["xn"], writes=[("y", kind, seq, ti, b)])

    SB = {}

    def sample_alloc():
        SB["shcar"] = wsb("sx_shcar", [128, 14, 16], F32)
        SB["shout"] = wsb("sx_shout", [128, 16, 16], F32)
        SB["ccar_s"] = wsb("sx_ccar_s", [128, 22, 16, 2], F32)
        SB["cout_s"] = wsb("sx_cout_s", [128, 24, 16, 2], F32)
        SB["S_s"] = wsb("sx_S_s", [128, 16, 4, 64], F32)
        SB["kcT"] = wsb("sx_kcT", [128, 16, 128], BF16)
        SB["vc"] = wsb("sx_vc", [128, 16, 128], BF16)
        SB["biasC"] = [wsb("sx_biasC%d" % kv, [128, 16, 4, 8], F32) for kv in range(2)]
        SB["biasN"] = [wsb("sx_biasN%d" % kv, [128, 16, 4, 8], F32) for kv in range(2)]
        SB["Apad"] = wsb("sx_Apad", [128, 16, 128], F32)
        SB["BPpad"] = [wsb("sx_BPpad", [128, 512], F32)] * 2
        SB["KPpad"] = [wsb("sx_KPpad", [128, 512], F32)] * 2
        SB["xn"] = xn

    def sample_const_setup():
        P.op("pool", lambda e: e.memset(SB["Apad"][:].rearrange("p b c -> p (b c)"), 0.0), writes=["Apad"])
        P.op("pool", lambda e: e.memset(SB["shout"][:].rearrange("p a b -> p (a b)"), 0.0), writes=["shout_init"])
        P.op("pool", lambda e: e.memset(SB["cout_s"][:].rearrange("p a b c -> p (a b c)"), 0.0), writes=["cout_init"])
        for kv in range(2):
            P.op("dve", lambda e, kv=kv: e.tensor_copy(SB["biasC"][kv][:], biasT[1][kv][:, :, 0:8].unsqueeze(1).broadcast_to([128, 16, 4, 8])),
                 reads=BIAS_KEYS[(1, kv)], writes=[("biasC", kv)])
            P.op("pool", lambda e, kv=kv: e.memset(SB["biasN"][kv][:].rearrange("p a b c -> p (a b c)"), NEGB), writes=[("biasN", kv)])
            for bb in range(16):
                P.dma("sp", SB["biasN"][kv][bb * 8:(bb + 1) * 8, bb, :, :], biasT[0][kv][0:8, :, 0:8],
                      reads=BIAS_KEYS[(0, kv)] + [("biasN", kv)], writes=[("biasNd", kv, bb)])
    BIASN_KEYS = {kv: [("biasN", kv)] + [("biasNd", kv, bb) for bb in range(16)] for kv in range(2)}

    def sample_setup(st):
        b0 = st * 16
        stgA = SB["xn"]
        for n in range(14):
            bnk = 4 + n % 4
            if n % 7 == 0:
                P.dma("sp", stgA[0:16, 0:896], sshift[b0:b0 + 16, n * 128:n * 128 + 896], writes=["xn"])
            P.op("pe", lambda e, n=n, bnk=bnk: e.transpose(BK[bnk][:, 0:16], stgA[0:16, (n % 7) * 128:(n % 7 + 1) * 128], cst[0:16, C_ID:C_ID + 16]), reads=["xn", "cst"], writes=[bk(bnk)])
            P.op("act", lambda e, n=n, bnk=bnk: e.copy(SB["shcar"][:, n, :], BK[bnk][:, 0:16]), reads=[bk(bnk)], writes=[("shcar", n)])
        for piece, npc in enumerate((8, 8, 6)):
            P.dma("sp", stgA[0:32, 0:npc * 128], sconv[st * 32:(st + 1) * 32, piece * 1024:piece * 1024 + npc * 128], writes=["xn"])
            for c in range(npc):
                f = piece * 8 + c
                bnk = 4 + c % 4
                P.op("pe", lambda e, c=c, bnk=bnk: e.transpose(BK[bnk][:, 0:32], stgA[0:32, c * 128:(c + 1) * 128], cst[0:32, C_ID:C_ID + 32]), reads=["xn", "cst"], writes=[bk(bnk)])
                P.op("act", lambda e, f=f, bnk=bnk: e.copy(SB["ccar_s"][:, f, :, :].rearrange("p b j -> p (b j)"), BK[bnk][:, 0:32]), reads=[bk(bnk)], writes=[("ccar_s", f)])
        for bb in range(16):
            hs = bb % 2
            P.dma("sp", stgA[0:64, hs * 512:(hs + 1) * 512].rearrange("p (h j) -> p h j", j=64), swkv[b0 + bb].rearrange("h i j -> i h j"), writes=[("stgAh", hs), "xn"] if bb < 2 else [("stgAh", hs)], reads=["xn"])
            bnk = 4 + bb % 4
            for p in range(4):
                P.op("pe", lambda e, p=p, bnk=bnk, hs=hs: e.transpose(BK[bnk][:, p * 64:(p + 1) * 64], stgA[0:64, hs * 512 + p * 128:hs * 512 + (p + 1) * 128], cst[0:64, C_ID:C_ID + 64]),
                     reads=[("stgAh", hs), "cst"], writes=[bk(bnk)])
            P.op("dve", lambda e, bb=bb, bnk=bnk: e.tensor_copy(SB["S_s"][:, bb, :, :].rearrange("p a i -> p (a i)"), BK[bnk][:, 0:256]), reads=[bk(bnk)], writes=[("S_s", bb)])
        for bb in range(16):
            bnk = 4 + bb % 4
            if bb % 8 == 0:
                P.dma("sp", stgA[:, 0:1024].rearrange("p (b c) -> p b c", c=128), ck[b0 + bb:b0 + bb + 8].rearrange("b k c -> k b c"), reads=[("stgAh", 0), ("stgAh", 1)], writes=["xn", ("stgAh", 0), ("stgAh", 1)])
            P.op("pe", lambda e, bb=bb, bnk=bnk: e.transpose(BK[bnk][:, 0:128], stgA[:, (bb % 8) * 128:(bb % 8 + 1) * 128], ident), reads=["xn", "cst"], writes=[bk(bnk)])
            P.op("act", lambda e, bb=bb, bnk=bnk: e.copy(SB["kcT"][:, bb, :], BK[bnk][:, 0:128]), reads=[bk(bnk)], writes=[("kcT", bb)])
        P.dma("pool", SB["vc"][:], cv[b0:b0 + 16].rearrange("b k c -> k b c"), writes=["vc"])
        P.dma("sp", sk[b0:b0 + 16, 0:120, :], ck[b0:b0 + 16, 8:128, :], writes=[("sk_old", st)])
        P.dma("sp", sv[b0:b0 + 16, 0:120, :], cv[b0:b0 + 16, 8:128, :], writes=[("sv_old", st)])

    def sample_attention():
        NT = 128
        kcT, vc = SB["kcT"], SB["vc"]
        P.op("pe", lambda e: e.transpose(BK[4][:, 0:128], kTf[:, 0:128], ident), reads=["kTf", "cst"], writes=[bk(4)])
        P.op("act", lambda e: e.copy(kvrow[:, 0, :], BK[4][:, 0:128]), reads=[bk(4)], writes=[("kvrow", 0)])
        for kv in range(2):
            ph = slice(kv * 64, kv * 64 + 64)
            for bb in range(16):
                P.op("pe", lambda e, bb=bb: e.matmul(BK[4][:, bb * 32:(bb + 1) * 32], kcT[ph, bb, :], qT[ph, :, bb * 8:(bb + 1) * 8], start=True, stop=True),
                     reads=[("kcT", bb)] + [("qT", c) for c in range(4)], writes=[bk(4)])
            for bb in range(16):
                P.op("pe", lambda e, bb=bb: e.matmul(BK[5][:, bb * 32:(bb + 1) * 32], kT[ph, 128:256], qT[ph, :, bb * 8:(bb + 1) * 8], start=True, stop=True),
                     reads=["kT"] + [("qT", c) for c in range(4)], writes=[bk(5)])
            P.op("dve", lambda e, kv=kv: e.scalar_tensor_tensor(BK[4][:], BK[4][:], 0.125, SB["biasC"][kv][:].rearrange("p a b c -> p (a b c)"), ALU.mult, ALU.add),
                 reads=[bk(4), ("biasC", kv)], writes=[bk(4)])
            P.op("dve", lambda e, kv=kv: e.scalar_tensor_tensor(BK[5][:], BK[5][:], 0.125, SB["biasN"][kv][:].rearrange("p a b c -> p (a b c)"), ALU.mult, ALU.add),
                 reads=[bk(5)] + BIASN_KEYS[kv], writes=[bk(5)])
            P.op("act", lambda e: e.activation(PT[0][:], BK[4][:], AF.Exp), reads=[bk(4)], writes=[("PT", 0)])
            P.op("act", lambda e: e.activation(PT[1][:], BK[5][:], AF.Exp), reads=[bk(5)], writes=[("PT", 1)])
            for bb in range(16):
                cs = slice(bb * 32, (bb + 1) * 32)
                P.op("pe", lambda e, bb=bb, cs=cs: e.matmul(BK[6][:, cs], vc[:, bb, :], PT[0][:, cs], start=True, stop=False), reads=["vc", ("PT", 0)], writes=[bk(6)])
                P.op("pe", lambda e, cs=cs: e.matmul(BK[6][:, cs], vtok[:, 1, :], PT[1][:, cs], start=False, stop=True), reads=[("vtok", 1), ("PT", 1)], writes=[bk(6)])
            for bb in range(16):
                cs = slice(bb * 32, (bb + 1) * 32)
                P.op("pe", lambda e, cs=cs: e.matmul(BK[7][:, cs], ones_bf[:], PT[0][:, cs], start=True, stop=False), reads=["ones_bf", ("PT", 0)], writes=[bk(7)])
                P.op("pe", lambda e, cs=cs: e.matmul(BK[7][:, cs], ones_bf[:], PT[1][:, cs], start=False, stop=True), reads=["ones_bf", ("PT", 1)], writes=[bk(7)])
            P.op("dve", lambda e: e.tensor_tensor(rd_t[ph, :].rearrange("p (b g t) -> p b g t", g=4, t=8), BK[7][ph, :].rearrange("p (b g t) -> p b g t", g=4, t=8), esink[ph, :, 0:8].unsqueeze(1).broadcast_to([64, 16, 4, 8]), ALU.add), reads=[bk(7)] + ESINK_KEYS, writes=["rd_t"])
            P.op("dve", lambda e: e.reciprocal(rd_t[ph, :], rd_t[ph, :]), reads=["rd_t"], writes=["rd_t"])
            P.op("dve", lambda e, kv=kv: e.tensor_tensor(attT[ph, :, 0:128].rearrange("p g (b t) -> p b g t", t=8), BK[6][ph, :].rearrange("p (b g t) -> p b g t", g=4, t=8),
                                                       rd_t[ph, :].rearrange("p (b g t) -> p b g t", g=4, t=8), ALU.mult),
                 reads=[bk(6), "rd_t"], writes=[("attT", kv, 0)])

    def sample_rwkv(hcol):
        S_s, Apad = SB["S_s"], SB["Apad"]
        Rpad = Apad
        y1s = VA[:].rearrange("p (a t) -> p a t", t=128)
        SKEYS = [("S_s", bb) for bb in range(16)]

        def diag(t):
            base = t[:, 0, 0:8]
            return bass.AP(base.tensor, base.offset, [list(base.ap[0]), [136, 16], [1, 8]])
        for p in range(4):
            P.op("pool", lambda e, p=p: e.tensor_copy(diag(Apad), art[:, p, 0, 0:128].rearrange("p (b t) -> p b t", t=8)), reads=[("art", p), "Apad"], writes=["Apad"])
            for h2 in range(2):
                ph = slice(h2 * 64, h2 * 64 + 64)
                for bb in range(16):
                    P.op("pe", lambda e, bb=bb, p=p, ph=ph, h2=h2: e.matmul(BK[5 + h2][:, p * 64:(p + 1) * 64], Apad[ph, bb, :], S_s[ph, bb, p, :], start=(bb == 0), stop=(bb == 15)),
                         reads=["Apad", ("S_s", bb)], writes=[bk(5 + h2)])
            P.op("pool", lambda e, p=p: e.tensor_copy(diag(Apad), art[:, p, 1, 0:128].rearrange("p (b t) -> p b t", t=8)), reads=[("art", p), "Apad"], writes=["Apad"])
            for h2 in range(2):
                ph = slice(h2 * 64, h2 * 64 + 64)
                yb = 4 if h2 == 0 else 7
                for bb in range(16):
                    P.op("pe", lambda e, bb=bb, p=p, ph=ph, yb=yb: e.matmul(BK[yb][ph, p * 128:(p + 1) * 128], S_s[ph, bb, p, :], Apad[ph, bb, :], start=(bb == 0), stop=(bb == 15)),
                         reads=["Apad", ("S_s", bb)], writes=[bk(yb)])
        for h2 in range(2):
            P.op("dve", lambda e, h2=h2: e.tensor_tensor(XT[:, h2 * 256:(h2 + 1) * 256], BK[5 + h2][:, 0:256], VA[:, h2 * 256:(h2 + 1) * 256], ALU.add),
                 reads=[bk(5 + h2), "VA"], writes=[("XT", h2)])
        for h in range(8):
            P.op("pe", lambda e, h=h: e.matmul(BK[5][:, hcol(h):hcol(h) + 64], Pv[h], XT[:, hcol(h):hcol(h) + 64], start=True, stop=True),
                 reads=[("Pv", h), ("XT", h % 2)], writes=[bk(5)])
        P.op("act", lambda e: e.copy(UT[:], BK[5][:]), reads=[bk(5)], writes=["UT"])
        for h in range(8):
            p, h2 = divmod(h, 2)
            ph = slice(h2 * 64, h2 * 64 + 64)
            P.op("pe", lambda e, h=h, p=p, ph=ph: e.matmul(BK[6][ph, p * 128:(p + 1) * 128], UT[:, hcol(h):hcol(h) + 64], AB[h][:, 128:256], start=True, stop=True),
                 reads=[("AB", h), "UT"], writes=[bk(6)])
        P.op("act", lambda e: e.copy(VA[0:64, :], BK[4][0:64, :]), reads=[bk(4)], writes=["VA"])
        P.op("act", lambda e: e.copy(VA[64:128, :], BK[7][64:128, :]), reads=[bk(7)], writes=["VA"])
        P.op("dve", lambda e: e.tensor_tensor(VA[:], BK[6][:], VA[:], ALU.add), reads=[bk(6), "VA", "VA"], writes=["VA", "VA"])
        P.op("pool", lambda e: e.tensor_tensor(Yt[:, :, 0:128], y1s, VK[:], ALU.add), reads=["VA", "VA", "VK"], writes=[("Yt", 0)])
        for bb in range(16):
            i2 = bb % 2
            bnk = 4 + bb % 4
            BPp, KPp = SB["BPpad"][i2], SB["KPpad"][i2]
            P.op("pool", lambda e, bb=bb, BPp=BPp: e.tensor_scalar(BPp[:], BPtok[:, 0, :], cst[:, C_BSEL + bb:C_BSEL + bb + 1], None, ALU.mult),
                 reads=[("BPtok", 0, p) for p in range(4)] + ["cst"], writes=["BPpad"])
            P.op("dve", lambda e, bb=bb, KPp=KPp: e.tensor_scalar(KPp[:], KPtok[:, 0, :], cst[:, C_BSEL + bb:C_BSEL + bb + 1], None, ALU.mult),
                 reads=[("KPtok", 0, p) for p in range(4)] + ["cst"], writes=["KPpad"])
            for h in range(8):
                p, h2 = divmod(h, 2)
                ph = slice(h2 * 64, h2 * 64 + 64)
                P.op("pe", lambda e, h=h, p=p, ph=ph, bnk=bnk, BPp=BPp: e.matmul(BK[bnk][ph, p * 64:(p + 1) * 64], BPp[:, h * 64:(h + 1) * 64], UT[:, hcol(h):hcol(h) + 64], start=True, stop=False),
                     reads=["BPpad", "UT"], writes=[bk(bnk)])
                P.op("pe", lambda e, h=h, p=p, ph=ph, bnk=bnk, KPp=KPp: e.matmul(BK[bnk][ph, p * 64:(p + 1) * 64], KPp[:, h * 64:(h + 1) * 64], Vtok[:, 0, h * 64:(h + 1) * 64], start=False, stop=True),
                     reads=["KPpad", ("Vtok", 0, p)], writes=[bk(bnk)])
            for p in range(4):
                P.op("dve", lambda e, p=p, bb=bb, bnk=bnk: e.scalar_tensor_tensor(S_s[:, bb, p, :], S_s[:, bb, p, :], cCt[:, p, bb:bb + 1], BK[bnk][:, p * 64:(p + 1) * 64], ALU.mult, ALU.add),
                     reads=[("S_s", bb), ("cCt", p), bk(bnk)], writes=[("S_s", bb)])
            for pp in range(2):
                P.op("pe", lambda e, pp=pp, bb=bb, bnk=bnk: e.transpose(BK[bnk][:, 256 + pp * 128:256 + (pp + 1) * 128], S_s[:, bb, 2 * pp:2 * pp + 2, :].rearrange("p a i -> p (a i)"), ident),
                     reads=[("S_s", bb), "cst"], writes=[bk(bnk)])
            stgO = XT[:, i2 * 256:(i2 + 1) * 256]
            P.op("act", lambda e, bnk=bnk, stgO=stgO: e.copy(stgO, BK[bnk][:, 256:512]), reads=[bk(bnk)], writes=[("XT", i2)])
            for pp in range(2):
                for pl in range(2):
                    pidx = 2 * pp + pl
                    P.dma("sp", swkvo[SB["b0"] + bb, 2 * pidx:2 * pidx + 2].rearrange("h i j -> i h j"), stgO[pl * 64:(pl + 1) * 64, pp * 128:(pp + 1) * 128].rearrange("p (h j) -> p h j", j=64),
                          reads=[("XT", i2)], writes=[("swkvo", bb, pidx)])

    def sample_outputs(st):
        b0 = st * 16
        stgA = SB["xn"]
        for bb in range(16):
            P.dma("sp", sk[b0 + bb, 120:128, :], kvrow[bb * 8:(bb + 1) * 8, 0, :], reads=[("kvrow", 0)], writes=[("sk_new", st, bb)])
            P.dma("sp", sv[b0 + bb, 120:128, :], kvrow[bb * 8:(bb + 1) * 8, 1, :], reads=[("kvrow", 1)], writes=[("sv_new", st, bb)])
        for g in range(2):
            P.op("pe", lambda e, g=g: e.transpose(BK[4 + g][:, 0:128], SB["shout"][:, g * 8:(g + 1) * 8, :].rearrange("p a b -> p (a b)"), ident),
                 reads=[("shout", n) for n in range(14)] + ["shout_init", "cst"], writes=[bk(4 + g)])
            P.op("act", lambda e, g=g: e.copy(stgA[:, g * 128:(g + 1) * 128], BK[4 + g][:, 0:128]), reads=[bk(4 + g)], writes=["xn"])
        for n in range(14):
            g, nl = divmod(n, 8)
            P.dma("sp", ssh[b0:b0 + 16, n * 128:(n + 1) * 128], stgA[nl * 16:(nl + 1) * 16, g * 128:(g + 1) * 128], reads=["xn"], writes=[("ssh", st, n)])
        for g in range(6):
            P.op("pe", lambda e, g=g: e.transpose(BK[4 + g % 4][:, 0:128], SB["cout_s"][:, g * 4:(g + 1) * 4, :, :].rearrange("p a b j -> p (a b j)"), ident),
                 reads=[("cout_s", f) for f in range(22)] + ["cout_init", "cst"], writes=[bk(4 + g % 4)])
            P.op("act", lambda e, g=g: e.copy(stgA[:, 256 + g * 128:256 + (g + 1) * 128], BK[4 + g % 4][:, 0:128]), reads=[bk(4 + g % 4)], writes=["xn"])
        for f in range(22):
            g, fl = divmod(f, 4)
            P.dma("sp", sconvo[st * 32:(st + 1) * 32, f * 128:(f + 1) * 128], stgA[fl * 32:(fl + 1) * 32, 256 + g * 128:256 + (g + 1) * 128], reads=["xn"], writes=[("sconvo", st, f)])

    tiles = [("p", t // (2048 // NTP), t % (2048 // NTP)) for t in range(n_ptiles)]
    if os.environ.get('KREP'):
        tiles = tiles * int(os.environ['KREP'])
    try:
        for (kind, seq, ti) in tiles:
            if os.environ.get('KBAR'):
                P.barrier()
            do_tile(kind, seq, ti)
        if do_sample:
            P.barrier()
            wstack[0].close()
            wstack[0] = ExitStack()
            alloc_work(128, "_s")
            sample_alloc()
            P.barrier()
            sample_const_setup()
            stop_at("SA")
            for st in range(n_stiles):
                SB["b0"] = st * 16
                sample_setup(st)
                stop_at("SB")
                do_tile("s", st, 0)
                sample_outputs(st)
    except _Stop:
        pass
    _pe = os.environ.get('PADE'); _pn = int(os.environ.get('PAD', '0'))
    dmy = P.sbuf("dmy", [128, 8], F32)
    for _i in range(_pn):
        if _pe == 'pe':
            P.op("pe", lambda e: e.matmul(BK[0][0:8, 0:8], cst[0:8, 0:8], cst[0:8, 0:8], start=True, stop=True), reads=["cst"], writes=[("ps", 0)])
        else:
            P.op(_pe, lambda e: e.memset(dmy[:], 0.0), writes=["dmy"])
    _pa = int(os.environ.get('KPADALL', '0'))
    for _i in range(_pa):
        P.op("pe", lambda e: e.matmul(BK[0][0:8, 0:8], cst[0:8, 0:8], cst[0:8, 0:8], start=True, stop=True), reads=["cst"], writes=[("ps", 0)])
        P.op("pe", lambda e: e.matmul(BK[1][0:8, 0:8], cst[0:8, 0:8], cst[0:8, 0:8], start=True, stop=True), reads=["cst"], writes=[("ps", 1)])
        P.op("dve", lambda e: e.memset(dmy[:], 0.0), writes=["dmy"])
        if _i % 2 == 0:
            P.op("pool", lambda e: e.memset(dmy[:, 0:4], 0.0), writes=["dmy2"])
    P.finish()
    return D


_STAGE = ""


NCORES = int(os.environ.get("KNCORES", "8"))


def kernel(**inputs):
    global STOP, NSEQ
    inp = {k: np.asarray(v) for k, v in inputs.items()}
    cst, oh = host_consts()
    STOP = ""
    NSEQ = 16 // NCORES
    NSTL = 8 // NCORES
    NB = NSTL * 16
    nc = bass.Bass("TRN2", target_bir_lowering=False)
    D = build(nc, n_ptiles=2048 // NTP * NSEQ, n_stiles=NSTL)
    in_maps = []
    for c in range(NCORES):
        bs = slice(NB * c, NB * (c + 1))
        m = {
            "xp": inp["x_prompt"][NSEQ * c:NSEQ * (c + 1)], "xsm": inp["x_sample"][bs].reshape(NB * 8, 1024),
            "ck": inp["cache_win_k"][0, bs].reshape(NB, 128, 128), "cv": inp["cache_win_v"][0, bs].reshape(NB, 128, 128),
            "sshift": inp["state_shift"][0, bs], "swkv": inp["state_wkv"][0, bs], "sconv": inp["state_conv"][0, bs].reshape(NB * 2, 2816),
            "rel_bias": inp["rel_bias"], "norm1_g": inp["norm1_g"], "w_in": inp["w_in"][0], "sinks": inp["sinks"][0],
            "mu_shift": inp["mu_shift"][0], "w0": inp["w0"][0], "w2": inp["w2"][0], "a0": inp["a0"][0], "a2": inp["a2"][0],
            "g2": inp["g2"][0], "k_k": inp["k_k"][0], "k_a": inp["k_a"][0], "r_k": inp["r_k"][0].reshape(512),
            "lnx_g": inp["lnx_g"][0], "lnx_b": inp["lnx_b"][0], "w_pa": inp["w_pa"][0], "w_pb": inp["w_pb"][0],
            "w_o": inp["w_o"][0], "norm2_g": inp["norm2_g"], "w_up": inp["w_up"][0], "conv_w": inp["conv_w"][0],
            "conv_b": inp["conv_b"][0], "w_down": inp["w_down"][0], "final_g": inp["final_g"].reshape(1, 1024),
            "consts": cst, "oh": oh,
        }
        in_maps.append({k: np.ascontiguousarray(v, dtype=np.float32) for k, v in m.items() if k in D})
    res = run_bass_kernel_spmd(nc, in_maps, core_ids=list(range(NCORES)))
    R = res.results

    def cat(name, shape):
        return np.concatenate([np.asarray(r[name], dtype=np.float32) for r in R], axis=0).reshape(shape)

    return (cat("yp", (16, 2048, 1024)), cat("ys", (128, 8, 1024)),
            cat("pk", (1, 16, 128, 2, 64)), cat("pv", (1, 16, 128, 2, 64)), cat("psh", (1, 16, 1792)),
            cat("pwkv", (1, 16, 8, 64, 64)), cat("pconv", (1, 16, 2, 2816)),
            cat("sk", (1, 128, 128, 2, 64)), cat("sv", (1, 128, 128, 2, 64)), cat("ssh", (1, 128, 1792)),
            cat("swkvo", (1, 128, 8, 64, 64)), cat("sconvo", (1, 128, 2, 2816)))
```

```python
import numpy as np
from contextlib import ExitStack
import concourse.bass as bass
import concourse.mybir as mybir
from concourse.bass_utils import run_bass_kernel_spmd

F32 = mybir.dt.float32
BF16 = mybir.dt.bfloat16
AF = mybir.ActivationFunctionType
ALU = mybir.AluOpType
AX = mybir.AxisListType

import os as _os0
SAME_ENGINE_SYNC = _os0.environ.get("SES", "1") == "1"
EPOCH = 30000
RELAX_FSZ = int(_os0.environ.get('KRELAX', '0'))
N_DMA_SEMS = {"sp": 8, "pool": 4}


class _Op:
    __slots__ = ("eng", "fn", "deps", "isdma", "ms", "dsem", "dval", "needed", "desc", "fsz")


class _Rec:
    def __init__(self):
        self.call = None

    def __getattr__(self, name):
        def f(*a, **k):
            assert self.call is None
            self.call = (name, a, k)
            return self
        return f


class Prog:
    def __init__(self, nc):
        self.nc = nc
        self.ops = []
        self.lastw = {}
        self.rd_c = {}
        self.rd_d = {}
        self.stack = ExitStack()
        self.last_op = {}
        self.pending = {}
        self.bar_from = 0

    def sbuf(self, name, shape, dtype):
        return self.stack.enter_context(self.nc.sbuf_tensor(name, list(shape), dtype))

    def psum(self, name, shape, dtype):
        return self.stack.enter_context(self.nc.psum_tensor(name, list(shape), dtype))

    def op(self, eng, fn, reads=(), writes=(), isdma=False):
        idx = len(self.ops)
        deps = set()
        for r in reads:
            w = self.lastw.get(r)
            if w is not None:
                deps.add(w)
        for r in writes:
            w = self.lastw.get(r)
            if w is not None:
                deps.add(w)
            for i in self.rd_c.get(r, {}).values():
                deps.add(i)
            for i in self.rd_d.get(r, ()):
                deps.add(i)
        for r in writes:
            self.lastw[r] = idx
            self.rd_c[r] = {}
            self.rd_d[r] = []
        ws = set(writes)
        for r in reads:
            if r in ws:
                continue
            if isdma:
                self.rd_d.setdefault(r, []).append(idx)
            else:
                self.rd_c.setdefault(r, {})[eng] = idx
        if eng in self.pending:
            deps.update(self.pending.pop(eng))
        rec = _Rec()
        fn(rec)
        name_, a_, k_ = rec.call
        o = _Op()
        o.desc = name_ + " w=" + str(list(writes))[:60]
        o.fsz = 0
        try:
            out_ap = k_.get("out", a_[0] if a_ else None)
            shp = tuple(out_ap.shape)
            n = 1
            for d_ in shp[1:]:
                n *= int(d_)
            o.fsz = n
        except Exception:
            o.fsz = 0
        o.eng, o.fn, o.deps, o.isdma = eng, (lambda e: getattr(e, name_)(*a_, **k_)), deps, isdma
        o.ms = None
        o.dsem = None
        o.dval = None
        o.needed = False
        self.ops.append(o)
        self.last_op[eng] = idx
        return idx

    def barrier(self):
        prev = set(self.last_op.values())
        prev.update(i for i in range(self.bar_from, len(self.ops)) if self.ops[i].isdma)
        self.bar_from = len(self.ops)
        for eng in ("pe", "act", "dve", "pool", "sp"):
            self.pending.setdefault(eng, set()).update(prev)

    def dma(self, q, out, in_, reads=(), writes=(), **kw):
        return self.op(q, lambda e: e.dma_start(out=out, in_=in_, **kw), reads, writes, isdma=True)

    def finish(self):
        nc = self.nc
        ops = self.ops
        last_dma = [i for i, o in enumerate(ops) if o.isdma]
        for i, o in enumerate(ops):
            best = {}
            nd = set()
            for d in o.deps:
                od = ops[d]
                if od.isdma:
                    nd.add(d)
                else:
                    if od.eng == o.eng and not o.isdma:
                        if od.eng == "pe" or not SAME_ENGINE_SYNC:
                            continue
                        if RELAX_FSZ and od.fsz >= RELAX_FSZ and od.eng in ("dve", "act"):
                            continue
                    if best.get(od.eng, -1) < d:
                        best[od.eng] = d
            nd.update(best.values())
            o.deps = nd
            for d in nd:
                ops[d].needed = True
        for i in last_dma:
            ops[i].needed = True
        tail_ops = []
        for en in ("pe", "act", "dve", "pool"):
            idxs = [i for i, o in enumerate(ops) if o.eng == en and not o.isdma]
            if idxs:
                ops[idxs[-1]].needed = True
                tail_ops.append(idxs[-1])
        cnt = {e: 0 for e in ("pe", "act", "dve", "pool", "sp")}
        dcount = {}
        nd_used = {"sp": 0, "pool": 0}
        for o in ops:
            if o.isdma:
                k = nd_used[o.eng] % N_DMA_SEMS[o.eng]
                nd_used[o.eng] += 1
                key = (o.eng, k)
                dcount[key] = dcount.get(key, 0) + 1
                o.dsem = key
                o.dval = 16 * dcount[key]
            elif o.needed:
                o.ms = cnt[o.eng]
                cnt[o.eng] += 1
        sems = {}
        for e in cnt:
            for ep in range(cnt[e] // EPOCH + 1):
                sems[(e, ep)] = self.stack.enter_context(nc.semaphore("m_%s_%d" % (e, ep)))
        dsems = {}
        for key in dcount:
            dsems[key] = self.stack.enter_context(nc.semaphore("d_%s_%d" % key))
        final_dma = {}
        for i in last_dma:
            final_dma[ops[i].dsem] = max(final_dma.get(ops[i].dsem, 0), ops[i].dval)
        by_eng = {e: [] for e in cnt}
        for o in ops:
            by_eng[o.eng].append(o)
        import os as _os
        dump = _os.environ.get("DUMP")

        def emit(ename, e):
            known = {}
            for o in by_eng[ename]:
                if o.isdma and o.dval > 16:
                    k = ("d",) + o.dsem
                    if known.get(k, 0) < o.dval - 16:
                        e.wait_ge(dsems[o.dsem], o.dval - 16)
                        known[k] = o.dval - 16
                for d in sorted(o.deps):
                    od = ops[d]
                    if od.isdma:
                        k = ("d",) + od.dsem
                        if known.get(k, 0) < od.dval:
                            e.wait_ge(dsems[od.dsem], od.dval)
                            known[k] = od.dval
                            if dump: print("   ", ename, "WAITD", od.dsem, od.dval)
                    else:
                        k = ("m", od.eng)
                        if known.get(k, -1) < od.ms:
                            ep = od.ms // EPOCH
                            e.wait_ge(sems[(od.eng, ep)], od.ms % EPOCH + 1)
                            known[k] = od.ms
                            if dump: print("   ", ename, "WAITM", od.eng, od.ms + 1)
                ins = o.fn(e)
                if dump: print(ename, "OP", o.desc, "ms", o.ms)
                if o.isdma:
                    ins.then_inc(dsems[o.dsem], 16)
                elif o.ms is not None:
                    ins.then_inc(sems[(o.eng, o.ms // EPOCH)], 1)
            if ename == "sp":
                for key, v in final_dma.items():
                    e.wait_ge(dsems[key], v)
                for i in tail_ops:
                    od = ops[i]
                    e.wait_ge(sems[(od.eng, od.ms // EPOCH)], od.ms % EPOCH + 1)

        with nc.Block() as block:
            @block.tensor
            def _(e):
                emit("pe", e)

            @block.scalar
            def _(e):
                emit("act", e)

            @block.vector
            def _(e):
                emit("dve", e)

            @block.gpsimd
            def _(e):
                emit("pool", e)

            @block.sync
            def _(e):
                emit("sp", e)
        self.stack.close()

import os
STOP = os.environ.get('KSTOP', '')
SKIP = os.environ.get('KSKIP', '').split(',')
NTP = 256
NSEQ = int(os.environ.get('KNSEQ', '2'))
KAPPA = 0.6065306597126334
NEGB = -30000.0
RW_DT = F32
C_ID, C_BO, C_BO64, C_ONE, C_M64, C_L64, C_M8, C_L8, C_BSEL, C_END = 0, 128, 256, 384, 512, 768, 896, 1152, 1280, 1296


def host_consts():
    c = np.zeros((128, C_END), np.float32)
    c[:, C_ID:C_ID + 128] = np.eye(128)
    blk = (np.arange(128)[:, None] // 64 == np.arange(128)[None] // 64)
    c[:, C_BO:C_BO + 128] = blk
    c[:, C_BO64:C_BO64 + 128] = blk / 64.0
    c[:, C_ONE:C_ONE + 128] = 1.0
    s = np.arange(128)[:, None]
    t = np.arange(128)[None]
    for (C, cm, cl) in ((64, C_M64, C_L64), (8, C_M8, C_L8)):
        same = (s // C == t // C)
        c[:, cm:cm + 128] = same & (s < t)
        c[:, cm + 128:cm + 256] = same & (s <= t)
        c[:, cl:cl + 128] = same & (s > t)
    c[:, C_BSEL:C_BSEL + 16] = (np.arange(128)[:, None] // 8 == np.arange(16)[None])
    def bucket(d):
        d = np.asarray(d)
        n = np.maximum(d, 0)
        nf = np.maximum(n, 1).astype(np.float32)
        large = 16 + (np.log(nf / np.float32(16)) / np.float32(np.log(128 / 16)) * np.float32(16)).astype(np.int32)
        return np.where(n < 16, n, np.minimum(large, 31))
    oh = np.zeros((33, 2, 384), np.float32)
    m = np.arange(384)
    for x in range(2):
        dist = (m - 128) if x == 0 else m
        valid = (dist >= 0) & (dist < 128) if x == 0 else (m >= 1) & (m < 128)
        b = bucket(np.clip(dist, 0, 255))
        for mm_ in range(384):
            if valid[mm_]:
                oh[b[mm_], x, mm_] = 1.0
            else:
                oh[32, x, mm_] = NEGB
    return c, oh.reshape(33, 768)


class _Stop(Exception):
    pass


def stop_at(tag):
    if STOP == tag:
        raise _Stop()


def build(nc, n_ptiles=16, do_sample=True, dbg=(), n_stiles=None):
    if n_stiles is None:
        n_stiles = 1 if do_sample else 0
    do_sample = n_stiles > 0
    NST = max(n_stiles, 1)
    P = Prog(nc)
    D = {}

    def din(name, shape):
        D[name] = nc.dram_tensor(name, list(shape), F32, kind="ExternalInput").ap()
        return D[name]

    def dout(name, shape):
        D[name] = nc.dram_tensor(name, list(shape), F32, kind="ExternalOutput").ap()
        return D[name]

    xp = din("xp", [NSEQ, 2048, 1024]); xsm = din("xsm", [NST * 128, 1024])
    ck = din("ck", [NST * 16, 128, 128]); cv = din("cv", [NST * 16, 128, 128])
    sshift = din("sshift", [NST * 16, 1792]); swkv = din("swkv", [NST * 16, 8, 64, 64]); sconv = din("sconv", [NST * 32, 2816])
    rel_bias = din("rel_bias", [32, 8]); norm1_g = din("norm1_g", [1, 1024]); w_in = din("w_in", [1024, 4608])
    sinks = din("sinks", [8]); mu_shift = din("mu_shift", [1792]); w0 = din("w0", [512]); w2 = din("w2", [64, 512])
    a0 = din("a0", [512]); a2 = din("a2", [64, 512]); g2 = din("g2", [128, 512]); k_k = din("k_k", [512])
    k_a = din("k_a", [512]); r_k = din("r_k", [512]); lnx_g = din("lnx_g", [512]); lnx_b = din("lnx_b", [512])
    w_pa = din("w_pa", [512, 1024]); w_pb = din("w_pb", [512, 1024]); w_o = din("w_o", [1024, 1024])
    norm2_g = din("norm2_g", [1, 1024]); w_up = din("w_up", [1024, 5632]); conv_w = din("conv_w", [3, 2816])
    conv_b = din("conv_b", [2816]); w_down = din("w_down", [2816, 1024]); final_g = din("final_g", [1, 1024])
    consts_d = din("consts", [128, C_END]); oh_d = din("oh", [33, 768])

    yp = dout("yp", [NSEQ, 2048, 1024]); ys = dout("ys", [NST * 128, 1024])
    pk = dout("pk", [NSEQ, 128, 128]); pv = dout("pv", [NSEQ, 128, 128]); psh = dout("psh", [NSEQ, 1792])
    pwkv = dout("pwkv", [NSEQ, 8, 64, 64]); pconv = dout("pconv", [NSEQ, 2, 2816])
    sk = dout("sk", [NST * 16, 128, 128]); sv = dout("sv", [NST * 16, 128, 128]); ssh = dout("ssh", [NST * 16, 1792])
    swkvo = dout("swkvo", [NST * 16, 8, 64, 64]); sconvo = dout("sconvo", [NST * 32, 2816])
    dbg_out = {}

    def scratch(name, shape, dtype=BF16):
        return nc.dram_tensor(name, list(shape), dtype, kind="Internal").ap()

    wsc_in = scratch("wsc_in", [9, 128, 4096]); wsc_pa = scratch("wsc_pa", [2, 128, 2048])
    wsc_pb = scratch("wsc_pb", [2, 128, 2048]); wsc_o = scratch("wsc_o", [2, 128, 4096])
    wsc_up = scratch("wsc_up", [11, 128, 4096]); wsc_dn = scratch("wsc_dn", [6, 128, 4096])
    E_d = scratch("E_d", [16, 128, 384], F32)

    chunks_src = []
    for j in range(4):
        chunks_src.append([(j * 64, 64), ((4 + j) * 64, 64)])
    chunks_src.append([(512, 128)]); chunks_src.append([(640, 128)])
    rwb = 768
    chunks_src.append([(rwb + 1536, 128)]); chunks_src.append([(rwb + 1664, 128)])
    for p in range(4):
        chunks_src += [[(rwb + p * 128, 128)], [(rwb + 512 + p * 128, 128)], [(rwb + 1024 + p * 128, 128)]]
    for g in range(16):
        chunks_src.append([(2560 + g * 128, 128)])
    for c, srcs in enumerate(chunks_src):
        b, cc = divmod(c, 4)
        dst = wsc_in[b].rearrange("p (k n) -> p k n", n=512)
        off = cc * 128
        for (lo, n) in srcs:
            P.dma("pool", dst[:, :, off:off + n], w_in[:, lo:lo + n].rearrange("(k p) n -> p k n", p=128),
                  writes=[("wsc_in", c, lo)])
            off += n
    for ch in range(2):
        dpa = wsc_pa[ch].rearrange("p (k n) -> p k n", n=512)
        for j in range(4):
            for half in range(2):
                r0 = (half * 4 + j) * 64
                P.dma("pool", dpa[half * 64:(half + 1) * 64, j, :], w_pa[r0:r0 + 64, ch * 512:(ch + 1) * 512],
                      writes=[("wsc_pa", ch, j, half)])
        P.dma("pool", wsc_pb[ch].rearrange("p (k n) -> p k n", n=512),
              w_pb[:, ch * 512:(ch + 1) * 512].rearrange("(k p) n -> p k n", p=128), writes=[("wsc_pb", ch)])
        P.dma("pool", wsc_o[ch].rearrange("p (k n) -> p k n", n=512),
              w_o[:, ch * 512:(ch + 1) * 512].rearrange("(k p) n -> p k n", p=128), writes=[("wsc_o", ch)])
    for b in range(11):
        dst = wsc_up[b].rearrange("p (k n) -> p k n", n=512)
        P.dma("pool", dst[:, :, 0:256], w_up[:, b * 256:(b + 1) * 256].rearrange("(k p) n -> p k n", p=128),
              writes=[("wsc_up", b, 0)])
        P.dma("pool", dst[:, :, 256:512], w_up[:, 2816 + b * 256:2816 + (b + 1) * 256].rearrange("(k p) n -> p k n", p=128),
              writes=[("wsc_up", b, 1)])
    DN_NK = (8, 8, 6)
    for ch in range(2):
        for rg in range(3):
            nk = DN_NK[rg]
            dst = wsc_dn[ch * 3 + rg].rearrange("p (k n) -> p k n", n=512)
            P.dma("pool", dst[:, 0:nk, :],
                  w_down[rg * 1024:rg * 1024 + nk * 128, ch * 512:(ch + 1) * 512].rearrange("(k p) n -> p k n", p=128),
                  writes=[("wsc_dn", ch * 3 + rg)])

    if STOP == 'A':
        P.finish(); return D
    blk_sched = []
    for b in range(9):
        keys = []
        for c in range(4 * b, 4 * b + 4):
            keys += [("wsc_in", c, lo) for (lo, n) in chunks_src[c]]
        blk_sched.append((wsc_in[b], 4096, keys))
    for ch in range(2):
        blk_sched.append((wsc_pa[ch], 2048, [("wsc_pa", ch, j, h) for j in range(4) for h in range(2)]))
        blk_sched.append((wsc_pb[ch], 2048, [("wsc_pb", ch)]))
    for ch in range(2):
        blk_sched.append((wsc_o[ch], 4096, [("wsc_o", ch)]))
    for b in range(11):
        blk_sched.append((wsc_up[b], 4096, [("wsc_up", b, 0), ("wsc_up", b, 1)]))
    for i in range(6):
        blk_sched.append((wsc_dn[i], DN_NK[i % 3] * 512, [("wsc_dn", i)]))
    NBLK_T = len(blk_sched)
    n_tiles_total = n_ptiles + n_stiles
    NSLOT = 3
    ring_tiles = [P.sbuf("ring%d" % i, [128, 4096], BF16) for i in range(NSLOT)]
    ring_state = {"loaded": 0, "consumed": 0}
    total_blocks = NBLK_T * n_tiles_total

    def ring_get(hold=0):
        k = ring_state["consumed"]
        while ring_state["loaded"] < min(k + NSLOT - hold, total_blocks):
            j = ring_state["loaded"]
            src, nel, keys = blk_sched[j % NBLK_T]
            s = j % NSLOT
            P.dma("sp", ring_tiles[s][:, 0:nel], src[:, 0:nel], reads=keys, writes=[("ring", s)])
            ring_state["loaded"] += 1
        ring_state["consumed"] += 1
        s = k % NSLOT
        return ring_tiles[s], ("ring", s)

    cst = P.sbuf("cst", [128, C_END], F32)
    P.dma("sp", cst[:], consts_d, writes=["cst"])
    ident = cst[:, C_ID:C_ID + 128]
    bo = cst[:, C_BO:C_BO + 128]
    bo64 = cst[:, C_BO64:C_BO64 + 128]
    ones = cst[:, C_ONE:C_ONE + 128]
    ones_bf = P.sbuf("ones_bf", [128, 128], BF16)
    P.op("dve", lambda e: e.tensor_copy(ones_bf[:], ones), reads=["cst"], writes=["ones_bf"])
    zeros_t = P.sbuf("zeros_t", [128, 64], F32)
    P.op("pool", lambda e: e.memset(zeros_t[:], 0.0), writes=["zeros_t"])

    def load_cols(name, vec, ncol):
        t = P.sbuf(name, [128, ncol], F32)
        P.dma("sp", t[:], vec.rearrange("(c p) -> p c", p=128), writes=[name], allow_slow_non_contiguous=True)
        return t

    mu_c = load_cols("mu_c", mu_shift, 14)
    om_c = P.sbuf("om_c", [128, 14], F32)
    P.op("dve", lambda e: e.tensor_scalar(om_c[:], mu_c[:], -1.0, 1.0, ALU.mult, ALU.add), reads=["mu_c"], writes=["om_c"])
    w0_c = load_cols("w0_c", w0, 4); a0_c = load_cols("a0_c", a0, 4); kk_c = load_cols("kk_c", k_k, 4)
    ka_c = load_cols("ka_c", k_a, 4); rk_c = load_cols("rk_c", r_k, 4); lg_c = load_cols("lg_c", lnx_g, 4)
    lb_c = load_cols("lb_c", lnx_b, 4); cb_c = load_cols("cb_c", conv_b, 22)
    cw_c = P.sbuf("cw_c", [128, 3, 22], F32)
    for j in range(3):
        P.dma("sp", cw_c[:, j, :], conv_w[j].rearrange("(c p) -> p c", p=128), writes=[("cw_c", j)], allow_slow_non_contiguous=True)
    CW_KEYS = [("cw_c", j) for j in range(3)]
    gb = {}
    g1c = load_cols("g1b", norm1_g.rearrange("a n -> (a n)"), 8)
    g2c = load_cols("g2b", norm2_g.rearrange("a n -> (a n)"), 8)
    gcol = {"g1b": g1c, "g2b": g2c}
    for nm, src in (("gfb", final_g),):
        gb[nm] = P.sbuf(nm, [128, 1024], F32)
        P.dma("sp", gb[nm][:], src.partition_broadcast(128).rearrange("p a n -> p (a n)"), writes=[nm])
    w2b = P.sbuf("w2b", [128, 512], BF16); g2bf = P.sbuf("g2bf", [128, 512], BF16)
    P.dma("pool", w2b[0:64, :], w2, writes=["w2b"])
    P.dma("pool", w2b[64:128, :], a2, writes=["a2b"])
    P.dma("pool", g2bf[:], g2, writes=["g2bf"])
    sk_t = P.sbuf("sk_t", [128, 4], F32)
    P.dma("sp", sk_t[0:64, :], sinks[0:4].partition_broadcast(64), writes=[("sk_t", 0)])
    P.dma("sp", sk_t[64:128, :], sinks[4:8].partition_broadcast(64), writes=[("sk_t", 1)])
    esk = P.sbuf("esk", [128, 4], F32)
    P.op("act", lambda e: e.activation(esk[:], sk_t[:], AF.Exp), reads=[("sk_t", 0), ("sk_t", 1)], writes=["esk"])
    esink = P.sbuf("esink", [128, 4, 128], F32)
    for g in range(4):
        P.op("act", lambda e, g=g: e.activation(esink[:, g, :], ones, AF.Copy, scale=esk[:, g:g + 1]),
             reads=["cst", "esk"], writes=[("esink", g)])
    ESINK_KEYS = [("esink", g) for g in range(4)]

    xn = P.sbuf("xn", [128, 1024], F32)
    if STOP == 'B':
        P.finish(); return D
    for _i in range(int(os.environ.get('KDUMMY', '0'))):
        P.dma('sp', zeros_t[:, 0:32], consts_d[:, 0:32], writes=['zeros_dummy'])
    for _i in range(int(os.environ.get('KBIG', '0'))):
        P.dma('sp', yp[1], xp[0], writes=['yp1_dummy'])
    BK = [P.psum("bank%d" % i, [128, 512], F32) for i in range(8)]

    def bk(i):
        return ("ps", i)

    rb = P.sbuf("rb", [33, 8], F32)
    Yt = P.sbuf("Yt", [128, 4, NTP], F32)
    bon = P.sbuf("bon", [128, 4, NTP], F32)
    Lh = Yt[0:33, :, :].rearrange("p a t -> p (a t)").rearrange("p (h t) -> p h t", t=128)
    oh_t = bon[0:33, :, :].rearrange("p a t -> p (a t)")[:, 0:768]
    P.dma("sp", rb[0:32, :], rel_bias, writes=["rb"])
    P.dma("sp", oh_t, oh_d, writes=["oh_t"])
    P.op("pool", lambda e: e.memset(Lh[32:33, :, :], 1.0), writes=["Lh1"])
    for h in range(8):
        P.op("act", lambda e, h=h: e.activation(Lh[0:32, h, :], cst[0:32, C_ONE:C_ONE + 128], AF.Copy, scale=rb[0:32, h:h + 1]),
             reads=["cst", "rb"], writes=[("Lh", h)])
    biasT = [[P.sbuf("biasT%d%d" % (x, kv), [128, 4, 128], F32) for kv in range(2)] for x in range(2)]
    for x in range(2):
        for h in range(8):
            bnk = 4 + (x * 8 + h) % 4
            P.op("pe", lambda e, h=h, x=x, bnk=bnk: e.matmul(BK[bnk][:, 0:384], Lh[:, h, :], oh_t[:, x * 384:(x + 1) * 384], start=True, stop=True),
                 reads=["Lh1", ("Lh", h), "oh_t"], writes=[bk(bnk)])
            P.op("dve", lambda e, bnk=bnk: e.tensor_copy(xn[:, 0:384], BK[bnk][:, 0:384]), reads=[bk(bnk)], writes=["xn"])
            P.dma("sp", E_d[x * 8 + h], xn[:, 0:384], reads=["xn"], writes=[("E_d", x, h)])
            kv, g = divmod(h, 4)
            skew = bass.AP(E_d.tensor, (x * 8 + h) * 128 * 384 + 128, [[383, 128], [1, 128]])
            P.dma("sp", biasT[x][kv][:, g, :], skew, reads=[("E_d", x, h)], writes=[("biasT", x, kv, g)])
    if STOP == 'C':
        P.finish(); return D
    if os.environ.get('NOBAR') is None:
        P.barrier()
    BIAS_KEYS = {(x, kv): [("biasT", x, kv, g) for g in range(4)] for x in range(2) for kv in range(2)}

    wstack = [ExitStack()]

    def wsb(name, shape, dtype):
        return wstack[0].enter_context(nc.sbuf_tensor(name, list(shape), dtype))

    xn2 = None
    Pvg = None
    xt = None
    stat = None
    hT = None
    qT = None
    kT = None
    kTf = None
    vTf = None
    vtok = None
    kvrow = None
    gates = None
    attT = None
    rwoT = None
    PT = None
    rd_t = None
    pbuf = None
    tmpA = None
    xs3 = None
    TW = None
    SG = None
    car = None
    art = None
    kt_ = None
    bt_ = None
    gT = None
    cCt = None
    KPtok = None
    BPtok = None
    Vtok = None
    rw = None
    S = None
    AK = None
    AB = None
    Mm_ = None
    Mt_ = None
    Pv = None
    VA = None
    VK = None
    XT = None
    UT = None
    y1 = None
    ugx = None
    cc_t = None
    gl_t = None
    actT = None
    ccar = None

    def alloc_work(NTM, sfx):
        nonlocal xn2, Pvg, xt, stat, hT, qT, kT, kTf, vTf, vtok, kvrow, gates, attT, rwoT, PT, rd_t, pbuf, tmpA, xs3, TW, SG, car, art, kt_, bt_, gT, cCt, KPtok, BPtok, Vtok, rw, S, AK, AB, Mm_, Mt_, Pv, VA, VK, XT, UT, y1, ugx, cc_t, gl_t, actT, ccar
        NB_MAX = NTM // 128
        xt = [wsb(("xt%d" % i) + sfx, [128, 1024], F32) for i in range(NB_MAX)]
        stat = wsb("stat" + sfx, [128, 32], F32)
        xn2 = wsb("xn2" + sfx, [128, 1024], F32) if NTM > 128 else None
        hT = wsb("hT" + sfx, [128, 8, NTM], BF16)
        qT = wsb("qT" + sfx, [128, 4, NTM], BF16)
        kT = wsb("kT" + sfx, [128, 128 + NTM], BF16)
        kTf = wsb("kTf" + sfx, [128, NTM], F32)
        vTf = wsb("vTf" + sfx, [128, NTM], F32)
        vtok = wsb("vtok" + sfx, [128, NB_MAX + 1, 128], BF16)
        kvrow = wsb("kvrow" + sfx, [128, 2, 128], F32)
        gates = wsb("gates" + sfx, [128, 16, NTM], BF16)
        attT = wsb("attT" + sfx, [128, 4, NTM], BF16)
        rwoT = wsb("rwoT" + sfx, [128, 4, NTM], BF16)
        PT = [wsb(("PT%d" % i) + sfx, [128, 512], BF16) for i in range(2)]
        rd_t = wsb("rd_t" + sfx, [128, 512], F32)
        pbuf = wsb("pbuf" + sfx, [128, NTM + 16], F32)
        tmpA = wsb("tmpA" + sfx, [128, NTM], F32)
        xs3 = [wsb(("xs3_%d" % i) + sfx, [128, NTM], F32) for i in range(3)]
        TW = wsb("TW" + sfx, [128, NTM], BF16)
        SG = wsb("SG" + sfx, [128, NTM], BF16)
        car = wsb("car" + sfx, [128, 128], F32)
        art = wsb("art" + sfx, [128, 4, 2, NTM], RW_DT)
        kt_ = wsb("kt_" + sfx, [128, 4, NTM], RW_DT)
        bt_ = wsb("bt_" + sfx, [128, 4, NTM], RW_DT)
        gT = wsb("gT" + sfx, [128, 4, NTM], F32)
        cCt = wsb("cCt" + sfx, [128, 4, NTM // 8], F32)
        KPtok = wsb("KPtok" + sfx, [128, NB_MAX, 512], RW_DT)
        BPtok = wsb("BPtok" + sfx, [128, NB_MAX, 512], RW_DT)
        Vtok = wsb("Vtok" + sfx, [128, NB_MAX, 512], RW_DT)
        rw = {n: wsb("rw_" + sfx + n, [128, NTM], F32) for n in ("sg", "cs", "t1", "t2", "t3", "a", "kkn", "k2", "ein", "einv", "eend", "nk")}
        S = wsb("S" + sfx, [128, 4, 64], F32)
        AK = [wsb(("AK%d" % h) + sfx, [128, 256], RW_DT) for h in range(8)]
        AB = [wsb(("AB%d" % h) + sfx, [128, 256], RW_DT) for h in range(8)]
        Mm_ = [wsb(("Mmg%d" % g) + sfx, [128, 4, 128], RW_DT) for g in range(2)]
        Mt_ = [wsb(("Mtg%d" % g) + sfx, [128, 4, 128], RW_DT) for g in range(2)]
        Pvg = [wsb(("Pvg%d" % g) + sfx, [128, 4, 128], RW_DT) for g in range(2)]
        Pv = [Pvg[h // 4][:, h % 4, :] for h in range(8)]
        VA = wsb("VA" + sfx, [128, 512], F32)
        VK = wsb("VK" + sfx, [128, 4, 128], F32)
        XT = wsb("XT" + sfx, [128, 512], RW_DT)
        UT = wsb("UT" + sfx, [128, 512], RW_DT)
        y1 = wsb("y1" + sfx, [128, 4, 64], F32)
        ugx = [wsb(("ugx%d" % i) + sfx, [128, NTM + 40], F32) for i in range(2)]
        cc_t = [wsb(("cc_t%d" % i) + sfx, [128, NTM], F32) for i in range(2)]
        gl_t = [wsb(("gl_t%d" % i) + sfx, [128, NTM], F32) for i in range(2)]
        actT = wsb("actT" + sfx, [128, 22, NTM], BF16)
        ccar = wsb("ccar" + sfx, [128, 2, 128], F32)

    P.stack.callback(lambda: wstack[0].close())
    alloc_work(NTP, "")
    P.op("pool", lambda e: e.memset(car[:], 0.0), writes=["car_init"])
    P.op("pool", lambda e: e.memset(ccar[:].rearrange("p a f -> p (a f)"), 0.0), writes=["ccar_init"])

    def debug_tap(name, ap_sb, shape, keys):
        if name in dbg:
            d_ = nc.dram_tensor("dbg_" + name, list(shape), ap_sb.dtype, kind="ExternalOutput").ap()
            D["dbg_" + name] = d_
            P.dma("sp", d_, ap_sb, reads=keys, writes=[("dbg", name)])

    def do_tile(kind, seq, ti):
        prompt = kind == "p"
        NT = NTP if prompt else 128
        nb = NT // 128
        NBs, TB = (1, NT) if prompt else (16, 8)
        C = 64 if prompt else 8
        LV = 5 if prompt else 2
        cm, cl = (C_M64, C_L64) if prompt else (C_M8, C_L8)
        first = prompt and ti == 0
        last = prompt and ti == 2048 // NTP - 1
        xrows = xp[seq, ti * NT:(ti + 1) * NT, :] if prompt else xsm[seq * 128:(seq + 1) * 128, :]
        yrows = yp[seq, ti * NT:(ti + 1) * NT, :] if prompt else ys[seq * 128:(seq + 1) * 128, :]

        for b in range(nb):
            P.dma("sp", xt[b][:], xrows[b * 128:(b + 1) * 128, :], writes=[("xt", b)])

        stop_at("D1")

        def norm_T(gname):
            def steps(b):
                so = 16 * b
                st = stat[:, so:so + 16]
                xnb = xn if b == 0 else xn2
                xk_ = "xn" if b == 0 else "xn2"
                sk_ = "stat%d_" % b
                bks = (4, 5) if b == 0 else (6, 7)

                def tr(half):
                    for q4 in range(4):
                        kc = half * 4 + q4
                        P.op("pe", lambda e, kc=kc, q4=q4: e.transpose(BK[bks[half]][:, q4 * 128:(q4 + 1) * 128], xnb[:, kc * 128:(kc + 1) * 128], ident),
                             reads=[xk_, "cst"], writes=[bk(bks[half])])

                def ev(half):
                    for q4 in range(4):
                        kc = half * 4 + q4
                        if half == 0:
                            P.op("act", lambda e, kc=kc, q4=q4: e.activation(hT[:, kc, b * 128:(b + 1) * 128], BK[bks[0]][:, q4 * 128:(q4 + 1) * 128], AF.Copy, scale=gcol[gname][:, kc:kc + 1]),
                                 reads=[bk(bks[0]), gname], writes=[("hT", kc)])
                        else:
                            P.op("dve", lambda e, kc=kc, q4=q4: e.tensor_scalar(hT[:, kc, b * 128:(b + 1) * 128], BK[bks[1]][:, q4 * 128:(q4 + 1) * 128], gcol[gname][:, kc:kc + 1], None, ALU.mult),
                                 reads=[bk(bks[1]), gname], writes=[("hT", kc)])
                return [
                    lambda: P.op("dve", lambda e: e.bn_stats(st[:, 0:6], xt[b][:, 0:512]), reads=[("xt", b)], writes=[sk_ + "0"]),
                    lambda: P.op("dve", lambda e: e.bn_stats(st[:, 6:12], xt[b][:, 512:1024]), reads=[("xt", b)], writes=[sk_ + "1"]),
                    lambda: P.op("dve", lambda e: e.bn_aggr(st[:, 12:14], st[:, 0:12]), reads=[sk_ + "0", sk_ + "1"], writes=[sk_ + "2"]),
                    lambda: P.op("dve", lambda e: e.scalar_tensor_tensor(st[:, 14:15], st[:, 12:13], st[:, 12:13], st[:, 13:14], ALU.mult, ALU.add), reads=[sk_ + "2"], writes=[sk_ + "3"]),
                    lambda: P.op("act", lambda e: e.activation(st[:, 15:16], st[:, 14:15], AF.Sqrt, bias=1e-6), reads=[sk_ + "3"], writes=[sk_ + "4"]),
                    lambda: P.op("dve", lambda e: e.reciprocal(st[:, 15:16], st[:, 15:16]), reads=[sk_ + "4"], writes=[sk_ + "4"]),
                    lambda: P.op("dve", lambda e: e.tensor_scalar(xnb[:], xt[b][:], st[:, 15:16], None, ALU.mult), reads=[("xt", b), sk_ + "4"], writes=[xk_]),
                    lambda: tr(0), lambda: tr(1), lambda: ev(0), lambda: ev(1),
                ]
            for group in zip(*[steps(b) for b in range(nb)]):
                for step in group:
                    step()
        HT_KEYS = [("hT", kc) for kc in range(8)]

        norm_T("g1b")
        stop_at("D")

        def v3(ap2, k):
            return ap2.rearrange("p (b t) -> p b t", t=k)

        ts_ctr = [0]

        def token_shift(bnk, n, dst, dst_key):
            ti_ = ts_ctr[0] % 2
            ts_ctr[0] += 1
            tA = tmpA if ti_ == 0 else pbuf
            tkey, tkey0 = ("tmpA", ti_), ("tmpA0", ti_)
            PS3 = v3(BK[bnk][:, 0:NT], TB)
            tA3 = v3(tA[:, 0:NT], TB)
            P.op("act", lambda e: e.activation(tA3[:, :, 1:TB], PS3[:, :, 0:TB - 1], AF.Copy, scale=mu_c[:, n:n + 1]), reads=[bk(bnk), "mu_c"], writes=[tkey])
            if prompt:
                if first:
                    P.op("pool", lambda e: e.memset(tA[:, 0:1], 0.0), writes=[tkey0])
                else:
                    P.op("pool", lambda e: e.tensor_scalar(tA[:, 0:1], car[:, n:n + 1], mu_c[:, n:n + 1], None, ALU.mult), reads=[("car", n), "mu_c"], writes=[tkey0])
            else:
                P.op("pool", lambda e: e.tensor_scalar(tA3[:, :, 0:1], SB["shcar"][:, n, :].unsqueeze(2), mu_c[:, n:n + 1], None, ALU.mult), reads=[("shcar", n), "mu_c"], writes=[tkey0])
            P.op("dve", lambda e: e.scalar_tensor_tensor(v3(dst, TB), PS3, om_c[:, n:n + 1], tA3, ALU.mult, ALU.add),
                 reads=[bk(bnk), tkey, tkey0, "om_c"], writes=[dst_key])
            if prompt:
                P.op("dve", lambda e: e.tensor_copy(car[:, n:n + 1], BK[bnk][:, NT - 1:NT]), reads=[bk(bnk)], writes=[("car", n)])
            else:
                P.op("dve", lambda e: e.tensor_copy(SB["shout"][:, n, :].unsqueeze(2), PS3[:, :, TB - 1:TB]), reads=[bk(bnk)], writes=[("shout", n)])

        def pair_process(p):
            xr, xk, xv = xs3[0][:, 0:NT], xs3[1][:, 0:NT], xs3[2][:, 0:NT]
            R = {n: rw[n][:, 0:NT] for n in rw}
            nch = NT // C
            b6, b7 = 6, 7
            P.op("pool", lambda e: e.tensor_scalar(R["kkn"], xk, kk_c[:, p:p + 1], None, ALU.mult), reads=["xs1", "kk_c"], writes=["r_kkn"])
            P.op("pool", lambda e: e.tensor_tensor(R["t2"], R["kkn"], R["kkn"], ALU.mult), reads=["r_kkn"], writes=["r_t2"])
            P.op("pe", lambda e: e.matmul(BK[b6][:, 0:NT], w2b[0:64, p * 128:(p + 1) * 128], TW[0:64, 0:NT], start=True, stop=True),
                 reads=["w2b", "TW"], writes=[bk(b6)])
            P.op("pe", lambda e: e.matmul(BK[b7][:, 0:NT], w2b[64:128, p * 128:(p + 1) * 128], TW[64:128, 0:NT], start=True, stop=True),
                 reads=["a2b", "TW"], writes=[bk(b7)])
            P.op("act", lambda e: e.activation(R["sg"], BK[b6][:, 0:NT], AF.Sigmoid, bias=w0_c[:, p:p + 1]), reads=[bk(b6), "w0_c"], writes=["r_sg"])
            P.op("act", lambda e: e.activation(R["a"], BK[b7][:, 0:NT], AF.Sigmoid, bias=a0_c[:, p:p + 1]), reads=[bk(b7), "a0_c"], writes=["r_a"])
            P.op("pe", lambda e: e.matmul(BK[b6][:, 0:NT], bo, R["t2"], start=True, stop=True), reads=["cst", "r_t2"], writes=[bk(b6)])
            for b in range(nb):
                tb_ = 4 + b % 2
                P.op("pe", lambda e, b=b, tb_=tb_: e.transpose(BK[tb_][:, 0:128], xs3[2][:, b * 128:(b + 1) * 128], ident), reads=["xs2", "cst"], writes=[bk(tb_)])
                P.op("act", lambda e, b=b, tb_=tb_: e.copy(Vtok[:, b, p * 128:(p + 1) * 128], BK[tb_][:, 0:128]), reads=[bk(tb_)], writes=[("Vtok", b, p)])
            for c in range(nch):
                P.op("dve", lambda e, c=c: e.tensor_tensor_scan(R["cs"][:, c * C:(c + 1) * C], ones[:, 0:C], R["sg"][:, c * C:(c + 1) * C], 0.0, ALU.mult, ALU.add),
                     reads=["r_sg", "cst"], writes=["r_cs"])
            P.op("act", lambda e: e.activation(R["t2"], BK[b6][:, 0:NT], AF.Sqrt), reads=[bk(b6)], writes=["r_t2"])
            P.op("dve", lambda e: e.tensor_scalar(R["t3"], R["a"], -1.0, ka_c[:, p:p + 1], ALU.add, ALU.mult), reads=["r_a", "ka_c"], writes=["r_t3"])
            P.op("dve", lambda e: e.scalar_tensor_tensor(R["k2"], R["t3"], 1.0, xk, ALU.add, ALU.mult), reads=["r_t3", "xs1"], writes=["r_k2"])
            P.op("act", lambda e: e.activation(R["ein"], R["cs"], AF.Exp, scale=-KAPPA), reads=["r_cs"], writes=["r_ein"])
            P.op("act", lambda e: e.activation(R["einv"], R["cs"], AF.Exp, scale=KAPPA), reads=["r_cs"], writes=["r_einv"])
            P.op("pool", lambda e: e.tensor_tensor(R["t1"], R["cs"], R["sg"], ALU.subtract), reads=["r_cs", "r_sg"], writes=["r_t1"])
            P.op("pool", lambda e: e.tensor_scalar(R["nk"][:, 0:nch], R["cs"][:, C - 1:NT:C], -KAPPA, None, ALU.mult), reads=["r_cs"], writes=["r_nk"])
            P.op("act", lambda e: e.activation(R["t1"], R["t1"], AF.Exp, scale=-KAPPA), reads=["r_t1"], writes=["r_t1"])
            for c in range(nch):
                P.op("act", lambda e, c=c: e.activation(R["eend"][:, c * C:(c + 1) * C], R["cs"][:, c * C:(c + 1) * C], AF.Exp, scale=KAPPA, bias=R["nk"][:, c:c + 1]),
                     reads=["r_cs", "r_nk"], writes=["r_eend"])
            P.op("dve", lambda e: e.tensor_scalar(R["t2"], R["t2"], 1e-12, None, ALU.max), reads=["r_t2"], writes=["r_t2"])
            P.op("dve", lambda e: e.reciprocal(R["t2"], R["t2"]), reads=["r_t2"], writes=["r_t2"])
            P.op("dve", lambda e: e.tensor_tensor(R["kkn"], R["kkn"], R["t2"], ALU.mult), reads=["r_kkn", "r_t2"], writes=["r_kkn"])
            P.op("pool", lambda e: e.tensor_copy(cCt[:, p, 0:nch], R["ein"][:, C - 1:NT:C]), reads=["r_ein"], writes=[("cCt", p)])
            P.op("pool", lambda e: e.tensor_tensor(art[:, p, 1, 0:NT], xr, R["ein"], ALU.mult), reads=["xs0", "r_ein"], writes=[("art", p)])
            P.op("dve", lambda e: e.tensor_tensor(kt_[:, p, 0:NT], R["k2"], R["einv"], ALU.mult), reads=["r_k2", "r_einv"], writes=[("kt", p)])
            P.op("dve", lambda e: e.scalar_tensor_tensor(art[:, p, 0, 0:NT], R["kkn"], -1.0, R["t1"], ALU.mult, ALU.mult), reads=["r_kkn", "r_t1"], writes=[("art", p)])
            P.op("pool", lambda e: e.tensor_tensor(R["t3"], R["kkn"], R["a"], ALU.mult), reads=["r_kkn", "r_a", "r_k2"], writes=["r_t3"])
            P.op("dve", lambda e: e.tensor_tensor(bt_[:, p, 0:NT], R["t3"], R["einv"], ALU.mult), reads=["r_t3", "r_einv"], writes=[("bt", p)])
            P.op("pool", lambda e: e.tensor_tensor(R["t2"], R["k2"], R["eend"], ALU.mult), reads=["r_k2", "r_eend", "r_kkn"], writes=["r_t2"])
            P.op("dve", lambda e: e.scalar_tensor_tensor(R["t1"], xr, rk_c[:, p:p + 1], R["k2"], ALU.mult, ALU.mult), reads=["xs0", "rk_c", "r_k2", ("art", p)], writes=["r_t1"])
            P.op("pool", lambda e: e.tensor_tensor(R["t3"], R["t3"], R["eend"], ALU.mult), reads=["r_t3", "r_eend", ("bt", p)], writes=["r_t3"])
            for b in range(nb):
                tb_ = 4 + b % 2
                P.op("pe", lambda e, b=b, tb_=tb_: e.transpose(BK[tb_][:, 0:128], R["t2"][:, b * 128:(b + 1) * 128], ident), reads=["r_t2", "cst"], writes=[bk(tb_)])
                P.op("act", lambda e, b=b, tb_=tb_: e.copy(KPtok[:, b, p * 128:(p + 1) * 128], BK[tb_][:, 0:128]), reads=[bk(tb_)], writes=[("KPtok", b, p)])
            P.op("pe", lambda e: e.matmul(BK[b7][:, 0:NT], bo, R["t1"], start=True, stop=True), reads=["cst", "r_t1"], writes=[bk(b7)])
            for b in range(nb):
                tb_ = 4 + b % 2
                P.op("pe", lambda e, b=b, tb_=tb_: e.transpose(BK[tb_][:, 0:128], R["t3"][:, b * 128:(b + 1) * 128], ident), reads=["r_t3", "cst"], writes=[bk(tb_)])
                P.op("act", lambda e, b=b, tb_=tb_: e.copy(BPtok[:, b, p * 128:(p + 1) * 128], BK[tb_][:, 0:128]), reads=[bk(tb_)], writes=[("BPtok", b, p)])
            P.op("dve", lambda e: e.tensor_tensor(bon[:, p, 0:NT], BK[b7][:, 0:NT], xv, ALU.mult), reads=[bk(b7), "xs2"], writes=[("bon", p)])
            P.op("pe", lambda e: e.matmul(BK[b6][:, 0:NT], g2bf[:, p * 128:(p + 1) * 128], SG[:, 0:NT], start=True, stop=True), reads=["g2bf", "SG"], writes=[bk(b6)])
            P.op("act", lambda e: e.copy(gT[:, p, 0:NT], BK[b6][:, 0:NT]), reads=[bk(b6)], writes=[("gT", p)])

        for blkb in range(9):
            slot, skey = ring_get()
            sl3 = slot[:, 0:4096].rearrange("p (k n) -> p k n", n=512)
            for cc in range(4):
                c = blkb * 4 + cc
                bnk = c % 4
                for kc in range(8):
                    P.op("pe", lambda e, kc=kc, cc=cc, bnk=bnk, sl3=sl3: e.matmul(BK[bnk][:, 0:NT], sl3[:, kc, cc * 128:(cc + 1) * 128], hT[:, kc, 0:NT], start=(kc == 0), stop=(kc == 7)),
                         reads=[skey] + HT_KEYS, writes=[bk(bnk)])
                stop_at("E%d" % c)
                if c < 4:
                    P.op("act", lambda e, c=c, bnk=bnk: e.copy(qT[:, c, 0:NT], BK[bnk][:, 0:NT]), reads=[bk(bnk)], writes=[("qT", c)])
                elif c == 4:
                    P.op("act", lambda e, bnk=bnk: e.copy(kTf[:, 0:NT], BK[bnk][:, 0:NT]), reads=[bk(bnk)], writes=["kTf"])
                    P.op("pool", lambda e: e.tensor_copy(kT[:, 128:128 + NT], kTf[:, 0:NT]), reads=["kTf"], writes=["kT"])
                elif c == 5:
                    P.op("act", lambda e, bnk=bnk: e.copy(vTf[:, 0:NT], BK[bnk][:, 0:NT]), reads=[bk(bnk)], writes=["vTf"])
                    for b in range(nb):
                        tb_ = 4 + b % 2
                        P.op("pe", lambda e, b=b, tb_=tb_: e.transpose(BK[tb_][:, 0:128], vTf[:, b * 128:(b + 1) * 128], ident), reads=["vTf", "cst"], writes=[bk(tb_)])
                        P.op("dve", lambda e, b=b, tb_=tb_: e.tensor_copy(vtok[:, 1 + b, :], BK[tb_][:, 0:128]), reads=[bk(tb_)], writes=[("vtok", 1 + b)])
                        if (prompt and last and b == nb - 1) or not prompt:
                            P.op("dve", lambda e, tb_=tb_: e.tensor_copy(kvrow[:, 1, :], BK[tb_][:, 0:128]), reads=[bk(tb_)], writes=[("kvrow", 1)])
                elif c == 6:
                    token_shift(bnk, 12, xs3[0][:, 0:NT], "xs0")
                    P.op("act", lambda e: e.activation(TW[0:64, 0:NT], xs3[0][0:64, 0:NT], AF.Tanh), reads=["xs0"], writes=["TW"])
                    P.op("pool", lambda e: e.tensor_copy(TW[64:128, 0:NT], xs3[0][64:128, 0:NT]), reads=["xs0"], writes=["TW"])
                elif c == 7:
                    token_shift(bnk, 13, xs3[0][:, 0:NT], "xs0")
                    P.op("act", lambda e: e.activation(SG[:, 0:NT], xs3[0][:, 0:NT], AF.Sigmoid), reads=["xs0"], writes=["SG"])
                elif c < 20:
                    p, which = divmod(c - 8, 3)
                    token_shift(bnk, which * 4 + p, xs3[which][:, 0:NT], "xs%d" % which)
                    if which == 2:
                        pair_process(p)
                else:
                    gi = c - 20
                    P.op("act", lambda e, gi=gi, bnk=bnk: e.activation(gates[:, gi, 0:NT], BK[bnk][:, 0:NT], AF.Sigmoid), reads=[bk(bnk)], writes=[("gates", gi)])

        stop_at("E")
        debug_tap("qT", qT[:, :, 0:NT], [128, 4, NT], [("qT", c) for c in range(4)])

        if prompt:
            for b in range(nb):
                gbk = ti * nb + b
                for kv in range(2):
                    ph = slice(kv * 64, kv * 64 + 64)
                    qv = qT[ph, :, b * 128:(b + 1) * 128]
                    xs_ = [0] + ([1] if gbk > 0 else [])
                    for x in xs_:
                        kcols = slice(128 + b * 128, 256 + b * 128) if x == 0 else slice(b * 128, 128 + b * 128)
                        bnk = 4 + x
                        P.op("pe", lambda e, kcols=kcols, bnk=bnk, qv=qv: e.matmul(BK[bnk][:], kT[ph, kcols], qv, start=True, stop=True),
                             reads=["kT"] + [("qT", c) for c in range(4)], writes=[bk(bnk)])
                        P.op("dve", lambda e, x=x, bnk=bnk: e.scalar_tensor_tensor(BK[bnk][:], BK[bnk][:], 0.125, biasT[x][kv][:].rearrange("p g q -> p (g q)"), ALU.mult, ALU.add),
                             reads=[bk(bnk)] + BIAS_KEYS[(x, kv)], writes=[bk(bnk)])
                        P.op("act", lambda e, x=x: e.activation(PT[x][:], BK[4 + x][:], AF.Exp), reads=[bk(4 + x)], writes=[("PT", x)])
                    for i, x in enumerate(xs_):
                        vb = 1 + b if x == 0 else b
                        P.op("pe", lambda e, x=x, vb=vb, i=i: e.matmul(BK[6][:], vtok[:, vb, :], PT[x][:], start=(i == 0), stop=(i == len(xs_) - 1)),
                             reads=[("vtok", vb), ("PT", x)], writes=[bk(6)])
                    for i, x in enumerate(xs_):
                        P.op("pe", lambda e, x=x, i=i: e.matmul(BK[7][:], ones_bf[:], PT[x][:], start=(i == 0), stop=(i == len(xs_) - 1)),
                             reads=["ones_bf", ("PT", x)], writes=[bk(7)])
                    P.op("dve", lambda e: e.tensor_tensor(rd_t[ph, :], BK[7][ph, :], esink[ph, :, :].rearrange("p g q -> p (g q)"), ALU.add),
                         reads=[bk(7)] + ESINK_KEYS, writes=["rd_t"])
                    P.op("dve", lambda e: e.reciprocal(rd_t[ph, :], rd_t[ph, :]), reads=["rd_t"], writes=["rd_t"])
                    P.op("dve", lambda e, b=b: e.tensor_tensor(attT[ph, :, b * 128:(b + 1) * 128], BK[6][ph, :].rearrange("p (g q) -> p g q", q=128), rd_t[ph, :].rearrange("p (g q) -> p g q", q=128), ALU.mult),
                         reads=[bk(6), "rd_t"], writes=[("attT", kv, b)])
            P.op("pool", lambda e: e.tensor_copy(kT[:, 0:128], kT[:, NT:NT + 128]), reads=["kT"], writes=["kT"])
            P.op("pool", lambda e: e.tensor_copy(vtok[:, 0, :], vtok[:, nb, :]), reads=[("vtok", nb)], writes=[("vtok", 0)])
            if last and 'pk' not in SKIP:
                P.op("pe", lambda e: e.transpose(BK[4][:, 0:128], kTf[:, NT - 128:NT], ident), reads=["kTf", "cst"], writes=[bk(4)])
                P.op("act", lambda e: e.copy(kvrow[:, 0, :], BK[4][:, 0:128]), reads=[bk(4)], writes=[("kvrow", 0)])
                P.dma("sp", pk[seq], kvrow[:, 0, :], reads=[("kvrow", 0)], writes=["pk"])
                P.dma("sp", pv[seq], kvrow[:, 1, :], reads=[("kvrow", 1)], writes=["pv"])
        else:
            sample_attention()
        ATT_KEYS = [("attT", kv, b) for kv in range(2) for b in range(nb)]
        stop_at("F")
        debug_tap("attT", attT[:, :, 0:NT], [128, 4, NT], ATT_KEYS)

        def hcol(h):
            return (h % 2) * 256 + (h // 2) * 64

        def rw_pre(b):
            bc = slice(b * 128, (b + 1) * 128)
            for h in range(8):
                p, h2 = divmod(h, 2)
                ph = slice(h2 * 64, h2 * 64 + 64)
                b0, b1, b2 = (4, 5, 6) if h % 2 == 0 else (0, 1, 2)
                P.op("pe", lambda e, p=p, ph=ph: e.matmul(BK[b0][:, 0:256], kt_[ph, p, bc], art[ph, p, :, bc], start=True, stop=True),
                     reads=[("kt", p), ("art", p)], writes=[bk(b0)])
                P.op("dve", lambda e, h=h: e.tensor_tensor(AK[h][:], BK[b0][:, 0:256], cst[:, cm:cm + 256], ALU.mult), reads=[bk(b0), "cst"], writes=[("AK", h)])
                P.op("pe", lambda e, p=p, ph=ph: e.matmul(BK[b1][:, 0:256], bt_[ph, p, bc], art[ph, p, :, bc], start=True, stop=True),
                     reads=[("bt", p), ("art", p)], writes=[bk(b1)])
                P.op("dve", lambda e, h=h: e.tensor_tensor(AB[h][:], BK[b1][:, 0:256], cst[:, cm:cm + 256], ALU.mult), reads=[bk(b1), "cst"], writes=[("AB", h)])
                P.op("pe", lambda e, p=p, ph=ph: e.matmul(BK[b2][:, 0:128], art[ph, p, 0, bc], bt_[ph, p, bc], start=True, stop=True),
                     reads=[("bt", p), ("art", p)], writes=[bk(b2)])
                P.op("dve", lambda e, h=h: e.tensor_tensor(Mt_[h // 4][:, h % 4, :], BK[b2][:, 0:128], cst[:, cl:cl + 128], ALU.mult), reads=[bk(b2), "cst"], writes=[("Mt", h // 4)])
                P.op("pool", lambda e, h=h: e.tensor_tensor(Pv[h], AB[h][:, 0:128], ident, ALU.add), reads=[("AB", h), "cst"], writes=[("Pv", h)])
            for lv in range(1, LV + 1):
                lastlv = lv == LV
                for g in range(2):
                    bMT, bM, bP = (4, 5, 6) if g == 0 else (0, 1, 2)
                    for j in range(4):
                        h = 4 * g + j
                        Mcur = AB[h][:, 0:128] if lv == 1 else Mm_[g][:, j, :]
                        mk_c = ("AB", h) if lv == 1 else ("Mm", g)
                        Mtcur = Mt_[g][:, j, :]
                        P.op("pe", lambda e, Mcur=Mcur, Mtcur=Mtcur, j=j: e.matmul(BK[bMT][:, j * 128:(j + 1) * 128], Mcur, Mtcur, start=True, stop=True),
                             reads=[mk_c, ("Mt", g)], writes=[bk(bMT)])
                        if not lastlv:
                            P.op("pe", lambda e, Mcur=Mcur, Mtcur=Mtcur, j=j: e.matmul(BK[bM][:, j * 128:(j + 1) * 128], Mtcur, Mcur, start=True, stop=True),
                                 reads=[mk_c, ("Mt", g)], writes=[bk(bM)])
                    P.op("act", lambda e, g=g: e.copy(Mt_[g][:].rearrange("p a t -> p (a t)"), BK[bMT][:]), reads=[bk(bMT)], writes=[("Mt", g)])
                    if not lastlv:
                        P.op("act", lambda e, g=g: e.copy(Mm_[g][:].rearrange("p a t -> p (a t)"), BK[bM][:]), reads=[bk(bM)], writes=[("Mm", g)])
                    for j in range(4):
                        h = 4 * g + j
                        P.op("pe", lambda e, j=j, h=h, g=g: e.matmul(BK[bP][:, j * 128:(j + 1) * 128], Mt_[g][:, j, :], Pv[h], start=True, stop=True),
                             reads=[("Mt", g), ("Pv", h)], writes=[bk(bP)])
                    P.op("dve", lambda e, g=g: e.tensor_tensor(Pvg[g][:].rearrange("p a t -> p (a t)"), BK[bP][:], Pvg[g][:].rearrange("p a t -> p (a t)"), ALU.add),
                         reads=[bk(bP)] + [("Pv", 4 * g + j) for j in range(4)], writes=[("Pv", 4 * g + j) for j in range(4)])
            for h in range(8):
                P.op("pe", lambda e, h=h, b=b: e.matmul(BK[7][:, hcol(h):hcol(h) + 64], AK[h][:, 0:128], Vtok[:, b, h * 64:(h + 1) * 64], start=True, stop=True),
                     reads=[("AK", h), ("Vtok", b, h // 2)], writes=[bk(7)])
            P.op("act", lambda e: e.copy(VA[:], BK[7][:]), reads=[bk(7)], writes=["VA"])
            for h in range(8):
                p, h2 = divmod(h, 2)
                ph = slice(h2 * 64, h2 * 64 + 64)
                P.op("pe", lambda e, h=h, b=b, p=p, ph=ph: e.matmul(BK[4][ph, p * 128:(p + 1) * 128], Vtok[:, b, h * 64:(h + 1) * 64], AK[h][:, 128:256], start=True, stop=True),
                     reads=[("AK", h), ("Vtok", b, p)], writes=[bk(4)])
            P.op("act", lambda e: e.copy(VK[:].rearrange("p a t -> p (a t)"), BK[4][:]), reads=[bk(4)], writes=["VK"])
            stop_at('G2')

        if prompt:
            if first:
                P.op("pool", lambda e: e.memset(S[:], 0.0), writes=["S"])
            for b in range(nb):
                rw_pre(b)
                for c2 in range(2):
                    cr = slice(c2 * 64, c2 * 64 + 64)
                    tc_ = slice(b * 128 + c2 * 64, b * 128 + c2 * 64 + 64)
                    gci = (b * 128 + c2 * 64) // 64
                    for h in range(8):
                        p, h2 = divmod(h, 2)
                        ph = slice(h2 * 64, h2 * 64 + 64)
                        P.op("pe", lambda e, h=h, p=p, ph=ph, h2=h2: e.matmul(BK[5 + h2][cr, p * 64:(p + 1) * 64], art[ph, p, 0, tc_], S[ph, p, :], start=True, stop=True),
                             reads=[("art", p), "S"], writes=[bk(5 + h2)])
                    for h in range(8):
                        p, h2 = divmod(h, 2)
                        ph = slice(h2 * 64, h2 * 64 + 64)
                        P.op("pe", lambda e, p=p, ph=ph, h2=h2: e.matmul(BK[h2][ph, p * 64:(p + 1) * 64], S[ph, p, :], art[ph, p, 1, tc_], start=True, stop=True),
                             reads=[("art", p), "S"], writes=[bk(h2)])
                    for h2 in range(2):
                        ph = slice(h2 * 64, h2 * 64 + 64)
                        P.op("act", lambda e, h2=h2, ph=ph: e.copy(y1[ph, :, :].rearrange("p a t -> p (a t)"), BK[h2][ph, 0:256]), reads=[bk(h2)], writes=[("y1", h2)])
                    for h2 in range(2):
                        P.op("dve", lambda e, h2=h2: e.tensor_tensor(XT[cr, h2 * 256:(h2 + 1) * 256], BK[5 + h2][cr, 0:256], VA[cr, h2 * 256:(h2 + 1) * 256], ALU.add),
                             reads=[bk(5 + h2), "VA"], writes=[("XT", h2)])
                    for h in range(8):
                        P.op("pe", lambda e, h=h: e.matmul(BK[7][cr, hcol(h):hcol(h) + 64], Pv[h][cr, c2 * 64:(c2 + 1) * 64], XT[cr, hcol(h):hcol(h) + 64], start=True, stop=True),
                             reads=[("Pv", h), ("XT", h % 2)], writes=[bk(7)])
                    P.op("act", lambda e: e.copy(UT[cr, :], BK[7][cr, :]), reads=[bk(7)], writes=["UT"])
                    stop_at('G3')
                    for h in range(8):
                        p, h2 = divmod(h, 2)
                        ph = slice(h2 * 64, h2 * 64 + 64)
                        P.op("pe", lambda e, h=h, p=p, ph=ph: e.matmul(BK[5][ph, 256 + p * 64:256 + (p + 1) * 64], BPtok[cr, b, h * 64:(h + 1) * 64], UT[cr, hcol(h):hcol(h) + 64], start=True, stop=False),
                             reads=[("BPtok", b, p), "UT"], writes=[bk(5)])
                        P.op("pe", lambda e, h=h, p=p, ph=ph: e.matmul(BK[5][ph, 256 + p * 64:256 + (p + 1) * 64], KPtok[cr, b, h * 64:(h + 1) * 64], Vtok[cr, b, h * 64:(h + 1) * 64], start=False, stop=True),
                             reads=[("KPtok", b, p), ("Vtok", b, p)], writes=[bk(5)])
                    for p in range(4):
                        P.op("dve", lambda e, p=p: e.scalar_tensor_tensor(S[:, p, :], S[:, p, :], cCt[:, p, gci:gci + 1], BK[5][:, 256 + p * 64:256 + (p + 1) * 64], ALU.mult, ALU.add),
                             reads=["S", ("cCt", p), bk(5)], writes=["S"])
                    for h in range(8):
                        p, h2 = divmod(h, 2)
                        ph = slice(h2 * 64, h2 * 64 + 64)
                        P.op("pe", lambda e, h=h, p=p, ph=ph: e.matmul(BK[6][ph, 256 + p * 64:256 + (p + 1) * 64], UT[cr, hcol(h):hcol(h) + 64], AB[h][cr, 128 + c2 * 64:128 + (c2 + 1) * 64], start=True, stop=True),
                             reads=[("AB", h), "UT"], writes=[bk(6)])
                    P.op("dve", lambda e: e.tensor_tensor(y1[:], BK[6][:, 256:512].rearrange("p (a t) -> p a t", t=64), y1[:], ALU.add), reads=[bk(6), ("y1", 0), ("y1", 1)], writes=[("y1", 0), ("y1", 1)])
                    P.op("pool", lambda e: e.tensor_tensor(Yt[:, :, tc_], y1[:], VK[:, :, c2 * 64:(c2 + 1) * 64], ALU.add), reads=[("y1", 0), ("y1", 1), "VK"], writes=[("Yt", b)])
                    stop_at('G4')
            if last and 'pwkv' not in SKIP:
                for pp in range(2):
                    P.op("pe", lambda e, pp=pp: e.transpose(BK[4][:, pp * 128:(pp + 1) * 128], S[:, 2 * pp:2 * pp + 2, :].rearrange("p a i -> p (a i)"), ident), reads=["S", "cst"], writes=[bk(4)])
                P.op("act", lambda e: e.copy(xn[:, 0:256], BK[4][:, 0:256]), reads=[bk(4)], writes=["xn"])
                for pp in range(2):
                    for pl in range(2):
                        pidx = 2 * pp + pl
                        P.dma("sp", pwkv[seq, 2 * pidx:2 * pidx + 2].rearrange("h i j -> i h j"), xn[pl * 64:(pl + 1) * 64, pp * 128:(pp + 1) * 128].rearrange("p (h j) -> p h j", j=64), reads=["xn"], writes=[("pwkv", pidx)])
                P.op("pe", lambda e: e.transpose(BK[4][:, 0:128], car[:, :], ident), reads=[("car", n) for n in range(14)] + ["cst", "car_init"], writes=[bk(4)])
                P.op("act", lambda e: e.copy(xn[0:14, 0:128], BK[4][0:14, 0:128]), reads=[bk(4)], writes=["xn"])
                P.dma("sp", psh[seq].rearrange("(c p) -> c p", p=128), xn[0:14, 0:128], reads=["xn"], writes=["psh"])
        else:
            rw_pre(0)
            sample_rwkv(hcol)
        YT_KEYS = [("Yt", b) for b in range(nb)]
        stop_at("G")
        debug_tap("Yt", Yt[:, :, 0:NT], [128, 4, NT], YT_KEYS)

        def post_steps(p):
            tn = ("sg", "cs", "t1", "t2", "t3", "a", "kkn", "k2")
            n1, n2 = tn[2 * p], tn[2 * p + 1]
            T1, T2 = rw[n1][:, 0:NT], rw[n2][:, 0:NT]
            k1, k2_ = "r_" + n1, "r_" + n2
            bm, bv = 4 + p, p
            return [
                lambda: P.op("pe", lambda e: e.matmul(BK[bm][:, 0:NT], bo64, Yt[:, p, 0:NT], start=True, stop=True), reads=YT_KEYS + ["cst"], writes=[bk(bm)]),
                lambda: P.op("dve", lambda e: e.tensor_tensor(T1, Yt[:, p, 0:NT], BK[bm][:, 0:NT], ALU.subtract), reads=YT_KEYS + [bk(bm)], writes=[k1]),
                lambda: P.op("pool", lambda e: e.tensor_tensor(T2, T1, T1, ALU.mult), reads=[k1], writes=[k2_]),
                lambda: P.op("pe", lambda e: e.matmul(BK[bv][:, 0:NT], bo64, T2, start=True, stop=True), reads=[k2_, "cst"], writes=[bk(bv)]),
                lambda: P.op("act", lambda e: e.activation(T2, BK[bv][:, 0:NT], AF.Sqrt, bias=64e-5), reads=[bk(bv)], writes=[k2_]),
                lambda: P.op("dve", lambda e: e.reciprocal(T2, T2), reads=[k2_], writes=[k2_]),
                lambda: P.op("dve", lambda e: e.tensor_tensor(T1, T1, T2, ALU.mult), reads=[k1, k2_], writes=[k1]),
                lambda: P.op("dve", lambda e: e.tensor_scalar(T1, T1, lg_c[:, p:p + 1], lb_c[:, p:p + 1], ALU.mult, ALU.add), reads=[k1, "lg_c", "lb_c"], writes=[k1]),
                lambda: P.op("pool", lambda e: e.tensor_tensor(T1, T1, bon[:, p, 0:NT], ALU.add), reads=[k1, ("bon", p)], writes=[k1]),
                lambda: P.op("dve", lambda e: e.tensor_tensor(rwoT[:, p, 0:NT], T1, gT[:, p, 0:NT], ALU.mult), reads=[k1, ("gT", p)], writes=[("rwoT", p)]),
            ]
        for group in zip(*[post_steps(p) for p in range(4)]):
            for step in group:
                step()
        RWO_KEYS = [("rwoT", p) for p in range(4)]
        stop_at("H")
        debug_tap("rwoT", rwoT[:, :, 0:NT], [128, 4, NT], RWO_KEYS)

        for ch in range(2):
            sa, ka_ = ring_get()
            sb_, kb_ = ring_get(hold=1)
            sa3 = sa[:, 0:2048].rearrange("p (k n) -> p k n", n=512)
            sb3 = sb_[:, 0:2048].rearrange("p (k n) -> p k n", n=512)
            for cc in range(4):
                oc = ch * 4 + cc
                ba, bb = (0, 1) if cc % 2 == 0 else (2, 3)
                for kc in range(4):
                    P.op("pe", lambda e, kc=kc, cc=cc, ba=ba, sa3=sa3: e.matmul(BK[ba][:, 0:NT], sa3[:, kc, cc * 128:(cc + 1) * 128], attT[:, kc, 0:NT], start=(kc == 0), stop=(kc == 3)),
                         reads=[ka_] + ATT_KEYS, writes=[bk(ba)])
                for kc in range(4):
                    P.op("pe", lambda e, kc=kc, cc=cc, bb=bb, sb3=sb3: e.matmul(BK[bb][:, 0:NT], sb3[:, kc, cc * 128:(cc + 1) * 128], rwoT[:, kc, 0:NT], start=(kc == 0), stop=(kc == 3)),
                         reads=[kb_] + RWO_KEYS, writes=[bk(bb)])
                tA = cc_t[cc % 2][:, 0:NT]
                tB = gl_t[cc % 2][:, 0:NT]
                P.op("dve", lambda e, oc=oc, ba=ba, tA=tA: e.tensor_tensor(tA, BK[ba][:, 0:NT], gates[:, oc, 0:NT], ALU.mult), reads=[bk(ba), ("gates", oc)], writes=[("cc_t", cc % 2)])
                P.op("dve", lambda e, oc=oc, bb=bb, tB=tB: e.tensor_tensor(tB, BK[bb][:, 0:NT], gates[:, 8 + oc, 0:NT], ALU.mult), reads=[bk(bb), ("gates", 8 + oc)], writes=[("gl_t", cc % 2)])
                P.op("pool", lambda e, oc=oc, tA=tA, tB=tB: e.tensor_tensor(hT[:, oc, 0:NT], tA, tB, ALU.add), reads=[("cc_t", cc % 2), ("gl_t", cc % 2)], writes=[("hT", oc)])
        MIX_KEYS = [("hT", oc) for oc in range(8)]
        debug_tap("mixT", hT[:, :, 0:NT], [128, 8, NT], MIX_KEYS)

        stop_at("I")
        for ch in range(2):
            so, ko = ring_get()
            so3 = so[:, 0:4096].rearrange("p (k n) -> p k n", n=512)
            for b in range(nb):
                bnk = (ch * nb + b) % 4
                for kc in range(8):
                    P.op("pe", lambda e, kc=kc, b=b, bnk=bnk, so3=so3: e.matmul(BK[bnk][:], hT[:, kc, b * 128:(b + 1) * 128], so3[:, kc, :], start=(kc == 0), stop=(kc == 7)),
                         reads=[ko] + MIX_KEYS, writes=[bk(bnk)])
                P.op("dve", lambda e, b=b, bnk=bnk, ch=ch: e.tensor_tensor(xt[b][:, ch * 512:(ch + 1) * 512], xt[b][:, ch * 512:(ch + 1) * 512], BK[bnk][:], ALU.add),
                     reads=[bk(bnk), ("xt", b)], writes=[("xt", b)])
        stop_at("J")
        debug_tap("x1", xt[0][:], [128, 1024], [("xt", 0)])

        norm_T("g2b")
        def ffn_steps(i, f, bo_):
            ug3 = v3(ugx[i][:, 0:NBs * (TB + 2)], TB + 2)
            ugk = ("ugx", i)
            c3 = v3(cc_t[i][:, 0:NT], TB)

            def carry():
                if prompt:
                    if first:
                        P.op("pool", lambda e: e.memset(ugx[i][:, 0:2], 0.0), writes=[("ugx0", i)])
                    else:
                        P.op("pool", lambda e: e.tensor_copy(ugx[i][:, 0:2], ccar[:, :, f]), reads=[("ccar", f)], writes=[("ugx0", i)])
                    P.op("pool", lambda e: e.tensor_copy(ccar[:, :, f], ugx[i][:, NT:NT + 2]), reads=[ugk], writes=[("ccar", f)])
                else:
                    P.op("pool", lambda e: e.tensor_copy(ug3[:, :, 0:2], SB["ccar_s"][:, f, :, :]), reads=[("ccar_s", f)], writes=[("ugx0", i)])
                    P.op("pool", lambda e: e.tensor_copy(SB["cout_s"][:, f, :, :], ug3[:, :, TB:TB + 2]), reads=[ugk], writes=[("cout_s", f)])
            return [
                lambda: P.op("act", lambda e: e.copy(ug3[:, :, 2:TB + 2], v3(BK[bo_ + i][:, 0:NT], TB)), reads=[bk(bo_ + i)], writes=[ugk]),
                carry,
                lambda: P.op("pool", lambda e: e.tensor_scalar(c3, ug3[:, :, 0:TB], cw_c[:, 0, f:f + 1], cb_c[:, f:f + 1], ALU.mult, ALU.add),
                             reads=[ugk, ("ugx0", i), "cb_c"] + CW_KEYS, writes=[("cc_t", i)]),
                lambda: P.op("dve", lambda e: e.scalar_tensor_tensor(c3, ug3[:, :, 1:TB + 1], cw_c[:, 1, f:f + 1], c3, ALU.mult, ALU.add),
                             reads=[ugk, ("ugx0", i), ("cc_t", i)] + CW_KEYS, writes=[("cc_t", i)]),
                lambda: P.op("dve", lambda e: e.scalar_tensor_tensor(c3, ug3[:, :, 2:TB + 2], cw_c[:, 2, f:f + 1], c3, ALU.mult, ALU.add),
                             reads=[ugk, ("cc_t", i)] + CW_KEYS, writes=[("cc_t", i)]),
                lambda: P.op("act", lambda e: e.activation(gl_t[i][:, 0:NT], cc_t[i][:, 0:NT], AF.Gelu_apprx_tanh), reads=[("cc_t", i)], writes=[("gl_t", i)]),
                lambda: P.op("dve", lambda e: e.tensor_tensor(actT[:, f, 0:NT], gl_t[i][:, 0:NT], BK[bo_ + 2 + i][:, 0:NT], ALU.mult), reads=[("gl_t", i), bk(bo_ + 2 + i)], writes=[("actT", f)]),
            ]
        for blkb in range(11):
            slot, skey = ring_get()
            sl3 = slot[:, 0:4096].rearrange("p (k n) -> p k n", n=512)
            bo_ = 4 * (blkb % 2)
            for cc in range(4):
                for kc in range(8):
                    P.op("pe", lambda e, kc=kc, cc=cc, sl3=sl3, bo_=bo_: e.matmul(BK[bo_ + cc][:, 0:NT], sl3[:, kc, cc * 128:(cc + 1) * 128], hT[:, kc, 0:NT], start=(kc == 0), stop=(kc == 7)),
                         reads=[skey] + HT_KEYS, writes=[bk(bo_ + cc)])
            for sa, sb in zip(ffn_steps(0, blkb * 2, bo_), ffn_steps(1, blkb * 2 + 1, bo_)):
                sa()
                sb()
        ACT_KEYS = [("actT", f) for f in range(22)]
        if prompt and last and 'pconv' not in SKIP:
            for j in range(2):
                P.op("pe", lambda e, j=j: e.transpose(BK[4][:, j * 128:(j + 1) * 128], ccar[:, j, :], ident), reads=[("ccar", f) for f in range(22)] + ["cst", "ccar_init"], writes=[bk(4)])
            P.op("act", lambda e: e.copy(xn[0:22, 0:256], BK[4][0:22, 0:256]), reads=[bk(4)], writes=["xn"])
            for j in range(2):
                P.dma("sp", pconv[seq, j].rearrange("(c p) -> c p", p=128), xn[0:22, j * 128:(j + 1) * 128], reads=["xn"], writes=[("pconv", j)])

        stop_at("K")
        for ch in range(2):
            for rg in range(3):
                sd, kd = ring_get()
                nk = DN_NK[rg]
                sd3 = sd[:, 0:nk * 512].rearrange("p (k n) -> p k n", n=512)
                for b in range(nb):
                    bnk = b % 4
                    for kc in range(nk):
                        f = rg * 8 + kc
                        P.op("pe", lambda e, kc=kc, f=f, b=b, bnk=bnk, sd3=sd3: e.matmul(BK[bnk][:], actT[:, f, b * 128:(b + 1) * 128], sd3[:, kc, :], start=(f == 0), stop=(f == 21)),
                             reads=[kd] + ACT_KEYS, writes=[bk(bnk)])
            for b in range(nb):
                bnk = b % 4
                P.op("dve", lambda e, b=b, bnk=bnk, ch=ch: e.tensor_tensor(xt[b][:, ch * 512:(ch + 1) * 512], xt[b][:, ch * 512:(ch + 1) * 512], BK[bnk][:], ALU.add),
                     reads=[bk(bnk), ("xt", b)], writes=[("xt", b)])

        for b in range(nb):
            P.op("dve", lambda e, b=b: e.bn_stats(stat[:, 0:6], xt[b][:, 0:512]), reads=[("xt", b)], writes=["stat0_0"])
            P.op("dve", lambda e, b=b: e.bn_stats(stat[:, 6:12], xt[b][:, 512:1024]), reads=[("xt", b)], writes=["stat0_1"])
            P.op("dve", lambda e: e.bn_aggr(stat[:, 12:14], stat[:, 0:12]), reads=["stat0_0", "stat0_1"], writes=["stat0_2"])
            P.op("dve", lambda e: e.scalar_tensor_tensor(stat[:, 14:15], stat[:, 12:13], stat[:, 12:13], stat[:, 13:14], ALU.mult, ALU.add), reads=["stat0_2"], writes=["stat0_3"])
            P.op("act", lambda e: e.activation(stat[:, 15:16], stat[:, 14:15], AF.Sqrt, bias=1e-6), reads=["stat0_3"], writes=["stat0_4"])
            P.op("dve", lambda e: e.reciprocal(stat[:, 15:16], stat[:, 15:16]), reads=["stat0_4"], writes=["stat0_4"])
            P.op("dve", lambda e, b=b: e.scalar_tensor_tensor(xn[:], xt[b][:], stat[:, 15:16], gb["gfb"][:], ALU.mult, ALU.mult), reads=[("xt", b), "stat0_4", "gfb"], writes=["xn"])
            P.dma("sp", yrows[b * 128:(b + 1) * 128, :], xn[:], reads=["xn"], writes=[("y", kind, seq, ti, b)])

    SB = {}

    def sample_alloc():
        SB["shcar"] = wsb("sx_shcar", [128, 14, 16], F32)
        SB["shout"] = wsb("sx_shout", [128, 16, 16], F32)
        SB["ccar_s"] = wsb("sx_ccar_s", [128, 22, 16, 2], F32)
        SB["cout_s"] = wsb("sx_cout_s", [128, 24, 16, 2], F32)
        SB["S_s"] = wsb("sx_S_s", [128, 16, 4, 64], F32)
        SB["kcT"] = wsb("sx_kcT", [128, 16, 128], BF16)
        SB["vc"] = wsb("sx_vc", [128, 16, 128], BF16)
        SB["biasC"] = [wsb("sx_biasC%d" % kv, [128, 16, 4, 8], F32) for kv in range(2)]
        SB["biasN"] = [wsb("sx_biasN%d" % kv, [128, 16, 4, 8], F32) for kv in range(2)]
        SB["Apad"] = wsb("sx_Apad", [128, 16, 128], F32)
        SB["BPpad"] = [wsb("sx_BPpad", [128, 512], F32)] * 2
        SB["KPpad"] = [wsb("sx_KPpad", [128, 512], F32)] * 2
        SB["xn"] = xn

    def sample_const_setup():
        P.op("pool", lambda e: e.memset(SB["Apad"][:].rearrange("p b c -> p (b c)"), 0.0), writes=["Apad"])
        P.op("pool", lambda e: e.memset(SB["shout"][:].rearrange("p a b -> p (a b)"), 0.0), writes=["shout_init"])
        P.op("pool", lambda e: e.memset(SB["cout_s"][:].rearrange("p a b c -> p (a b c)"), 0.0), writes=["cout_init"])
        for kv in range(2):
            P.op("dve", lambda e, kv=kv: e.tensor_copy(SB["biasC"][kv][:], biasT[1][kv][:, :, 0:8].unsqueeze(1).broadcast_to([128, 16, 4, 8])),
                 reads=BIAS_KEYS[(1, kv)], writes=[("biasC", kv)])
            P.op("pool", lambda e, kv=kv: e.memset(SB["biasN"][kv][:].rearrange("p a b c -> p (a b c)"), NEGB), writes=[("biasN", kv)])
            for bb in range(16):
                P.dma("sp", SB["biasN"][kv][bb * 8:(bb + 1) * 8, bb, :, :], biasT[0][kv][0:8, :, 0:8],
                      reads=BIAS_KEYS[(0, kv)] + [("biasN", kv)], writes=[("biasNd", kv, bb)])
    BIASN_KEYS = {kv: [("biasN", kv)] + [("biasNd", kv, bb) for bb in range(16)] for kv in range(2)}

    def sample_setup(st):
        b0 = st * 16
        stgA = SB["xn"]
        for n in range(14):
            bnk = 4 + n % 4
            if n % 7 == 0:
                P.dma("sp", stgA[0:16, 0:896], sshift[b0:b0 + 16, n * 128:n * 128 + 896], writes=["xn"])
            P.op("pe", lambda e, n=n, bnk=bnk: e.transpose(BK[bnk][:, 0:16], stgA[0:16, (n % 7) * 128:(n % 7 + 1) * 128], cst[0:16, C_ID:C_ID + 16]), reads=["xn", "cst"], writes=[bk(bnk)])
            P.op("act", lambda e, n=n, bnk=bnk: e.copy(SB["shcar"][:, n, :], BK[bnk][:, 0:16]), reads=[bk(bnk)], writes=[("shcar", n)])
        for piece, npc in enumerate((8, 8, 6)):
            P.dma("sp", stgA[0:32, 0:npc * 128], sconv[st * 32:(st + 1) * 32, piece * 1024:piece * 1024 + npc * 128], writes=["xn"])
            for c in range(npc):
                f = piece * 8 + c
                bnk = 4 + c % 4
                P.op("pe", lambda e, c=c, bnk=bnk: e.transpose(BK[bnk][:, 0:32], stgA[0:32, c * 128:(c + 1) * 128], cst[0:32, C_ID:C_ID + 32]), reads=["xn", "cst"], writes=[bk(bnk)])
                P.op("act", lambda e, f=f, bnk=bnk: e.copy(SB["ccar_s"][:, f, :, :].rearrange("p b j -> p (b j)"), BK[bnk][:, 0:32]), reads=[bk(bnk)], writes=[("ccar_s", f)])
        for bb in range(16):
            hs = bb % 2
            P.dma("sp", stgA[0:64, hs * 512:(hs + 1) * 512].rearrange("p (h j) -> p h j", j=64), swkv[b0 + bb].rearrange("h i j -> i h j"), writes=[("stgAh", hs), "xn"] if bb < 2 else [("stgAh", hs)], reads=["xn"])
            bnk = 4 + bb % 4
            for p in range(4):
                P.op("pe", lambda e, p=p, bnk=bnk, hs=hs: e.transpose(BK[bnk][:, p * 64:(p + 1) * 64], stgA[0:64, hs * 512 + p * 128:hs * 512 + (p + 1) * 128], cst[0:64, C_ID:C_ID + 64]),
                     reads=[("stgAh", hs), "cst"], writes=[bk(bnk)])
            P.op("dve", lambda e, bb=bb, bnk=bnk: e.tensor_copy(SB["S_s"][:, bb, :, :].rearrange("p a i -> p (a i)"), BK[bnk][:, 0:256]), reads=[bk(bnk)], writes=[("S_s", bb)])
        for bb in range(16):
            bnk = 4 + bb % 4
            if bb % 8 == 0:
                P.dma("sp", stgA[:, 0:1024].rearrange("p (b c) -> p b c", c=128), ck[b0 + bb:b0 + bb + 8].rearrange("b k c -> k b c"), reads=[("stgAh", 0), ("stgAh", 1)], writes=["xn", ("stgAh", 0), ("stgAh", 1)])
            P.op("pe", lambda e, bb=bb, bnk=bnk: e.transpose(BK[bnk][:, 0:128], stgA[:, (bb % 8) * 128:(bb % 8 + 1) * 128], ident), reads=["xn", "cst"], writes=[bk(bnk)])
            P.op("act", lambda e, bb=bb, bnk=bnk: e.copy(SB["kcT"][:, bb, :], BK[bnk][:, 0:128]), reads=[bk(bnk)], writes=[("kcT", bb)])
        P.dma("pool", SB["vc"][:], cv[b0:b0 + 16].rearrange("b k c -> k b c"), writes=["vc"])
        P.dma("sp", sk[b0:b0 + 16, 0:120, :], ck[b0:b0 + 16, 8:128, :], writes=[("sk_old", st)])
        P.dma("sp", sv[b0:b0 + 16, 0:120, :], cv[b0:b0 + 16, 8:128, :], writes=[("sv_old", st)])

    def sample_attention():
        NT = 128
        kcT, vc = SB["kcT"], SB["vc"]
        P.op("pe", lambda e: e.transpose(BK[4][:, 0:128], kTf[:, 0:128], ident), reads=["kTf", "cst"], writes=[bk(4)])
        P.op("act", lambda e: e.copy(kvrow[:, 0, :], BK[4][:, 0:128]), reads=[bk(4)], writes=[("kvrow", 0)])
        for kv in range(2):
            ph = slice(kv * 64, kv * 64 + 64)
            for bb in range(16):
                P.op("pe", lambda e, bb=bb: e.matmul(BK[4][:, bb * 32:(bb + 1) * 32], kcT[ph, bb, :], qT[ph, :, bb * 8:(bb + 1) * 8], start=True, stop=True),
                     reads=[("kcT", bb)] + [("qT", c) for c in range(4)], writes=[bk(4)])
            for bb in range(16):
                P.op("pe", lambda e, bb=bb: e.matmul(BK[5][:, bb * 32:(bb + 1) * 32], kT[ph, 128:256], qT[ph, :, bb * 8:(bb + 1) * 8], start=True, stop=True),
                     reads=["kT"] + [("qT", c) for c in range(4)], writes=[bk(5)])
            P.op("dve", lambda e, kv=kv: e.scalar_tensor_tensor(BK[4][:], BK[4][:], 0.125, SB["biasC"][kv][:].rearrange("p a b c -> p (a b c)"), ALU.mult, ALU.add),
                 reads=[bk(4), ("biasC", kv)], writes=[bk(4)])
            P.op("dve", lambda e, kv=kv: e.scalar_tensor_tensor(BK[5][:], BK[5][:], 0.125, SB["biasN"][kv][:].rearrange("p a b c -> p (a b c)"), ALU.mult, ALU.add),
                 reads=[bk(5)] + BIASN_KEYS[kv], writes=[bk(5)])
            P.op("act", lambda e: e.activation(PT[0][:], BK[4][:], AF.Exp), reads=[bk(4)], writes=[("PT", 0)])
            P.op("act", lambda e: e.activation(PT[1][:], BK[5][:], AF.Exp), reads=[bk(5)], writes=[("PT", 1)])
            for bb in range(16):
                cs = slice(bb * 32, (bb + 1) * 32)
                P.op("pe", lambda e, bb=bb, cs=cs: e.matmul(BK[6][:, cs], vc[:, bb, :], PT[0][:, cs], start=True, stop=False), reads=["vc", ("PT", 0)], writes=[bk(6)])
                P.op("pe", lambda e, cs=cs: e.matmul(BK[6][:, cs], vtok[:, 1, :], PT[1][:, cs], start=False, stop=True), reads=[("vtok", 1), ("PT", 1)], writes=[bk(6)])
            for bb in range(16):
                cs = slice(bb * 32, (bb + 1) * 32)
                P.op("pe", lambda e, cs=cs: e.matmul(BK[7][:, cs], ones_bf[:], PT[0][:, cs], start=True, stop=False), reads=["ones_bf", ("PT", 0)], writes=[bk(7)])
                P.op("pe", lambda e, cs=cs: e.matmul(BK[7][:, cs], ones_bf[:], PT[1][:, cs], start=False, stop=True), reads=["ones_bf", ("PT", 1)], writes=[bk(7)])
            P.op("dve", lambda e: e.tensor_tensor(rd_t[ph, :].rearrange("p (b g t) -> p b g t", g=4, t=8), BK[7][ph, :].rearrange("p (b g t) -> p b g t", g=4, t=8), esink[ph, :, 0:8].unsqueeze(1).broadcast_to([64, 16, 4, 8]), ALU.add), reads=[bk(7)] + ESINK_KEYS, writes=["rd_t"])
            P.op("dve", lambda e: e.reciprocal(rd_t[ph, :], rd_t[ph, :]), reads=["rd_t"], writes=["rd_t"])
            P.op("dve", lambda e, kv=kv: e.tensor_tensor(attT[ph, :, 0:128].rearrange("p g (b t) -> p b g t", t=8), BK[6][ph, :].rearrange("p (b g t) -> p b g t", g=4, t=8),
                                                       rd_t[ph, :].rearrange("p (b g t) -> p b g t", g=4, t=8), ALU.mult),
                 reads=[bk(6), "rd_t"], writes=[("attT", kv, 0)])

    def sample_rwkv(hcol):
        S_s, Apad = SB["S_s"], SB["Apad"]
        Rpad = Apad
        y1s = VA[:].rearrange("p (a t) -> p a t", t=128)
        SKEYS = [("S_s", bb) for bb in range(16)]

        def diag(t):
            base = t[:, 0, 0:8]
            return bass.AP(base.tensor, base.offset, [list(base.ap[0]), [136, 16], [1, 8]])
        for p in range(4):
            P.op("pool", lambda e, p=p: e.tensor_copy(diag(Apad), art[:, p, 0, 0:128].rearrange("p (b t) -> p b t", t=8)), reads=[("art", p), "Apad"], writes=["Apad"])
            for h2 in range(2):
                ph = slice(h2 * 64, h2 * 64 + 64)
                for bb in range(16):
                    P.op("pe", lambda e, bb=bb, p=p, ph=ph, h2=h2: e.matmul(BK[5 + h2][:, p * 64:(p + 1) * 64], Apad[ph, bb, :], S_s[ph, bb, p, :], start=(bb == 0), stop=(bb == 15)),
                         reads=["Apad", ("S_s", bb)], writes=[bk(5 + h2)])
            P.op("pool", lambda e, p=p: e.tensor_copy(diag(Apad), art[:, p, 1, 0:128].rearrange("p (b t) -> p b t", t=8)), reads=[("art", p), "Apad"], writes=["Apad"])
            for h2 in range(2):
                ph = slice(h2 * 64, h2 * 64 + 64)
                yb = 4 if h2 == 0 else 7
                for bb in range(16):
                    P.op("pe", lambda e, bb=bb, p=p, ph=ph, yb=yb: e.matmul(BK[yb][ph, p * 128:(p + 1) * 128], S_s[ph, bb, p, :], Apad[ph, bb, :], start=(bb == 0), stop=(bb == 15)),
                         reads=["Apad", ("S_s", bb)], writes=[bk(yb)])
        for h2 in range(2):
            P.op("dve", lambda e, h2=h2: e.tensor_tensor(XT[:, h2 * 256:(h2 + 1) * 256], BK[5 + h2][:, 0:256], VA[:, h2 * 256:(h2 + 1) * 256], ALU.add),
                 reads=[bk(5 + h2), "VA"], writes=[("XT", h2)])
        for h in range(8):
            P.op("pe", lambda e, h=h: e.matmul(BK[5][:, hcol(h):hcol(h) + 64], Pv[h], XT[:, hcol(h):hcol(h) + 64], start=True, stop=True),
                 reads=[("Pv", h), ("XT", h % 2)], writes=[bk(5)])
        P.op("act", lambda e: e.copy(UT[:], BK[5][:]), reads=[bk(5)], writes=["UT"])
        for h in range(8):
            p, h2 = divmod(h, 2)
            ph = slice(h2 * 64, h2 * 64 + 64)
            P.op("pe", lambda e, h=h, p=p, ph=ph: e.matmul(BK[6][ph, p * 128:(p + 1) * 128], UT[:, hcol(h):hcol(h) + 64], AB[h][:, 128:256], start=True, stop=True),
                 reads=[("AB", h), "UT"], writes=[bk(6)])
        P.op("act", lambda e: e.copy(VA[0:64, :], BK[4][0:64, :]), reads=[bk(4)], writes=["VA"])
        P.op("act", lambda e: e.copy(VA[64:128, :], BK[7][64:128, :]), reads=[bk(7)], writes=["VA"])
        P.op("dve", lambda e: e.tensor_tensor(VA[:], BK[6][:], VA[:], ALU.add), reads=[bk(6), "VA", "VA"], writes=["VA", "VA"])
        P.op("pool", lambda e: e.tensor_tensor(Yt[:, :, 0:128], y1s, VK[:], ALU.add), reads=["VA", "VA", "VK"], writes=[("Yt", 0)])
        for bb in range(16):
            i2 = bb % 2
            bnk = 4 + bb % 4
            BPp, KPp = SB["BPpad"][i2], SB["KPpad"][i2]
            P.op("pool", lambda e, bb=bb, BPp=BPp: e.tensor_scalar(BPp[:], BPtok[:, 0, :], cst[:, C_BSEL + bb:C_BSEL + bb + 1], None, ALU.mult),
                 reads=[("BPtok", 0, p) for p in range(4)] + ["cst"], writes=["BPpad"])
            P.op("dve", lambda e, bb=bb, KPp=KPp: e.tensor_scalar(KPp[:], KPtok[:, 0, :], cst[:, C_BSEL + bb:C_BSEL + bb + 1], None, ALU.mult),
                 reads=[("KPtok", 0, p) for p in range(4)] + ["cst"], writes=["KPpad"])
            for h in range(8):
                p, h2 = divmod(h, 2)
                ph = slice(h2 * 64, h2 * 64 + 64)
                P.op("pe", lambda e, h=h, p=p, ph=ph, bnk=bnk, BPp=BPp: e.matmul(BK[bnk][ph, p * 64:(p + 1) * 64], BPp[:, h * 64:(h + 1) * 64], UT[:, hcol(h):hcol(h) + 64], start=True, stop=False),
                     reads=["BPpad", "UT"], writes=[bk(bnk)])
                P.op("pe", lambda e, h=h, p=p, ph=ph, bnk=bnk, KPp=KPp: e.matmul(BK[bnk][ph, p * 64:(p + 1) * 64], KPp[:, h * 64:(h + 1) * 64], Vtok[:, 0, h * 64:(h + 1) * 64], start=False, stop=True),
                     reads=["KPpad", ("Vtok", 0, p)], writes=[bk(bnk)])
            for p in range(4):
                P.op("dve", lambda e, p=p, bb=bb, bnk=bnk: e.scalar_tensor_tensor(S_s[:, bb, p, :], S_s[:, bb, p, :], cCt[:, p, bb:bb + 1], BK[bnk][:, p * 64:(p + 1) * 64], ALU.mult, ALU.add),
                     reads=[("S_s", bb), ("cCt", p), bk(bnk)], writes=[("S_s", bb)])
            for pp in range(2):
                P.op("pe", lambda e, pp=pp, bb=bb, bnk=bnk: e.transpose(BK[bnk][:, 256 + pp * 128:256 + (pp + 1) * 128], S_s[:, bb, 2 * pp:2 * pp + 2, :].rearrange("p a i -> p (a i)"), ident),
                     reads=[("S_s", bb), "cst"], writes=[bk(bnk)])
            stgO = XT[:, i2 * 256:(i2 + 1) * 256]
            P.op("act", lambda e, bnk=bnk, stgO=stgO: e.copy(stgO, BK[bnk][:, 256:512]), reads=[bk(bnk)], writes=[("XT", i2)])
            for pp in range(2):
                for pl in range(2):
                    pidx = 2 * pp + pl
                    P.dma("sp", swkvo[SB["b0"] + bb, 2 * pidx:2 * pidx + 2].rearrange("h i j -> i h j"), stgO[pl * 64:(pl + 1) * 64, pp * 128:(pp + 1) * 128].rearrange("p (h j) -> p h j", j=64),
                          reads=[("XT", i2)], writes=[("swkvo", bb, pidx)])

    def sample_outputs(st):
        b0 = st * 16
        stgA = SB["xn"]
        for bb in range(16):
            P.dma("sp", sk[b0 + bb, 120:128, :], kvrow[bb * 8:(bb + 1) * 8, 0, :], reads=[("kvrow", 0)], writes=[("sk_new", st, bb)])
            P.dma("sp", sv[b0 + bb, 120:128, :], kvrow[bb * 8:(bb + 1) * 8, 1, :], reads=[("kvrow", 1)], writes=[("sv_new", st, bb)])
        for g in range(2):
            P.op("pe", lambda e, g=g: e.transpose(BK[4 + g][:, 0:128], SB["shout"][:, g * 8:(g + 1) * 8, :].rearrange("p a b -> p (a b)"), ident),
                 reads=[("shout", n) for n in range(14)] + ["shout_init", "cst"], writes=[bk(4 + g)])
            P.op("act", lambda e, g=g: e.copy(stgA[:, g * 128:(g + 1) * 128], BK[4 + g][:, 0:128]), reads=[bk(4 + g)], writes=["xn"])
        for n in range(14):
            g, nl = divmod(n, 8)
            P.dma("sp", ssh[b0:b0 + 16, n * 128:(n + 1) * 128], stgA[nl * 16:(nl + 1) * 16, g * 128:(g + 1) * 128], reads=["xn"], writes=[("ssh", st, n)])
        for g in range(6):
            P.op("pe", lambda e, g=g: e.transpose(BK[4 + g % 4][:, 0:128], SB["cout_s"][:, g * 4:(g + 1) * 4, :, :].rearrange("p a b j -> p (a b j)"), ident),
                 reads=[("cout_s", f) for f in range(22)] + ["cout_init", "cst"], writes=[bk(4 + g % 4)])
            P.op("act", lambda e, g=g: e.copy(stgA[:, 256 + g * 128:256 + (g + 1) * 128], BK[4 + g % 4][:, 0:128]), reads=[bk(4 + g % 4)], writes=["xn"])
        for f in range(22):
            g, fl = divmod(f, 4)
            P.dma("sp", sconvo[st * 32:(st + 1) * 32, f * 128:(f + 1) * 128], stgA[fl * 32:(fl + 1) * 32, 256 + g * 128:256 + (g + 1) * 128], reads=["xn"], writes=[("sconvo", st, f)])

    tiles = [("p", t // (2048 // NTP), t % (2048 // NTP)) for t in range(n_ptiles)]
    if os.environ.get('KREP'):
        tiles = tiles * int(os.environ['KREP'])
    try:
        for (kind, seq, ti) in tiles:
            if os.environ.get('KBAR'):
                P.barrier()
            do_tile(kind, seq, ti)
        if do_sample:
            P.barrier()
            wstack[0].close()
            wstack[0] = ExitStack()
            alloc_work(128, "_s")
            sample_alloc()
            P.barrier()
            sample_const_setup()
            stop_at("SA")
            for st in range(n_stiles):
                SB["b0"] = st * 16
                sample_setup(st)
                stop_at("SB")
                do_tile("s", st, 0)
                sample_outputs(st)
    except _Stop:
        pass
    _pe = os.environ.get('PADE'); _pn = int(os.environ.get('PAD', '0'))
    dmy = P.sbuf("dmy", [128, 8], F32)
    for _i in range(_pn):
        if _pe == 'pe':
            P.op("pe", lambda e: e.matmul(BK[0][0:8, 0:8], cst[0:8, 0:8], cst[0:8, 0:8], start=True, stop=True), reads=["cst"], writes=[("ps", 0)])
        else:
            P.op(_pe, lambda e: e.memset(dmy[:], 0.0), writes=["dmy"])
    _pa = int(os.environ.get('KPADALL', '0'))
    for _i in range(_pa):
        P.op("pe", lambda e: e.matmul(BK[0][0:8, 0:8], cst[0:8, 0:8], cst[0:8, 0:8], start=True, stop=True), reads=["cst"], writes=[("ps", 0)])
        P.op("pe", lambda e: e.matmul(BK[1][0:8, 0:8], cst[0:8, 0:8], cst[0:8, 0:8], start=True, stop=True), reads=["cst"], writes=[("ps", 1)])
        P.op("dve", lambda e: e.memset(dmy[:], 0.0), writes=["dmy"])
        if _i % 2 == 0:
            P.op("pool", lambda e: e.memset(dmy[:, 0:4], 0.0), writes=["dmy2"])
    P.finish()
    return D


_STAGE = ""


NCORES = int(os.environ.get("KNCORES", "8"))


def kernel(**inputs):
    global STOP, NSEQ
    inp = {k: np.asarray(v) for k, v in inputs.items()}
    cst, oh = host_consts()
    STOP = ""
    NSEQ = 16 // NCORES
    NSTL = 8 // NCORES
    NB = NSTL * 16
    nc = bass.Bass("TRN2", target_bir_lowering=False)
    D = build(nc, n_ptiles=2048 // NTP * NSEQ, n_stiles=NSTL)
    in_maps = []
    for c in range(NCORES):
        bs = slice(NB * c, NB * (c + 1))
        m = {
            "xp": inp["x_prompt"][NSEQ * c:NSEQ * (c + 1)], "xsm": inp["x_sample"][bs].reshape(NB * 8, 1024),
            "ck": inp["cache_win_k"][0, bs].reshape(NB, 128, 128), "cv": inp["cache_win_v"][0, bs].reshape(NB, 128, 128),
            "sshift": inp["state_shift"][0, bs], "swkv": inp["state_wkv"][0, bs], "sconv": inp["state_conv"][0, bs].reshape(NB * 2, 2816),
            "rel_bias": inp["rel_bias"], "norm1_g": inp["norm1_g"], "w_in": inp["w_in"][0], "sinks": inp["sinks"][0],
            "mu_shift": inp["mu_shift"][0], "w0": inp["w0"][0], "w2": inp["w2"][0], "a0": inp["a0"][0], "a2": inp["a2"][0],
            "g2": inp["g2"][0], "k_k": inp["k_k"][0], "k_a": inp["k_a"][0], "r_k": inp["r_k"][0].reshape(512),
            "lnx_g": inp["lnx_g"][0], "lnx_b": inp["lnx_b"][0], "w_pa": inp["w_pa"][0], "w_pb": inp["w_pb"][0],
            "w_o": inp["w_o"][0], "norm2_g": inp["norm2_g"], "w_up": inp["w_up"][0], "conv_w": inp["conv_w"][0],
            "conv_b": inp["conv_b"][0], "w_down": inp["w_down"][0], "final_g": inp["final_g"].reshape(1, 1024),
            "consts": cst, "oh": oh,
        }
        in_maps.append({k: np.ascontiguousarray(v, dtype=np.float32) for k, v in m.items() if k in D})
    res = run_bass_kernel_spmd(nc, in_maps, core_ids=list(range(NCORES)))
    R = res.results

    def cat(name, shape):
        return np.concatenate([np.asarray(r[name], dtype=np.float32) for r in R], axis=0).reshape(shape)

    return (cat("yp", (16, 2048, 1024)), cat("ys", (128, 8, 1024)),
            cat("pk", (1, 16, 128, 2, 64)), cat("pv", (1, 16, 128, 2, 64)), cat("psh", (1, 16, 1792)),
            cat("pwkv", (1, 16, 8, 64, 64)), cat("pconv", (1, 16, 2, 2816)),
            cat("sk", (1, 128, 128, 2, 64)), cat("sv", (1, 128, 128, 2, 64)), cat("ssh", (1, 128, 1792)),
            cat("swkvo", (1, 128, 8, 64, 64)), cat("sconvo", (1, 128, 2, 2816)))
```

```python
import numpy as np
from contextlib import ExitStack
import concourse.bass as bass
import concourse.mybir as mybir
from concourse.bass_utils import run_bass_kernel_spmd

F32 = mybir.dt.float32
BF16 = mybir.dt.bfloat16
AF = mybir.ActivationFunctionType
ALU = mybir.AluOpType
AX = mybir.AxisListType

import os as _os0
SAME_ENGINE_SYNC = _os0.environ.get("SES", "1") == "1"
EPOCH = 30000
RELAX_FSZ = int(_os0.environ.get('KRELAX', '0'))
N_DMA_SEMS = {"sp": 8, "pool": 4}


class _Op:
    __slots__ = ("eng", "fn", "deps", "isdma", "ms", "dsem", "dval", "needed", "desc", "fsz")


class _Rec:
    def __init__(self):
        self.call = None

    def __getattr__(self, name):
        def f(*a, **k):
            assert self.call is None
            self.call = (name, a, k)
            return self
        return f


class Prog:
    def __init__(self, nc):
        self.nc = nc
        self.ops = []
        self.lastw = {}
        self.rd_c = {}
        self.rd_d = {}
        self.stack = ExitStack()
        self.last_op = {}
        self.pending = {}
        self.bar_from = 0

    def sbuf(self, name, shape, dtype):
        return self.stack.enter_context(self.nc.sbuf_tensor(name, list(shape), dtype))

    def psum(self, name, shape, dtype):
        return self.stack.enter_context(self.nc.psum_tensor(name, list(shape), dtype))

    def op(self, eng, fn, reads=(), writes=(), isdma=False):
        idx = len(self.ops)
        deps = set()
        for r in reads:
            w = self.lastw.get(r)
            if w is not None:
                deps.add(w)
        for r in writes:
            w = self.lastw.get(r)
            if w is not None:
                deps.add(w)
            for i in self.rd_c.get(r, {}).values():
                deps.add(i)
            for i in self.rd_d.get(r, ()):
                deps.add(i)
        for r in writes:
            self.lastw[r] = idx
            self.rd_c[r] = {}
            self.rd_d[r] = []
        ws = set(writes)
        for r in reads:
            if r in ws:
                continue
            if isdma:
                self.rd_d.setdefault(r, []).append(idx)
            else:
                self.rd_c.setdefault(r, {})[eng] = idx
        if eng in self.pending:
            deps.update(self.pending.pop(eng))
        rec = _Rec()
        fn(rec)
        name_, a_, k_ = rec.call
        o = _Op()
        o.desc = name_ + " w=" + str(list(writes))[:60]
        o.fsz = 0
        try:
            out_ap = k_.get("out", a_[0] if a_ else None)
            shp = tuple(out_ap.shape)
            n = 1
            for d_ in shp[1:]:
                n *= int(d_)
            o.fsz = n
        except Exception:
            o.fsz = 0
        o.eng, o.fn, o.deps, o.isdma = eng, (lambda e: getattr(e, name_)(*a_, **k_)), deps, isdma
        o.ms = None
        o.dsem = None
        o.dval = None
        o.needed = False
        self.ops.append(o)
        self.last_op[eng] = idx
        return idx

    def barrier(self):
        prev = set(self.last_op.values())
        prev.update(i for i in range(self.bar_from, len(self.ops)) if self.ops[i].isdma)
        self.bar_from = len(self.ops)
        for eng in ("pe", "act", "dve", "pool", "sp"):
            self.pending.setdefault(eng, set()).update(prev)

    def dma(self, q, out, in_, reads=(), writes=(), **kw):
        return self.op(q, lambda e: e.dma_start(out=out, in_=in_, **kw), reads, writes, isdma=True)

    def finish(self):
        nc = self.nc
        ops = self.ops
        last_dma = [i for i, o in enumerate(ops) if o.isdma]
        for i, o in enumerate(ops):
            best = {}
            nd = set()
            for d in o.deps:
                od = ops[d]
                if od.isdma:
                    nd.add(d)
                else:
                    if od.eng == o.eng and not o.isdma:
                        if od.eng == "pe" or not SAME_ENGINE_SYNC:
                            continue
                        if RELAX_FSZ and od.fsz >= RELAX_FSZ and od.eng in ("dve", "act"):
                            continue
                    if best.get(od.eng, -1) < d:
                        best[od.eng] = d
            nd.update(best.values())
            o.deps = nd
            for d in nd:
                ops[d].needed = True
        for i in last_dma:
            ops[i].needed = True
        tail_ops = []
        for en in ("pe", "act", "dve", "pool"):
            idxs = [i for i, o in enumerate(ops) if o.eng == en and not o.isdma]
            if idxs:
                ops[idxs[-1]].needed = True
                tail_ops.append(idxs[-1])
        cnt = {e: 0 for e in ("pe", "act", "dve", "pool", "sp")}
        dcount = {}
        nd_used = {"sp": 0, "pool": 0}
        for o in ops:
            if o.isdma:
                k = nd_used[o.eng] % N_DMA_SEMS[o.eng]
                nd_used[o.eng] += 1
                key = (o.eng, k)
                dcount[key] = dcount.get(key, 0) + 1
                o.dsem = key
                o.dval = 16 * dcount[key]
            elif o.needed:
                o.ms = cnt[o.eng]
                cnt[o.eng] += 1
        sems = {}
        for e in cnt:
            for ep in range(cnt[e] // EPOCH + 1):
                sems[(e, ep)] = self.stack.enter_context(nc.semaphore("m_%s_%d" % (e, ep)))
        dsems = {}
        for key in dcount:
            dsems[key] = self.stack.enter_context(nc.semaphore("d_%s_%d" % key))
        final_dma = {}
        for i in last_dma:
            final_dma[ops[i].dsem] = max(final_dma.get(ops[i].dsem, 0), ops[i].dval)
        by_eng = {e: [] for e in cnt}
        for o in ops:
            by_eng[o.eng].append(o)
        import os as _os
        dump = _os.environ.get("DUMP")

        def emit(ename, e):
            known = {}
            for o in by_eng[ename]:
                if o.isdma and o.dval > 16:
                    k = ("d",) + o.dsem
                    if known.get(k, 0) < o.dval - 16:
                        e.wait_ge(dsems[o.dsem], o.dval - 16)
                        known[k] = o.dval - 16
                for d in sorted(o.deps):
                    od = ops[d]
                    if od.isdma:
                        k = ("d",) + od.dsem
                        if known.get(k, 0) < od.dval:
                            e.wait_ge(dsems[od.dsem], od.dval)
                            known[k] = od.dval
                            if dump: print("   ", ename, "WAITD", od.dsem, od.dval)
                    else:
                        k = ("m", od.eng)
                        if known.get(k, -1) < od.ms:
                            ep = od.ms // EPOCH
                            e.wait_ge(sems[(od.eng, ep)], od.ms % EPOCH + 1)
                            known[k] = od.ms
                            if dump: print("   ", ename, "WAITM", od.eng, od.ms + 1)
                ins = o.fn(e)
                if dump: print(ename, "OP", o.desc, "ms", o.ms)
                if o.isdma:
                    ins.then_inc(dsems[o.dsem], 16)
                elif o.ms is not None:
                    ins.then_inc(sems[(o.eng, o.ms // EPOCH)], 1)
            if ename == "sp":
                for key, v in final_dma.items():
                    e.wait_ge(dsems[key], v)
                for i in tail_ops:
                    od = ops[i]
                    e.wait_ge(sems[(od.eng, od.ms // EPOCH)], od.ms % EPOCH + 1)

        with nc.Block() as block:
            @block.tensor
            def _(e):
                emit("pe", e)

            @block.scalar
            def _(e):
                emit("act", e)

            @block.vector
            def _(e):
                emit("dve", e)

            @block.gpsimd
            def _(e):
                emit("pool", e)

            @block.sync
            def _(e):
                emit("sp", e)
        self.stack.close()

import os
STOP = os.environ.get('KSTOP', '')
SKIP = os.environ.get('KSKIP', '').split(',')
NTP = 256
NSEQ = int(os.environ.get('KNSEQ', '2'))
KAPPA = 0.6065306597126334
NEGB = -30000.0
RW_DT = F32
C_ID, C_BO, C_BO64, C_ONE, C_M64, C_L64, C_M8, C_L8, C_BSEL, C_END = 0, 128, 256, 384, 512, 768, 896, 1152, 1280, 1296


def host_consts():
    c = np.zeros((128, C_END), np.float32)
    c[:, C_ID:C_ID + 128] = np.eye(128)
    blk = (np.arange(128)[:, None] // 64 == np.arange(128)[None] // 64)
    c[:, C_BO:C_BO + 128] = blk
    c[:, C_BO64:C_BO64 + 128] = blk / 64.0
    c[:, C_ONE:C_ONE + 128] = 1.0
    s = np.arange(128)[:, None]
    t = np.arange(128)[None]
    for (C, cm, cl) in ((64, C_M64, C_L64), (8, C_M8, C_L8)):
        same = (s // C == t // C)
        c[:, cm:cm + 128] = same & (s < t)
        c[:, cm + 128:cm + 256] = same & (s <= t)
        c[:, cl:cl + 128] = same & (s > t)
    c[:, C_BSEL:C_BSEL + 16] = (np.arange(128)[:, None] // 8 == np.arange(16)[None])
    def bucket(d):
        d = np.asarray(d)
        n = np.maximum(d, 0)
        nf = np.maximum(n, 1).astype(np.float32)
        large = 16 + (np.log(nf / np.float32(16)) / np.float32(np.log(128 / 16)) * np.float32(16)).astype(np.int32)
        return np.where(n < 16, n, np.minimum(large, 31))
    oh = np.zeros((33, 2, 384), np.float32)
    m = np.arange(384)
    for x in range(2):
        dist = (m - 128) if x == 0 else m
        valid = (dist >= 0) & (dist < 128) if x == 0 else (m >= 1) & (m < 128)
        b = bucket(np.clip(dist, 0, 255))
        for mm_ in range(384):
            if valid[mm_]:
                oh[b[mm_], x, mm_] = 1.0
            else:
                oh[32, x, mm_] = NEGB
    return c, oh.reshape(33, 768)


class _Stop(Exception):
    pass


def stop_at(tag):
    if STOP == tag:
        raise _Stop()


def build(nc, n_ptiles=16, do_sample=True, dbg=(), n_stiles=None):
    if n_stiles is None:
        n_stiles = 1 if do_sample else 0
    do_sample = n_stiles > 0
    NST = max(n_stiles, 1)
    P = Prog(nc)
    D = {}

    def din(name, shape):
        D[name] = nc.dram_tensor(name, list(shape), F32, kind="ExternalInput").ap()
        return D[name]

    def dout(name, shape):
        D[name] = nc.dram_tensor(name, list(shape), F32, kind="ExternalOutput").ap()
        return D[name]

    xp = din("xp", [NSEQ, 2048, 1024]); xsm = din("xsm", [NST * 128, 1024])
    ck = din("ck", [NST * 16, 128, 128]); cv = din("cv", [NST * 16, 128, 128])
    sshift = din("sshift", [NST * 16, 1792]); swkv = din("swkv", [NST * 16, 8, 64, 64]); sconv = din("sconv", [NST * 32, 2816])
    rel_bias = din("rel_bias", [32, 8]); norm1_g = din("norm1_g", [1, 1024]); w_in = din("w_in", [1024, 4608])
    sinks = din("sinks", [8]); mu_shift = din("mu_shift", [1792]); w0 = din("w0", [512]); w2 = din("w2", [64, 512])
    a0 = din("a0", [512]); a2 = din("a2", [64, 512]); g2 = din("g2", [128, 512]); k_k = din("k_k", [512])
    k_a = din("k_a", [512]); r_k = din("r_k", [512]); lnx_g = din("lnx_g", [512]); lnx_b = din("lnx_b", [512])
    w_pa = din("w_pa", [512, 1024]); w_pb = din("w_pb", [512, 1024]); w_o = din("w_o", [1024, 1024])
    norm2_g = din("norm2_g", [1, 1024]); w_up = din("w_up", [1024, 5632]); conv_w = din("conv_w", [3, 2816])
    conv_b = din("conv_b", [2816]); w_down = din("w_down", [2816, 1024]); final_g = din("final_g", [1, 1024])
    consts_d = din("consts", [128, C_END]); oh_d = din("oh", [33, 768])

    yp = dout("yp", [NSEQ, 2048, 1024]); ys = dout("ys", [NST * 128, 1024])
    pk = dout("pk", [NSEQ, 128, 128]); pv = dout("pv", [NSEQ, 128, 128]); psh = dout("psh", [NSEQ, 1792])
    pwkv = dout("pwkv", [NSEQ, 8, 64, 64]); pconv = dout("pconv", [NSEQ, 2, 2816])
    sk = dout("sk", [NST * 16, 128, 128]); sv = dout("sv", [NST * 16, 128, 128]); ssh = dout("ssh", [NST * 16, 1792])
    swkvo = dout("swkvo", [NST * 16, 8, 64, 64]); sconvo = dout("sconvo", [NST * 32, 2816])
    dbg_out = {}

    def scratch(name, shape, dtype=BF16):
        return nc.dram_tensor(name, list(shape), dtype, kind="Internal").ap()

    wsc_in = scratch("wsc_in", [9, 128, 4096]); wsc_pa = scratch("wsc_pa", [2, 128, 2048])
    wsc_pb = scratch("wsc_pb", [2, 128, 2048]); wsc_o = scratch("wsc_o", [2, 128, 4096])
    wsc_up = scratch("wsc_up", [11, 128, 4096]); wsc_dn = scratch("wsc_dn", [6, 128, 4096])
    E_d = scratch("E_d", [16, 128, 384], F32)

    chunks_src = []
    for j in range(4):
        chunks_src.append([(j * 64, 64), ((4 + j) * 64, 64)])
    chunks_src.append([(512, 128)]); chunks_src.append([(640, 128)])
    rwb = 768
    chunks_src.append([(rwb + 1536, 128)]); chunks_src.append([(rwb + 1664, 128)])
    for p in range(4):
        chunks_src += [[(rwb + p * 128, 128)], [(rwb + 512 + p * 128, 128)], [(rwb + 1024 + p * 128, 128)]]
    for g in range(16):
        chunks_src.append([(2560 + g * 128, 128)])
    for c, srcs in enumerate(chunks_src):
        b, cc = divmod(c, 4)
        dst = wsc_in[b].rearrange("p (k n) -> p k n", n=512)
        off = cc * 128
        for (lo, n) in srcs:
            P.dma("pool", dst[:, :, off:off + n], w_in[:, lo:lo + n].rearrange("(k p) n -> p k n", p=128),
                  writes=[("wsc_in", c, lo)])
            off += n
    for ch in range(2):
        dpa = wsc_pa[ch].rearrange("p (k n) -> p k n", n=512)
        for j in range(4):
            for half in range(2):
                r0 = (half * 4 + j) * 64
                P.dma("pool", dpa[half * 64:(half + 1) * 64, j, :], w_pa[r0:r0 + 64, ch * 512:(ch + 1) * 512],
                      writes=[("wsc_pa", ch, j, half)])
        P.dma("pool", wsc_pb[ch].rearrange("p (k n) -> p k n", n=512),
              w_pb[:, ch * 512:(ch + 1) * 512].rearrange("(k p) n -> p k n", p=128), writes=[("wsc_pb", ch)])
        P.dma("pool", wsc_o[ch].rearrange("p (k n) -> p k n", n=512),
              w_o[:, ch * 512:(ch + 1) * 512].rearrange("(k p) n -> p k n", p=128), writes=[("wsc_o", ch)])
    for b in range(11):
        dst = wsc_up[b].rearrange("p (k n) -> p k n", n=512)
        P.dma("pool", dst[:, :, 0:256], w_up[:, b * 256:(b + 1) * 256].rearrange("(k p) n -> p k n", p=128),
              writes=[("wsc_up", b, 0)])
        P.dma("pool", dst[:, :, 256:512], w_up[:, 2816 + b * 256:2816 + (b + 1) * 256].rearrange("(k p) n -> p k n", p=128),
              writes=[("wsc_up", b, 1)])
    DN_NK = (8, 8, 6)
    for ch in range(2):
        for rg in range(3):
            nk = DN_NK[rg]
            dst = wsc_dn[ch * 3 + rg].rearrange("p (k n) -> p k n", n=512)
            P.dma("pool", dst[:, 0:nk, :],
                  w_down[rg * 1024:rg * 1024 + nk * 128, ch * 512:(ch + 1) * 512].rearrange("(k p) n -> p k n", p=128),
                  writes=[("wsc_dn", ch * 3 + rg)])

    if STOP == 'A':
        P.finish(); return D
    blk_sched = []
    for b in range(9):
        keys = []
        for c in range(4 * b, 4 * b + 4):
            keys += [("wsc_in", c, lo) for (lo, n) in chunks_src[c]]
        blk_sched.append((wsc_in[b], 4096, keys))
    for ch in range(2):
        blk_sched.append((wsc_pa[ch], 2048, [("wsc_pa", ch, j, h) for j in range(4) for h in range(2)]))
        blk_sched.append((wsc_pb[ch], 2048, [("wsc_pb", ch)]))
    for ch in range(2):
        blk_sched.append((wsc_o[ch], 4096, [("wsc_o", ch)]))
    for b in range(11):
        blk_sched.append((wsc_up[b], 4096, [("wsc_up", b, 0), ("wsc_up", b, 1)]))
    for i in range(6):
        blk_sched.append((wsc_dn[i], DN_NK[i % 3] * 512, [("wsc_dn", i)]))
    NBLK_T = len(blk_sched)
    n_tiles_total = n_ptiles + n_stiles
    NSLOT = 3
    ring_tiles = [P.sbuf("ring%d" % i, [128, 4096], BF16) for i in range(NSLOT)]
    ring_state = {"loaded": 0, "consumed": 0}
    total_blocks = NBLK_T * n_tiles_total

    def ring_get(hold=0):
        k = ring_state["consumed"]
        while ring_state["loaded"] < min(k + NSLOT - hold, total_blocks):
            j = ring_state["loaded"]
            src, nel, keys = blk_sched[j % NBLK_T]
            s = j % NSLOT
            P.dma("sp", ring_tiles[s][:, 0:nel], src[:, 0:nel], reads=keys, writes=[("ring", s)])
            ring_state["loaded"] += 1
        ring_state["consumed"] += 1
        s = k % NSLOT
        return ring_tiles[s], ("ring", s)

    cst = P.sbuf("cst", [128, C_END], F32)
    P.dma("sp", cst[:], consts_d, writes=["cst"])
    ident = cst[:, C_ID:C_ID + 128]
    bo = cst[:, C_BO:C_BO + 128]
    bo64 = cst[:, C_BO64:C_BO64 + 128]
    ones = cst[:, C_ONE:C_ONE + 128]
    ones_bf = P.sbuf("ones_bf", [128, 128], BF16)
    P.op("dve", lambda e: e.tensor_copy(ones_bf[:], ones), reads=["cst"], writes=["ones_bf"])
    zeros_t = P.sbuf("zeros_t", [128, 64], F32)
    P.op("pool", lambda e: e.memset(zeros_t[:], 0.0), writes=["zeros_t"])

    def load_cols(name, vec, ncol):
        t = P.sbuf(name, [128, ncol], F32)
        P.dma("sp", t[:], vec.rearrange("(c p) -> p c", p=128), writes=[name], allow_slow_non_contiguous=True)
        return t

    mu_c = load_cols("mu_c", mu_shift, 14)
    om_c = P.sbuf("om_c", [128, 14], F32)
    P.op("dve", lambda e: e.tensor_scalar(om_c[:], mu_c[:], -1.0, 1.0, ALU.mult, ALU.add), reads=["mu_c"], writes=["om_c"])
    w0_c = load_cols("w0_c", w0, 4); a0_c = load_cols("a0_c", a0, 4); kk_c = load_cols("kk_c", k_k, 4)
    ka_c = load_cols("ka_c", k_a, 4); rk_c = load_cols("rk_c", r_k, 4); lg_c = load_cols("lg_c", lnx_g, 4)
    lb_c = load_cols("lb_c", lnx_b, 4); cb_c = load_cols("cb_c", conv_b, 22)
    cw_c = P.sbuf("cw_c", [128, 3, 22], F32)
    for j in range(3):
        P.dma("sp", cw_c[:, j, :], conv_w[j].rearrange("(c p) -> p c", p=128), writes=[("cw_c", j)], allow_slow_non_contiguous=True)
    CW_KEYS = [("cw_c", j) for j in range(3)]
    gb = {}
    g1c = load_cols("g1b", norm1_g.rearrange("a n -> (a n)"), 8)
    g2c = load_cols("g2b", norm2_g.rearrange("a n -> (a n)"), 8)
    gcol = {"g1b": g1c, "g2b": g2c}
    for nm, src in (("gfb", final_g),):
        gb[nm] = P.sbuf(nm, [128, 1024], F32)
        P.dma("sp", gb[nm][:], src.partition_broadcast(128).rearrange("p a n -> p (a n)"), writes=[nm])
    w2b = P.sbuf("w2b", [128, 512], BF16); g2bf = P.sbuf("g2bf", [128, 512], BF16)
    P.dma("pool", w2b[0:64, :], w2, writes=["w2b"])
    P.dma("pool", w2b[64:128, :], a2, writes=["a2b"])
    P.dma("pool", g2bf[:], g2, writes=["g2bf"])
    sk_t = P.sbuf("sk_t", [128, 4], F32)
    P.dma("sp", sk_t[0:64, :], sinks[0:4].partition_broadcast(64), writes=[("sk_t", 0)])
    P.dma("sp", sk_t[64:128, :], sinks[4:8].partition_broadcast(64), writes=[("sk_t", 1)])
    esk = P.sbuf("esk", [128, 4], F32)
    P.op("act", lambda e: e.activation(esk[:], sk_t[:], AF.Exp), reads=[("sk_t", 0), ("sk_t", 1)], writes=["esk"])
    esink = P.sbuf("esink", [128, 4, 128], F32)
    for g in range(4):
        P.op("act", lambda e, g=g: e.activation(esink[:, g, :], ones, AF.Copy, scale=esk[:, g:g + 1]),
             reads=["cst", "esk"], writes=[("esink", g)])
    ESINK_KEYS = [("esink", g) for g in range(4)]

    xn = P.sbuf("xn", [128, 1024], F32)
    if STOP == 'B':
        P.finish(); return D
    for _i in range(int(os.environ.get('KDUMMY', '0'))):
        P.dma('sp', zeros_t[:, 0:32], consts_d[:, 0:32], writes=['zeros_dummy'])
    for _i in range(int(os.environ.get('KBIG', '0'))):
        P.dma('sp', yp[1], xp[0], writes=['yp1_dummy'])
    BK = [P.psum("bank%d" % i, [128, 512], F32) for i in range(8)]

    def bk(i):
        return ("ps", i)

    rb = P.sbuf("rb", [33, 8], F32)
    Yt = P.sbuf("Yt", [128, 4, NTP], F32)
    bon = P.sbuf("bon", [128, 4, NTP], F32)
    Lh = Yt[0:33, :, :].rearrange("p a t -> p (a t)").rearrange("p (h t) -> p h t", t=128)
    oh_t = bon[0:33, :, :].rearrange("p a t -> p (a t)")[:, 0:768]
    P.dma("sp", rb[0:32, :], rel_bias, writes=["rb"])
    P.dma("sp", oh_t, oh_d, writes=["oh_t"])
    P.op("pool", lambda e: e.memset(Lh[32:33, :, :], 1.0), writes=["Lh1"])
    for h in range(8):
        P.op("act", lambda e, h=h: e.activation(Lh[0:32, h, :], cst[0:32, C_ONE:C_ONE + 128], AF.Copy, scale=rb[0:32, h:h + 1]),
             reads=["cst", "rb"], writes=[("Lh", h)])
    biasT = [[P.sbuf("biasT%d%d" % (x, kv), [128, 4, 128], F32) for kv in range(2)] for x in range(2)]
    for x in range(2):
        for h in range(8):
            bnk = 4 + (x * 8 + h) % 4
            P.op("pe", lambda e, h=h, x=x, bnk=bnk: e.matmul(BK[bnk][:, 0:384], Lh[:, h, :], oh_t[:, x * 384:(x + 1) * 384], start=True, stop=True),
                 reads=["Lh1", ("Lh", h), "oh_t"], writes=[bk(bnk)])
            P.op("dve", lambda e, bnk=bnk: e.tensor_copy(xn[:, 0:384], BK[bnk][:, 0:384]), reads=[bk(bnk)], writes=["xn"])
            P.dma("sp", E_d[x * 8 + h], xn[:, 0:384], reads=["xn"], writes=[("E_d", x, h)])
            kv, g = divmod(h, 4)
            skew = bass.AP(E_d.tensor, (x * 8 + h) * 128 * 384 + 128, [[383, 128], [1, 128]])
            P.dma("sp", biasT[x][kv][:, g, :], skew, reads=[("E_d", x, h)], writes=[("biasT", x, kv, g)])
    if STOP == 'C':
        P.finish(); return D
    if os.environ.get('NOBAR') is None:
        P.barrier()
    BIAS_KEYS = {(x, kv): [("biasT", x, kv, g) for g in range(4)] for x in range(2) for kv in range(2)}

    wstack = [ExitStack()]

    def wsb(name, shape, dtype):
        return wstack[0].enter_context(nc.sbuf_tensor(name, list(shape), dtype))

    PT2 = None
    xn2 = None
    Pvg = None
    xt = None
    stat = None
    hT = None
    qT = None
    kT = None
    kTf = None
    vTf = None
    vtok = None
    kvrow = None
    gates = None
    attT = None
    rwoT = None
    PT = None
    rd_t = None
    pbuf = None
    tmpA = None
    xs3 = None
    TW = None
    SG = None
    car = None
    art = None
    kt_ = None
    bt_ = None
    gT = None
    cCt = None
    KPtok = None
    BPtok = None
    Vtok = None
    rw = None
    S = None
    AK = None
    AB = None
    Mm_ = None
    Mt_ = None
    Pv = None
    VA = None
    VK = None
    XT = None
    UT = None
    y1 = None
    ugx = None
    cc_t = None
    gl_t = None
    actT = None
    ccar = None

    def alloc_work(NTM, sfx):
        nonlocal PT2, xn2, Pvg, xt, stat, hT, qT, kT, kTf, vTf, vtok, kvrow, gates, attT, rwoT, PT, rd_t, pbuf, tmpA, xs3, TW, SG, car, art, kt_, bt_, gT, cCt, KPtok, BPtok, Vtok, rw, S, AK, AB, Mm_, Mt_, Pv, VA, VK, XT, UT, y1, ugx, cc_t, gl_t, actT, ccar
        NB_MAX = NTM // 128
        xt = [wsb(("xt%d" % i) + sfx, [128, 1024], F32) for i in range(NB_MAX)]
        stat = wsb("stat" + sfx, [128, 32], F32)
        xn2 = wsb("xn2" + sfx, [128, 1024], F32) if NTM > 128 else None
        hT = wsb("hT" + sfx, [128, 8, NTM], BF16)
        qT = wsb("qT" + sfx, [128, 4, NTM], BF16)
        kT = wsb("kT" + sfx, [128, 128 + NTM], BF16)
        kTf = wsb("kTf" + sfx, [128, NTM], F32)
        vTf = wsb("vTf" + sfx, [128, NTM], F32)
        vtok = wsb("vtok" + sfx, [128, NB_MAX + 1, 128], BF16)
        kvrow = wsb("kvrow" + sfx, [128, 2, 128], F32)
        gates = wsb("gates" + sfx, [128, 16, NTM], BF16)
        attT = wsb("attT" + sfx, [128, 4, NTM], BF16)
        rwoT = wsb("rwoT" + sfx, [128, 4, NTM], BF16)
        PT = [wsb(("PT%d" % i) + sfx, [128, 512], BF16) for i in range(2)]
        PT2 = [wsb(("PTb%d" % i) + sfx, [128, 512], BF16) for i in range(2)] if NTM > 128 else PT
        rd_t = wsb("rd_t" + sfx, [128, 512], F32)
        pbuf = wsb("pbuf" + sfx, [128, NTM + 16], F32)
        tmpA = wsb("tmpA" + sfx, [128, NTM], F32)
        xs3 = [wsb(("xs3_%d" % i) + sfx, [128, NTM], F32) for i in range(3)]
        TW = wsb("TW" + sfx, [128, NTM], BF16)
        SG = wsb("SG" + sfx, [128, NTM], BF16)
        car = wsb("car" + sfx, [128, 128], F32)
        art = wsb("art" + sfx, [128, 4, 2, NTM], RW_DT)
        kt_ = wsb("kt_" + sfx, [128, 4, NTM], RW_DT)
        bt_ = wsb("bt_" + sfx, [128, 4, NTM], RW_DT)
        gT = wsb("gT" + sfx, [128, 4, NTM], F32)
        cCt = wsb("cCt" + sfx, [128, 4, NTM // 8], F32)
        KPtok = wsb("KPtok" + sfx, [128, NB_MAX, 512], RW_DT)
        BPtok = wsb("BPtok" + sfx, [128, NB_MAX, 512], RW_DT)
        Vtok = wsb("Vtok" + sfx, [128, NB_MAX, 512], RW_DT)
        rw = {n: wsb("rw_" + sfx + n, [128, NTM], F32) for n in ("sg", "cs", "t1", "t2", "t3", "a", "kkn", "k2", "ein", "einv", "eend", "nk")}
        S = wsb("S" + sfx, [128, 4, 64], F32)
        AK = [wsb(("AK%d" % h) + sfx, [128, 256], RW_DT) for h in range(8)]
        AB = [wsb(("AB%d" % h) + sfx, [128, 256], RW_DT) for h in range(8)]
        Mm_ = [wsb(("Mmg%d" % g) + sfx, [128, 4, 128], RW_DT) for g in range(2)]
        Mt_ = [wsb(("Mtg%d" % g) + sfx, [128, 4, 128], RW_DT) for g in range(2)]
        Pvg = [wsb(("Pvg%d" % g) + sfx, [128, 4, 128], RW_DT) for g in range(2)]
        Pv = [Pvg[h // 4][:, h % 4, :] for h in range(8)]
        VA = wsb("VA" + sfx, [128, 512], F32)
        VK = wsb("VK" + sfx, [128, 4, 128], F32)
        XT = wsb("XT" + sfx, [128, 512], RW_DT)
        UT = wsb("UT" + sfx, [128, 512], RW_DT)
        y1 = wsb("y1" + sfx, [128, 4, 64], F32)
        ugx = [wsb(("ugx%d" % i) + sfx, [128, NTM + 40], F32) for i in range(2)]
        cc_t = [wsb(("cc_t%d" % i) + sfx, [128, NTM], F32) for i in range(2)]
        gl_t = [wsb(("gl_t%d" % i) + sfx, [128, NTM], F32) for i in range(2)]
        actT = wsb("actT" + sfx, [128, 22, NTM], BF16)
        ccar = wsb("ccar" + sfx, [128, 2, 128], F32)

    P.stack.callback(lambda: wstack[0].close())
    alloc_work(NTP, "")
    P.op("pool", lambda e: e.memset(car[:], 0.0), writes=["car_init"])
    P.op("pool", lambda e: e.memset(ccar[:].rearrange("p a f -> p (a f)"), 0.0), writes=["ccar_init"])

    def debug_tap(name, ap_sb, shape, keys):
        if name in dbg:
            d_ = nc.dram_tensor("dbg_" + name, list(shape), ap_sb.dtype, kind="ExternalOutput").ap()
            D["dbg_" + name] = d_
            P.dma("sp", d_, ap_sb, reads=keys, writes=[("dbg", name)])

    def do_tile(kind, seq, ti):
        prompt = kind == "p"
        NT = NTP if prompt else 128
        nb = NT // 128
        NBs, TB = (1, NT) if prompt else (16, 8)
        C = 64 if prompt else 8
        LV = 5 if prompt else 2
        cm, cl = (C_M64, C_L64) if prompt else (C_M8, C_L8)
        first = prompt and ti == 0
        last = prompt and ti == 2048 // NTP - 1
        xrows = xp[seq, ti * NT:(ti + 1) * NT, :] if prompt else xsm[seq * 128:(seq + 1) * 128, :]
        yrows = yp[seq, ti * NT:(ti + 1) * NT, :] if prompt else ys[seq * 128:(seq + 1) * 128, :]

        for b in range(nb):
            P.dma("sp", xt[b][:], xrows[b * 128:(b + 1) * 128, :], writes=[("xt", b)])

        stop_at("D1")

        def norm_T(gname):
            def steps(b):
                so = 16 * b
                st = stat[:, so:so + 16]
                xnb = xn if b == 0 else xn2
                xk_ = "xn" if b == 0 else "xn2"
                sk_ = "stat%d_" % b
                bks = (4, 5) if b == 0 else (6, 7)

                def tr(half):
                    for q4 in range(4):
                        kc = half * 4 + q4
                        P.op("pe", lambda e, kc=kc, q4=q4: e.transpose(BK[bks[half]][:, q4 * 128:(q4 + 1) * 128], xnb[:, kc * 128:(kc + 1) * 128], ident),
                             reads=[xk_, "cst"], writes=[bk(bks[half])])

                def ev(half):
                    for q4 in range(4):
                        kc = half * 4 + q4
                        if half == 0:
                            P.op("act", lambda e, kc=kc, q4=q4: e.activation(hT[:, kc, b * 128:(b + 1) * 128], BK[bks[0]][:, q4 * 128:(q4 + 1) * 128], AF.Copy, scale=gcol[gname][:, kc:kc + 1]),
                                 reads=[bk(bks[0]), gname], writes=[("hT", kc)])
                        else:
                            P.op("dve", lambda e, kc=kc, q4=q4: e.tensor_scalar(hT[:, kc, b * 128:(b + 1) * 128], BK[bks[1]][:, q4 * 128:(q4 + 1) * 128], gcol[gname][:, kc:kc + 1], None, ALU.mult),
                                 reads=[bk(bks[1]), gname], writes=[("hT", kc)])
                return [
                    lambda: P.op("dve", lambda e: e.bn_stats(st[:, 0:6], xt[b][:, 0:512]), reads=[("xt", b)], writes=[sk_ + "0"]),
                    lambda: P.op("dve", lambda e: e.bn_stats(st[:, 6:12], xt[b][:, 512:1024]), reads=[("xt", b)], writes=[sk_ + "1"]),
                    lambda: P.op("dve", lambda e: e.bn_aggr(st[:, 12:14], st[:, 0:12]), reads=[sk_ + "0", sk_ + "1"], writes=[sk_ + "2"]),
                    lambda: P.op("dve", lambda e: e.scalar_tensor_tensor(st[:, 14:15], st[:, 12:13], st[:, 12:13], st[:, 13:14], ALU.mult, ALU.add), reads=[sk_ + "2"], writes=[sk_ + "3"]),
                    lambda: P.op("act", lambda e: e.activation(st[:, 15:16], st[:, 14:15], AF.Sqrt, bias=1e-6), reads=[sk_ + "3"], writes=[sk_ + "4"]),
                    lambda: P.op("dve", lambda e: e.reciprocal(st[:, 15:16], st[:, 15:16]), reads=[sk_ + "4"], writes=[sk_ + "4"]),
                    lambda: P.op("dve", lambda e: e.tensor_scalar(xnb[:], xt[b][:], st[:, 15:16], None, ALU.mult), reads=[("xt", b), sk_ + "4"], writes=[xk_]),
                    lambda: tr(0), lambda: tr(1), lambda: ev(0), lambda: ev(1),
                ]
            for group in zip(*[steps(b) for b in range(nb)]):
                for step in group:
                    step()
        HT_KEYS = [("hT", kc) for kc in range(8)]

        norm_T("g1b")
        stop_at("D")

        def v3(ap2, k):
            return ap2.rearrange("p (b t) -> p b t", t=k)

        ts_ctr = [0]

        def token_shift(bnk, n, dst, dst_key):
            ti_ = ts_ctr[0] % 2
            ts_ctr[0] += 1
            tA = tmpA if ti_ == 0 else pbuf
            tkey, tkey0 = ("tmpA", ti_), ("tmpA0", ti_)
            PS3 = v3(BK[bnk][:, 0:NT], TB)
            tA3 = v3(tA[:, 0:NT], TB)
            P.op("act", lambda e: e.activation(tA3[:, :, 1:TB], PS3[:, :, 0:TB - 1], AF.Copy, scale=mu_c[:, n:n + 1]), reads=[bk(bnk), "mu_c"], writes=[tkey])
            if prompt:
                if first:
                    P.op("pool", lambda e: e.memset(tA[:, 0:1], 0.0), writes=[tkey0])
                else:
                    P.op("pool", lambda e: e.tensor_scalar(tA[:, 0:1], car[:, n:n + 1], mu_c[:, n:n + 1], None, ALU.mult), reads=[("car", n), "mu_c"], writes=[tkey0])
            else:
                P.op("pool", lambda e: e.tensor_scalar(tA3[:, :, 0:1], SB["shcar"][:, n, :].unsqueeze(2), mu_c[:, n:n + 1], None, ALU.mult), reads=[("shcar", n), "mu_c"], writes=[tkey0])
            P.op("dve", lambda e: e.scalar_tensor_tensor(v3(dst, TB), PS3, om_c[:, n:n + 1], tA3, ALU.mult, ALU.add),
                 reads=[bk(bnk), tkey, tkey0, "om_c"], writes=[dst_key])
            if prompt:
                P.op("dve", lambda e: e.tensor_copy(car[:, n:n + 1], BK[bnk][:, NT - 1:NT]), reads=[bk(bnk)], writes=[("car", n)])
            else:
                P.op("dve", lambda e: e.tensor_copy(SB["shout"][:, n, :].unsqueeze(2), PS3[:, :, TB - 1:TB]), reads=[bk(bnk)], writes=[("shout", n)])

        def pair_process(p):
            xr, xk, xv = xs3[0][:, 0:NT], xs3[1][:, 0:NT], xs3[2][:, 0:NT]
            R = {n: rw[n][:, 0:NT] for n in rw}
            nch = NT // C
            b6, b7 = 6, 7
            P.op("pool", lambda e: e.tensor_scalar(R["kkn"], xk, kk_c[:, p:p + 1], None, ALU.mult), reads=["xs1", "kk_c"], writes=["r_kkn"])
            P.op("pool", lambda e: e.tensor_tensor(R["t2"], R["kkn"], R["kkn"], ALU.mult), reads=["r_kkn"], writes=["r_t2"])
            P.op("pe", lambda e: e.matmul(BK[b6][:, 0:NT], w2b[0:64, p * 128:(p + 1) * 128], TW[0:64, 0:NT], start=True, stop=True),
                 reads=["w2b", "TW"], writes=[bk(b6)])
            P.op("pe", lambda e: e.matmul(BK[b7][:, 0:NT], w2b[64:128, p * 128:(p + 1) * 128], TW[64:128, 0:NT], start=True, stop=True),
                 reads=["a2b", "TW"], writes=[bk(b7)])
            P.op("act", lambda e: e.activation(R["sg"], BK[b6][:, 0:NT], AF.Sigmoid, bias=w0_c[:, p:p + 1]), reads=[bk(b6), "w0_c"], writes=["r_sg"])
            P.op("act", lambda e: e.activation(R["a"], BK[b7][:, 0:NT], AF.Sigmoid, bias=a0_c[:, p:p + 1]), reads=[bk(b7), "a0_c"], writes=["r_a"])
            P.op("pe", lambda e: e.matmul(BK[b6][:, 0:NT], bo, R["t2"], start=True, stop=True), reads=["cst", "r_t2"], writes=[bk(b6)])
            for b in range(nb):
                tb_ = 4 + b % 2
                P.op("pe", lambda e, b=b, tb_=tb_: e.transpose(BK[tb_][:, 0:128], xs3[2][:, b * 128:(b + 1) * 128], ident), reads=["xs2", "cst"], writes=[bk(tb_)])
                P.op("act", lambda e, b=b, tb_=tb_: e.copy(Vtok[:, b, p * 128:(p + 1) * 128], BK[tb_][:, 0:128]), reads=[bk(tb_)], writes=[("Vtok", b, p)])
            for c in range(nch):
                P.op("dve", lambda e, c=c: e.tensor_tensor_scan(R["cs"][:, c * C:(c + 1) * C], ones[:, 0:C], R["sg"][:, c * C:(c + 1) * C], 0.0, ALU.mult, ALU.add),
                     reads=["r_sg", "cst"], writes=["r_cs"])
            P.op("act", lambda e: e.activation(R["t2"], BK[b6][:, 0:NT], AF.Sqrt), reads=[bk(b6)], writes=["r_t2"])
            P.op("dve", lambda e: e.tensor_scalar(R["t3"], R["a"], -1.0, ka_c[:, p:p + 1], ALU.add, ALU.mult), reads=["r_a", "ka_c"], writes=["r_t3"])
            P.op("dve", lambda e: e.scalar_tensor_tensor(R["k2"], R["t3"], 1.0, xk, ALU.add, ALU.mult), reads=["r_t3", "xs1"], writes=["r_k2"])
            P.op("act", lambda e: e.activation(R["ein"], R["cs"], AF.Exp, scale=-KAPPA), reads=["r_cs"], writes=["r_ein"])
            P.op("act", lambda e: e.activation(R["einv"], R["cs"], AF.Exp, scale=KAPPA), reads=["r_cs"], writes=["r_einv"])
            P.op("pool", lambda e: e.tensor_tensor(R["t1"], R["cs"], R["sg"], ALU.subtract), reads=["r_cs", "r_sg"], writes=["r_t1"])
            P.op("pool", lambda e: e.tensor_scalar(R["nk"][:, 0:nch], R["cs"][:, C - 1:NT:C], -KAPPA, None, ALU.mult), reads=["r_cs"], writes=["r_nk"])
            P.op("act", lambda e: e.activation(R["t1"], R["t1"], AF.Exp, scale=-KAPPA), reads=["r_t1"], writes=["r_t1"])
            for c in range(nch):
                P.op("act", lambda e, c=c: e.activation(R["eend"][:, c * C:(c + 1) * C], R["cs"][:, c * C:(c + 1) * C], AF.Exp, scale=KAPPA, bias=R["nk"][:, c:c + 1]),
                     reads=["r_cs", "r_nk"], writes=["r_eend"])
            P.op("dve", lambda e: e.tensor_scalar(R["t2"], R["t2"], 1e-12, None, ALU.max), reads=["r_t2"], writes=["r_t2"])
            P.op("dve", lambda e: e.reciprocal(R["t2"], R["t2"]), reads=["r_t2"], writes=["r_t2"])
            P.op("dve", lambda e: e.tensor_tensor(R["kkn"], R["kkn"], R["t2"], ALU.mult), reads=["r_kkn", "r_t2"], writes=["r_kkn"])
            P.op("pool", lambda e: e.tensor_copy(cCt[:, p, 0:nch], R["ein"][:, C - 1:NT:C]), reads=["r_ein"], writes=[("cCt", p)])
            P.op("pool", lambda e: e.tensor_tensor(art[:, p, 1, 0:NT], xr, R["ein"], ALU.mult), reads=["xs0", "r_ein"], writes=[("art", p)])
            P.op("dve", lambda e: e.tensor_tensor(kt_[:, p, 0:NT], R["k2"], R["einv"], ALU.mult), reads=["r_k2", "r_einv"], writes=[("kt", p)])
            P.op("dve", lambda e: e.scalar_tensor_tensor(art[:, p, 0, 0:NT], R["kkn"], -1.0, R["t1"], ALU.mult, ALU.mult), reads=["r_kkn", "r_t1"], writes=[("art", p)])
            P.op("pool", lambda e: e.tensor_tensor(R["t3"], R["kkn"], R["a"], ALU.mult), reads=["r_kkn", "r_a", "r_k2"], writes=["r_t3"])
            P.op("dve", lambda e: e.tensor_tensor(bt_[:, p, 0:NT], R["t3"], R["einv"], ALU.mult), reads=["r_t3", "r_einv"], writes=[("bt", p)])
            P.op("pool", lambda e: e.tensor_tensor(R["t2"], R["k2"], R["eend"], ALU.mult), reads=["r_k2", "r_eend", "r_kkn"], writes=["r_t2"])
            P.op("dve", lambda e: e.scalar_tensor_tensor(R["t1"], xr, rk_c[:, p:p + 1], R["k2"], ALU.mult, ALU.mult), reads=["xs0", "rk_c", "r_k2", ("art", p)], writes=["r_t1"])
            P.op("pool", lambda e: e.tensor_tensor(R["t3"], R["t3"], R["eend"], ALU.mult), reads=["r_t3", "r_eend", ("bt", p)], writes=["r_t3"])
            for b in range(nb):
                tb_ = 4 + b % 2
                P.op("pe", lambda e, b=b, tb_=tb_: e.transpose(BK[tb_][:, 0:128], R["t2"][:, b * 128:(b + 1) * 128], ident), reads=["r_t2", "cst"], writes=[bk(tb_)])
                P.op("act", lambda e, b=b, tb_=tb_: e.copy(KPtok[:, b, p * 128:(p + 1) * 128], BK[tb_][:, 0:128]), reads=[bk(tb_)], writes=[("KPtok", b, p)])
            P.op("pe", lambda e: e.matmul(BK[b7][:, 0:NT], bo, R["t1"], start=True, stop=True), reads=["cst", "r_t1"], writes=[bk(b7)])
            for b in range(nb):
                tb_ = 4 + b % 2
                P.op("pe", lambda e, b=b, tb_=tb_: e.transpose(BK[tb_][:, 0:128], R["t3"][:, b * 128:(b + 1) * 128], ident), reads=["r_t3", "cst"], writes=[bk(tb_)])
                P.op("act", lambda e, b=b, tb_=tb_: e.copy(BPtok[:, b, p * 128:(p + 1) * 128], BK[tb_][:, 0:128]), reads=[bk(tb_)], writes=[("BPtok", b, p)])
            P.op("dve", lambda e: e.tensor_tensor(bon[:, p, 0:NT], BK[b7][:, 0:NT], xv, ALU.mult), reads=[bk(b7), "xs2"], writes=[("bon", p)])
            P.op("pe", lambda e: e.matmul(BK[b6][:, 0:NT], g2bf[:, p * 128:(p + 1) * 128], SG[:, 0:NT], start=True, stop=True), reads=["g2bf", "SG"], writes=[bk(b6)])
            P.op("act", lambda e: e.copy(gT[:, p, 0:NT], BK[b6][:, 0:NT]), reads=[bk(b6)], writes=[("gT", p)])

        for blkb in range(9):
            slot, skey = ring_get()
            sl3 = slot[:, 0:4096].rearrange("p (k n) -> p k n", n=512)
            for cc in range(4):
                c = blkb * 4 + cc
                bnk = c % 4
                for kc in range(8):
                    P.op("pe", lambda e, kc=kc, cc=cc, bnk=bnk, sl3=sl3: e.matmul(BK[bnk][:, 0:NT], sl3[:, kc, cc * 128:(cc + 1) * 128], hT[:, kc, 0:NT], start=(kc == 0), stop=(kc == 7)),
                         reads=[skey] + HT_KEYS, writes=[bk(bnk)])
                stop_at("E%d" % c)
                if c < 4:
                    P.op("act", lambda e, c=c, bnk=bnk: e.copy(qT[:, c, 0:NT], BK[bnk][:, 0:NT]), reads=[bk(bnk)], writes=[("qT", c)])
                elif c == 4:
                    P.op("act", lambda e, bnk=bnk: e.copy(kTf[:, 0:NT], BK[bnk][:, 0:NT]), reads=[bk(bnk)], writes=["kTf"])
                    P.op("pool", lambda e: e.tensor_copy(kT[:, 128:128 + NT], kTf[:, 0:NT]), reads=["kTf"], writes=["kT"])
                elif c == 5:
                    P.op("act", lambda e, bnk=bnk: e.copy(vTf[:, 0:NT], BK[bnk][:, 0:NT]), reads=[bk(bnk)], writes=["vTf"])
                    for b in range(nb):
                        tb_ = 4 + b % 2
                        P.op("pe", lambda e, b=b, tb_=tb_: e.transpose(BK[tb_][:, 0:128], vTf[:, b * 128:(b + 1) * 128], ident), reads=["vTf", "cst"], writes=[bk(tb_)])
                        P.op("dve", lambda e, b=b, tb_=tb_: e.tensor_copy(vtok[:, 1 + b, :], BK[tb_][:, 0:128]), reads=[bk(tb_)], writes=[("vtok", 1 + b)])
                        if (prompt and last and b == nb - 1) or not prompt:
                            P.op("dve", lambda e, tb_=tb_: e.tensor_copy(kvrow[:, 1, :], BK[tb_][:, 0:128]), reads=[bk(tb_)], writes=[("kvrow", 1)])
                elif c == 6:
                    token_shift(bnk, 12, xs3[0][:, 0:NT], "xs0")
                    P.op("act", lambda e: e.activation(TW[0:64, 0:NT], xs3[0][0:64, 0:NT], AF.Tanh), reads=["xs0"], writes=["TW"])
                    P.op("pool", lambda e: e.tensor_copy(TW[64:128, 0:NT], xs3[0][64:128, 0:NT]), reads=["xs0"], writes=["TW"])
                elif c == 7:
                    token_shift(bnk, 13, xs3[0][:, 0:NT], "xs0")
                    P.op("act", lambda e: e.activation(SG[:, 0:NT], xs3[0][:, 0:NT], AF.Sigmoid), reads=["xs0"], writes=["SG"])
                elif c < 20:
                    p, which = divmod(c - 8, 3)
                    token_shift(bnk, which * 4 + p, xs3[which][:, 0:NT], "xs%d" % which)
                    if which == 2:
                        pair_process(p)
                else:
                    gi = c - 20
                    P.op("act", lambda e, gi=gi, bnk=bnk: e.activation(gates[:, gi, 0:NT], BK[bnk][:, 0:NT], AF.Sigmoid), reads=[bk(bnk)], writes=[("gates", gi)])

        stop_at("E")
        debug_tap("qT", qT[:, :, 0:NT], [128, 4, NT], [("qT", c) for c in range(4)])

        if prompt:
            def att_steps(b, kv):
                gbk = ti * nb + b
                ph = slice(kv * 64, kv * 64 + 64)
                qv = qT[ph, :, b * 128:(b + 1) * 128]
                xs_ = [0] + ([1] if gbk > 0 else [])
                bb = 4 if kv == 0 else 0
                PTk = PT if kv == 0 else PT2

                def sc():
                    for x in xs_:
                        kcols = slice(128 + b * 128, 256 + b * 128) if x == 0 else slice(b * 128, 128 + b * 128)
                        P.op("pe", lambda e, kcols=kcols, x=x: e.matmul(BK[bb + x][:], kT[ph, kcols], qv, start=True, stop=True),
                             reads=["kT"] + [("qT", c) for c in range(4)], writes=[bk(bb + x)])

                def bias():
                    for x in xs_:
                        P.op("dve", lambda e, x=x: e.scalar_tensor_tensor(BK[bb + x][:], BK[bb + x][:], 0.125, biasT[x][kv][:].rearrange("p g q -> p (g q)"), ALU.mult, ALU.add),
                             reads=[bk(bb + x)] + BIAS_KEYS[(x, kv)], writes=[bk(bb + x)])

                def ex():
                    for x in xs_:
                        P.op("act", lambda e, x=x: e.activation(PTk[x][:], BK[bb + x][:], AF.Exp), reads=[bk(bb + x)], writes=[("PT", kv, x)])

                def pv():
                    for i, x in enumerate(xs_):
                        vb = 1 + b if x == 0 else b
                        P.op("pe", lambda e, x=x, vb=vb, i=i: e.matmul(BK[bb + 2][:], vtok[:, vb, :], PTk[x][:], start=(i == 0), stop=(i == len(xs_) - 1)),
                             reads=[("vtok", vb), ("PT", kv, x)], writes=[bk(bb + 2)])
                    for i, x in enumerate(xs_):
                        P.op("pe", lambda e, x=x, i=i: e.matmul(BK[bb + 3][:], ones_bf[:], PTk[x][:], start=(i == 0), stop=(i == len(xs_) - 1)),
                             reads=["ones_bf", ("PT", kv, x)], writes=[bk(bb + 3)])
                return [
                    sc, bias, ex, pv,
                    lambda: P.op("dve", lambda e: e.tensor_tensor(rd_t[ph, :], BK[bb + 3][ph, :], esink[ph, :, :].rearrange("p g q -> p (g q)"), ALU.add),
                                 reads=[bk(bb + 3)] + ESINK_KEYS, writes=[("rd_t", kv)]),
                    lambda: P.op("dve", lambda e: e.reciprocal(rd_t[ph, :], rd_t[ph, :]), reads=[("rd_t", kv)], writes=[("rd_t", kv)]),
                    lambda: P.op("dve", lambda e: e.tensor_tensor(attT[ph, :, b * 128:(b + 1) * 128], BK[bb + 2][ph, :].rearrange("p (g q) -> p g q", q=128), rd_t[ph, :].rearrange("p (g q) -> p g q", q=128), ALU.mult),
                                 reads=[bk(bb + 2), ("rd_t", kv)], writes=[("attT", kv, b)]),
                ]
            for b in range(nb):
                for s_a, s_b in zip(att_steps(b, 0), att_steps(b, 1)):
                    s_a()
                    s_b()
            P.op("pool", lambda e: e.tensor_copy(kT[:, 0:128], kT[:, NT:NT + 128]), reads=["kT"], writes=["kT"])
            P.op("pool", lambda e: e.tensor_copy(vtok[:, 0, :], vtok[:, nb, :]), reads=[("vtok", nb)], writes=[("vtok", 0)])
            if last and 'pk' not in SKIP:
                P.op("pe", lambda e: e.transpose(BK[4][:, 0:128], kTf[:, NT - 128:NT], ident), reads=["kTf", "cst"], writes=[bk(4)])
                P.op("act", lambda e: e.copy(kvrow[:, 0, :], BK[4][:, 0:128]), reads=[bk(4)], writes=[("kvrow", 0)])
                P.dma("sp", pk[seq], kvrow[:, 0, :], reads=[("kvrow", 0)], writes=["pk"])
                P.dma("sp", pv[seq], kvrow[:, 1, :], reads=[("kvrow", 1)], writes=["pv"])
        else:
            sample_attention()
        ATT_KEYS = [("attT", kv, b) for kv in range(2) for b in range(nb)]
        stop_at("F")
        debug_tap("attT", attT[:, :, 0:NT], [128, 4, NT], ATT_KEYS)

        def hcol(h):
            return (h % 2) * 256 + (h // 2) * 64

        def rw_pre(b):
            bc = slice(b * 128, (b + 1) * 128)
            for h in range(8):
                p, h2 = divmod(h, 2)
                ph = slice(h2 * 64, h2 * 64 + 64)
                b0, b1, b2 = (4, 5, 6) if h % 2 == 0 else (0, 1, 2)
                P.op("pe", lambda e, p=p, ph=ph: e.matmul(BK[b0][:, 0:256], kt_[ph, p, bc], art[ph, p, :, bc], start=True, stop=True),
                     reads=[("kt", p), ("art", p)], writes=[bk(b0)])
                P.op("dve", lambda e, h=h: e.tensor_tensor(AK[h][:], BK[b0][:, 0:256], cst[:, cm:cm + 256], ALU.mult), reads=[bk(b0), "cst"], writes=[("AK", h)])
                P.op("pe", lambda e, p=p, ph=ph: e.matmul(BK[b1][:, 0:256], bt_[ph, p, bc], art[ph, p, :, bc], start=True, stop=True),
                     reads=[("bt", p), ("art", p)], writes=[bk(b1)])
                P.op("dve", lambda e, h=h: e.tensor_tensor(AB[h][:], BK[b1][:, 0:256], cst[:, cm:cm + 256], ALU.mult), reads=[bk(b1), "cst"], writes=[("AB", h)])
                P.op("pe", lambda e, p=p, ph=ph: e.matmul(BK[b2][:, 0:128], art[ph, p, 0, bc], bt_[ph, p, bc], start=True, stop=True),
                     reads=[("bt", p), ("art", p)], writes=[bk(b2)])
                P.op("dve", lambda e, h=h: e.tensor_tensor(Mt_[h // 4][:, h % 4, :], BK[b2][:, 0:128], cst[:, cl:cl + 128], ALU.mult), reads=[bk(b2), "cst"], writes=[("Mt", h // 4)])
                P.op("pool", lambda e, h=h: e.tensor_tensor(Pv[h], AB[h][:, 0:128], ident, ALU.add), reads=[("AB", h), "cst"], writes=[("Pv", h)])
            for lv in range(1, LV + 1):
                lastlv = lv == LV
                for g in range(2):
                    bMT, bM, bP = (4, 5, 6) if g == 0 else (0, 1, 2)
                    for j in range(4):
                        h = 4 * g + j
                        Mcur = AB[h][:, 0:128] if lv == 1 else Mm_[g][:, j, :]
                        mk_c = ("AB", h) if lv == 1 else ("Mm", g)
                        Mtcur = Mt_[g][:, j, :]
                        P.op("pe", lambda e, Mcur=Mcur, Mtcur=Mtcur, j=j: e.matmul(BK[bMT][:, j * 128:(j + 1) * 128], Mcur, Mtcur, start=True, stop=True),
                             reads=[mk_c, ("Mt", g)], writes=[bk(bMT)])
                        if not lastlv:
                            P.op("pe", lambda e, Mcur=Mcur, Mtcur=Mtcur, j=j: e.matmul(BK[bM][:, j * 128:(j + 1) * 128], Mtcur, Mcur, start=True, stop=True),
                                 reads=[mk_c, ("Mt", g)], writes=[bk(bM)])
                    P.op("act", lambda e, g=g: e.copy(Mt_[g][:].rearrange("p a t -> p (a t)"), BK[bMT][:]), reads=[bk(bMT)], writes=[("Mt", g)])
                    if not lastlv:
                        P.op("act", lambda e, g=g: e.copy(Mm_[g][:].rearrange("p a t -> p (a t)"), BK[bM][:]), reads=[bk(bM)], writes=[("Mm", g)])
                    for j in range(4):
                        h = 4 * g + j
                        P.op("pe", lambda e, j=j, h=h, g=g: e.matmul(BK[bP][:, j * 128:(j + 1) * 128], Mt_[g][:, j, :], Pv[h], start=True, stop=True),
                             reads=[("Mt", g), ("Pv", h)], writes=[bk(bP)])
                    P.op("dve", lambda e, g=g: e.tensor_tensor(Pvg[g][:].rearrange("p a t -> p (a t)"), BK[bP][:], Pvg[g][:].rearrange("p a t -> p (a t)"), ALU.add),
                         reads=[bk(bP)] + [("Pv", 4 * g + j) for j in range(4)], writes=[("Pv", 4 * g + j) for j in range(4)])
            for h in range(8):
                P.op("pe", lambda e, h=h, b=b: e.matmul(BK[7][:, hcol(h):hcol(h) + 64], AK[h][:, 0:128], Vtok[:, b, h * 64:(h + 1) * 64], start=True, stop=True),
                     reads=[("AK", h), ("Vtok", b, h // 2)], writes=[bk(7)])
            P.op("act", lambda e: e.copy(VA[:], BK[7][:]), reads=[bk(7)], writes=["VA"])
            for h in range(8):
                p, h2 = divmod(h, 2)
                ph = slice(h2 * 64, h2 * 64 + 64)
                P.op("pe", lambda e, h=h, b=b, p=p, ph=ph: e.matmul(BK[4][ph, p * 128:(p + 1) * 128], Vtok[:, b, h * 64:(h + 1) * 64], AK[h][:, 128:256], start=True, stop=True),
                     reads=[("AK", h), ("Vtok", b, p)], writes=[bk(4)])
            P.op("act", lambda e: e.copy(VK[:].rearrange("p a t -> p (a t)"), BK[4][:]), reads=[bk(4)], writes=["VK"])
            stop_at('G2')

        if prompt:
            if first:
                P.op("pool", lambda e: e.memset(S[:], 0.0), writes=["S"])
            for b in range(nb):
                rw_pre(b)
                for c2 in range(2):
                    cr = slice(c2 * 64, c2 * 64 + 64)
                    tc_ = slice(b * 128 + c2 * 64, b * 128 + c2 * 64 + 64)
                    gci = (b * 128 + c2 * 64) // 64
                    for h in range(8):
                        p, h2 = divmod(h, 2)
                        ph = slice(h2 * 64, h2 * 64 + 64)
                        P.op("pe", lambda e, h=h, p=p, ph=ph, h2=h2: e.matmul(BK[5 + h2][cr, p * 64:(p + 1) * 64], art[ph, p, 0, tc_], S[ph, p, :], start=True, stop=True),
                             reads=[("art", p), "S"], writes=[bk(5 + h2)])
                    for h in range(8):
                        p, h2 = divmod(h, 2)
                        ph = slice(h2 * 64, h2 * 64 + 64)
                        P.op("pe", lambda e, p=p, ph=ph, h2=h2: e.matmul(BK[h2][ph, p * 64:(p + 1) * 64], S[ph, p, :], art[ph, p, 1, tc_], start=True, stop=True),
                             reads=[("art", p), "S"], writes=[bk(h2)])
                    for h2 in range(2):
                        ph = slice(h2 * 64, h2 * 64 + 64)
                        P.op("act", lambda e, h2=h2, ph=ph: e.copy(y1[ph, :, :].rearrange("p a t -> p (a t)"), BK[h2][ph, 0:256]), reads=[bk(h2)], writes=[("y1", h2)])
                    for h2 in range(2):
                        P.op("dve", lambda e, h2=h2: e.tensor_tensor(XT[cr, h2 * 256:(h2 + 1) * 256], BK[5 + h2][cr, 0:256], VA[cr, h2 * 256:(h2 + 1) * 256], ALU.add),
                             reads=[bk(5 + h2), "VA"], writes=[("XT", h2)])
                    for h in range(8):
                        P.op("pe", lambda e, h=h: e.matmul(BK[7][cr, hcol(h):hcol(h) + 64], Pv[h][cr, c2 * 64:(c2 + 1) * 64], XT[cr, hcol(h):hcol(h) + 64], start=True, stop=True),
                             reads=[("Pv", h), ("XT", h % 2)], writes=[bk(7)])
                    P.op("act", lambda e: e.copy(UT[cr, :], BK[7][cr, :]), reads=[bk(7)], writes=["UT"])
                    stop_at('G3')
                    for h in range(8):
                        p, h2 = divmod(h, 2)
                        ph = slice(h2 * 64, h2 * 64 + 64)
                        P.op("pe", lambda e, h=h, p=p, ph=ph: e.matmul(BK[5][ph, 256 + p * 64:256 + (p + 1) * 64], BPtok[cr, b, h * 64:(h + 1) * 64], UT[cr, hcol(h):hcol(h) + 64], start=True, stop=False),
                             reads=[("BPtok", b, p), "UT"], writes=[bk(5)])
                        P.op("pe", lambda e, h=h, p=p, ph=ph: e.matmul(BK[5][ph, 256 + p * 64:256 + (p + 1) * 64], KPtok[cr, b, h * 64:(h + 1) * 64], Vtok[cr, b, h * 64:(h + 1) * 64], start=False, stop=True),
                             reads=[("KPtok", b, p), ("Vtok", b, p)], writes=[bk(5)])
                    for p in range(4):
                        P.op("dve", lambda e, p=p: e.scalar_tensor_tensor(S[:, p, :], S[:, p, :], cCt[:, p, gci:gci + 1], BK[5][:, 256 + p * 64:256 + (p + 1) * 64], ALU.mult, ALU.add),
                             reads=["S", ("cCt", p), bk(5)], writes=["S"])
                    for h in range(8):
                        p, h2 = divmod(h, 2)
                        ph = slice(h2 * 64, h2 * 64 + 64)
                        P.op("pe", lambda e, h=h, p=p, ph=ph: e.matmul(BK[6][ph, 256 + p * 64:256 + (p + 1) * 64], UT[cr, hcol(h):hcol(h) + 64], AB[h][cr, 128 + c2 * 64:128 + (c2 + 1) * 64], start=True, stop=True),
                             reads=[("AB", h), "UT"], writes=[bk(6)])
                    P.op("dve", lambda e: e.tensor_tensor(y1[:], BK[6][:, 256:512].rearrange("p (a t) -> p a t", t=64), y1[:], ALU.add), reads=[bk(6), ("y1", 0), ("y1", 1)], writes=[("y1", 0), ("y1", 1)])
                    P.op("pool", lambda e: e.tensor_tensor(Yt[:, :, tc_], y1[:], VK[:, :, c2 * 64:(c2 + 1) * 64], ALU.add), reads=[("y1", 0), ("y1", 1), "VK"], writes=[("Yt", b)])
                    stop_at('G4')
            if last and 'pwkv' not in SKIP:
                for pp in range(2):
                    P.op("pe", lambda e, pp=pp: e.transpose(BK[4][:, pp * 128:(pp + 1) * 128], S[:, 2 * pp:2 * pp + 2, :].rearrange("p a i -> p (a i)"), ident), reads=["S", "cst"], writes=[bk(4)])
                P.op("act", lambda e: e.copy(xn[:, 0:256], BK[4][:, 0:256]), reads=[bk(4)], writes=["xn"])
                for pp in range(2):
                    for pl in range(2):
                        pidx = 2 * pp + pl
                        P.dma("sp", pwkv[seq, 2 * pidx:2 * pidx + 2].rearrange("h i j -> i h j"), xn[pl * 64:(pl + 1) * 64, pp * 128:(pp + 1) * 128].rearrange("p (h j) -> p h j", j=64), reads=["xn"], writes=[("pwkv", pidx)])
                P.op("pe", lambda e: e.transpose(BK[4][:, 0:128], car[:, :], ident), reads=[("car", n) for n in range(14)] + ["cst", "car_init"], writes=[bk(4)])
                P.op("act", lambda e: e.copy(xn[0:14, 0:128], BK[4][0:14, 0:128]), reads=[bk(4)], writes=["xn"])
                P.dma("sp", psh[seq].rearrange("(c p) -> c p", p=128), xn[0:14, 0:128], reads=["xn"], writes=["psh"])
        else:
            rw_pre(0)
            sample_rwkv(hcol)
        YT_KEYS = [("Yt", b) for b in range(nb)]
        stop_at("G")
        debug_tap("Yt", Yt[:, :, 0:NT], [128, 4, NT], YT_KEYS)

        def post_steps(p):
            tn = ("sg", "cs", "t1", "t2", "t3", "a", "kkn", "k2")
            n1, n2 = tn[2 * p], tn[2 * p + 1]
            T1, T2 = rw[n1][:, 0:NT], rw[n2][:, 0:NT]
            k1, k2_ = "r_" + n1, "r_" + n2
            bm, bv = 4 + p, p
            return [
                lambda: P.op("pe", lambda e: e.matmul(BK[bm][:, 0:NT], bo64, Yt[:, p, 0:NT], start=True, stop=True), reads=YT_KEYS + ["cst"], writes=[bk(bm)]),
                lambda: P.op("dve", lambda e: e.tensor_tensor(T1, Yt[:, p, 0:NT], BK[bm][:, 0:NT], ALU.subtract), reads=YT_KEYS + [bk(bm)], writes=[k1]),
                lambda: P.op("pool", lambda e: e.tensor_tensor(T2, T1, T1, ALU.mult), reads=[k1], writes=[k2_]),
                lambda: P.op("pe", lambda e: e.matmul(BK[bv][:, 0:NT], bo64, T2, start=True, stop=True), reads=[k2_, "cst"], writes=[bk(bv)]),
                lambda: P.op("act", lambda e: e.activation(T2, BK[bv][:, 0:NT], AF.Sqrt, bias=64e-5), reads=[bk(bv)], writes=[k2_]),
                lambda: P.op("dve", lambda e: e.reciprocal(T2, T2), reads=[k2_], writes=[k2_]),
                lambda: P.op("dve", lambda e: e.tensor_tensor(T1, T1, T2, ALU.mult), reads=[k1, k2_], writes=[k1]),
                lambda: P.op("dve", lambda e: e.tensor_scalar(T1, T1, lg_c[:, p:p + 1], lb_c[:, p:p + 1], ALU.mult, ALU.add), reads=[k1, "lg_c", "lb_c"], writes=[k1]),
                lambda: P.op("pool", lambda e: e.tensor_tensor(T1, T1, bon[:, p, 0:NT], ALU.add), reads=[k1, ("bon", p)], writes=[k1]),
                lambda: P.op("dve", lambda e: e.tensor_tensor(rwoT[:, p, 0:NT], T1, gT[:, p, 0:NT], ALU.mult), reads=[k1, ("gT", p)], writes=[("rwoT", p)]),
            ]
        for group in zip(*[post_steps(p) for p in range(4)]):
            for step in group:
                step()
        RWO_KEYS = [("rwoT", p) for p in range(4)]
        stop_at("H")
        debug_tap("rwoT", rwoT[:, :, 0:NT], [128, 4, NT], RWO_KEYS)

        for ch in range(2):
            sa, ka_ = ring_get()
            sb_, kb_ = ring_get(hold=1)
            sa3 = sa[:, 0:2048].rearrange("p (k n) -> p k n", n=512)
            sb3 = sb_[:, 0:2048].rearrange("p (k n) -> p k n", n=512)
            for cc in range(4):
                oc = ch * 4 + cc
                ba, bb = (0, 1) if cc % 2 == 0 else (2, 3)
                for kc in range(4):
                    P.op("pe", lambda e, kc=kc, cc=cc, ba=ba, sa3=sa3: e.matmul(BK[ba][:, 0:NT], sa3[:, kc, cc * 128:(cc + 1) * 128], attT[:, kc, 0:NT], start=(kc == 0), stop=(kc == 3)),
                         reads=[ka_] + ATT_KEYS, writes=[bk(ba)])
                for kc in range(4):
                    P.op("pe", lambda e, kc=kc, cc=cc, bb=bb, sb3=sb3: e.matmul(BK[bb][:, 0:NT], sb3[:, kc, cc * 128:(cc + 1) * 128], rwoT[:, kc, 0:NT], start=(kc == 0), stop=(kc == 3)),
                         reads=[kb_] + RWO_KEYS, writes=[bk(bb)])
                tA = cc_t[cc % 2][:, 0:NT]
                tB = gl_t[cc % 2][:, 0:NT]
                P.op("dve", lambda e, oc=oc, ba=ba, tA=tA: e.tensor_tensor(tA, BK[ba][:, 0:NT], gates[:, oc, 0:NT], ALU.mult), reads=[bk(ba), ("gates", oc)], writes=[("cc_t", cc % 2)])
                P.op("dve", lambda e, oc=oc, bb=bb, tB=tB: e.tensor_tensor(tB, BK[bb][:, 0:NT], gates[:, 8 + oc, 0:NT], ALU.mult), reads=[bk(bb), ("gates", 8 + oc)], writes=[("gl_t", cc % 2)])
                P.op("pool", lambda e, oc=oc, tA=tA, tB=tB: e.tensor_tensor(hT[:, oc, 0:NT], tA, tB, ALU.add), reads=[("cc_t", cc % 2), ("gl_t", cc % 2)], writes=[("hT", oc)])
        MIX_KEYS = [("hT", oc) for oc in range(8)]
        debug_tap("mixT", hT[:, :, 0:NT], [128, 8, NT], MIX_KEYS)

        stop_at("I")
        for ch in range(2):
            so, ko = ring_get()
            so3 = so[:, 0:4096].rearrange("p (k n) -> p k n", n=512)
            for b in range(nb):
                bnk = (ch * nb + b) % 4
                for kc in range(8):
                    P.op("pe", lambda e, kc=kc, b=b, bnk=bnk, so3=so3: e.matmul(BK[bnk][:], hT[:, kc, b * 128:(b + 1) * 128], so3[:, kc, :], start=(kc == 0), stop=(kc == 7)),
                         reads=[ko] + MIX_KEYS, writes=[bk(bnk)])
                P.op("dve", lambda e, b=b, bnk=bnk, ch=ch: e.tensor_tensor(xt[b][:, ch * 512:(ch + 1) * 512], xt[b][:, ch * 512:(ch + 1) * 512], BK[bnk][:], ALU.add),
                     reads=[bk(bnk), ("xt", b)], writes=[("xt", b)])
        stop_at("J")
        debug_tap("x1", xt[0][:], [128, 1024], [("xt", 0)])

        norm_T("g2b")
        def ffn_steps(i, f, bo_):
            ug3 = v3(ugx[i][:, 0:NBs * (TB + 2)], TB + 2)
            ugk = ("ugx", i)
            c3 = v3(cc_t[i][:, 0:NT], TB)

            def carry():
                if prompt:
                    if first:
                        P.op("pool", lambda e: e.memset(ugx[i][:, 0:2], 0.0), writes=[("ugx0", i)])
                    else:
                        P.op("pool", lambda e: e.tensor_copy(ugx[i][:, 0:2], ccar[:, :, f]), reads=[("ccar", f)], writes=[("ugx0", i)])
                    P.op("pool", lambda e: e.tensor_copy(ccar[:, :, f], ugx[i][:, NT:NT + 2]), reads=[ugk], writes=[("ccar", f)])
                else:
                    P.op("pool", lambda e: e.tensor_copy(ug3[:, :, 0:2], SB["ccar_s"][:, f, :, :]), reads=[("ccar_s", f)], writes=[("ugx0", i)])
                    P.op("pool", lambda e: e.tensor_copy(SB["cout_s"][:, f, :, :], ug3[:, :, TB:TB + 2]), reads=[ugk], writes=[("cout_s", f)])
            return [
                lambda: P.op("act", lambda e: e.copy(ug3[:, :, 2:TB + 2], v3(BK[bo_ + i][:, 0:NT], TB)), reads=[bk(bo_ + i)], writes=[ugk]),
                carry,
                lambda: P.op("pool", lambda e: e.tensor_scalar(c3, ug3[:, :, 0:TB], cw_c[:, 0, f:f + 1], cb_c[:, f:f + 1], ALU.mult, ALU.add),
                             reads=[ugk, ("ugx0", i), "cb_c"] + CW_KEYS, writes=[("cc_t", i)]),
                lambda: P.op("dve", lambda e: e.scalar_tensor_tensor(c3, ug3[:, :, 1:TB + 1], cw_c[:, 1, f:f + 1], c3, ALU.mult, ALU.add),
                             reads=[ugk, ("ugx0", i), ("cc_t", i)] + CW_KEYS, writes=[("cc_t", i)]),
                lambda: P.op("dve", lambda e: e.scalar_tensor_tensor(c3, ug3[:, :, 2:TB + 2], cw_c[:, 2, f:f + 1], c3, ALU.mult, ALU.add),
                             reads=[ugk, ("cc_t", i)] + CW_KEYS, writes=[("cc_t", i)]),
                lambda: P.op("act", lambda e: e.activation(gl_t[i][:, 0:NT], cc_t[i][:, 0:NT], AF.Gelu_apprx_tanh), reads=[("cc_t", i)], writes=[("gl_t", i)]),
                lambda: P.op("dve", lambda e: e.tensor_tensor(actT[:, f, 0:NT], gl_t[i][:, 0:NT], BK[bo_ + 2 + i][:, 0:NT], ALU.mult), reads=[("gl_t", i), bk(bo_ + 2 + i)], writes=[("actT", f)]),
            ]
        for blkb in range(11):
            slot, skey = ring_get()
            sl3 = slot[:, 0:4096].rearrange("p (k n) -> p k n", n=512)
            bo_ = 4 * (blkb % 2)
            for cc in range(4):
                for kc in range(8):
                    P.op("pe", lambda e, kc=kc, cc=cc, sl3=sl3, bo_=bo_: e.matmul(BK[bo_ + cc][:, 0:NT], sl3[:, kc, cc * 128:(cc + 1) * 128], hT[:, kc, 0:NT], start=(kc == 0), stop=(kc == 7)),
                         reads=[skey] + HT_KEYS, writes=[bk(bo_ + cc)])
            for sa, sb in zip(ffn_steps(0, blkb * 2, bo_), ffn_steps(1, blkb * 2 + 1, bo_)):
                sa()
                sb()
        ACT_KEYS = [("actT", f) for f in range(22)]
        if prompt and last and 'pconv' not in SKIP:
            for j in range(2):
                P.op("pe", lambda e, j=j: e.transpose(BK[4][:, j * 128:(j + 1) * 128], ccar[:, j, :], ident), reads=[("ccar", f) for f in range(22)] + ["cst", "ccar_init"], writes=[bk(4)])
            P.op("act", lambda e: e.copy(xn[0:22, 0:256], BK[4][0:22, 0:256]), reads=[bk(4)], writes=["xn"])
            for j in range(2):
                P.dma("sp", pconv[seq, j].rearrange("(c p) -> c p", p=128), xn[0:22, j * 128:(j + 1) * 128], reads=["xn"], writes=[("pconv", j)])

        stop_at("K")
        for ch in range(2):
            for rg in range(3):
                sd, kd = ring_get()
                nk = DN_NK[rg]
                sd3 = sd[:, 0:nk * 512].rearrange("p (k n) -> p k n", n=512)
                for b in range(nb):
                    bnk = b % 4
                    for kc in range(nk):
                        f = rg * 8 + kc
                        P.op("pe", lambda e, kc=kc, f=f, b=b, bnk=bnk, sd3=sd3: e.matmul(BK[bnk][:], actT[:, f, b * 128:(b + 1) * 128], sd3[:, kc, :], start=(f == 0), stop=(f == 21)),
                             reads=[kd] + ACT_KEYS, writes=[bk(bnk)])
            for b in range(nb):
                bnk = b % 4
                P.op("dve", lambda e, b=b, bnk=bnk, ch=ch: e.tensor_tensor(xt[b][:, ch * 512:(ch + 1) * 512], xt[b][:, ch * 512:(ch + 1) * 512], BK[bnk][:], ALU.add),
                     reads=[bk(bnk), ("xt", b)], writes=[("xt", b)])

        for b in range(nb):
            P.op("dve", lambda e, b=b: e.bn_stats(stat[:, 0:6], xt[b][:, 0:512]), reads=[("xt", b)], writes=["stat0_0"])
            P.op("dve", lambda e, b=b: e.bn_stats(stat[:, 6:12], xt[b][:, 512:1024]), reads=[("xt", b)], writes=["stat0_1"])
            P.op("dve", lambda e: e.bn_aggr(stat[:, 12:14], stat[:, 0:12]), reads=["stat0_0", "stat0_1"], writes=["stat0_2"])
            P.op("dve", lambda e: e.scalar_tensor_tensor(stat[:, 14:15], stat[:, 12:13], stat[:, 12:13], stat[:, 13:14], ALU.mult, ALU.add), reads=["stat0_2"], writes=["stat0_3"])
            P.op("act", lambda e: e.activation(stat[:, 15:16], stat[:, 14:15], AF.Sqrt, bias=1e-6), reads=["stat0_3"], writes=["stat0_4"])
            P.op("dve", lambda e: e.reciprocal(stat[:, 15:16], stat[:, 15:16]), reads=["stat0_4"], writes=["stat0_4"])
            P.op("dve", lambda e, b=b: e.scalar_tensor_tensor(xn[:], xt[b][:], stat[:, 15:16], gb["gfb"][:], ALU.mult, ALU.mult), reads=[("xt", b), "stat0_4", "gfb"], writes=["xn"])
            P.dma("sp", yrows[b * 128:(b + 1) * 128, :], xn[:], reads=["xn"], writes=[("y", kind, seq, ti, b)])

    SB = {}

    def sample_alloc():
        SB["shcar"] = wsb("sx_shcar", [128, 14, 16], F32)
        SB["shout"] = wsb("sx_shout", [128, 16, 16], F32)
        SB["ccar_s"] = wsb("sx_ccar_s", [128, 22, 16, 2], F32)
        SB["cout_s"] = wsb("sx_cout_s", [128, 24, 16, 2], F32)
        SB["S_s"] = wsb("sx_S_s", [128, 16, 4, 64], F32)
        SB["kcT"] = wsb("sx_kcT", [128, 16, 128], BF16)
        SB["vc"] = wsb("sx_vc", [128, 16, 128], BF16)
        SB["biasC"] = [wsb("sx_biasC%d" % kv, [128, 16, 4, 8], F32) for kv in range(2)]
        SB["biasN"] = [wsb("sx_biasN%d" % kv, [128, 16, 4, 8], F32) for kv in range(2)]
        SB["Apad"] = wsb("sx_Apad", [128, 16, 128], F32)
        SB["BPpad"] = [wsb("sx_BPpad", [128, 512], F32)] * 2
        SB["KPpad"] = [wsb("sx_KPpad", [128, 512], F32)] * 2
        SB["xn"] = xn

    def sample_const_setup():
        P.op("pool", lambda e: e.memset(SB["Apad"][:].rearrange("p b c -> p (b c)"), 0.0), writes=["Apad"])
        P.op("pool", lambda e: e.memset(SB["shout"][:].rearrange("p a b -> p (a b)"), 0.0), writes=["shout_init"])
        P.op("pool", lambda e: e.memset(SB["cout_s"][:].rearrange("p a b c -> p (a b c)"), 0.0), writes=["cout_init"])
        for kv in range(2):
            P.op("dve", lambda e, kv=kv: e.tensor_copy(SB["biasC"][kv][:], biasT[1][kv][:, :, 0:8].unsqueeze(1).broadcast_to([128, 16, 4, 8])),
                 reads=BIAS_KEYS[(1, kv)], writes=[("biasC", kv)])
            P.op("pool", lambda e, kv=kv: e.memset(SB["biasN"][kv][:].rearrange("p a b c -> p (a b c)"), NEGB), writes=[("biasN", kv)])
            for bb in range(16):
                P.dma("sp", SB["biasN"][kv][bb * 8:(bb + 1) * 8, bb, :, :], biasT[0][kv][0:8, :, 0:8],
                      reads=BIAS_KEYS[(0, kv)] + [("biasN", kv)], writes=[("biasNd", kv, bb)])
    BIASN_KEYS = {kv: [("biasN", kv)] + [("biasNd", kv, bb) for bb in range(16)] for kv in range(2)}

    def sample_setup(st):
        b0 = st * 16
        stgA = SB["xn"]
        for n in range(14):
            bnk = 4 + n % 4
            if n % 7 == 0:
                P.dma("sp", stgA[0:16, 0:896], sshift[b0:b0 + 16, n * 128:n * 128 + 896], writes=["xn"])
            P.op("pe", lambda e, n=n, bnk=bnk: e.transpose(BK[bnk][:, 0:16], stgA[0:16, (n % 7) * 128:(n % 7 + 1) * 128], cst[0:16, C_ID:C_ID + 16]), reads=["xn", "cst"], writes=[bk(bnk)])
            P.op("act", lambda e, n=n, bnk=bnk: e.copy(SB["shcar"][:, n, :], BK[bnk][:, 0:16]), reads=[bk(bnk)], writes=[("shcar", n)])
        for piece, npc in enumerate((8, 8, 6)):
            P.dma("sp", stgA[0:32, 0:npc * 128], sconv[st * 32:(st + 1) * 32, piece * 1024:piece * 1024 + npc * 128], writes=["xn"])
            for c in range(npc):
                f = piece * 8 + c
                bnk = 4 + c % 4
                P.op("pe", lambda e, c=c, bnk=bnk: e.transpose(BK[bnk][:, 0:32], stgA[0:32, c * 128:(c + 1) * 128], cst[0:32, C_ID:C_ID + 32]), reads=["xn", "cst"], writes=[bk(bnk)])
                P.op("act", lambda e, f=f, bnk=bnk: e.copy(SB["ccar_s"][:, f, :, :].rearrange("p b j -> p (b j)"), BK[bnk][:, 0:32]), reads=[bk(bnk)], writes=[("ccar_s", f)])
        for bb in range(16):
            hs = bb % 2
            P.dma("sp", stgA[0:64, hs * 512:(hs + 1) * 512].rearrange("p (h j) -> p h j", j=64), swkv[b0 + bb].rearrange("h i j -> i h j"), writes=[("stgAh", hs), "xn"] if bb < 2 else [("stgAh", hs)], reads=["xn"])
            bnk = 4 + bb % 4
            for p in range(4):
                P.op("pe", lambda e, p=p, bnk=bnk, hs=hs: e.transpose(BK[bnk][:, p * 64:(p + 1) * 64], stgA[0:64, hs * 512 + p * 128:hs * 512 + (p + 1) * 128], cst[0:64, C_ID:C_ID + 64]),
                     reads=[("stgAh", hs), "cst"], writes=[bk(bnk)])
            P.op("dve", lambda e, bb=bb, bnk=bnk: e.tensor_copy(SB["S_s"][:, bb, :, :].rearrange("p a i -> p (a i)"), BK[bnk][:, 0:256]), reads=[bk(bnk)], writes=[("S_s", bb)])
        for bb in range(16):
            bnk = 4 + bb % 4
            if bb % 8 == 0:
                P.dma("sp", stgA[:, 0:1024].rearrange("p (b c) -> p b c", c=128), ck[b0 + bb:b0 + bb + 8].rearrange("b k c -> k b c"), reads=[("stgAh", 0), ("stgAh", 1)], writes=["xn", ("stgAh", 0), ("stgAh", 1)])
            P.op("pe", lambda e, bb=bb, bnk=bnk: e.transpose(BK[bnk][:, 0:128], stgA[:, (bb % 8) * 128:(bb % 8 + 1) * 128], ident), reads=["xn", "cst"], writes=[bk(bnk)])
            P.op("act", lambda e, bb=bb, bnk=bnk: e.copy(SB["kcT"][:, bb, :], BK[bnk][:, 0:128]), reads=[bk(bnk)], writes=[("kcT", bb)])
        P.dma("pool", SB["vc"][:], cv[b0:b0 + 16].rearrange("b k c -> k b c"), writes=["vc"])
        P.dma("sp", sk[b0:b0 + 16, 0:120, :], ck[b0:b0 + 16, 8:128, :], writes=[("sk_old", st)])
        P.dma("sp", sv[b0:b0 + 16, 0:120, :], cv[b0:b0 + 16, 8:128, :], writes=[("sv_old", st)])

    def sample_attention():
        NT = 128
        kcT, vc = SB["kcT"], SB["vc"]
        P.op("pe", lambda e: e.transpose(BK[4][:, 0:128], kTf[:, 0:128], ident), reads=["kTf", "cst"], writes=[bk(4)])
        P.op("act", lambda e: e.copy(kvrow[:, 0, :], BK[4][:, 0:128]), reads=[bk(4)], writes=[("kvrow", 0)])
        for kv in range(2):
            ph = slice(kv * 64, kv * 64 + 64)
            for bb in range(16):
                P.op("pe", lambda e, bb=bb: e.matmul(BK[4][:, bb * 32:(bb + 1) * 32], kcT[ph, bb, :], qT[ph, :, bb * 8:(bb + 1) * 8], start=True, stop=True),
                     reads=[("kcT", bb)] + [("qT", c) for c in range(4)], writes=[bk(4)])
            for bb in range(16):
                P.op("pe", lambda e, bb=bb: e.matmul(BK[5][:, bb * 32:(bb + 1) * 32], kT[ph, 128:256], qT[ph, :, bb * 8:(bb + 1) * 8], start=True, stop=True),
                     reads=["kT"] + [("qT", c) for c in range(4)], writes=[bk(5)])
            P.op("dve", lambda e, kv=kv: e.scalar_tensor_tensor(BK[4][:], BK[4][:], 0.125, SB["biasC"][kv][:].rearrange("p a b c -> p (a b c)"), ALU.mult, ALU.add),
                 reads=[bk(4), ("biasC", kv)], writes=[bk(4)])
            P.op("dve", lambda e, kv=kv: e.scalar_tensor_tensor(BK[5][:], BK[5][:], 0.125, SB["biasN"][kv][:].rearrange("p a b c -> p (a b c)"), ALU.mult, ALU.add),
                 reads=[bk(5)] + BIASN_KEYS[kv], writes=[bk(5)])
            P.op("act", lambda e: e.activation(PT[0][:], BK[4][:], AF.Exp), reads=[bk(4)], writes=[("PT", 0)])
            P.op("act", lambda e: e.activation(PT[1][:], BK[5][:], AF.Exp), reads=[bk(5)], writes=[("PT", 1)])
            for bb in range(16):
                cs = slice(bb * 32, (bb + 1) * 32)
                P.op("pe", lambda e, bb=bb, cs=cs: e.matmul(BK[6][:, cs], vc[:, bb, :], PT[0][:, cs], start=True, stop=False), reads=["vc", ("PT", 0)], writes=[bk(6)])
                P.op("pe", lambda e, cs=cs: e.matmul(BK[6][:, cs], vtok[:, 1, :], PT[1][:, cs], start=False, stop=True), reads=[("vtok", 1), ("PT", 1)], writes=[bk(6)])
            for bb in range(16):
                cs = slice(bb * 32, (bb + 1) * 32)
                P.op("pe", lambda e, cs=cs: e.matmul(BK[7][:, cs], ones_bf[:], PT[0][:, cs], start=True, stop=False), reads=["ones_bf", ("PT", 0)], writes=[bk(7)])
                P.op("pe", lambda e, cs=cs: e.matmul(BK[7][:, cs], ones_bf[:], PT[1][:, cs], start=False, stop=True), reads=["ones_bf", ("PT", 1)], writes=[bk(7)])
            P.op("dve", lambda e: e.tensor_tensor(rd_t[ph, :].rearrange("p (b g t) -> p b g t", g=4, t=8), BK[7][ph, :].rearrange("p (b g t) -> p b g t", g=4, t=8), esink[ph, :, 0:8].unsqueeze(1).broadcast_to([64, 16, 4, 8]), ALU.add), reads=[bk(7)] + ESINK_KEYS, writes=["rd_t"])
            P.op("dve", lambda e: e.reciprocal(rd_t[ph, :], rd_t[ph, :]), reads=["rd_t"], writes=["rd_t"])
            P.op("dve", lambda e, kv=kv: e.tensor_tensor(attT[ph, :, 0:128].rearrange("p g (b t) -> p b g t", t=8), BK[6][ph, :].rearrange("p (b g t) -> p b g t", g=4, t=8),
                                                       rd_t[ph, :].rearrange("p (b g t) -> p b g t", g=4, t=8), ALU.mult),
                 reads=[bk(6), "rd_t"], writes=[("attT", kv, 0)])

    def sample_rwkv(hcol):
        S_s, Apad = SB["S_s"], SB["Apad"]
        Rpad = Apad
        y1s = VA[:].rearrange("p (a t) -> p a t", t=128)
        SKEYS = [("S_s", bb) for bb in range(16)]

        def diag(t):
            base = t[:, 0, 0:8]
            return bass.AP(base.tensor, base.offset, [list(base.ap[0]), [136, 16], [1, 8]])
        for p in range(4):
            P.op("pool", lambda e, p=p: e.tensor_copy(diag(Apad), art[:, p, 0, 0:128].rearrange("p (b t) -> p b t", t=8)), reads=[("art", p), "Apad"], writes=["Apad"])
            for h2 in range(2):
                ph = slice(h2 * 64, h2 * 64 + 64)
                for bb in range(16):
                    P.op("pe", lambda e, bb=bb, p=p, ph=ph, h2=h2: e.matmul(BK[5 + h2][:, p * 64:(p + 1) * 64], Apad[ph, bb, :], S_s[ph, bb, p, :], start=(bb == 0), stop=(bb == 15)),
                         reads=["Apad", ("S_s", bb)], writes=[bk(5 + h2)])
            P.op("pool", lambda e, p=p: e.tensor_copy(diag(Apad), art[:, p, 1, 0:128].rearrange("p (b t) -> p b t", t=8)), reads=[("art", p), "Apad"], writes=["Apad"])
            for h2 in range(2):
                ph = slice(h2 * 64, h2 * 64 + 64)
                yb = 4 if h2 == 0 else 7
                for bb in range(16):
                    P.op("pe", lambda e, bb=bb, p=p, ph=ph, yb=yb: e.matmul(BK[yb][ph, p * 128:(p + 1) * 128], S_s[ph, bb, p, :], Apad[ph, bb, :], start=(bb == 0), stop=(bb == 15)),
                         reads=["Apad", ("S_s", bb)], writes=[bk(yb)])
        for h2 in range(2):
            P.op("dve", lambda e, h2=h2: e.tensor_tensor(XT[:, h2 * 256:(h2 + 1) * 256], BK[5 + h2][:, 0:256], VA[:, h2 * 256:(h2 + 1) * 256], ALU.add),
                 reads=[bk(5 + h2), "VA"], writes=[("XT", h2)])
        for h in range(8):
            P.op("pe", lambda e, h=h: e.matmul(BK[5][:, hcol(h):hcol(h) + 64], Pv[h], XT[:, hcol(h):hcol(h) + 64], start=True, stop=True),
                 reads=[("Pv", h), ("XT", h % 2)], writes=[bk(5)])
        P.op("act", lambda e: e.copy(UT[:], BK[5][:]), reads=[bk(5)], writes=["UT"])
        for h in range(8):
            p, h2 = divmod(h, 2)
            ph = slice(h2 * 64, h2 * 64 + 64)
            P.op("pe", lambda e, h=h, p=p, ph=ph: e.matmul(BK[6][ph, p * 128:(p + 1) * 128], UT[:, hcol(h):hcol(h) + 64], AB[h][:, 128:256], start=True, stop=True),
                 reads=[("AB", h), "UT"], writes=[bk(6)])
        P.op("act", lambda e: e.copy(VA[0:64, :], BK[4][0:64, :]), reads=[bk(4)], writes=["VA"])
        P.op("act", lambda e: e.copy(VA[64:128, :], BK[7][64:128, :]), reads=[bk(7)], writes=["VA"])
        P.op("dve", lambda e: e.tensor_tensor(VA[:], BK[6][:], VA[:], ALU.add), reads=[bk(6), "VA", "VA"], writes=["VA", "VA"])
        P.op("pool", lambda e: e.tensor_tensor(Yt[:, :, 0:128], y1s, VK[:], ALU.add), reads=["VA", "VA", "VK"], writes=[("Yt", 0)])
        for bb in range(16):
            i2 = bb % 2
            bnk = 4 + bb % 4
            BPp, KPp = SB["BPpad"][i2], SB["KPpad"][i2]
            P.op("pool", lambda e, bb=bb, BPp=BPp: e.tensor_scalar(BPp[:], BPtok[:, 0, :], cst[:, C_BSEL + bb:C_BSEL + bb + 1], None, ALU.mult),
                 reads=[("BPtok", 0, p) for p in range(4)] + ["cst"], writes=["BPpad"])
            P.op("dve", lambda e, bb=bb, KPp=KPp: e.tensor_scalar(KPp[:], KPtok[:, 0, :], cst[:, C_BSEL + bb:C_BSEL + bb + 1], None, ALU.mult),
                 reads=[("KPtok", 0, p) for p in range(4)] + ["cst"], writes=["KPpad"])
            for h in range(8):
                p, h2 = divmod(h, 2)
                ph = slice(h2 * 64, h2 * 64 + 64)
                P.op("pe", lambda e, h=h, p=p, ph=ph, bnk=bnk, BPp=BPp: e.matmul(BK[bnk][ph, p * 64:(p + 1) * 64], BPp[:, h * 64:(h + 1) * 64], UT[:, hcol(h):hcol(h) + 64], start=True, stop=False),
                     reads=["BPpad", "UT"], writes=[bk(bnk)])
                P.op("pe", lambda e, h=h, p=p, ph=ph, bnk=bnk, KPp=KPp: e.matmul(BK[bnk][ph, p * 64:(p + 1) * 64], KPp[:, h * 64:(h + 1) * 64], Vtok[:, 0, h * 64:(h + 1) * 64], start=False, stop=True),
                     reads=["KPpad", ("Vtok", 0, p)], writes=[bk(bnk)])
            for p in range(4):
                P.op("dve", lambda e, p=p, bb=bb, bnk=bnk: e.scalar_tensor_tensor(S_s[:, bb, p, :], S_s[:, bb, p, :], cCt[:, p, bb:bb + 1], BK[bnk][:, p * 64:(p + 1) * 64], ALU.mult, ALU.add),
                     reads=[("S_s", bb), ("cCt", p), bk(bnk)], writes=[("S_s", bb)])
            for pp in range(2):
                P.op("pe", lambda e, pp=pp, bb=bb, bnk=bnk: e.transpose(BK[bnk][:, 256 + pp * 128:256 + (pp + 1) * 128], S_s[:, bb, 2 * pp:2 * pp + 2, :].rearrange("p a i -> p (a i)"), ident),
                     reads=[("S_s", bb), "cst"], writes=[bk(bnk)])
            stgO = XT[:, i2 * 256:(i2 + 1) * 256]
            P.op("act", lambda e, bnk=bnk, stgO=stgO: e.copy(stgO, BK[bnk][:, 256:512]), reads=[bk(bnk)], writes=[("XT", i2)])
            for pp in range(2):
                for pl in range(2):
                    pidx = 2 * pp + pl
                    P.dma("sp", swkvo[SB["b0"] + bb, 2 * pidx:2 * pidx + 2].rearrange("h i j -> i h j"), stgO[pl * 64:(pl + 1) * 64, pp * 128:(pp + 1) * 128].rearrange("p (h j) -> p h j", j=64),
                          reads=[("XT", i2)], writes=[("swkvo", bb, pidx)])

    def sample_outputs(st):
        b0 = st * 16
        stgA = SB["xn"]
        for bb in range(16):
            P.dma("sp", sk[b0 + bb, 120:128, :], kvrow[bb * 8:(bb + 1) * 8, 0, :], reads=[("kvrow", 0)], writes=[("sk_new", st, bb)])
            P.dma("sp", sv[b0 + bb, 120:128, :], kvrow[bb * 8:(bb + 1) * 8, 1, :], reads=[("kvrow", 1)], writes=[("sv_new", st, bb)])
        for g in range(2):
            P.op("pe", lambda e, g=g: e.transpose(BK[4 + g][:, 0:128], SB["shout"][:, g * 8:(g + 1) * 8, :].rearrange("p a b -> p (a b)"), ident),
                 reads=[("shout", n) for n in range(14)] + ["shout_init", "cst"], writes=[bk(4 + g)])
            P.op("act", lambda e, g=g: e.copy(stgA[:, g * 128:(g + 1) * 128], BK[4 + g][:, 0:128]), reads=[bk(4 + g)], writes=["xn"])
        for n in range(14):
            g, nl = divmod(n, 8)
            P.dma("sp", ssh[b0:b0 + 16, n * 128:(n + 1) * 128], stgA[nl * 16:(nl + 1) * 16, g * 128:(g + 1) * 128], reads=["xn"], writes=[("ssh", st, n)])
        for g in range(6):
            P.op("pe", lambda e, g=g: e.transpose(BK[4 + g % 4][:, 0:128], SB["cout_s"][:, g * 4:(g + 1) * 4, :, :].rearrange("p a b j -> p (a b j)"), ident),
                 reads=[("cout_s", f) for f in range(22)] + ["cout_init", "cst"], writes=[bk(4 + g % 4)])
            P.op("act", lambda e, g=g: e.copy(stgA[:, 256 + g * 128:256 + (g + 1) * 128], BK[4 + g % 4][:, 0:128]), reads=[bk(4 + g % 4)], writes=["xn"])
        for f in range(22):
            g, fl = divmod(f, 4)
            P.dma("sp", sconvo[st * 32:(st + 1) * 32, f * 128:(f + 1) * 128], stgA[fl * 32:(fl + 1) * 32, 256 + g * 128:256 + (g + 1) * 128], reads=["xn"], writes=[("sconvo", st, f)])

    tiles = [("p", t // (2048 // NTP), t % (2048 // NTP)) for t in range(n_ptiles)]
    if os.environ.get('KREP'):
        tiles = tiles * int(os.environ['KREP'])
    try:
        for (kind, seq, ti) in tiles:
            if os.environ.get('KBAR'):
                P.barrier()
            do_tile(kind, seq, ti)
        if do_sample:
            P.barrier()
            wstack[0].close()
            wstack[0] = ExitStack()
            alloc_work(128, "_s")
            sample_alloc()
            P.barrier()
            sample_const_setup()
            stop_at("SA")
            for st in range(n_stiles):
                SB["b0"] = st * 16
                sample_setup(st)
                stop_at("SB")
                do_tile("s", st, 0)
                sample_outputs(st)
    except _Stop:
        pass
    _pe = os.environ.get('PADE'); _pn = int(os.environ.get('PAD', '0'))
    dmy = P.sbuf("dmy", [128, 8], F32)
    for _i in range(_pn):
        if _pe == 'pe':
            P.op("pe", lambda e: e.matmul(BK[0][0:8, 0:8], cst[0:8, 0:8], cst[0:8, 0:8], start=True, stop=True), reads=["cst"], writes=[("ps", 0)])
        else:
            P.op(_pe, lambda e: e.memset(dmy[:], 0.0), writes=["dmy"])
    _pa = int(os.environ.get('KPADALL', '0'))
    for _i in range(_pa):
        P.op("pe", lambda e: e.matmul(BK[0][0:8, 0:8], cst[0:8, 0:8], cst[0:8, 0:8], start=True, stop=True), reads=["cst"], writes=[("ps", 0)])
        P.op("pe", lambda e: e.matmul(BK[1][0:8, 0:8], cst[0:8, 0:8], cst[0:8, 0:8], start=True, stop=True), reads=["cst"], writes=[("ps", 1)])
        P.op("dve", lambda e: e.memset(dmy[:], 0.0), writes=["dmy"])
        if _i % 2 == 0:
            P.op("pool", lambda e: e.memset(dmy[:, 0:4], 0.0), writes=["dmy2"])
    P.finish()
    return D


_STAGE = ""


NCORES = int(os.environ.get("KNCORES", "8"))


def kernel(**inputs):
    global STOP, NSEQ
    inp = {k: np.asarray(v) for k, v in inputs.items()}
    cst, oh = host_consts()
    STOP = ""
    NSEQ = 16 // NCORES
    NSTL = 8 // NCORES
    NB = NSTL * 16
    nc = bass.Bass("TRN2", target_bir_lowering=False)
    D = build(nc, n_ptiles=2048 // NTP * NSEQ, n_stiles=NSTL)
    in_maps = []
    for c in range(NCORES):
        bs = slice(NB * c, NB * (c + 1))
        m = {
            "xp": inp["x_prompt"][NSEQ * c:NSEQ * (c + 1)], "xsm": inp["x_sample"][bs].reshape(NB * 8, 1024),
            "ck": inp["cache_win_k"][0, bs].reshape(NB, 128, 128), "cv": inp["cache_win_v"][0, bs].reshape(NB, 128, 128),
            "sshift": inp["state_shift"][0, bs], "swkv": inp["state_wkv"][0, bs], "sconv": inp["state_conv"][0, bs].reshape(NB * 2, 2816),
            "rel_bias": inp["rel_bias"], "norm1_g": inp["norm1_g"], "w_in": inp["w_in"][0], "sinks": inp["sinks"][0],
            "mu_shift": inp["mu_shift"][0], "w0": inp["w0"][0], "w2": inp["w2"][0], "a0": inp["a0"][0], "a2": inp["a2"][0],
            "g2": inp["g2"][0], "k_k": inp["k_k"][0], "k_a": inp["k_a"][0], "r_k": inp["r_k"][0].reshape(512),
            "lnx_g": inp["lnx_g"][0], "lnx_b": inp["lnx_b"][0], "w_pa": inp["w_pa"][0], "w_pb": inp["w_pb"][0],
            "w_o": inp["w_o"][0], "norm2_g": inp["norm2_g"], "w_up": inp["w_up"][0], "conv_w": inp["conv_w"][0],
            "conv_b": inp["conv_b"][0], "w_down": inp["w_down"][0], "final_g": inp["final_g"].reshape(1, 1024),
            "consts": cst, "oh": oh,
        }
        in_maps.append({k: np.ascontiguousarray(v, dtype=np.float32) for k, v in m.items() if k in D})
    res = run_bass_kernel_spmd(nc, in_maps, core_ids=list(range(NCORES)))
    R = res.results

    def cat(name, shape):
        return np.concatenate([np.asarray(r[name], dtype=np.float32) for r in R], axis=0).reshape(shape)

    return (cat("yp", (16, 2048, 1024)), cat("ys", (128, 8, 1024)),
            cat("pk", (1, 16, 128, 2, 64)), cat("pv", (1, 16, 128, 2, 64)), cat("psh", (1, 16, 1792)),
            cat("pwkv", (1, 16, 8, 64, 64)), cat("pconv", (1, 16, 2, 2816)),
            cat("sk", (1, 128, 128, 2, 64)), cat("sv", (1, 128, 128, 2, 64)), cat("ssh", (1, 128, 1792)),
            cat("swkvo", (1, 128, 8, 64, 64)), cat("sconvo", (1, 128, 2, 2816)))
```

```python
import numpy as np
from contextlib import ExitStack
import concourse.bass as bass
import concourse.mybir as mybir
from concourse.bass_utils import run_bass_kernel_spmd

F32 = mybir.dt.float32
BF16 = mybir.dt.bfloat16
AF = mybir.ActivationFunctionType
ALU = mybir.AluOpType
AX = mybir.AxisListType

import os as _os0
SAME_ENGINE_SYNC = _os0.environ.get("SES", "1") == "1"
EPOCH = 30000
RELAX_FSZ = int(_os0.environ.get('KRELAX', '0'))
N_DMA_SEMS = {"sp": 8, "pool": 4}


class _Op:
    __slots__ = ("eng", "fn", "deps", "isdma", "ms", "dsem", "dval", "needed", "desc", "fsz")


class _Rec:
    def __init__(self):
        self.call = None

    def __getattr__(self, name):
        def f(*a, **k):
            assert self.call is None
            self.call = (name, a, k)
            return self
        return f


class Prog:
    def __init__(self, nc):
        self.nc = nc
        self.ops = []
        self.lastw = {}
        self.rd_c = {}
        self.rd_d = {}
        self.stack = ExitStack()
        self.last_op = {}
        self.pending = {}
        self.bar_from = 0

    def sbuf(self, name, shape, dtype):
        return self.stack.enter_context(self.nc.sbuf_tensor(name, list(shape), dtype))

    def psum(self, name, shape, dtype):
        return self.stack.enter_context(self.nc.psum_tensor(name, list(shape), dtype))

    def op(self, eng, fn, reads=(), writes=(), isdma=False):
        idx = len(self.ops)
        deps = set()
        for r in reads:
            w = self.lastw.get(r)
            if w is not None:
                deps.add(w)
        for r in writes:
            w = self.lastw.get(r)
            if w is not None:
                deps.add(w)
            for i in self.rd_c.get(r, {}).values():
                deps.add(i)
            for i in self.rd_d.get(r, ()):
                deps.add(i)
        for r in writes:
            self.lastw[r] = idx
            self.rd_c[r] = {}
            self.rd_d[r] = []
        ws = set(writes)
        for r in reads:
            if r in ws:
                continue
            if isdma:
                self.rd_d.setdefault(r, []).append(idx)
            else:
                self.rd_c.setdefault(r, {})[eng] = idx
        if eng in self.pending:
            deps.update(self.pending.pop(eng))
        rec = _Rec()
        fn(rec)
        name_, a_, k_ = rec.call
        o = _Op()
        o.desc = name_ + " w=" + str(list(writes))[:60]
        o.fsz = 0
        try:
            out_ap = k_.get("out", a_[0] if a_ else None)
            shp = tuple(out_ap.shape)
            n = 1
            for d_ in shp[1:]:
                n *= int(d_)
            o.fsz = n
        except Exception:
            o.fsz = 0
        o.eng, o.fn, o.deps, o.isdma = eng, (lambda e: getattr(e, name_)(*a_, **k_)), deps, isdma
        o.ms = None
        o.dsem = None
        o.dval = None
        o.needed = False
        self.ops.append(o)
        self.last_op[eng] = idx
        return idx

    def barrier(self):
        prev = set(self.last_op.values())
        prev.update(i for i in range(self.bar_from, len(self.ops)) if self.ops[i].isdma)
        self.bar_from = len(self.ops)
        for eng in ("pe", "act", "dve", "pool", "sp"):
            self.pending.setdefault(eng, set()).update(prev)

    def dma(self, q, out, in_, reads=(), writes=(), **kw):
        return self.op(q, lambda e: e.dma_start(out=out, in_=in_, **kw), reads, writes, isdma=True)

    def finish(self):
        nc = self.nc
        ops = self.ops
        last_dma = [i for i, o in enumerate(ops) if o.isdma]
        for i, o in enumerate(ops):
            best = {}
            nd = set()
            for d in o.deps:
                od = ops[d]
                if od.isdma:
                    nd.add(d)
                else:
                    if od.eng == o.eng and not o.isdma:
                        if od.eng == "pe" or not SAME_ENGINE_SYNC:
                            continue
                        if RELAX_FSZ and od.fsz >= RELAX_FSZ and od.eng in ("dve", "act"):
                            continue
                    if best.get(od.eng, -1) < d:
                        best[od.eng] = d
            nd.update(best.values())
            o.deps = nd
            for d in nd:
                ops[d].needed = True
        for i in last_dma:
            ops[i].needed = True
        tail_ops = []
        for en in ("pe", "act", "dve", "pool"):
            idxs = [i for i, o in enumerate(ops) if o.eng == en and not o.isdma]
            if idxs:
                ops[idxs[-1]].needed = True
                tail_ops.append(idxs[-1])
        cnt = {e: 0 for e in ("pe", "act", "dve", "pool", "sp")}
        dcount = {}
        nd_used = {"sp": 0, "pool": 0}
        for o in ops:
            if o.isdma:
                k = nd_used[o.eng] % N_DMA_SEMS[o.eng]
                nd_used[o.eng] += 1
                key = (o.eng, k)
                dcount[key] = dcount.get(key, 0) + 1
                o.dsem = key
                o.dval = 16 * dcount[key]
            elif o.needed:
                o.ms = cnt[o.eng]
                cnt[o.eng] += 1
        sems = {}
        for e in cnt:
            for ep in range(cnt[e] // EPOCH + 1):
                sems[(e, ep)] = self.stack.enter_context(nc.semaphore("m_%s_%d" % (e, ep)))
        dsems = {}
        for key in dcount:
            dsems[key] = self.stack.enter_context(nc.semaphore("d_%s_%d" % key))
        final_dma = {}
        for i in last_dma:
            final_dma[ops[i].dsem] = max(final_dma.get(ops[i].dsem, 0), ops[i].dval)
        by_eng = {e: [] for e in cnt}
        for o in ops:
            by_eng[o.eng].append(o)
        import os as _os
        dump = _os.environ.get("DUMP")

        def emit(ename, e):
            known = {}
            for o in by_eng[ename]:
                if o.isdma and o.dval > 16:
                    k = ("d",) + o.dsem
                    if known.get(k, 0) < o.dval - 16:
                        e.wait_ge(dsems[o.dsem], o.dval - 16)
                        known[k] = o.dval - 16
                for d in sorted(o.deps):
                    od = ops[d]
                    if od.isdma:
                        k = ("d",) + od.dsem
                        if known.get(k, 0) < od.dval:
                            e.wait_ge(dsems[od.dsem], od.dval)
                            known[k] = od.dval
                            if dump: print("   ", ename, "WAITD", od.dsem, od.dval)
                    else:
                        k = ("m", od.eng)
                        if known.get(k, -1) < od.ms:
                            ep = od.ms // EPOCH
                            e.wait_ge(sems[(od.eng, ep)], od.ms % EPOCH + 1)
                            known[k] = od.ms
                            if dump: print("   ", ename, "WAITM", od.eng, od.ms + 1)
                ins = o.fn(e)
                if dump: print(ename, "OP", o.desc, "ms", o.ms)
                if o.isdma:
                    ins.then_inc(dsems[o.dsem], 16)
                elif o.ms is not None:
                    ins.then_inc(sems[(o.eng, o.ms // EPOCH)], 1)
            if ename == "sp":
                for key, v in final_dma.items():
                    e.wait_ge(dsems[key], v)
                for i in tail_ops:
                    od = ops[i]
                    e.wait_ge(sems[(od.eng, od.ms // EPOCH)], od.ms % EPOCH + 1)

        with nc.Block() as block:
            @block.tensor
            def _(e):
                emit("pe", e)

            @block.scalar
            def _(e):
                emit("act", e)

            @block.vector
            def _(e):
                emit("dve", e)

            @block.gpsimd
            def _(e):
                emit("pool", e)

            @block.sync
            def _(e):
                emit("sp", e)
        self.stack.close()

import os
STOP = os.environ.get('KSTOP', '')
SKIP = os.environ.get('KSKIP', '').split(',')
NTP = 256
NSEQ = int(os.environ.get('KNSEQ', '2'))
KAPPA = 0.6065306597126334
NEGB = -30000.0
RW_DT = F32
C_ID, C_BO, C_BO64, C_ONE, C_M64, C_L64, C_M8, C_L8, C_BSEL, C_END = 0, 128, 256, 384, 512, 768, 896, 1152, 1280, 1296


def host_consts():
    c = np.zeros((128, C_END), np.float32)
    c[:, C_ID:C_ID + 128] = np.eye(128)
    blk = (np.arange(128)[:, None] // 64 == np.arange(128)[None] // 64)
    c[:, C_BO:C_BO + 128] = blk
    c[:, C_BO64:C_BO64 + 128] = blk / 64.0
    c[:, C_ONE:C_ONE + 128] = 1.0
    s = np.arange(128)[:, None]
    t = np.arange(128)[None]
    for (C, cm, cl) in ((64, C_M64, C_L64), (8, C_M8, C_L8)):
        same = (s // C == t // C)
        c[:, cm:cm + 128] = same & (s < t)
        c[:, cm + 128:cm + 256] = same & (s <= t)
        c[:, cl:cl + 128] = same & (s > t)
    c[:, C_BSEL:C_BSEL + 16] = (np.arange(128)[:, None] // 8 == np.arange(16)[None])
    def bucket(d):
        d = np.asarray(d)
        n = np.maximum(d, 0)
        nf = np.maximum(n, 1).astype(np.float32)
        large = 16 + (np.log(nf / np.float32(16)) / np.float32(np.log(128 / 16)) * np.float32(16)).astype(np.int32)
        return np.where(n < 16, n, np.minimum(large, 31))
    oh = np.zeros((33, 2, 384), np.float32)
    m = np.arange(384)
    for x in range(2):
        dist = (m - 128) if x == 0 else m
        valid = (dist >= 0) & (dist < 128) if x == 0 else (m >= 1) & (m < 128)
        b = bucket(np.clip(dist, 0, 255))
        for mm_ in range(384):
            if valid[mm_]:
                oh[b[mm_], x, mm_] = 1.0
            else:
                oh[32, x, mm_] = NEGB
    return c, oh.reshape(33, 768)


class _Stop(Exception):
    pass


def stop_at(tag):
    if STOP == tag:
        raise _Stop()


def build(nc, n_ptiles=16, do_sample=True, dbg=(), n_stiles=None):
    if n_stiles is None:
        n_stiles = 1 if do_sample else 0
    do_sample = n_stiles > 0
    NST = max(n_stiles, 1)
    P = Prog(nc)
    D = {}

    def din(name, shape):
        D[name] = nc.dram_tensor(name, list(shape), F32, kind="ExternalInput").ap()
        return D[name]

    def dout(name, shape):
        D[name] = nc.dram_tensor(name, list(shape), F32, kind="ExternalOutput").ap()
        return D[name]

    xp = din("xp", [NSEQ, 2048, 1024]); xsm = din("xsm", [NST * 128, 1024])
    ck = din("ck", [NST * 16, 128, 128]); cv = din("cv", [NST * 16, 128, 128])
    sshift = din("sshift", [NST * 16, 1792]); swkv = din("swkv", [NST * 16, 8, 64, 64]); sconv = din("sconv", [NST * 32, 2816])
    rel_bias = din("rel_bias", [32, 8]); norm1_g = din("norm1_g", [1, 1024]); w_in = din("w_in", [1024, 4608])
    sinks = din("sinks", [8]); mu_shift = din("mu_shift", [1792]); w0 = din("w0", [512]); w2 = din("w2", [64, 512])
    a0 = din("a0", [512]); a2 = din("a2", [64, 512]); g2 = din("g2", [128, 512]); k_k = din("k_k", [512])
    k_a = din("k_a", [512]); r_k = din("r_k", [512]); lnx_g = din("lnx_g", [512]); lnx_b = din("lnx_b", [512])
    w_pa = din("w_pa", [512, 1024]); w_pb = din("w_pb", [512, 1024]); w_o = din("w_o", [1024, 1024])
    norm2_g = din("norm2_g", [1, 1024]); w_up = din("w_up", [1024, 5632]); conv_w = din("conv_w", [3, 2816])
    conv_b = din("conv_b", [2816]); w_down = din("w_down", [2816, 1024]); final_g = din("final_g", [1, 1024])
    consts_d = din("consts", [128, C_END]); oh_d = din("oh", [33, 768])

    yp = dout("yp", [NSEQ, 2048, 1024]); ys = dout("ys", [NST * 128, 1024])
    pk = dout("pk", [NSEQ, 128, 128]); pv = dout("pv", [NSEQ, 128, 128]); psh = dout("psh", [NSEQ, 1792])
    pwkv = dout("pwkv", [NSEQ, 8, 64, 64]); pconv = dout("pconv", [NSEQ, 2, 2816])
    sk = dout("sk", [NST * 16, 128, 128]); sv = dout("sv", [NST * 16, 128, 128]); ssh = dout("ssh", [NST * 16, 1792])
    swkvo = dout("swkvo", [NST * 16, 8, 64, 64]); sconvo = dout("sconvo", [NST * 32, 2816])
    dbg_out = {}

    def scratch(name, shape, dtype=BF16):
        return nc.dram_tensor(name, list(shape), dtype, kind="Internal").ap()

    wsc_in = scratch("wsc_in", [9, 128, 4096]); wsc_pa = scratch("wsc_pa", [2, 128, 2048])
    wsc_pb = scratch("wsc_pb", [2, 128, 2048]); wsc_o = scratch("wsc_o", [2, 128, 4096])
    wsc_up = scratch("wsc_up", [11, 128, 4096]); wsc_dn = scratch("wsc_dn", [6, 128, 4096])
    E_d = scratch("E_d", [16, 128, 384], F32)

    chunks_src = []
    for j in range(4):
        chunks_src.append([(j * 64, 64), ((4 + j) * 64, 64)])
    chunks_src.append([(512, 128)]); chunks_src.append([(640, 128)])
    rwb = 768
    chunks_src.append([(rwb + 1536, 128)]); chunks_src.append([(rwb + 1664, 128)])
    for p in range(4):
        chunks_src += [[(rwb + p * 128, 128)], [(rwb + 512 + p * 128, 128)], [(rwb + 1024 + p * 128, 128)]]
    for g in range(16):
        chunks_src.append([(2560 + g * 128, 128)])
    for c, srcs in enumerate(chunks_src):
        b, cc = divmod(c, 4)
        dst = wsc_in[b].rearrange("p (k n) -> p k n", n=512)
        off = cc * 128
        for (lo, n) in srcs:
            P.dma("pool", dst[:, :, off:off + n], w_in[:, lo:lo + n].rearrange("(k p) n -> p k n", p=128),
                  writes=[("wsc_in", c, lo)])
            off += n
    for ch in range(2):
        dpa = wsc_pa[ch].rearrange("p (k n) -> p k n", n=512)
        for j in range(4):
            for half in range(2):
                r0 = (half * 4 + j) * 64
                P.dma("pool", dpa[half * 64:(half + 1) * 64, j, :], w_pa[r0:r0 + 64, ch * 512:(ch + 1) * 512],
                      writes=[("wsc_pa", ch, j, half)])
        P.dma("pool", wsc_pb[ch].rearrange("p (k n) -> p k n", n=512),
              w_pb[:, ch * 512:(ch + 1) * 512].rearrange("(k p) n -> p k n", p=128), writes=[("wsc_pb", ch)])
        P.dma("pool", wsc_o[ch].rearrange("p (k n) -> p k n", n=512),
              w_o[:, ch * 512:(ch + 1) * 512].rearrange("(k p) n -> p k n", p=128), writes=[("wsc_o", ch)])
    for b in range(11):
        dst = wsc_up[b].rearrange("p (k n) -> p k n", n=512)
        P.dma("pool", dst[:, :, 0:256], w_up[:, b * 256:(b + 1) * 256].rearrange("(k p) n -> p k n", p=128),
              writes=[("wsc_up", b, 0)])
        P.dma("pool", dst[:, :, 256:512], w_up[:, 2816 + b * 256:2816 + (b + 1) * 256].rearrange("(k p) n -> p k n", p=128),
              writes=[("wsc_up", b, 1)])
    DN_NK = (8, 8, 6)
    for ch in range(2):
        for rg in range(3):
            nk = DN_NK[rg]
            dst = wsc_dn[ch * 3 + rg].rearrange("p (k n) -> p k n", n=512)
            P.dma("pool", dst[:, 0:nk, :],
                  w_down[rg * 1024:rg * 1024 + nk * 128, ch * 512:(ch + 1) * 512].rearrange("(k p) n -> p k n", p=128),
                  writes=[("wsc_dn", ch * 3 + rg)])

    if STOP == 'A':
        P.finish(); return D
    blk_sched = []
    for b in range(9):
        keys = []
        for c in range(4 * b, 4 * b + 4):
            keys += [("wsc_in", c, lo) for (lo, n) in chunks_src[c]]
        blk_sched.append((wsc_in[b], 4096, keys))
    for ch in range(2):
        blk_sched.append((wsc_pa[ch], 2048, [("wsc_pa", ch, j, h) for j in range(4) for h in range(2)]))
        blk_sched.append((wsc_pb[ch], 2048, [("wsc_pb", ch)]))
    for ch in range(2):
        blk_sched.append((wsc_o[ch], 4096, [("wsc_o", ch)]))
    for b in range(11):
        blk_sched.append((wsc_up[b], 4096, [("wsc_up", b, 0), ("wsc_up", b, 1)]))
    for i in range(6):
        blk_sched.append((wsc_dn[i], DN_NK[i % 3] * 512, [("wsc_dn", i)]))
    NBLK_T = len(blk_sched)
    n_tiles_total = n_ptiles + n_stiles
    NSLOT = 3
    ring_tiles = [P.sbuf("ring%d" % i, [128, 4096], BF16) for i in range(NSLOT)]
    ring_state = {"loaded": 0, "consumed": 0}
    total_blocks = NBLK_T * n_tiles_total

    def ring_get(hold=0):
        k = ring_state["consumed"]
        while ring_state["loaded"] < min(k + NSLOT - hold, total_blocks):
            j = ring_state["loaded"]
            src, nel, keys = blk_sched[j % NBLK_T]
            s = j % NSLOT
            P.dma("sp", ring_tiles[s][:, 0:nel], src[:, 0:nel], reads=keys, writes=[("ring", s)])
            ring_state["loaded"] += 1
        ring_state["consumed"] += 1
        s = k % NSLOT
        return ring_tiles[s], ("ring", s)

    cst = P.sbuf("cst", [128, C_END], F32)
    P.dma("sp", cst[:], consts_d, writes=["cst"])
    ident = cst[:, C_ID:C_ID + 128]
    bo = cst[:, C_BO:C_BO + 128]
    bo64 = cst[:, C_BO64:C_BO64 + 128]
    ones = cst[:, C_ONE:C_ONE + 128]
    ones_bf = P.sbuf("ones_bf", [128, 128], BF16)
    P.op("dve", lambda e: e.tensor_copy(ones_bf[:], ones), reads=["cst"], writes=["ones_bf"])
    zeros_t = P.sbuf("zeros_t", [128, 64], F32)
    P.op("pool", lambda e: e.memset(zeros_t[:], 0.0), writes=["zeros_t"])

    def load_cols(name, vec, ncol):
        t = P.sbuf(name, [128, ncol], F32)
        P.dma("sp", t[:], vec.rearrange("(c p) -> p c", p=128), writes=[name], allow_slow_non_contiguous=True)
        return t

    mu_c = load_cols("mu_c", mu_shift, 14)
    om_c = P.sbuf("om_c", [128, 14], F32)
    P.op("dve", lambda e: e.tensor_scalar(om_c[:], mu_c[:], -1.0, 1.0, ALU.mult, ALU.add), reads=["mu_c"], writes=["om_c"])
    w0_c = load_cols("w0_c", w0, 4); a0_c = load_cols("a0_c", a0, 4); kk_c = load_cols("kk_c", k_k, 4)
    ka_c = load_cols("ka_c", k_a, 4); rk_c = load_cols("rk_c", r_k, 4); lg_c = load_cols("lg_c", lnx_g, 4)
    lb_c = load_cols("lb_c", lnx_b, 4); cb_c = load_cols("cb_c", conv_b, 22)
    cw_c = P.sbuf("cw_c", [128, 3, 22], F32)
    for j in range(3):
        P.dma("sp", cw_c[:, j, :], conv_w[j].rearrange("(c p) -> p c", p=128), writes=[("cw_c", j)], allow_slow_non_contiguous=True)
    CW_KEYS = [("cw_c", j) for j in range(3)]
    gb = {}
    g1c = load_cols("g1b", norm1_g.rearrange("a n -> (a n)"), 8)
    g2c = load_cols("g2b", norm2_g.rearrange("a n -> (a n)"), 8)
    gcol = {"g1b": g1c, "g2b": g2c}
    for nm, src in (("gfb", final_g),):
        gb[nm] = P.sbuf(nm, [128, 1024], F32)
        P.dma("sp", gb[nm][:], src.partition_broadcast(128).rearrange("p a n -> p (a n)"), writes=[nm])
    w2b = P.sbuf("w2b", [128, 512], BF16); g2bf = P.sbuf("g2bf", [128, 512], BF16)
    P.dma("pool", w2b[0:64, :], w2, writes=["w2b"])
    P.dma("pool", w2b[64:128, :], a2, writes=["a2b"])
    P.dma("pool", g2bf[:], g2, writes=["g2bf"])
    sk_t = P.sbuf("sk_t", [128, 4], F32)
    P.dma("sp", sk_t[0:64, :], sinks[0:4].partition_broadcast(64), writes=[("sk_t", 0)])
    P.dma("sp", sk_t[64:128, :], sinks[4:8].partition_broadcast(64), writes=[("sk_t", 1)])
    esk = P.sbuf("esk", [128, 4], F32)
    P.op("act", lambda e: e.activation(esk[:], sk_t[:], AF.Exp), reads=[("sk_t", 0), ("sk_t", 1)], writes=["esk"])
    esink = P.sbuf("esink", [128, 4, 128], F32)
    for g in range(4):
        P.op("act", lambda e, g=g: e.activation(esink[:, g, :], ones, AF.Copy, scale=esk[:, g:g + 1]),
             reads=["cst", "esk"], writes=[("esink", g)])
    ESINK_KEYS = [("esink", g) for g in range(4)]

    xn = P.sbuf("xn", [128, 1024], F32)
    if STOP == 'B':
        P.finish(); return D
    for _i in range(int(os.environ.get('KDUMMY', '0'))):
        P.dma('sp', zeros_t[:, 0:32], consts_d[:, 0:32], writes=['zeros_dummy'])
    for _i in range(int(os.environ.get('KBIG', '0'))):
        P.dma('sp', yp[1], xp[0], writes=['yp1_dummy'])
    BK = [P.psum("bank%d" % i, [128, 512], F32) for i in range(8)]

    def bk(i):
        return ("ps", i)

    rb = P.sbuf("rb", [33, 8], F32)
    Yt = P.sbuf("Yt", [128, 4, NTP], F32)
    bon = P.sbuf("bon", [128, 4, NTP], F32)
    Lh = Yt[0:33, :, :].rearrange("p a t -> p (a t)").rearrange("p (h t) -> p h t", t=128)
    oh_t = bon[0:33, :, :].rearrange("p a t -> p (a t)")[:, 0:768]
    P.dma("sp", rb[0:32, :], rel_bias, writes=["rb"])
    P.dma("sp", oh_t, oh_d, writes=["oh_t"])
    P.op("pool", lambda e: e.memset(Lh[32:33, :, :], 1.0), writes=["Lh1"])
    for h in range(8):
        P.op("act", lambda e, h=h: e.activation(Lh[0:32, h, :], cst[0:32, C_ONE:C_ONE + 128], AF.Copy, scale=rb[0:32, h:h + 1]),
             reads=["cst", "rb"], writes=[("Lh", h)])
    biasT = [[P.sbuf("biasT%d%d" % (x, kv), [128, 4, 128], F32) for kv in range(2)] for x in range(2)]
    for x in range(2):
        for h in range(8):
            bnk = 4 + (x * 8 + h) % 4
            P.op("pe", lambda e, h=h, x=x, bnk=bnk: e.matmul(BK[bnk][:, 0:384], Lh[:, h, :], oh_t[:, x * 384:(x + 1) * 384], start=True, stop=True),
                 reads=["Lh1", ("Lh", h), "oh_t"], writes=[bk(bnk)])
            P.op("dve", lambda e, bnk=bnk: e.tensor_copy(xn[:, 0:384], BK[bnk][:, 0:384]), reads=[bk(bnk)], writes=["xn"])
            P.dma("sp", E_d[x * 8 + h], xn[:, 0:384], reads=["xn"], writes=[("E_d", x, h)])
            kv, g = divmod(h, 4)
            skew = bass.AP(E_d.tensor, (x * 8 + h) * 128 * 384 + 128, [[383, 128], [1, 128]])
            P.dma("sp", biasT[x][kv][:, g, :], skew, reads=[("E_d", x, h)], writes=[("biasT", x, kv, g)])
    if STOP == 'C':
        P.finish(); return D
    if os.environ.get('NOBAR') is None:
        P.barrier()
    BIAS_KEYS = {(x, kv): [("biasT", x, kv, g) for g in range(4)] for x in range(2) for kv in range(2)}

    wstack = [ExitStack()]

    def wsb(name, shape, dtype):
        return wstack[0].enter_context(nc.sbuf_tensor(name, list(shape), dtype))

    PT2 = None
    xn2 = None
    Pvg = None
    xt = None
    stat = None
    hT = None
    qT = None
    kT = None
    kTf = None
    vTf = None
    vtok = None
    kvrow = None
    gates = None
    attT = None
    rwoT = None
    PT = None
    rd_t = None
    pbuf = None
    tmpA = None
    xs3 = None
    TW = None
    SG = None
    car = None
    art = None
    kt_ = None
    bt_ = None
    gT = None
    cCt = None
    KPtok = None
    BPtok = None
    Vtok = None
    rw = None
    S = None
    AK = None
    AB = None
    Mm_ = None
    Mt_ = None
    Pv = None
    VA = None
    VK = None
    XT = None
    UT = None
    y1 = None
    ugx = None
    cc_t = None
    gl_t = None
    actT = None
    ccar = None

    def alloc_work(NTM, sfx):
        nonlocal PT2, xn2, Pvg, xt, stat, hT, qT, kT, kTf, vTf, vtok, kvrow, gates, attT, rwoT, PT, rd_t, pbuf, tmpA, xs3, TW, SG, car, art, kt_, bt_, gT, cCt, KPtok, BPtok, Vtok, rw, S, AK, AB, Mm_, Mt_, Pv, VA, VK, XT, UT, y1, ugx, cc_t, gl_t, actT, ccar
        NB_MAX = NTM // 128
        xt = [wsb(("xt%d" % i) + sfx, [128, 1024], F32) for i in range(NB_MAX)]
        stat = wsb("stat" + sfx, [128, 32], F32)
        xn2 = wsb("xn2" + sfx, [128, 1024], F32) if NTM > 128 else None
        hT = wsb("hT" + sfx, [128, 8, NTM], BF16)
        qT = wsb("qT" + sfx, [128, 4, NTM], BF16)
        kT = wsb("kT" + sfx, [128, 128 + NTM], BF16)
        kTf = wsb("kTf" + sfx, [128, NTM], F32)
        vTf = wsb("vTf" + sfx, [128, NTM], F32)
        vtok = wsb("vtok" + sfx, [128, NB_MAX + 1, 128], BF16)
        kvrow = wsb("kvrow" + sfx, [128, 2, 128], F32)
        gates = wsb("gates" + sfx, [128, 16, NTM], BF16)
        attT = wsb("attT" + sfx, [128, 4, NTM], BF16)
        rwoT = wsb("rwoT" + sfx, [128, 4, NTM], BF16)
        PT = [wsb(("PT%d" % i) + sfx, [128, 512], BF16) for i in range(2)]
        PT2 = [wsb(("PTb%d" % i) + sfx, [128, 512], BF16) for i in range(2)] if NTM > 128 else PT
        rd_t = wsb("rd_t" + sfx, [128, 512], F32)
        pbuf = wsb("pbuf" + sfx, [128, NTM + 16], F32)
        tmpA = wsb("tmpA" + sfx, [128, NTM], F32)
        xs3 = [wsb(("xs3_%d" % i) + sfx, [128, NTM], F32) for i in range(3)]
        TW = wsb("TW" + sfx, [128, NTM], BF16)
        SG = wsb("SG" + sfx, [128, NTM], BF16)
        car = wsb("car" + sfx, [128, 128], F32)
        art = wsb("art" + sfx, [128, 4, 2, NTM], RW_DT)
        kt_ = wsb("kt_" + sfx, [128, 4, NTM], RW_DT)
        bt_ = wsb("bt_" + sfx, [128, 4, NTM], RW_DT)
        gT = wsb("gT" + sfx, [128, 4, NTM], F32)
        cCt = wsb("cCt" + sfx, [128, 4, NTM // 8], F32)
        KPtok = wsb("KPtok" + sfx, [128, NB_MAX, 512], RW_DT)
        BPtok = wsb("BPtok" + sfx, [128, NB_MAX, 512], RW_DT)
        Vtok = wsb("Vtok" + sfx, [128, NB_MAX, 512], RW_DT)
        rw = {n: wsb("rw_" + sfx + n, [128, NTM], F32) for n in ("sg", "cs", "t1", "t2", "t3", "a", "kkn", "k2", "ein", "einv", "eend", "nk")}
        S = wsb("S" + sfx, [128, 4, 64], F32)
        AK = [wsb(("AK%d" % h) + sfx, [128, 256], RW_DT) for h in range(8)]
        AB = [wsb(("AB%d" % h) + sfx, [128, 256], RW_DT) for h in range(8)]
        Mm_ = [wsb(("Mmg%d" % g) + sfx, [128, 4, 128], RW_DT) for g in range(2)]
        Mt_ = [wsb(("Mtg%d" % g) + sfx, [128, 4, 128], RW_DT) for g in range(2)]
        Pvg = [wsb(("Pvg%d" % g) + sfx, [128, 4, 128], RW_DT) for g in range(2)]
        Pv = [Pvg[h // 4][:, h % 4, :] for h in range(8)]
        VA = wsb("VA" + sfx, [128, 512], F32)
        VK = wsb("VK" + sfx, [128, 4, 128], F32)
        XT = wsb("XT" + sfx, [128, 512], RW_DT)
        UT = wsb("UT" + sfx, [128, 512], RW_DT)
        y1 = wsb("y1" + sfx, [128, 4, 64], F32)
        ugx = [wsb(("ugx%d" % i) + sfx, [128, NTM + 40], F32) for i in range(2)]
        cc_t = [wsb(("cc_t%d" % i) + sfx, [128, NTM], F32) for i in range(2)]
        gl_t = [wsb(("gl_t%d" % i) + sfx, [128, NTM], F32) for i in range(2)]
        actT = wsb("actT" + sfx, [128, 22, NTM], BF16)
        ccar = wsb("ccar" + sfx, [128, 2, 128], F32)

    P.stack.callback(lambda: wstack[0].close())
    alloc_work(NTP, "")
    P.op("pool", lambda e: e.memset(car[:], 0.0), writes=["car_init"])
    P.op("pool", lambda e: e.memset(ccar[:].rearrange("p a f -> p (a f)"), 0.0), writes=["ccar_init"])

    def debug_tap(name, ap_sb, shape, keys):
        if name in dbg:
            d_ = nc.dram_tensor("dbg_" + name, list(shape), ap_sb.dtype, kind="ExternalOutput").ap()
            D["dbg_" + name] = d_
            P.dma("sp", d_, ap_sb, reads=keys, writes=[("dbg", name)])

    def do_tile(kind, seq, ti):
        prompt = kind == "p"
        NT = NTP if prompt else 128
        nb = NT // 128
        NBs, TB = (1, NT) if prompt else (16, 8)
        C = 64 if prompt else 8
        LV = 5 if prompt else 2
        cm, cl = (C_M64, C_L64) if prompt else (C_M8, C_L8)
        first = prompt and ti == 0
        last = prompt and ti == 2048 // NTP - 1
        xrows = xp[seq, ti * NT:(ti + 1) * NT, :] if prompt else xsm[seq * 128:(seq + 1) * 128, :]
        yrows = yp[seq, ti * NT:(ti + 1) * NT, :] if prompt else ys[seq * 128:(seq + 1) * 128, :]

        for b in range(nb):
            P.dma("sp", xt[b][:], xrows[b * 128:(b + 1) * 128, :], writes=[("xt", b)])

        stop_at("D1")

        def norm_T(gname):
            def steps(b):
                so = 16 * b
                st = stat[:, so:so + 16]
                xnb = xn if b == 0 else xn2
                xk_ = "xn" if b == 0 else "xn2"
                sk_ = "stat%d_" % b
                bks = (4, 5) if b == 0 else (6, 7)

                def tr(half):
                    for q4 in range(4):
                        kc = half * 4 + q4
                        P.op("pe", lambda e, kc=kc, q4=q4: e.transpose(BK[bks[half]][:, q4 * 128:(q4 + 1) * 128], xnb[:, kc * 128:(kc + 1) * 128], ident),
                             reads=[xk_, "cst"], writes=[bk(bks[half])])

                def ev(half):
                    for q4 in range(4):
                        kc = half * 4 + q4
                        if half == 0:
                            P.op("act", lambda e, kc=kc, q4=q4: e.activation(hT[:, kc, b * 128:(b + 1) * 128], BK[bks[0]][:, q4 * 128:(q4 + 1) * 128], AF.Copy, scale=gcol[gname][:, kc:kc + 1]),
                                 reads=[bk(bks[0]), gname], writes=[("hT", kc)])
                        else:
                            P.op("dve", lambda e, kc=kc, q4=q4: e.tensor_scalar(hT[:, kc, b * 128:(b + 1) * 128], BK[bks[1]][:, q4 * 128:(q4 + 1) * 128], gcol[gname][:, kc:kc + 1], None, ALU.mult),
                                 reads=[bk(bks[1]), gname], writes=[("hT", kc)])
                return [
                    lambda: P.op("dve", lambda e: e.bn_stats(st[:, 0:6], xt[b][:, 0:512]), reads=[("xt", b)], writes=[sk_ + "0"]),
                    lambda: P.op("dve", lambda e: e.bn_stats(st[:, 6:12], xt[b][:, 512:1024]), reads=[("xt", b)], writes=[sk_ + "1"]),
                    lambda: P.op("dve", lambda e: e.bn_aggr(st[:, 12:14], st[:, 0:12]), reads=[sk_ + "0", sk_ + "1"], writes=[sk_ + "2"]),
                    lambda: P.op("dve", lambda e: e.scalar_tensor_tensor(st[:, 14:15], st[:, 12:13], st[:, 12:13], st[:, 13:14], ALU.mult, ALU.add), reads=[sk_ + "2"], writes=[sk_ + "3"]),
                    lambda: P.op("act", lambda e: e.activation(st[:, 15:16], st[:, 14:15], AF.Sqrt, bias=1e-6), reads=[sk_ + "3"], writes=[sk_ + "4"]),
                    lambda: P.op("dve", lambda e: e.reciprocal(st[:, 15:16], st[:, 15:16]), reads=[sk_ + "4"], writes=[sk_ + "4"]),
                    lambda: P.op("dve", lambda e: e.tensor_scalar(xnb[:], xt[b][:], st[:, 15:16], None, ALU.mult), reads=[("xt", b), sk_ + "4"], writes=[xk_]),
                    lambda: tr(0), lambda: tr(1), lambda: ev(0), lambda: ev(1),
                ]
            for group in zip(*[steps(b) for b in range(nb)]):
                for step in group:
                    step()
        HT_KEYS = [("hT", kc) for kc in range(8)]

        norm_T("g1b")
        stop_at("D")

        def v3(ap2, k):
            return ap2.rearrange("p (b t) -> p b t", t=k)

        ts_ctr = [0]

        def token_shift(bnk, n, dst, dst_key):
            ti_ = ts_ctr[0] % 2
            ts_ctr[0] += 1
            tA = tmpA if ti_ == 0 else pbuf
            tkey, tkey0 = ("tmpA", ti_), ("tmpA0", ti_)
            PS3 = v3(BK[bnk][:, 0:NT], TB)
            tA3 = v3(tA[:, 0:NT], TB)
            P.op("act", lambda e: e.activation(tA3[:, :, 1:TB], PS3[:, :, 0:TB - 1], AF.Copy, scale=mu_c[:, n:n + 1]), reads=[bk(bnk), "mu_c"], writes=[tkey])
            if prompt:
                if first:
                    P.op("pool", lambda e: e.memset(tA[:, 0:1], 0.0), writes=[tkey0])
                else:
                    P.op("pool", lambda e: e.tensor_scalar(tA[:, 0:1], car[:, n:n + 1], mu_c[:, n:n + 1], None, ALU.mult), reads=[("car", n), "mu_c"], writes=[tkey0])
            else:
                P.op("pool", lambda e: e.tensor_scalar(tA3[:, :, 0:1], SB["shcar"][:, n, :].unsqueeze(2), mu_c[:, n:n + 1], None, ALU.mult), reads=[("shcar", n), "mu_c"], writes=[tkey0])
            P.op("dve", lambda e: e.scalar_tensor_tensor(v3(dst, TB), PS3, om_c[:, n:n + 1], tA3, ALU.mult, ALU.add),
                 reads=[bk(bnk), tkey, tkey0, "om_c"], writes=[dst_key])
            if prompt:
                P.op("dve", lambda e: e.tensor_copy(car[:, n:n + 1], BK[bnk][:, NT - 1:NT]), reads=[bk(bnk)], writes=[("car", n)])
            else:
                P.op("dve", lambda e: e.tensor_copy(SB["shout"][:, n, :].unsqueeze(2), PS3[:, :, TB - 1:TB]), reads=[bk(bnk)], writes=[("shout", n)])

        def xs_set(p):
            if p % 2 == 0:
                return (xs3[0], xs3[1], xs3[2]), ("xs0", "xs1", "xs2")
            return (cc_t[0], cc_t[1], gl_t[0]), (("cc_t", 0), ("cc_t", 1), ("gl_t", 0))

        def pair_process(p):
            xbufs, xkeys = xs_set(p)
            xr, xk, xv = xbufs[0][:, 0:NT], xbufs[1][:, 0:NT], xbufs[2][:, 0:NT]
            kx0, kx1, kx2 = xkeys
            R = {n: rw[n][:, 0:NT] for n in rw}
            nch = NT // C
            b6, b7 = 6, 7
            P.op("pool", lambda e: e.tensor_scalar(R["kkn"], xk, kk_c[:, p:p + 1], None, ALU.mult), reads=[kx1, "kk_c"], writes=["r_kkn"])
            P.op("pool", lambda e: e.tensor_tensor(R["t2"], R["kkn"], R["kkn"], ALU.mult), reads=["r_kkn"], writes=["r_t2"])
            P.op("pe", lambda e: e.matmul(BK[b6][:, 0:NT], w2b[0:64, p * 128:(p + 1) * 128], TW[0:64, 0:NT], start=True, stop=True),
                 reads=["w2b", "TW"], writes=[bk(b6)])
            P.op("pe", lambda e: e.matmul(BK[b7][:, 0:NT], w2b[64:128, p * 128:(p + 1) * 128], TW[64:128, 0:NT], start=True, stop=True),
                 reads=["a2b", "TW"], writes=[bk(b7)])
            P.op("act", lambda e: e.activation(R["sg"], BK[b6][:, 0:NT], AF.Sigmoid, bias=w0_c[:, p:p + 1]), reads=[bk(b6), "w0_c"], writes=["r_sg"])
            P.op("act", lambda e: e.activation(R["a"], BK[b7][:, 0:NT], AF.Sigmoid, bias=a0_c[:, p:p + 1]), reads=[bk(b7), "a0_c"], writes=["r_a"])
            P.op("pe", lambda e: e.matmul(BK[b6][:, 0:NT], bo, R["t2"], start=True, stop=True), reads=["cst", "r_t2"], writes=[bk(b6)])
            for b in range(nb):
                tb_ = 4 + b % 2
                P.op("pe", lambda e, b=b, tb_=tb_: e.transpose(BK[tb_][:, 0:128], xbufs[2][:, b * 128:(b + 1) * 128], ident), reads=[kx2, "cst"], writes=[bk(tb_)])
                P.op("act", lambda e, b=b, tb_=tb_: e.copy(Vtok[:, b, p * 128:(p + 1) * 128], BK[tb_][:, 0:128]), reads=[bk(tb_)], writes=[("Vtok", b, p)])
            for c in range(nch):
                P.op("dve", lambda e, c=c: e.tensor_tensor_scan(R["cs"][:, c * C:(c + 1) * C], ones[:, 0:C], R["sg"][:, c * C:(c + 1) * C], 0.0, ALU.mult, ALU.add),
                     reads=["r_sg", "cst"], writes=["r_cs"])
            P.op("act", lambda e: e.activation(R["t2"], BK[b6][:, 0:NT], AF.Sqrt), reads=[bk(b6)], writes=["r_t2"])
            P.op("dve", lambda e: e.tensor_scalar(R["t3"], R["a"], -1.0, ka_c[:, p:p + 1], ALU.add, ALU.mult), reads=["r_a", "ka_c"], writes=["r_t3"])
            P.op("dve", lambda e: e.scalar_tensor_tensor(R["k2"], R["t3"], 1.0, xk, ALU.add, ALU.mult), reads=["r_t3", kx1], writes=["r_k2"])
            P.op("act", lambda e: e.activation(R["ein"], R["cs"], AF.Exp, scale=-KAPPA), reads=["r_cs"], writes=["r_ein"])
            P.op("act", lambda e: e.activation(R["einv"], R["cs"], AF.Exp, scale=KAPPA), reads=["r_cs"], writes=["r_einv"])
            P.op("pool", lambda e: e.tensor_tensor(R["t1"], R["cs"], R["sg"], ALU.subtract), reads=["r_cs", "r_sg"], writes=["r_t1"])
            P.op("pool", lambda e: e.tensor_scalar(R["nk"][:, 0:nch], R["cs"][:, C - 1:NT:C], -KAPPA, None, ALU.mult), reads=["r_cs"], writes=["r_nk"])
            P.op("act", lambda e: e.activation(R["t1"], R["t1"], AF.Exp, scale=-KAPPA), reads=["r_t1"], writes=["r_t1"])
            for c in range(nch):
                P.op("act", lambda e, c=c: e.activation(R["eend"][:, c * C:(c + 1) * C], R["cs"][:, c * C:(c + 1) * C], AF.Exp, scale=KAPPA, bias=R["nk"][:, c:c + 1]),
                     reads=["r_cs", "r_nk"], writes=["r_eend"])
            P.op("dve", lambda e: e.tensor_scalar(R["t2"], R["t2"], 1e-12, None, ALU.max), reads=["r_t2"], writes=["r_t2"])
            P.op("dve", lambda e: e.reciprocal(R["t2"], R["t2"]), reads=["r_t2"], writes=["r_t2"])
            P.op("dve", lambda e: e.tensor_tensor(R["kkn"], R["kkn"], R["t2"], ALU.mult), reads=["r_kkn", "r_t2"], writes=["r_kkn"])
            P.op("pool", lambda e: e.tensor_copy(cCt[:, p, 0:nch], R["ein"][:, C - 1:NT:C]), reads=["r_ein"], writes=[("cCt", p)])
            P.op("pool", lambda e: e.tensor_tensor(art[:, p, 1, 0:NT], xr, R["ein"], ALU.mult), reads=[kx0, "r_ein"], writes=[("art", p)])
            P.op("dve", lambda e: e.tensor_tensor(kt_[:, p, 0:NT], R["k2"], R["einv"], ALU.mult), reads=["r_k2", "r_einv"], writes=[("kt", p)])
            P.op("dve", lambda e: e.scalar_tensor_tensor(art[:, p, 0, 0:NT], R["kkn"], -1.0, R["t1"], ALU.mult, ALU.mult), reads=["r_kkn", "r_t1"], writes=[("art", p)])
            P.op("pool", lambda e: e.tensor_tensor(R["t3"], R["kkn"], R["a"], ALU.mult), reads=["r_kkn", "r_a", "r_k2"], writes=["r_t3"])
            P.op("dve", lambda e: e.tensor_tensor(bt_[:, p, 0:NT], R["t3"], R["einv"], ALU.mult), reads=["r_t3", "r_einv"], writes=[("bt", p)])
            P.op("pool", lambda e: e.tensor_tensor(R["t2"], R["k2"], R["eend"], ALU.mult), reads=["r_k2", "r_eend", "r_kkn"], writes=["r_t2"])
            P.op("dve", lambda e: e.scalar_tensor_tensor(R["t1"], xr, rk_c[:, p:p + 1], R["k2"], ALU.mult, ALU.mult), reads=[kx0, "rk_c", "r_k2", ("art", p)], writes=["r_t1"])
            P.op("pool", lambda e: e.tensor_tensor(R["t3"], R["t3"], R["eend"], ALU.mult), reads=["r_t3", "r_eend", ("bt", p)], writes=["r_t3"])
            for b in range(nb):
                tb_ = 4 + b % 2
                P.op("pe", lambda e, b=b, tb_=tb_: e.transpose(BK[tb_][:, 0:128], R["t2"][:, b * 128:(b + 1) * 128], ident), reads=["r_t2", "cst"], writes=[bk(tb_)])
                P.op("act", lambda e, b=b, tb_=tb_: e.copy(KPtok[:, b, p * 128:(p + 1) * 128], BK[tb_][:, 0:128]), reads=[bk(tb_)], writes=[("KPtok", b, p)])
            P.op("pe", lambda e: e.matmul(BK[b7][:, 0:NT], bo, R["t1"], start=True, stop=True), reads=["cst", "r_t1"], writes=[bk(b7)])
            for b in range(nb):
                tb_ = 4 + b % 2
                P.op("pe", lambda e, b=b, tb_=tb_: e.transpose(BK[tb_][:, 0:128], R["t3"][:, b * 128:(b + 1) * 128], ident), reads=["r_t3", "cst"], writes=[bk(tb_)])
                P.op("act", lambda e, b=b, tb_=tb_: e.copy(BPtok[:, b, p * 128:(p + 1) * 128], BK[tb_][:, 0:128]), reads=[bk(tb_)], writes=[("BPtok", b, p)])
            P.op("dve", lambda e: e.tensor_tensor(bon[:, p, 0:NT], BK[b7][:, 0:NT], xv, ALU.mult), reads=[bk(b7), kx2], writes=[("bon", p)])
            P.op("pe", lambda e: e.matmul(BK[b6][:, 0:NT], g2bf[:, p * 128:(p + 1) * 128], SG[:, 0:NT], start=True, stop=True), reads=["g2bf", "SG"], writes=[bk(b6)])
            P.op("act", lambda e: e.copy(gT[:, p, 0:NT], BK[b6][:, 0:NT]), reads=[bk(b6)], writes=[("gT", p)])

        pend = [None]
        for blkb in range(9):
            slot, skey = ring_get()
            sl3 = slot[:, 0:4096].rearrange("p (k n) -> p k n", n=512)
            for cc in range(4):
                c = blkb * 4 + cc
                bnk = c % 4
                for kc in range(8):
                    P.op("pe", lambda e, kc=kc, cc=cc, bnk=bnk, sl3=sl3: e.matmul(BK[bnk][:, 0:NT], sl3[:, kc, cc * 128:(cc + 1) * 128], hT[:, kc, 0:NT], start=(kc == 0), stop=(kc == 7)),
                         reads=[skey] + HT_KEYS, writes=[bk(bnk)])
                stop_at("E%d" % c)
                if c < 4:
                    P.op("act", lambda e, c=c, bnk=bnk: e.copy(qT[:, c, 0:NT], BK[bnk][:, 0:NT]), reads=[bk(bnk)], writes=[("qT", c)])
                elif c == 4:
                    P.op("act", lambda e, bnk=bnk: e.copy(kTf[:, 0:NT], BK[bnk][:, 0:NT]), reads=[bk(bnk)], writes=["kTf"])
                    P.op("pool", lambda e: e.tensor_copy(kT[:, 128:128 + NT], kTf[:, 0:NT]), reads=["kTf"], writes=["kT"])
                elif c == 5:
                    P.op("act", lambda e, bnk=bnk: e.copy(vTf[:, 0:NT], BK[bnk][:, 0:NT]), reads=[bk(bnk)], writes=["vTf"])
                    for b in range(nb):
                        tb_ = 4 + b % 2
                        P.op("pe", lambda e, b=b, tb_=tb_: e.transpose(BK[tb_][:, 0:128], vTf[:, b * 128:(b + 1) * 128], ident), reads=["vTf", "cst"], writes=[bk(tb_)])
                        P.op("dve", lambda e, b=b, tb_=tb_: e.tensor_copy(vtok[:, 1 + b, :], BK[tb_][:, 0:128]), reads=[bk(tb_)], writes=[("vtok", 1 + b)])
                        if (prompt and last and b == nb - 1) or not prompt:
                            P.op("dve", lambda e, tb_=tb_: e.tensor_copy(kvrow[:, 1, :], BK[tb_][:, 0:128]), reads=[bk(tb_)], writes=[("kvrow", 1)])
                elif c == 6:
                    token_shift(bnk, 12, xs3[0][:, 0:NT], "xs0")
                    P.op("act", lambda e: e.activation(TW[0:64, 0:NT], xs3[0][0:64, 0:NT], AF.Tanh), reads=["xs0"], writes=["TW"])
                    P.op("pool", lambda e: e.tensor_copy(TW[64:128, 0:NT], xs3[0][64:128, 0:NT]), reads=["xs0"], writes=["TW"])
                elif c == 7:
                    token_shift(bnk, 13, xs3[0][:, 0:NT], "xs0")
                    P.op("act", lambda e: e.activation(SG[:, 0:NT], xs3[0][:, 0:NT], AF.Sigmoid), reads=["xs0"], writes=["SG"])
                elif c < 20:
                    p, which = divmod(c - 8, 3)
                    xb_, xk_ = xs_set(p)
                    token_shift(bnk, which * 4 + p, xb_[which][:, 0:NT], xk_[which])
                    if which == 2:
                        if pend[0] is not None:
                            pair_process(pend[0])
                        pend[0] = p
                else:
                    gi = c - 20
                    P.op("act", lambda e, gi=gi, bnk=bnk: e.activation(gates[:, gi, 0:NT], BK[bnk][:, 0:NT], AF.Sigmoid), reads=[bk(bnk)], writes=[("gates", gi)])
                    if c == 23 and pend[0] is not None:
                        pair_process(pend[0])
                        pend[0] = None

        stop_at("E")
        debug_tap("qT", qT[:, :, 0:NT], [128, 4, NT], [("qT", c) for c in range(4)])

        if prompt:
            def att_steps(b, kv):
                gbk = ti * nb + b
                ph = slice(kv * 64, kv * 64 + 64)
                qv = qT[ph, :, b * 128:(b + 1) * 128]
                xs_ = [0] + ([1] if gbk > 0 else [])
                bb = 4 if kv == 0 else 0
                PTk = PT if kv == 0 else PT2

                def sc():
                    for x in xs_:
                        kcols = slice(128 + b * 128, 256 + b * 128) if x == 0 else slice(b * 128, 128 + b * 128)
                        P.op("pe", lambda e, kcols=kcols, x=x: e.matmul(BK[bb + x][:], kT[ph, kcols], qv, start=True, stop=True),
                             reads=["kT"] + [("qT", c) for c in range(4)], writes=[bk(bb + x)])

                def bias():
                    for x in xs_:
                        P.op("dve", lambda e, x=x: e.scalar_tensor_tensor(BK[bb + x][:], BK[bb + x][:], 0.125, biasT[x][kv][:].rearrange("p g q -> p (g q)"), ALU.mult, ALU.add),
                             reads=[bk(bb + x)] + BIAS_KEYS[(x, kv)], writes=[bk(bb + x)])

                def ex():
                    for x in xs_:
                        P.op("act", lambda e, x=x: e.activation(PTk[x][:], BK[bb + x][:], AF.Exp), reads=[bk(bb + x)], writes=[("PT", kv, x)])

                def pv():
                    for i, x in enumerate(xs_):
                        vb = 1 + b if x == 0 else b
                        P.op("pe", lambda e, x=x, vb=vb, i=i: e.matmul(BK[bb + 2][:], vtok[:, vb, :], PTk[x][:], start=(i == 0), stop=(i == len(xs_) - 1)),
                             reads=[("vtok", vb), ("PT", kv, x)], writes=[bk(bb + 2)])
                    for i, x in enumerate(xs_):
                        P.op("pe", lambda e, x=x, i=i: e.matmul(BK[bb + 3][:], ones_bf[:], PTk[x][:], start=(i == 0), stop=(i == len(xs_) - 1)),
                             reads=["ones_bf", ("PT", kv, x)], writes=[bk(bb + 3)])
                return [
                    sc, bias, ex, pv,
                    lambda: P.op("dve", lambda e: e.tensor_tensor(rd_t[ph, :], BK[bb + 3][ph, :], esink[ph, :, :].rearrange("p g q -> p (g q)"), ALU.add),
                                 reads=[bk(bb + 3)] + ESINK_KEYS, writes=[("rd_t", kv)]),
                    lambda: P.op("dve", lambda e: e.reciprocal(rd_t[ph, :], rd_t[ph, :]), reads=[("rd_t", kv)], writes=[("rd_t", kv)]),
                    lambda: P.op("dve", lambda e: e.tensor_tensor(attT[ph, :, b * 128:(b + 1) * 128], BK[bb + 2][ph, :].rearrange("p (g q) -> p g q", q=128), rd_t[ph, :].rearrange("p (g q) -> p g q", q=128), ALU.mult),
                                 reads=[bk(bb + 2), ("rd_t", kv)], writes=[("attT", kv, b)]),
                ]
            for b in range(nb):
                for s_a, s_b in zip(att_steps(b, 0), att_steps(b, 1)):
                    s_a()
                    s_b()
            P.op("pool", lambda e: e.tensor_copy(kT[:, 0:128], kT[:, NT:NT + 128]), reads=["kT"], writes=["kT"])
            P.op("pool", lambda e: e.tensor_copy(vtok[:, 0, :], vtok[:, nb, :]), reads=[("vtok", nb)], writes=[("vtok", 0)])
            if last and 'pk' not in SKIP:
                P.op("pe", lambda e: e.transpose(BK[4][:, 0:128], kTf[:, NT - 128:NT], ident), reads=["kTf", "cst"], writes=[bk(4)])
                P.op("act", lambda e: e.copy(kvrow[:, 0, :], BK[4][:, 0:128]), reads=[bk(4)], writes=[("kvrow", 0)])
                P.dma("sp", pk[seq], kvrow[:, 0, :], reads=[("kvrow", 0)], writes=["pk"])
                P.dma("sp", pv[seq], kvrow[:, 1, :], reads=[("kvrow", 1)], writes=["pv"])
        else:
            sample_attention()
        ATT_KEYS = [("attT", kv, b) for kv in range(2) for b in range(nb)]
        stop_at("F")
        debug_tap("attT", attT[:, :, 0:NT], [128, 4, NT], ATT_KEYS)

        def hcol(h):
            return (h % 2) * 256 + (h // 2) * 64

        def rw_pre(b):
            bc = slice(b * 128, (b + 1) * 128)
            for h in range(8):
                p, h2 = divmod(h, 2)
                ph = slice(h2 * 64, h2 * 64 + 64)
                b0, b1, b2 = (4, 5, 6) if h % 2 == 0 else (0, 1, 2)
                P.op("pe", lambda e, p=p, ph=ph: e.matmul(BK[b0][:, 0:256], kt_[ph, p, bc], art[ph, p, :, bc], start=True, stop=True),
                     reads=[("kt", p), ("art", p)], writes=[bk(b0)])
                P.op("dve", lambda e, h=h: e.tensor_tensor(AK[h][:], BK[b0][:, 0:256], cst[:, cm:cm + 256], ALU.mult), reads=[bk(b0), "cst"], writes=[("AK", h)])
                P.op("pe", lambda e, p=p, ph=ph: e.matmul(BK[b1][:, 0:256], bt_[ph, p, bc], art[ph, p, :, bc], start=True, stop=True),
                     reads=[("bt", p), ("art", p)], writes=[bk(b1)])
                P.op("dve", lambda e, h=h: e.tensor_tensor(AB[h][:], BK[b1][:, 0:256], cst[:, cm:cm + 256], ALU.mult), reads=[bk(b1), "cst"], writes=[("AB", h)])
                P.op("pe", lambda e, p=p, ph=ph: e.matmul(BK[b2][:, 0:128], art[ph, p, 0, bc], bt_[ph, p, bc], start=True, stop=True),
                     reads=[("bt", p), ("art", p)], writes=[bk(b2)])
                P.op("dve", lambda e, h=h: e.tensor_tensor(Mt_[h // 4][:, h % 4, :], BK[b2][:, 0:128], cst[:, cl:cl + 128], ALU.mult), reads=[bk(b2), "cst"], writes=[("Mt", h // 4)])
                P.op("pool", lambda e, h=h: e.tensor_tensor(Pv[h], AB[h][:, 0:128], ident, ALU.add), reads=[("AB", h), "cst"], writes=[("Pv", h)])
            for lv in range(1, LV + 1):
                lastlv = lv == LV
                for g in range(2):
                    bMT, bM, bP = (4, 5, 6) if g == 0 else (0, 1, 2)
                    for j in range(4):
                        h = 4 * g + j
                        Mcur = AB[h][:, 0:128] if lv == 1 else Mm_[g][:, j, :]
                        mk_c = ("AB", h) if lv == 1 else ("Mm", g)
                        Mtcur = Mt_[g][:, j, :]
                        P.op("pe", lambda e, Mcur=Mcur, Mtcur=Mtcur, j=j: e.matmul(BK[bMT][:, j * 128:(j + 1) * 128], Mcur, Mtcur, start=True, stop=True),
                             reads=[mk_c, ("Mt", g)], writes=[bk(bMT)])
                        if not lastlv:
                            P.op("pe", lambda e, Mcur=Mcur, Mtcur=Mtcur, j=j: e.matmul(BK[bM][:, j * 128:(j + 1) * 128], Mtcur, Mcur, start=True, stop=True),
                                 reads=[mk_c, ("Mt", g)], writes=[bk(bM)])
                    P.op("act", lambda e, g=g: e.copy(Mt_[g][:].rearrange("p a t -> p (a t)"), BK[bMT][:]), reads=[bk(bMT)], writes=[("Mt", g)])
                    if not lastlv:
                        P.op("act", lambda e, g=g: e.copy(Mm_[g][:].rearrange("p a t -> p (a t)"), BK[bM][:]), reads=[bk(bM)], writes=[("Mm", g)])
                    for j in range(4):
                        h = 4 * g + j
                        P.op("pe", lambda e, j=j, h=h, g=g: e.matmul(BK[bP][:, j * 128:(j + 1) * 128], Mt_[g][:, j, :], Pv[h], start=True, stop=True),
                             reads=[("Mt", g), ("Pv", h)], writes=[bk(bP)])
                    P.op("dve", lambda e, g=g: e.tensor_tensor(Pvg[g][:].rearrange("p a t -> p (a t)"), BK[bP][:], Pvg[g][:].rearrange("p a t -> p (a t)"), ALU.add),
                         reads=[bk(bP)] + [("Pv", 4 * g + j) for j in range(4)], writes=[("Pv", 4 * g + j) for j in range(4)])
            for h in range(8):
                P.op("pe", lambda e, h=h, b=b: e.matmul(BK[7][:, hcol(h):hcol(h) + 64], AK[h][:, 0:128], Vtok[:, b, h * 64:(h + 1) * 64], start=True, stop=True),
                     reads=[("AK", h), ("Vtok", b, h // 2)], writes=[bk(7)])
            P.op("act", lambda e: e.copy(VA[:], BK[7][:]), reads=[bk(7)], writes=["VA"])
            for h in range(8):
                p, h2 = divmod(h, 2)
                ph = slice(h2 * 64, h2 * 64 + 64)
                P.op("pe", lambda e, h=h, b=b, p=p, ph=ph: e.matmul(BK[4][ph, p * 128:(p + 1) * 128], Vtok[:, b, h * 64:(h + 1) * 64], AK[h][:, 128:256], start=True, stop=True),
                     reads=[("AK", h), ("Vtok", b, p)], writes=[bk(4)])
            P.op("act", lambda e: e.copy(VK[:].rearrange("p a t -> p (a t)"), BK[4][:]), reads=[bk(4)], writes=["VK"])
            stop_at('G2')

        if prompt:
            if first:
                P.op("pool", lambda e: e.memset(S[:], 0.0), writes=["S"])
            for b in range(nb):
                rw_pre(b)
                for c2 in range(2):
                    cr = slice(c2 * 64, c2 * 64 + 64)
                    tc_ = slice(b * 128 + c2 * 64, b * 128 + c2 * 64 + 64)
                    gci = (b * 128 + c2 * 64) // 64
                    for h in range(8):
                        p, h2 = divmod(h, 2)
                        ph = slice(h2 * 64, h2 * 64 + 64)
                        P.op("pe", lambda e, h=h, p=p, ph=ph, h2=h2: e.matmul(BK[5 + h2][cr, p * 64:(p + 1) * 64], art[ph, p, 0, tc_], S[ph, p, :], start=True, stop=True),
                             reads=[("art", p), "S"], writes=[bk(5 + h2)])
                    for h in range(8):
                        p, h2 = divmod(h, 2)
                        ph = slice(h2 * 64, h2 * 64 + 64)
                        P.op("pe", lambda e, p=p, ph=ph, h2=h2: e.matmul(BK[h2][ph, p * 64:(p + 1) * 64], S[ph, p, :], art[ph, p, 1, tc_], start=True, stop=True),
                             reads=[("art", p), "S"], writes=[bk(h2)])
                    for h2 in range(2):
                        ph = slice(h2 * 64, h2 * 64 + 64)
                        P.op("act", lambda e, h2=h2, ph=ph: e.copy(y1[ph, :, :].rearrange("p a t -> p (a t)"), BK[h2][ph, 0:256]), reads=[bk(h2)], writes=[("y1", h2)])
                    for h2 in range(2):
                        P.op("dve", lambda e, h2=h2: e.tensor_tensor(XT[cr, h2 * 256:(h2 + 1) * 256], BK[5 + h2][cr, 0:256], VA[cr, h2 * 256:(h2 + 1) * 256], ALU.add),
                             reads=[bk(5 + h2), "VA"], writes=[("XT", h2)])
                    for h in range(8):
                        P.op("pe", lambda e, h=h: e.matmul(BK[7][cr, hcol(h):hcol(h) + 64], Pv[h][cr, c2 * 64:(c2 + 1) * 64], XT[cr, hcol(h):hcol(h) + 64], start=True, stop=True),
                             reads=[("Pv", h), ("XT", h % 2)], writes=[bk(7)])
                    P.op("act", lambda e: e.copy(UT[cr, :], BK[7][cr, :]), reads=[bk(7)], writes=["UT"])
                    stop_at('G3')
                    for h in range(8):
                        p, h2 = divmod(h, 2)
                        ph = slice(h2 * 64, h2 * 64 + 64)
                        P.op("pe", lambda e, h=h, p=p, ph=ph: e.matmul(BK[5][ph, 256 + p * 64:256 + (p + 1) * 64], BPtok[cr, b, h * 64:(h + 1) * 64], UT[cr, hcol(h):hcol(h) + 64], start=True, stop=False),
                             reads=[("BPtok", b, p), "UT"], writes=[bk(5)])
                        P.op("pe", lambda e, h=h, p=p, ph=ph: e.matmul(BK[5][ph, 256 + p * 64:256 + (p + 1) * 64], KPtok[cr, b, h * 64:(h + 1) * 64], Vtok[cr, b, h * 64:(h + 1) * 64], start=False, stop=True),
                             reads=[("KPtok", b, p), ("Vtok", b, p)], writes=[bk(5)])
                    for p in range(4):
                        P.op("dve", lambda e, p=p: e.scalar_tensor_tensor(S[:, p, :], S[:, p, :], cCt[:, p, gci:gci + 1], BK[5][:, 256 + p * 64:256 + (p + 1) * 64], ALU.mult, ALU.add),
                             reads=["S", ("cCt", p), bk(5)], writes=["S"])
                    for h in range(8):
                        p, h2 = divmod(h, 2)
                        ph = slice(h2 * 64, h2 * 64 + 64)
                        P.op("pe", lambda e, h=h, p=p, ph=ph: e.matmul(BK[6][ph, 256 + p * 64:256 + (p + 1) * 64], UT[cr, hcol(h):hcol(h) + 64], AB[h][cr, 128 + c2 * 64:128 + (c2 + 1) * 64], start=True, stop=True),
                             reads=[("AB", h), "UT"], writes=[bk(6)])
                    P.op("dve", lambda e: e.tensor_tensor(y1[:], BK[6][:, 256:512].rearrange("p (a t) -> p a t", t=64), y1[:], ALU.add), reads=[bk(6), ("y1", 0), ("y1", 1)], writes=[("y1", 0), ("y1", 1)])
                    P.op("pool", lambda e: e.tensor_tensor(Yt[:, :, tc_], y1[:], VK[:, :, c2 * 64:(c2 + 1) * 64], ALU.add), reads=[("y1", 0), ("y1", 1), "VK"], writes=[("Yt", b)])
                    stop_at('G4')
            if last and 'pwkv' not in SKIP:
                for pp in range(2):
                    P.op("pe", lambda e, pp=pp: e.transpose(BK[4][:, pp * 128:(pp + 1) * 128], S[:, 2 * pp:2 * pp + 2, :].rearrange("p a i -> p (a i)"), ident), reads=["S", "cst"], writes=[bk(4)])
                P.op("act", lambda e: e.copy(xn[:, 0:256], BK[4][:, 0:256]), reads=[bk(4)], writes=["xn"])
                for pp in range(2):
                    for pl in range(2):
                        pidx = 2 * pp + pl
                        P.dma("sp", pwkv[seq, 2 * pidx:2 * pidx + 2].rearrange("h i j -> i h j"), xn[pl * 64:(pl + 1) * 64, pp * 128:(pp + 1) * 128].rearrange("p (h j) -> p h j", j=64), reads=["xn"], writes=[("pwkv", pidx)])
                P.op("pe", lambda e: e.transpose(BK[4][:, 0:128], car[:, :], ident), reads=[("car", n) for n in range(14)] + ["cst", "car_init"], writes=[bk(4)])
                P.op("act", lambda e: e.copy(xn[0:14, 0:128], BK[4][0:14, 0:128]), reads=[bk(4)], writes=["xn"])
                P.dma("sp", psh[seq].rearrange("(c p) -> c p", p=128), xn[0:14, 0:128], reads=["xn"], writes=["psh"])
        else:
            rw_pre(0)
            sample_rwkv(hcol)
        YT_KEYS = [("Yt", b) for b in range(nb)]
        stop_at("G")
        debug_tap("Yt", Yt[:, :, 0:NT], [128, 4, NT], YT_KEYS)

        def post_steps(p):
            tn = ("sg", "cs", "t1", "t2", "t3", "a", "kkn", "k2")
            n1, n2 = tn[2 * p], tn[2 * p + 1]
            T1, T2 = rw[n1][:, 0:NT], rw[n2][:, 0:NT]
            k1, k2_ = "r_" + n1, "r_" + n2
            bm, bv = 4 + p, p
            return [
                lambda: P.op("pe", lambda e: e.matmul(BK[bm][:, 0:NT], bo64, Yt[:, p, 0:NT], start=True, stop=True), reads=YT_KEYS + ["cst"], writes=[bk(bm)]),
                lambda: P.op("dve", lambda e: e.tensor_tensor(T1, Yt[:, p, 0:NT], BK[bm][:, 0:NT], ALU.subtract), reads=YT_KEYS + [bk(bm)], writes=[k1]),
                lambda: P.op("pool", lambda e: e.tensor_tensor(T2, T1, T1, ALU.mult), reads=[k1], writes=[k2_]),
                lambda: P.op("pe", lambda e: e.matmul(BK[bv][:, 0:NT], bo64, T2, start=True, stop=True), reads=[k2_, "cst"], writes=[bk(bv)]),
                lambda: P.op("act", lambda e: e.activation(T2, BK[bv][:, 0:NT], AF.Sqrt, bias=64e-5), reads=[bk(bv)], writes=[k2_]),
                lambda: P.op("dve", lambda e: e.reciprocal(T2, T2), reads=[k2_], writes=[k2_]),
                lambda: P.op("dve", lambda e: e.tensor_tensor(T1, T1, T2, ALU.mult), reads=[k1, k2_], writes=[k1]),
                lambda: P.op("dve", lambda e: e.tensor_scalar(T1, T1, lg_c[:, p:p + 1], lb_c[:, p:p + 1], ALU.mult, ALU.add), reads=[k1, "lg_c", "lb_c"], writes=[k1]),
                lambda: P.op("pool", lambda e: e.tensor_tensor(T1, T1, bon[:, p, 0:NT], ALU.add), reads=[k1, ("bon", p)], writes=[k1]),
                lambda: P.op("dve", lambda e: e.tensor_tensor(rwoT[:, p, 0:NT], T1, gT[:, p, 0:NT], ALU.mult), reads=[k1, ("gT", p)], writes=[("rwoT", p)]),
            ]
        for group in zip(*[post_steps(p) for p in range(4)]):
            for step in group:
                step()
        RWO_KEYS = [("rwoT", p) for p in range(4)]
        stop_at("H")
        debug_tap("rwoT", rwoT[:, :, 0:NT], [128, 4, NT], RWO_KEYS)

        for ch in range(2):
            sa, ka_ = ring_get()
            sb_, kb_ = ring_get(hold=1)
            sa3 = sa[:, 0:2048].rearrange("p (k n) -> p k n", n=512)
            sb3 = sb_[:, 0:2048].rearrange("p (k n) -> p k n", n=512)
            for cc in range(4):
                oc = ch * 4 + cc
                ba, bb = (0, 1) if cc % 2 == 0 else (2, 3)
                for kc in range(4):
                    P.op("pe", lambda e, kc=kc, cc=cc, ba=ba, sa3=sa3: e.matmul(BK[ba][:, 0:NT], sa3[:, kc, cc * 128:(cc + 1) * 128], attT[:, kc, 0:NT], start=(kc == 0), stop=(kc == 3)),
                         reads=[ka_] + ATT_KEYS, writes=[bk(ba)])
                for kc in range(4):
                    P.op("pe", lambda e, kc=kc, cc=cc, bb=bb, sb3=sb3: e.matmul(BK[bb][:, 0:NT], sb3[:, kc, cc * 128:(cc + 1) * 128], rwoT[:, kc, 0:NT], start=(kc == 0), stop=(kc == 3)),
                         reads=[kb_] + RWO_KEYS, writes=[bk(bb)])
                tA = cc_t[cc % 2][:, 0:NT]
                tB = gl_t[cc % 2][:, 0:NT]
                P.op("dve", lambda e, oc=oc, ba=ba, tA=tA: e.tensor_tensor(tA, BK[ba][:, 0:NT], gates[:, oc, 0:NT], ALU.mult), reads=[bk(ba), ("gates", oc)], writes=[("cc_t", cc % 2)])
                P.op("dve", lambda e, oc=oc, bb=bb, tB=tB: e.tensor_tensor(tB, BK[bb][:, 0:NT], gates[:, 8 + oc, 0:NT], ALU.mult), reads=[bk(bb), ("gates", 8 + oc)], writes=[("gl_t", cc % 2)])
                P.op("pool", lambda e, oc=oc, tA=tA, tB=tB: e.tensor_tensor(hT[:, oc, 0:NT], tA, tB, ALU.add), reads=[("cc_t", cc % 2), ("gl_t", cc % 2)], writes=[("hT", oc)])
        MIX_KEYS = [("hT", oc) for oc in range(8)]
        debug_tap("mixT", hT[:, :, 0:NT], [128, 8, NT], MIX_KEYS)

        stop_at("I")
        for ch in range(2):
            so, ko = ring_get()
            so3 = so[:, 0:4096].rearrange("p (k n) -> p k n", n=512)
            for b in range(nb):
                bnk = (ch * nb + b) % 4
                for kc in range(8):
                    P.op("pe", lambda e, kc=kc, b=b, bnk=bnk, so3=so3: e.matmul(BK[bnk][:], hT[:, kc, b * 128:(b + 1) * 128], so3[:, kc, :], start=(kc == 0), stop=(kc == 7)),
                         reads=[ko] + MIX_KEYS, writes=[bk(bnk)])
                P.op("dve", lambda e, b=b, bnk=bnk, ch=ch: e.tensor_tensor(xt[b][:, ch * 512:(ch + 1) * 512], xt[b][:, ch * 512:(ch + 1) * 512], BK[bnk][:], ALU.add),
                     reads=[bk(bnk), ("xt", b)], writes=[("xt", b)])
        stop_at("J")
        debug_tap("x1", xt[0][:], [128, 1024], [("xt", 0)])

        norm_T("g2b")
        def ffn_steps(i, f, bo_):
            ug3 = v3(ugx[i][:, 0:NBs * (TB + 2)], TB + 2)
            ugk = ("ugx", i)
            c3 = v3(cc_t[i][:, 0:NT], TB)

            def carry():
                if prompt:
                    if first:
                        P.op("pool", lambda e: e.memset(ugx[i][:, 0:2], 0.0), writes=[("ugx0", i)])
                    else:
                        P.op("pool", lambda e: e.tensor_copy(ugx[i][:, 0:2], ccar[:, :, f]), reads=[("ccar", f)], writes=[("ugx0", i)])
                    P.op("pool", lambda e: e.tensor_copy(ccar[:, :, f], ugx[i][:, NT:NT + 2]), reads=[ugk], writes=[("ccar", f)])
                else:
                    P.op("pool", lambda e: e.tensor_copy(ug3[:, :, 0:2], SB["ccar_s"][:, f, :, :]), reads=[("ccar_s", f)], writes=[("ugx0", i)])
                    P.op("pool", lambda e: e.tensor_copy(SB["cout_s"][:, f, :, :], ug3[:, :, TB:TB + 2]), reads=[ugk], writes=[("cout_s", f)])
            return [
                lambda: P.op("act", lambda e: e.copy(ug3[:, :, 2:TB + 2], v3(BK[bo_ + i][:, 0:NT], TB)), reads=[bk(bo_ + i)], writes=[ugk]),
                carry,
                lambda: P.op("pool", lambda e: e.tensor_scalar(c3, ug3[:, :, 0:TB], cw_c[:, 0, f:f + 1], cb_c[:, f:f + 1], ALU.mult, ALU.add),
                             reads=[ugk, ("ugx0", i), "cb_c"] + CW_KEYS, writes=[("cc_t", i)]),
                lambda: P.op("dve", lambda e: e.scalar_tensor_tensor(c3, ug3[:, :, 1:TB + 1], cw_c[:, 1, f:f + 1], c3, ALU.mult, ALU.add),
                             reads=[ugk, ("ugx0", i), ("cc_t", i)] + CW_KEYS, writes=[("cc_t", i)]),
                lambda: P.op("dve", lambda e: e.scalar_tensor_tensor(c3, ug3[:, :, 2:TB + 2], cw_c[:, 2, f:f + 1], c3, ALU.mult, ALU.add),
                             reads=[ugk, ("cc_t", i)] + CW_KEYS, writes=[("cc_t", i)]),
                lambda: P.op("act", lambda e: e.activation(gl_t[i][:, 0:NT], cc_t[i][:, 0:NT], AF.Gelu_apprx_tanh), reads=[("cc_t", i)], writes=[("gl_t", i)]),
                lambda: P.op("dve", lambda e: e.tensor_tensor(actT[:, f, 0:NT], gl_t[i][:, 0:NT], BK[bo_ + 2 + i][:, 0:NT], ALU.mult), reads=[("gl_t", i), bk(bo_ + 2 + i)], writes=[("actT", f)]),
            ]
        for blkb in range(11):
            slot, skey = ring_get()
            sl3 = slot[:, 0:4096].rearrange("p (k n) -> p k n", n=512)
            bo_ = 4 * (blkb % 2)
            for cc in range(4):
                for kc in range(8):
                    P.op("pe", lambda e, kc=kc, cc=cc, sl3=sl3, bo_=bo_: e.matmul(BK[bo_ + cc][:, 0:NT], sl3[:, kc, cc * 128:(cc + 1) * 128], hT[:, kc, 0:NT], start=(kc == 0), stop=(kc == 7)),
                         reads=[skey] + HT_KEYS, writes=[bk(bo_ + cc)])
            for sa, sb in zip(ffn_steps(0, blkb * 2, bo_), ffn_steps(1, blkb * 2 + 1, bo_)):
                sa()
                sb()
        ACT_KEYS = [("actT", f) for f in range(22)]
        if prompt and last and 'pconv' not in SKIP:
            for j in range(2):
                P.op("pe", lambda e, j=j: e.transpose(BK[4][:, j * 128:(j + 1) * 128], ccar[:, j, :], ident), reads=[("ccar", f) for f in range(22)] + ["cst", "ccar_init"], writes=[bk(4)])
            P.op("act", lambda e: e.copy(xn[0:22, 0:256], BK[4][0:22, 0:256]), reads=[bk(4)], writes=["xn"])
            for j in range(2):
                P.dma("sp", pconv[seq, j].rearrange("(c p) -> c p", p=128), xn[0:22, j * 128:(j + 1) * 128], reads=["xn"], writes=[("pconv", j)])

        stop_at("K")
        for ch in range(2):
            for rg in range(3):
                sd, kd = ring_get()
                nk = DN_NK[rg]
                sd3 = sd[:, 0:nk * 512].rearrange("p (k n) -> p k n", n=512)
                for b in range(nb):
                    bnk = b % 4
                    for kc in range(nk):
                        f = rg * 8 + kc
                        P.op("pe", lambda e, kc=kc, f=f, b=b, bnk=bnk, sd3=sd3: e.matmul(BK[bnk][:], actT[:, f, b * 128:(b + 1) * 128], sd3[:, kc, :], start=(f == 0), stop=(f == 21)),
                             reads=[kd] + ACT_KEYS, writes=[bk(bnk)])
            for b in range(nb):
                bnk = b % 4
                P.op("dve", lambda e, b=b, bnk=bnk, ch=ch: e.tensor_tensor(xt[b][:, ch * 512:(ch + 1) * 512], xt[b][:, ch * 512:(ch + 1) * 512], BK[bnk][:], ALU.add),
                     reads=[bk(bnk), ("xt", b)], writes=[("xt", b)])

        for b in range(nb):
            P.op("dve", lambda e, b=b: e.bn_stats(stat[:, 0:6], xt[b][:, 0:512]), reads=[("xt", b)], writes=["stat0_0"])
            P.op("dve", lambda e, b=b: e.bn_stats(stat[:, 6:12], xt[b][:, 512:1024]), reads=[("xt", b)], writes=["stat0_1"])
            P.op("dve", lambda e: e.bn_aggr(stat[:, 12:14], stat[:, 0:12]), reads=["stat0_0", "stat0_1"], writes=["stat0_2"])
            P.op("dve", lambda e: e.scalar_tensor_tensor(stat[:, 14:15], stat[:, 12:13], stat[:, 12:13], stat[:, 13:14], ALU.mult, ALU.add), reads=["stat0_2"], writes=["stat0_3"])
            P.op("act", lambda e: e.activation(stat[:, 15:16], stat[:, 14:15], AF.Sqrt, bias=1e-6), reads=["stat0_3"], writes=["stat0_4"])
            P.op("dve", lambda e: e.reciprocal(stat[:, 15:16], stat[:, 15:16]), reads=["stat0_4"], writes=["stat0_4"])
            P.op("dve", lambda e, b=b: e.scalar_tensor_tensor(xn[:], xt[b][:], stat[:, 15:16], gb["gfb"][:], ALU.mult, ALU.mult), reads=[("xt", b), "stat0_4", "gfb"], writes=["xn"])
            P.dma("sp", yrows[b * 128:(b + 1) * 128, :], xn[:], reads=["xn"], writes=[("y", kind, seq, ti, b)])

    SB = {}

    def sample_alloc():
        SB["shcar"] = wsb("sx_shcar", [128, 14, 16], F32)
        SB["shout"] = wsb("sx_shout", [128, 16, 16], F32)
        SB["ccar_s"] = wsb("sx_ccar_s", [128, 22, 16, 2], F32)
        SB["cout_s"] = wsb("sx_cout_s", [128, 24, 16, 2], F32)
        SB["S_s"] = wsb("sx_S_s", [128, 16, 4, 64], F32)
        SB["kcT"] = wsb("sx_kcT", [128, 16, 128], BF16)
        SB["vc"] = wsb("sx_vc", [128, 16, 128], BF16)
        SB["biasC"] = [wsb("sx_biasC%d" % kv, [128, 16, 4, 8], F32) for kv in range(2)]
        SB["biasN"] = [wsb("sx_biasN%d" % kv, [128, 16, 4, 8], F32) for kv in range(2)]
        SB["Apad"] = wsb("sx_Apad", [128, 16, 128], F32)
        SB["BPpad"] = [wsb("sx_BPpad", [128, 512], F32)] * 2
        SB["KPpad"] = [wsb("sx_KPpad", [128, 512], F32)] * 2
        SB["xn"] = xn

    def sample_const_setup():
        P.op("pool", lambda e: e.memset(SB["Apad"][:].rearrange("p b c -> p (b c)"), 0.0), writes=["Apad"])
        P.op("pool", lambda e: e.memset(SB["shout"][:].rearrange("p a b -> p (a b)"), 0.0), writes=["shout_init"])
        P.op("pool", lambda e: e.memset(SB["cout_s"][:].rearrange("p a b c -> p (a b c)"), 0.0), writes=["cout_init"])
        for kv in range(2):
            P.op("dve", lambda e, kv=kv: e.tensor_copy(SB["biasC"][kv][:], biasT[1][kv][:, :, 0:8].unsqueeze(1).broadcast_to([128, 16, 4, 8])),
                 reads=BIAS_KEYS[(1, kv)], writes=[("biasC", kv)])
            P.op("pool", lambda e, kv=kv: e.memset(SB["biasN"][kv][:].rearrange("p a b c -> p (a b c)"), NEGB), writes=[("biasN", kv)])
            for bb in range(16):
                P.dma("sp", SB["biasN"][kv][bb * 8:(bb + 1) * 8, bb, :, :], biasT[0][kv][0:8, :, 0:8],
                      reads=BIAS_KEYS[(0, kv)] + [("biasN", kv)], writes=[("biasNd", kv, bb)])
    BIASN_KEYS = {kv: [("biasN", kv)] + [("biasNd", kv, bb) for bb in range(16)] for kv in range(2)}

    def sample_setup(st):
        b0 = st * 16
        stgA = SB["xn"]
        for n in range(14):
            bnk = 4 + n % 4
            if n % 7 == 0:
                P.dma("sp", stgA[0:16, 0:896], sshift[b0:b0 + 16, n * 128:n * 128 + 896], writes=["xn"])
            P.op("pe", lambda e, n=n, bnk=bnk: e.transpose(BK[bnk][:, 0:16], stgA[0:16, (n % 7) * 128:(n % 7 + 1) * 128], cst[0:16, C_ID:C_ID + 16]), reads=["xn", "cst"], writes=[bk(bnk)])
            P.op("act", lambda e, n=n, bnk=bnk: e.copy(SB["shcar"][:, n, :], BK[bnk][:, 0:16]), reads=[bk(bnk)], writes=[("shcar", n)])
        for piece, npc in enumerate((8, 8, 6)):
            P.dma("sp", stgA[0:32, 0:npc * 128], sconv[st * 32:(st + 1) * 32, piece * 1024:piece * 1024 + npc * 128], writes=["xn"])
            for c in range(npc):
                f = piece * 8 + c
                bnk = 4 + c % 4
                P.op("pe", lambda e, c=c, bnk=bnk: e.transpose(BK[bnk][:, 0:32], stgA[0:32, c * 128:(c + 1) * 128], cst[0:32, C_ID:C_ID + 32]), reads=["xn", "cst"], writes=[bk(bnk)])
                P.op("act", lambda e, f=f, bnk=bnk: e.copy(SB["ccar_s"][:, f, :, :].rearrange("p b j -> p (b j)"), BK[bnk][:, 0:32]), reads=[bk(bnk)], writes=[("ccar_s", f)])
        for bb in range(16):
            hs = bb % 2
            P.dma("sp", stgA[0:64, hs * 512:(hs + 1) * 512].rearrange("p (h j) -> p h j", j=64), swkv[b0 + bb].rearrange("h i j -> i h j"), writes=[("stgAh", hs), "xn"] if bb < 2 else [("stgAh", hs)], reads=["xn"])
            bnk = 4 + bb % 4
            for p in range(4):
                P.op("pe", lambda e, p=p, bnk=bnk, hs=hs: e.transpose(BK[bnk][:, p * 64:(p + 1) * 64], stgA[0:64, hs * 512 + p * 128:hs * 512 + (p + 1) * 128], cst[0:64, C_ID:C_ID + 64]),
                     reads=[("stgAh", hs), "cst"], writes=[bk(bnk)])
            P.op("dve", lambda e, bb=bb, bnk=bnk: e.tensor_copy(SB["S_s"][:, bb, :, :].rearrange("p a i -> p (a i)"), BK[bnk][:, 0:256]), reads=[bk(bnk)], writes=[("S_s", bb)])
        for bb in range(16):
            bnk = 4 + bb % 4
            if bb % 8 == 0:
                P.dma("sp", stgA[:, 0:1024].rearrange("p (b c) -> p b c", c=128), ck[b0 + bb:b0 + bb + 8].rearrange("b k c -> k b c"), reads=[("stgAh", 0), ("stgAh", 1)], writes=["xn", ("stgAh", 0), ("stgAh", 1)])
            P.op("pe", lambda e, bb=bb, bnk=bnk: e.transpose(BK[bnk][:, 0:128], stgA[:, (bb % 8) * 128:(bb % 8 + 1) * 128], ident), reads=["xn", "cst"], writes=[bk(bnk)])
            P.op("act", lambda e, bb=bb, bnk=bnk: e.copy(SB["kcT"][:, bb, :], BK[bnk][:, 0:128]), reads=[bk(bnk)], writes=[("kcT", bb)])
        P.dma("pool", SB["vc"][:], cv[b0:b0 + 16].rearrange("b k c -> k b c"), writes=["vc"])
        P.dma("sp", sk[b0:b0 + 16, 0:120, :], ck[b0:b0 + 16, 8:128, :], writes=[("sk_old", st)])
        P.dma("sp", sv[b0:b0 + 16, 0:120, :], cv[b0:b0 + 16, 8:128, :], writes=[("sv_old", st)])

    def sample_attention():
        NT = 128
        kcT, vc = SB["kcT"], SB["vc"]
        P.op("pe", lambda e: e.transpose(BK[4][:, 0:128], kTf[:, 0:128], ident), reads=["kTf", "cst"], writes=[bk(4)])
        P.op("act", lambda e: e.copy(kvrow[:, 0, :], BK[4][:, 0:128]), reads=[bk(4)], writes=[("kvrow", 0)])
        for kv in range(2):
            ph = slice(kv * 64, kv * 64 + 64)
            for bb in range(16):
                P.op("pe", lambda e, bb=bb: e.matmul(BK[4][:, bb * 32:(bb + 1) * 32], kcT[ph, bb, :], qT[ph, :, bb * 8:(bb + 1) * 8], start=True, stop=True),
                     reads=[("kcT", bb)] + [("qT", c) for c in range(4)], writes=[bk(4)])
            for bb in range(16):
                P.op("pe", lambda e, bb=bb: e.matmul(BK[5][:, bb * 32:(bb + 1) * 32], kT[ph, 128:256], qT[ph, :, bb * 8:(bb + 1) * 8], start=True, stop=True),
                     reads=["kT"] + [("qT", c) for c in range(4)], writes=[bk(5)])
            P.op("dve", lambda e, kv=kv: e.scalar_tensor_tensor(BK[4][:], BK[4][:], 0.125, SB["biasC"][kv][:].rearrange("p a b c -> p (a b c)"), ALU.mult, ALU.add),
                 reads=[bk(4), ("biasC", kv)], writes=[bk(4)])
            P.op("dve", lambda e, kv=kv: e.scalar_tensor_tensor(BK[5][:], BK[5][:], 0.125, SB["biasN"][kv][:].rearrange("p a b c -> p (a b c)"), ALU.mult, ALU.add),
                 reads=[bk(5)] + BIASN_KEYS[kv], writes=[bk(5)])
            P.op("act", lambda e: e.activation(PT[0][:], BK[4][:], AF.Exp), reads=[bk(4)], writes=[("PT", 0)])
            P.op("act", lambda e: e.activation(PT[1][:], BK[5][:], AF.Exp), reads=[bk(5)], writes=[("PT", 1)])
            for bb in range(16):
                cs = slice(bb * 32, (bb + 1) * 32)
                P.op("pe", lambda e, bb=bb, cs=cs: e.matmul(BK[6][:, cs], vc[:, bb, :], PT[0][:, cs], start=True, stop=False), reads=["vc", ("PT", 0)], writes=[bk(6)])
                P.op("pe", lambda e, cs=cs: e.matmul(BK[6][:, cs], vtok[:, 1, :], PT[1][:, cs], start=False, stop=True), reads=[("vtok", 1), ("PT", 1)], writes=[bk(6)])
            for bb in range(16):
                cs = slice(bb * 32, (bb + 1) * 32)
                P.op("pe", lambda e, cs=cs: e.matmul(BK[7][:, cs], ones_bf[:], PT[0][:, cs], start=True, stop=False), reads=["ones_bf", ("PT", 0)], writes=[bk(7)])
                P.op("pe", lambda e, cs=cs: e.matmul(BK[7][:, cs], ones_bf[:], PT[1][:, cs], start=False, stop=True), reads=["ones_bf", ("PT", 1)], writes=[bk(7)])
            P.op("dve", lambda e: e.tensor_tensor(rd_t[ph, :].rearrange("p (b g t) -> p b g t", g=4, t=8), BK[7][ph, :].rearrange("p (b g t) -> p b g t", g=4, t=8), esink[ph, :, 0:8].unsqueeze(1).broadcast_to([64, 16, 4, 8]), ALU.add), reads=[bk(7)] + ESINK_KEYS, writes=["rd_t"])
            P.op("dve", lambda e: e.reciprocal(rd_t[ph, :], rd_t[ph, :]), reads=["rd_t"], writes=["rd_t"])
            P.op("dve", lambda e, kv=kv: e.tensor_tensor(attT[ph, :, 0:128].rearrange("p g (b t) -> p b g t", t=8), BK[6][ph, :].rearrange("p (b g t) -> p b g t", g=4, t=8),
                                                       rd_t[ph, :].rearrange("p (b g t) -> p b g t", g=4, t=8), ALU.mult),
                 reads=[bk(6), "rd_t"], writes=[("attT", kv, 0)])

    def sample_rwkv(hcol):
        S_s, Apad = SB["S_s"], SB["Apad"]
        Rpad = Apad
        y1s = VA[:].rearrange("p (a t) -> p a t", t=128)
        SKEYS = [("S_s", bb) for bb in range(16)]

        def diag(t):
            base = t[:, 0, 0:8]
            return bass.AP(base.tensor, base.offset, [list(base.ap[0]), [136, 16], [1, 8]])
        for p in range(4):
            P.op("pool", lambda e, p=p: e.tensor_copy(diag(Apad), art[:, p, 0, 0:128].rearrange("p (b t) -> p b t", t=8)), reads=[("art", p), "Apad"], writes=["Apad"])
            for h2 in range(2):
                ph = slice(h2 * 64, h2 * 64 + 64)
                for bb in range(16):
                    P.op("pe", lambda e, bb=bb, p=p, ph=ph, h2=h2: e.matmul(BK[5 + h2][:, p * 64:(p + 1) * 64], Apad[ph, bb, :], S_s[ph, bb, p, :], start=(bb == 0), stop=(bb == 15)),
                         reads=["Apad", ("S_s", bb)], writes=[bk(5 + h2)])
            P.op("pool", lambda e, p=p: e.tensor_copy(diag(Apad), art[:, p, 1, 0:128].rearrange("p (b t) -> p b t", t=8)), reads=[("art", p), "Apad"], writes=["Apad"])
            for h2 in range(2):
                ph = slice(h2 * 64, h2 * 64 + 64)
                yb = 4 if h2 == 0 else 7
                for bb in range(16):
                    P.op("pe", lambda e, bb=bb, p=p, ph=ph, yb=yb: e.matmul(BK[yb][ph, p * 128:(p + 1) * 128], S_s[ph, bb, p, :], Apad[ph, bb, :], start=(bb == 0), stop=(bb == 15)),
                         reads=["Apad", ("S_s", bb)], writes=[bk(yb)])
        for h2 in range(2):
            P.op("dve", lambda e, h2=h2: e.tensor_tensor(XT[:, h2 * 256:(h2 + 1) * 256], BK[5 + h2][:, 0:256], VA[:, h2 * 256:(h2 + 1) * 256], ALU.add),
                 reads=[bk(5 + h2), "VA"], writes=[("XT", h2)])
        for h in range(8):
            P.op("pe", lambda e, h=h: e.matmul(BK[5][:, hcol(h):hcol(h) + 64], Pv[h], XT[:, hcol(h):hcol(h) + 64], start=True, stop=True),
                 reads=[("Pv", h), ("XT", h % 2)], writes=[bk(5)])
        P.op("act", lambda e: e.copy(UT[:], BK[5][:]), reads=[bk(5)], writes=["UT"])
        for h in range(8):
            p, h2 = divmod(h, 2)
            ph = slice(h2 * 64, h2 * 64 + 64)
            P.op("pe", lambda e, h=h, p=p, ph=ph: e.matmul(BK[6][ph, p * 128:(p + 1) * 128], UT[:, hcol(h):hcol(h) + 64], AB[h][:, 128:256], start=True, stop=True),
                 reads=[("AB", h), "UT"], writes=[bk(6)])
        P.op("act", lambda e: e.copy(VA[0:64, :], BK[4][0:64, :]), reads=[bk(4)], writes=["VA"])
        P.op("act", lambda e: e.copy(VA[64:128, :], BK[7][64:128, :]), reads=[bk(7)], writes=["VA"])
        P.op("dve", lambda e: e.tensor_tensor(VA[:], BK[6][:], VA[:], ALU.add), reads=[bk(6), "VA", "VA"], writes=["VA", "VA"])
        P.op("pool", lambda e: e.tensor_tensor(Yt[:, :, 0:128], y1s, VK[:], ALU.add), reads=["VA", "VA", "VK"], writes=[("Yt", 0)])
        for bb in range(16):
            i2 = bb % 2
            bnk = 4 + bb % 4
            BPp, KPp = SB["BPpad"][i2], SB["KPpad"][i2]
            P.op("pool", lambda e, bb=bb, BPp=BPp: e.tensor_scalar(BPp[:], BPtok[:, 0, :], cst[:, C_BSEL + bb:C_BSEL + bb + 1], None, ALU.mult),
                 reads=[("BPtok", 0, p) for p in range(4)] + ["cst"], writes=["BPpad"])
            P.op("dve", lambda e, bb=bb, KPp=KPp: e.tensor_scalar(KPp[:], KPtok[:, 0, :], cst[:, C_BSEL + bb:C_BSEL + bb + 1], None, ALU.mult),
                 reads=[("KPtok", 0, p) for p in range(4)] + ["cst"], writes=["KPpad"])
            for h in range(8):
                p, h2 = divmod(h, 2)
                ph = slice(h2 * 64, h2 * 64 + 64)
                P.op("pe", lambda e, h=h, p=p, ph=ph, bnk=bnk, BPp=BPp: e.matmul(BK[bnk][ph, p * 64:(p + 1) * 64], BPp[:, h * 64:(h + 1) * 64], UT[:, hcol(h):hcol(h) + 64], start=True, stop=False),
                     reads=["BPpad", "UT"], writes=[bk(bnk)])
                P.op("pe", lambda e, h=h, p=p, ph=ph, bnk=bnk, KPp=KPp: e.matmul(BK[bnk][ph, p * 64:(p + 1) * 64], KPp[:, h * 64:(h + 1) * 64], Vtok[:, 0, h * 64:(h + 1) * 64], start=False, stop=True),
                     reads=["KPpad", ("Vtok", 0, p)], writes=[bk(bnk)])
            for p in range(4):
                P.op("dve", lambda e, p=p, bb=bb, bnk=bnk: e.scalar_tensor_tensor(S_s[:, bb, p, :], S_s[:, bb, p, :], cCt[:, p, bb:bb + 1], BK[bnk][:, p * 64:(p + 1) * 64], ALU.mult, ALU.add),
                     reads=[("S_s", bb), ("cCt", p), bk(bnk)], writes=[("S_s", bb)])
            for pp in range(2):
                P.op("pe", lambda e, pp=pp, bb=bb, bnk=bnk: e.transpose(BK[bnk][:, 256 + pp * 128:256 + (pp + 1) * 128], S_s[:, bb, 2 * pp:2 * pp + 2, :].rearrange("p a i -> p (a i)"), ident),
                     reads=[("S_s", bb), "cst"], writes=[bk(bnk)])
            stgO = XT[:, i2 * 256:(i2 + 1) * 256]
            P.op("act", lambda e, bnk=bnk, stgO=stgO: e.copy(stgO, BK[bnk][:, 256:512]), reads=[bk(bnk)], writes=[("XT", i2)])
            for pp in range(2):
                for pl in range(2):
                    pidx = 2 * pp + pl
                    P.dma("sp", swkvo[SB["b0"] + bb, 2 * pidx:2 * pidx + 2].rearrange("h i j -> i h j"), stgO[pl * 64:(pl + 1) * 64, pp * 128:(pp + 1) * 128].rearrange("p (h j) -> p h j", j=64),
                          reads=[("XT", i2)], writes=[("swkvo", bb, pidx)])

    def sample_outputs(st):
        b0 = st * 16
        stgA = SB["xn"]
        for bb in range(16):
            P.dma("sp", sk[b0 + bb, 120:128, :], kvrow[bb * 8:(bb + 1) * 8, 0, :], reads=[("kvrow", 0)], writes=[("sk_new", st, bb)])
            P.dma("sp", sv[b0 + bb, 120:128, :], kvrow[bb * 8:(bb + 1) * 8, 1, :], reads=[("kvrow", 1)], writes=[("sv_new", st, bb)])
        for g in range(2):
            P.op("pe", lambda e, g=g: e.transpose(BK[4 + g][:, 0:128], SB["shout"][:, g * 8:(g + 1) * 8, :].rearrange("p a b -> p (a b)"), ident),
                 reads=[("shout", n) for n in range(14)] + ["shout_init", "cst"], writes=[bk(4 + g)])
            P.op("act", lambda e, g=g: e.copy(stgA[:, g * 128:(g + 1) * 128], BK[4 + g][:, 0:128]), reads=[bk(4 + g)], writes=["xn"])
        for n in range(14):
            g, nl = divmod(n, 8)
            P.dma("sp", ssh[b0:b0 + 16, n * 128:(n + 1) * 128], stgA[nl * 16:(nl + 1) * 16, g * 128:(g + 1) * 128], reads=["xn"], writes=[("ssh", st, n)])
        for g in range(6):
            P.op("pe", lambda e, g=g: e.transpose(BK[4 + g % 4][:, 0:128], SB["cout_s"][:, g * 4:(g + 1) * 4, :, :].rearrange("p a b j -> p (a b j)"), ident),
                 reads=[("cout_s", f) for f in range(22)] + ["cout_init", "cst"], writes=[bk(4 + g % 4)])
            P.op("act", lambda e, g=g: e.copy(stgA[:, 256 + g * 128:256 + (g + 1) * 128], BK[4 + g % 4][:, 0:128]), reads=[bk(4 + g % 4)], writes=["xn"])
        for f in range(22):
            g, fl = divmod(f, 4)
            P.dma("sp", sconvo[st * 32:(st + 1) * 32, f * 128:(f + 1) * 128], stgA[fl * 32:(fl + 1) * 32, 256 + g * 128:256 + (g + 1) * 128], reads=["xn"], writes=[("sconvo", st, f)])

    tiles = [("p", t // (2048 // NTP), t % (2048 // NTP)) for t in range(n_ptiles)]
    if os.environ.get('KREP'):
        tiles = tiles * int(os.environ['KREP'])
    try:
        for (kind, seq, ti) in tiles:
            if os.environ.get('KBAR'):
                P.barrier()
            do_tile(kind, seq, ti)
        if do_sample:
            P.barrier()
            wstack[0].close()
            wstack[0] = ExitStack()
            alloc_work(128, "_s")
            sample_alloc()
            P.barrier()
            sample_const_setup()
            stop_at("SA")
            for st in range(n_stiles):
                SB["b0"] = st * 16
                sample_setup(st)
                stop_at("SB")
                do_tile("s", st, 0)
                sample_outputs(st)
    except _Stop:
        pass
    _pe = os.environ.get('PADE'); _pn = int(os.environ.get('PAD', '0'))
    dmy = P.sbuf("dmy", [128, 8], F32)
    for _i in range(_pn):
        if _pe == 'pe':
            P.op("pe", lambda e: e.matmul(BK[0][0:8, 0:8], cst[0:8, 0:8], cst[0:8, 0:8], start=True, stop=True), reads=["cst"], writes=[("ps", 0)])
        else:
            P.op(_pe, lambda e: e.memset(dmy[:], 0.0), writes=["dmy"])
    _pa = int(os.environ.get('KPADALL', '0'))
    for _i in range(_pa):
        P.op("pe", lambda e: e.matmul(BK[0][0:8, 0:8], cst[0:8, 0:8], cst[0:8, 0:8], start=True, stop=True), reads=["cst"], writes=[("ps", 0)])
        P.op("pe", lambda e: e.matmul(BK[1][0:8, 0:8], cst[0:8, 0:8], cst[0:8, 0:8], start=True, stop=True), reads=["cst"], writes=[("ps", 1)])
        P.op("dve", lambda e: e.memset(dmy[:], 0.0), writes=["dmy"])
        if _i % 2 == 0:
            P.op("pool", lambda e: e.memset(dmy[:, 0:4], 0.0), writes=["dmy2"])
    P.finish()
    return D


_STAGE = ""


NCORES = int(os.environ.get("KNCORES", "8"))


def kernel(**inputs):
    global STOP, NSEQ
    inp = {k: np.asarray(v) for k, v in inputs.items()}
    cst, oh = host_consts()
    STOP = ""
    NSEQ = 16 // NCORES
    NSTL = 8 // NCORES
    NB = NSTL * 16
    nc = bass.Bass("TRN2", target_bir_lowering=False)
    D = build(nc, n_ptiles=2048 // NTP * NSEQ, n_stiles=NSTL)
    in_maps = []
    for c in range(NCORES):
        bs = slice(NB * c, NB * (c + 1))
        m = {
            "xp": inp["x_prompt"][NSEQ * c:NSEQ * (c + 1)], "xsm": inp["x_sample"][bs].reshape(NB * 8, 1024),
            "ck": inp["cache_win_k"][0, bs].reshape(NB, 128, 128), "cv": inp["cache_win_v"][0, bs].reshape(NB, 128, 128),
            "sshift": inp["state_shift"][0, bs], "swkv": inp["state_wkv"][0, bs], "sconv": inp["state_conv"][0, bs].reshape(NB * 2, 2816),
            "rel_bias": inp["rel_bias"], "norm1_g": inp["norm1_g"], "w_in": inp["w_in"][0], "sinks": inp["sinks"][0],
            "mu_shift": inp["mu_shift"][0], "w0": inp["w0"][0], "w2": inp["w2"][0], "a0": inp["a0"][0], "a2": inp["a2"][0],
            "g2": inp["g2"][0], "k_k": inp["k_k"][0], "k_a": inp["k_a"][0], "r_k": inp["r_k"][0].reshape(512),
            "lnx_g": inp["lnx_g"][0], "lnx_b": inp["lnx_b"][0], "w_pa": inp["w_pa"][0], "w_pb": inp["w_pb"][0],
            "w_o": inp["w_o"][0], "norm2_g": inp["norm2_g"], "w_up": inp["w_up"][0], "conv_w": inp["conv_w"][0],
            "conv_b": inp["conv_b"][0], "w_down": inp["w_down"][0], "final_g": inp["final_g"].reshape(1, 1024),
            "consts": cst, "oh": oh,
        }
        in_maps.append({k: np.ascontiguousarray(v, dtype=np.float32) for k, v in m.items() if k in D})
    res = run_bass_kernel_spmd(nc, in_maps, core_ids=list(range(NCORES)))
    R = res.results

    def cat(name, shape):
        return np.concatenate([np.asarray(r[name], dtype=np.float32) for r in R], axis=0).reshape(shape)

    return (cat("yp", (16, 2048, 1024)), cat("ys", (128, 8, 1024)),
            cat("pk", (1, 16, 128, 2, 64)), cat("pv", (1, 16, 128, 2, 64)), cat("psh", (1, 16, 1792)),
            cat("pwkv", (1, 16, 8, 64, 64)), cat("pconv", (1, 16, 2, 2816)),
            cat("sk", (1, 128, 128, 2, 64)), cat("sv", (1, 128, 128, 2, 64)), cat("ssh", (1, 128, 1792)),
            cat("swkvo", (1, 128, 8, 64, 64)), cat("sconvo", (1, 128, 2, 2816)))
```

```python
import numpy as np
from contextlib import ExitStack
import concourse.bass as bass
import concourse.mybir as mybir
from concourse.bass_utils import run_bass_kernel_spmd

F32 = mybir.dt.float32
BF16 = mybir.dt.bfloat16
AF = mybir.ActivationFunctionType
ALU = mybir.AluOpType
AX = mybir.AxisListType

import os as _os0
SAME_ENGINE_SYNC = _os0.environ.get("SES", "1") == "1"
EPOCH = 30000
RELAX_FSZ = int(_os0.environ.get('KRELAX', '0'))
N_DMA_SEMS = {"sp": 8, "pool": 4}


class _Op:
    __slots__ = ("eng", "fn", "deps", "isdma", "ms", "dsem", "dval", "needed", "desc", "fsz")


class _Rec:
    def __init__(self):
        self.call = None

    def __getattr__(self, name):
        def f(*a, **k):
            assert self.call is None
            self.call = (name, a, k)
            return self
        return f


class Prog:
    def __init__(self, nc):
        self.nc = nc
        self.ops = []
        self.lastw = {}
        self.rd_c = {}
        self.rd_d = {}
        self.stack = ExitStack()
        self.last_op = {}
        self.pending = {}
        self.bar_from = 0

    def sbuf(self, name, shape, dtype):
        return self.stack.enter_context(self.nc.sbuf_tensor(name, list(shape), dtype))

    def psum(self, name, shape, dtype):
        return self.stack.enter_context(self.nc.psum_tensor(name, list(shape), dtype))

    def op(self, eng, fn, reads=(), writes=(), isdma=False):
        idx = len(self.ops)
        deps = set()
        for r in reads:
            w = self.lastw.get(r)
            if w is not None:
                deps.add(w)
        for r in writes:
            w = self.lastw.get(r)
            if w is not None:
                deps.add(w)
            for i in self.rd_c.get(r, {}).values():
                deps.add(i)
            for i in self.rd_d.get(r, ()):
                deps.add(i)
        for r in writes:
            self.lastw[r] = idx
            self.rd_c[r] = {}
            self.rd_d[r] = []
        ws = set(writes)
        for r in reads:
            if r in ws:
                continue
            if isdma:
                self.rd_d.setdefault(r, []).append(idx)
            else:
                self.rd_c.setdefault(r, {})[eng] = idx
        if eng in self.pending:
            deps.update(self.pending.pop(eng))
        rec = _Rec()
        fn(rec)
        name_, a_, k_ = rec.call
        o = _Op()
        o.desc = name_ + " w=" + str(list(writes))[:60]
        o.fsz = 0
        try:
            out_ap = k_.get("out", a_[0] if a_ else None)
            shp = tuple(out_ap.shape)
            n = 1
            for d_ in shp[1:]:
                n *= int(d_)
            o.fsz = n
        except Exception:
            o.fsz = 0
        o.eng, o.fn, o.deps, o.isdma = eng, (lambda e: getattr(e, name_)(*a_, **k_)), deps, isdma
        o.ms = None
        o.dsem = None
        o.dval = None
        o.needed = False
        self.ops.append(o)
        self.last_op[eng] = idx
        return idx

    def barrier(self):
        prev = set(self.last_op.values())
        prev.update(i for i in range(self.bar_from, len(self.ops)) if self.ops[i].isdma)
        self.bar_from = len(self.ops)
        for eng in ("pe", "act", "dve", "pool", "sp"):
            self.pending.setdefault(eng, set()).update(prev)

    def dma(self, q, out, in_, reads=(), writes=(), **kw):
        return self.op(q, lambda e: e.dma_start(out=out, in_=in_, **kw), reads, writes, isdma=True)

    def finish(self):
        nc = self.nc
        ops = self.ops
        last_dma = [i for i, o in enumerate(ops) if o.isdma]
        for i, o in enumerate(ops):
            best = {}
            nd = set()
            for d in o.deps:
                od = ops[d]
                if od.isdma:
                    nd.add(d)
                else:
                    if od.eng == o.eng and not o.isdma:
                        if od.eng == "pe" or not SAME_ENGINE_SYNC:
                            continue
                        if RELAX_FSZ and od.fsz >= RELAX_FSZ and od.eng in ("dve", "act"):
                            continue
                    if best.get(od.eng, -1) < d:
                        best[od.eng] = d
            nd.update(best.values())
            o.deps = nd
            for d in nd:
                ops[d].needed = True
        for i in last_dma:
            ops[i].needed = True
        tail_ops = []
        for en in ("pe", "act", "dve", "pool"):
            idxs = [i for i, o in enumerate(ops) if o.eng == en and not o.isdma]
            if idxs:
                ops[idxs[-1]].needed = True
                tail_ops.append(idxs[-1])
        cnt = {e: 0 for e in ("pe", "act", "dve", "pool", "sp")}
        dcount = {}
        nd_used = {"sp": 0, "pool": 0}
        for o in ops:
            if o.isdma:
                k = nd_used[o.eng] % N_DMA_SEMS[o.eng]
                nd_used[o.eng] += 1
                key = (o.eng, k)
                dcount[key] = dcount.get(key, 0) + 1
                o.dsem = key
                o.dval = 16 * dcount[key]
            elif o.needed:
                o.ms = cnt[o.eng]
                cnt[o.eng] += 1
        sems = {}
        for e in cnt:
            for ep in range(cnt[e] // EPOCH + 1):
                sems[(e, ep)] = self.stack.enter_context(nc.semaphore("m_%s_%d" % (e, ep)))
        dsems = {}
        for key in dcount:
            dsems[key] = self.stack.enter_context(nc.semaphore("d_%s_%d" % key))
        final_dma = {}
        for i in last_dma:
            final_dma[ops[i].dsem] = max(final_dma.get(ops[i].dsem, 0), ops[i].dval)
        by_eng = {e: [] for e in cnt}
        for o in ops:
            by_eng[o.eng].append(o)
        import os as _os
        dump = _os.environ.get("DUMP")

        def emit(ename, e):
            known = {}
            for o in by_eng[ename]:
                if o.isdma and o.dval > 16:
                    k = ("d",) + o.dsem
                    if known.get(k, 0) < o.dval - 16:
                        e.wait_ge(dsems[o.dsem], o.dval - 16)
                        known[k] = o.dval - 16
                for d in sorted(o.deps):
                    od = ops[d]
                    if od.isdma:
                        k = ("d",) + od.dsem
                        if known.get(k, 0) < od.dval:
                            e.wait_ge(dsems[od.dsem], od.dval)
                            known[k] = od.dval
                            if dump: print("   ", ename, "WAITD", od.dsem, od.dval)
                    else:
                        k = ("m", od.eng)
                        if known.get(k, -1) < od.ms:
                            ep = od.ms // EPOCH
                            e.wait_ge(sems[(od.eng, ep)], od.ms % EPOCH + 1)
                            known[k] = od.ms
                            if dump: print("   ", ename, "WAITM", od.eng, od.ms + 1)
                ins = o.fn(e)
                if dump: print(ename, "OP", o.desc, "ms", o.ms)
                if o.isdma:
                    ins.then_inc(dsems[o.dsem], 16)
                elif o.ms is not None:
                    ins.then_inc(sems[(o.eng, o.ms // EPOCH)], 1)
            if ename == "sp":
                for key, v in final_dma.items():
                    e.wait_ge(dsems[key], v)
                for i in tail_ops:
                    od = ops[i]
                    e.wait_ge(sems[(od.eng, od.ms // EPOCH)], od.ms % EPOCH + 1)

        with nc.Block() as block:
            @block.tensor
            def _(e):
                emit("pe", e)

            @block.scalar
            def _(e):
                emit("act", e)

            @block.vector
            def _(e):
                emit("dve", e)

            @block.gpsimd
            def _(e):
                emit("pool", e)

            @block.sync
            def _(e):
                emit("sp", e)
        self.stack.close()

import os
STOP = os.environ.get('KSTOP', '')
SKIP = os.environ.get('KSKIP', '').split(',')
NTP = 256
NSEQ = int(os.environ.get('KNSEQ', '2'))
KAPPA = 0.6065306597126334
NEGB = -30000.0
RW_DT = F32
C_ID, C_BO, C_BO64, C_ONE, C_M64, C_L64, C_M8, C_L8, C_BSEL, C_END = 0, 128, 256, 384, 512, 768, 896, 1152, 1280, 1296


def host_consts():
    c = np.zeros((128, C_END), np.float32)
    c[:, C_ID:C_ID + 128] = np.eye(128)
    blk = (np.arange(128)[:, None] // 64 == np.arange(128)[None] // 64)
    c[:, C_BO:C_BO + 128] = blk
    c[:, C_BO64:C_BO64 + 128] = blk / 64.0
    c[:, C_ONE:C_ONE + 128] = 1.0
    s = np.arange(128)[:, None]
    t = np.arange(128)[None]
    for (C, cm, cl) in ((64, C_M64, C_L64), (8, C_M8, C_L8)):
        same = (s // C == t // C)
        c[:, cm:cm + 128] = same & (s < t)
        c[:, cm + 128:cm + 256] = same & (s <= t)
        c[:, cl:cl + 128] = same & (s > t)
    c[:, C_BSEL:C_BSEL + 16] = (np.arange(128)[:, None] // 8 == np.arange(16)[None])
    def bucket(d):
        d = np.asarray(d)
        n = np.maximum(d, 0)
        nf = np.maximum(n, 1).astype(np.float32)
        large = 16 + (np.log(nf / np.float32(16)) / np.float32(np.log(128 / 16)) * np.float32(16)).astype(np.int32)
        return np.where(n < 16, n, np.minimum(large, 31))
    oh = np.zeros((33, 2, 384), np.float32)
    m = np.arange(384)
    for x in range(2):
        dist = (m - 128) if x == 0 else m
        valid = (dist >= 0) & (dist < 128) if x == 0 else (m >= 1) & (m < 128)
        b = bucket(np.clip(dist, 0, 255))
        for mm_ in range(384):
            if valid[mm_]:
                oh[b[mm_], x, mm_] = 1.0
            else:
                oh[32, x, mm_] = NEGB
    return c, oh.reshape(33, 768)


class _Stop(Exception):
    pass


def stop_at(tag):
    if STOP == tag:
        raise _Stop()


def build(nc, n_ptiles=16, do_sample=True, dbg=(), n_stiles=None):
    if n_stiles is None:
        n_stiles = 1 if do_sample else 0
    do_sample = n_stiles > 0
    NST = max(n_stiles, 1)
    P = Prog(nc)
    D = {}

    def din(name, shape):
        D[name] = nc.dram_tensor(name, list(shape), F32, kind="ExternalInput").ap()
        return D[name]

    def dout(name, shape):
        D[name] = nc.dram_tensor(name, list(shape), F32, kind="ExternalOutput").ap()
        return D[name]

    xp = din("xp", [NSEQ, 2048, 1024]); xsm = din("xsm", [NST * 128, 1024])
    ck = din("ck", [NST * 16, 128, 128]); cv = din("cv", [NST * 16, 128, 128])
    sshift = din("sshift", [NST * 16, 1792]); swkv = din("swkv", [NST * 16, 8, 64, 64]); sconv = din("sconv", [NST * 32, 2816])
    rel_bias = din("rel_bias", [32, 8]); norm1_g = din("norm1_g", [1, 1024]); w_in = din("w_in", [1024, 4608])
    sinks = din("sinks", [8]); mu_shift = din("mu_shift", [1792]); w0 = din("w0", [512]); w2 = din("w2", [64, 512])
    a0 = din("a0", [512]); a2 = din("a2", [64, 512]); g2 = din("g2", [128, 512]); k_k = din("k_k", [512])
    k_a = din("k_a", [512]); r_k = din("r_k", [512]); lnx_g = din("lnx_g", [512]); lnx_b = din("lnx_b", [512])
    w_pa = din("w_pa", [512, 1024]); w_pb = din("w_pb", [512, 1024]); w_o = din("w_o", [1024, 1024])
    norm2_g = din("norm2_g", [1, 1024]); w_up = din("w_up", [1024, 5632]); conv_w = din("conv_w", [3, 2816])
    conv_b = din("conv_b", [2816]); w_down = din("w_down", [2816, 1024]); final_g = din("final_g", [1, 1024])
    consts_d = din("consts", [128, C_END]); oh_d = din("oh", [33, 768])

    yp = dout("yp", [NSEQ, 2048, 1024]); ys = dout("ys", [NST * 128, 1024])
    pk = dout("pk", [NSEQ, 128, 128]); pv = dout("pv", [NSEQ, 128, 128]); psh = dout("psh", [NSEQ, 1792])
    pwkv = dout("pwkv", [NSEQ, 8, 64, 64]); pconv = dout("pconv", [NSEQ, 2, 2816])
    sk = dout("sk", [NST * 16, 128, 128]); sv = dout("sv", [NST * 16, 128, 128]); ssh = dout("ssh", [NST * 16, 1792])
    swkvo = dout("swkvo", [NST * 16, 8, 64, 64]); sconvo = dout("sconvo", [NST * 32, 2816])
    dbg_out = {}

    def scratch(name, shape, dtype=BF16):
        return nc.dram_tensor(name, list(shape), dtype, kind="Internal").ap()

    wsc_in = scratch("wsc_in", [9, 128, 4096]); wsc_pa = scratch("wsc_pa", [2, 128, 2048])
    wsc_pb = scratch("wsc_pb", [2, 128, 2048]); wsc_o = scratch("wsc_o", [2, 128, 4096])
    wsc_up = scratch("wsc_up", [11, 128, 4096]); wsc_dn = scratch("wsc_dn", [6, 128, 4096])
    E_d = scratch("E_d", [16, 128, 384], F32)

    chunks_src = []
    for j in range(4):
        chunks_src.append([(j * 64, 64), ((4 + j) * 64, 64)])
    chunks_src.append([(512, 128)]); chunks_src.append([(640, 128)])
    rwb = 768
    chunks_src.append([(rwb + 1536, 128)]); chunks_src.append([(rwb + 1664, 128)])
    for p in range(4):
        chunks_src += [[(rwb + p * 128, 128)], [(rwb + 512 + p * 128, 128)], [(rwb + 1024 + p * 128, 128)]]
    for g in range(16):
        chunks_src.append([(2560 + g * 128, 128)])
    for c, srcs in enumerate(chunks_src):
        b, cc = divmod(c, 4)
        dst = wsc_in[b].rearrange("p (k n) -> p k n", n=512)
        off = cc * 128
        for (lo, n) in srcs:
            P.dma("pool", dst[:, :, off:off + n], w_in[:, lo:lo + n].rearrange("(k p) n -> p k n", p=128),
                  writes=[("wsc_in", c, lo)])
            off += n
    for ch in range(2):
        dpa = wsc_pa[ch].rearrange("p (k n) -> p k n", n=512)
        for j in range(4):
            for half in range(2):
                r0 = (half * 4 + j) * 64
                P.dma("pool", dpa[half * 64:(half + 1) * 64, j, :], w_pa[r0:r0 + 64, ch * 512:(ch + 1) * 512],
                      writes=[("wsc_pa", ch, j, half)])
        P.dma("pool", wsc_pb[ch].rearrange("p (k n) -> p k n", n=512),
              w_pb[:, ch * 512:(ch + 1) * 512].rearrange("(k p) n -> p k n", p=128), writes=[("wsc_pb", ch)])
        P.dma("pool", wsc_o[ch].rearrange("p (k n) -> p k n", n=512),
              w_o[:, ch * 512:(ch + 1) * 512].rearrange("(k p) n -> p k n", p=128), writes=[("wsc_o", ch)])
    for b in range(11):
        dst = wsc_up[b].rearrange("p (k n) -> p k n", n=512)
        P.dma("pool", dst[:, :, 0:256], w_up[:, b * 256:(b + 1) * 256].rearrange("(k p) n -> p k n", p=128),
              writes=[("wsc_up", b, 0)])
        P.dma("pool", dst[:, :, 256:512], w_up[:, 2816 + b * 256:2816 + (b + 1) * 256].rearrange("(k p) n -> p k n", p=128),
              writes=[("wsc_up", b, 1)])
    DN_NK = (8, 8, 6)
    for ch in range(2):
        for rg in range(3):
            nk = DN_NK[rg]
            dst = wsc_dn[ch * 3 + rg].rearrange("p (k n) -> p k n", n=512)
            P.dma("pool", dst[:, 0:nk, :],
                  w_down[rg * 1024:rg * 1024 + nk * 128, ch * 512:(ch + 1) * 512].rearrange("(k p) n -> p k n", p=128),
                  writes=[("wsc_dn", ch * 3 + rg)])

    if STOP == 'A':
        P.finish(); return D
    blk_sched = []
    for b in range(9):
        keys = []
        for c in range(4 * b, 4 * b + 4):
            keys += [("wsc_in", c, lo) for (lo, n) in chunks_src[c]]
        blk_sched.append((wsc_in[b], 4096, keys))
    for ch in range(2):
        blk_sched.append((wsc_pa[ch], 2048, [("wsc_pa", ch, j, h) for j in range(4) for h in range(2)]))
        blk_sched.append((wsc_pb[ch], 2048, [("wsc_pb", ch)]))
    for ch in range(2):
        blk_sched.append((wsc_o[ch], 4096, [("wsc_o", ch)]))
    for b in range(11):
        blk_sched.append((wsc_up[b], 4096, [("wsc_up", b, 0), ("wsc_up", b, 1)]))
    for i in range(6):
        blk_sched.append((wsc_dn[i], DN_NK[i % 3] * 512, [("wsc_dn", i)]))
    NBLK_T = len(blk_sched)
    n_tiles_total = n_ptiles + n_stiles
    NSLOT = 3
    ring_tiles = [P.sbuf("ring%d" % i, [128, 4096], BF16) for i in range(NSLOT)]
    ring_state = {"loaded": 0, "consumed": 0}
    total_blocks = NBLK_T * n_tiles_total

    def ring_get(hold=0):
        k = ring_state["consumed"]
        while ring_state["loaded"] < min(k + NSLOT - hold, total_blocks):
            j = ring_state["loaded"]
            src, nel, keys = blk_sched[j % NBLK_T]
            s = j % NSLOT
            P.dma("sp", ring_tiles[s][:, 0:nel], src[:, 0:nel], reads=keys, writes=[("ring", s)])
            ring_state["loaded"] += 1
        ring_state["consumed"] += 1
        s = k % NSLOT
        return ring_tiles[s], ("ring", s)

    cst = P.sbuf("cst", [128, C_END], F32)
    P.dma("sp", cst[:], consts_d, writes=["cst"])
    ident = cst[:, C_ID:C_ID + 128]
    bo = cst[:, C_BO:C_BO + 128]
    bo64 = cst[:, C_BO64:C_BO64 + 128]
    ones = cst[:, C_ONE:C_ONE + 128]
    ones_bf = P.sbuf("ones_bf", [128, 128], BF16)
    P.op("dve", lambda e: e.tensor_copy(ones_bf[:], ones), reads=["cst"], writes=["ones_bf"])
    zeros_t = P.sbuf("zeros_t", [128, 64], F32)
    P.op("pool", lambda e: e.memset(zeros_t[:], 0.0), writes=["zeros_t"])

    def load_cols(name, vec, ncol):
        t = P.sbuf(name, [128, ncol], F32)
        P.dma("sp", t[:], vec.rearrange("(c p) -> p c", p=128), writes=[name], allow_slow_non_contiguous=True)
        return t

    mu_c = load_cols("mu_c", mu_shift, 14)
    om_c = P.sbuf("om_c", [128, 14], F32)
    P.op("dve", lambda e: e.tensor_scalar(om_c[:], mu_c[:], -1.0, 1.0, ALU.mult, ALU.add), reads=["mu_c"], writes=["om_c"])
    w0_c = load_cols("w0_c", w0, 4); a0_c = load_cols("a0_c", a0, 4); kk_c = load_cols("kk_c", k_k, 4)
    ka_c = load_cols("ka_c", k_a, 4); rk_c = load_cols("rk_c", r_k, 4); lg_c = load_cols("lg_c", lnx_g, 4)
    lb_c = load_cols("lb_c", lnx_b, 4); cb_c = load_cols("cb_c", conv_b, 22)
    cw_c = P.sbuf("cw_c", [128, 3, 22], F32)
    for j in range(3):
        P.dma("sp", cw_c[:, j, :], conv_w[j].rearrange("(c p) -> p c", p=128), writes=[("cw_c", j)], allow_slow_non_contiguous=True)
    CW_KEYS = [("cw_c", j) for j in range(3)]
    gb = {}
    g1c = load_cols("g1b", norm1_g.rearrange("a n -> (a n)"), 8)
    g2c = load_cols("g2b", norm2_g.rearrange("a n -> (a n)"), 8)
    gcol = {"g1b": g1c, "g2b": g2c}
    for nm, src in (("gfb", final_g),):
        gb[nm] = P.sbuf(nm, [128, 1024], F32)
        P.dma("sp", gb[nm][:], src.partition_broadcast(128).rearrange("p a n -> p (a n)"), writes=[nm])
    w2b = P.sbuf("w2b", [128, 512], BF16); g2bf = P.sbuf("g2bf", [128, 512], BF16)
    P.dma("pool", w2b[0:64, :], w2, writes=["w2b"])
    P.dma("pool", w2b[64:128, :], a2, writes=["a2b"])
    P.dma("pool", g2bf[:], g2, writes=["g2bf"])
    sk_t = P.sbuf("sk_t", [128, 4], F32)
    P.dma("sp", sk_t[0:64, :], sinks[0:4].partition_broadcast(64), writes=[("sk_t", 0)])
    P.dma("sp", sk_t[64:128, :], sinks[4:8].partition_broadcast(64), writes=[("sk_t", 1)])
    esk = P.sbuf("esk", [128, 4], F32)
    P.op("act", lambda e: e.activation(esk[:], sk_t[:], AF.Exp), reads=[("sk_t", 0), ("sk_t", 1)], writes=["esk"])
    esink = P.sbuf("esink", [128, 4, 128], F32)
    for g in range(4):
        P.op("act", lambda e, g=g: e.activation(esink[:, g, :], ones, AF.Copy, scale=esk[:, g:g + 1]),
             reads=["cst", "esk"], writes=[("esink", g)])
    ESINK_KEYS = [("esink", g) for g in range(4)]

    xn = P.sbuf("xn", [128, 1024], F32)
    if STOP == 'B':
        P.finish(); return D
    for _i in range(int(os.environ.get('KDUMMY', '0'))):
        P.dma('sp', zeros_t[:, 0:32], consts_d[:, 0:32], writes=['zeros_dummy'])
    for _i in range(int(os.environ.get('KBIG', '0'))):
        P.dma('sp', yp[1], xp[0], writes=['yp1_dummy'])
    BK = [P.psum("bank%d" % i, [128, 512], F32) for i in range(8)]

    def bk(i):
        return ("ps", i)

    rb = P.sbuf("rb", [33, 8], F32)
    Yt = P.sbuf("Yt", [128, 4, NTP], F32)
    bon = P.sbuf("bon", [128, 4, NTP], F32)
    Lh = Yt[0:33, :, :].rearrange("p a t -> p (a t)").rearrange("p (h t) -> p h t", t=128)
    oh_t = bon[0:33, :, :].rearrange("p a t -> p (a t)")[:, 0:768]
    P.dma("sp", rb[0:32, :], rel_bias, writes=["rb"])
    P.dma("sp", oh_t, oh_d, writes=["oh_t"])
    P.op("pool", lambda e: e.memset(Lh[32:33, :, :], 1.0), writes=["Lh1"])
    for h in range(8):
        P.op("act", lambda e, h=h: e.activation(Lh[0:32, h, :], cst[0:32, C_ONE:C_ONE + 128], AF.Copy, scale=rb[0:32, h:h + 1]),
             reads=["cst", "rb"], writes=[("Lh", h)])
    biasT = [[P.sbuf("biasT%d%d" % (x, kv), [128, 4, 128], F32) for kv in range(2)] for x in range(2)]
    for x in range(2):
        for h in range(8):
            bnk = 4 + (x * 8 + h) % 4
            P.op("pe", lambda e, h=h, x=x, bnk=bnk: e.matmul(BK[bnk][:, 0:384], Lh[:, h, :], oh_t[:, x * 384:(x + 1) * 384], start=True, stop=True),
                 reads=["Lh1", ("Lh", h), "oh_t"], writes=[bk(bnk)])
            P.op("dve", lambda e, bnk=bnk: e.tensor_copy(xn[:, 0:384], BK[bnk][:, 0:384]), reads=[bk(bnk)], writes=["xn"])
            P.dma("sp", E_d[x * 8 + h], xn[:, 0:384], reads=["xn"], writes=[("E_d", x, h)])
            kv, g = divmod(h, 4)
            skew = bass.AP(E_d.tensor, (x * 8 + h) * 128 * 384 + 128, [[383, 128], [1, 128]])
            P.dma("sp", biasT[x][kv][:, g, :], skew, reads=[("E_d", x, h)], writes=[("biasT", x, kv, g)])
    if STOP == 'C':
        P.finish(); return D
    if os.environ.get('NOBAR') is None:
        P.barrier()
    BIAS_KEYS = {(x, kv): [("biasT", x, kv, g) for g in range(4)] for x in range(2) for kv in range(2)}

    wstack = [ExitStack()]

    def wsb(name, shape, dtype):
        return wstack[0].enter_context(nc.sbuf_tensor(name, list(shape), dtype))

    PT2 = None
    xn2 = None
    Pvg = None
    xt = None
    stat = None
    hT = None
    qT = None
    kT = None
    kTf = None
    vTf = None
    vtok = None
    kvrow = None
    gates = None
    attT = None
    rwoT = None
    PT = None
    rd_t = None
    pbuf = None
    tmpA = None
    xs3 = None
    TW = None
    SG = None
    car = None
    art = None
    kt_ = None
    bt_ = None
    gT = None
    cCt = None
    KPtok = None
    BPtok = None
    Vtok = None
    rw = None
    S = None
    AK = None
    AB = None
    Mm_ = None
    Mt_ = None
    Pv = None
    VA = None
    VK = None
    XT = None
    UT = None
    y1 = None
    ugx = None
    cc_t = None
    gl_t = None
    actT = None
    ccar = None

    def alloc_work(NTM, sfx):
        nonlocal PT2, xn2, Pvg, xt, stat, hT, qT, kT, kTf, vTf, vtok, kvrow, gates, attT, rwoT, PT, rd_t, pbuf, tmpA, xs3, TW, SG, car, art, kt_, bt_, gT, cCt, KPtok, BPtok, Vtok, rw, S, AK, AB, Mm_, Mt_, Pv, VA, VK, XT, UT, y1, ugx, cc_t, gl_t, actT, ccar
        NB_MAX = NTM // 128
        xt = [wsb(("xt%d" % i) + sfx, [128, 1024], F32) for i in range(NB_MAX)]
        stat = wsb("stat" + sfx, [128, 32], F32)
        xn2 = wsb("xn2" + sfx, [128, 1024], F32) if NTM > 128 else None
        hT = wsb("hT" + sfx, [128, 8, NTM], BF16)
        qT = wsb("qT" + sfx, [128, 4, NTM], BF16)
        kT = wsb("kT" + sfx, [128, 128 + NTM], BF16)
        kTf = wsb("kTf" + sfx, [128, NTM], F32)
        vTf = wsb("vTf" + sfx, [128, NTM], F32)
        vtok = wsb("vtok" + sfx, [128, NB_MAX + 1, 128], BF16)
        kvrow = wsb("kvrow" + sfx, [128, 2, 128], F32)
        gates = wsb("gates" + sfx, [128, 16, NTM], BF16)
        attT = wsb("attT" + sfx, [128, 4, NTM], BF16)
        rwoT = wsb("rwoT" + sfx, [128, 4, NTM], BF16)
        PT = [wsb(("PT%d" % i) + sfx, [128, 512], BF16) for i in range(2)]
        PT2 = [wsb(("PTb%d" % i) + sfx, [128, 512], BF16) for i in range(2)] if NTM > 128 else PT
        rd_t = wsb("rd_t" + sfx, [128, 512], F32)
        pbuf = wsb("pbuf" + sfx, [128, NTM + 16], F32)
        tmpA = wsb("tmpA" + sfx, [128, NTM], F32)
        xs3 = [wsb(("xs3_%d" % i) + sfx, [128, NTM], F32) for i in range(3)]
        TW = wsb("TW" + sfx, [128, NTM], BF16)
        SG = wsb("SG" + sfx, [128, NTM], BF16)
        car = wsb("car" + sfx, [128, 128], F32)
        art = wsb("art" + sfx, [128, 4, 2, NTM], RW_DT)
        kt_ = wsb("kt_" + sfx, [128, 4, NTM], RW_DT)
        bt_ = wsb("bt_" + sfx, [128, 4, NTM], RW_DT)
        gT = wsb("gT" + sfx, [128, 4, NTM], F32)
        cCt = wsb("cCt" + sfx, [128, 4, NTM // 8], F32)
        KPtok = wsb("KPtok" + sfx, [128, NB_MAX, 512], RW_DT)
        BPtok = wsb("BPtok" + sfx, [128, NB_MAX, 512], RW_DT)
        Vtok = wsb("Vtok" + sfx, [128, NB_MAX, 512], RW_DT)
        rw = {n: wsb("rw_" + sfx + n, [128, NTM], F32) for n in ("sg", "cs", "t1", "t2", "t3", "a", "kkn", "k2", "ein", "einv", "eend", "nk")}
        S = wsb("S" + sfx, [128, 4, 64], F32)
        AK = [wsb(("AK%d" % h) + sfx, [128, 256], RW_DT) for h in range(8)]
        AB = [wsb(("AB%d" % h) + sfx, [128, 256], RW_DT) for h in range(8)]
        Mm_ = [wsb(("Mmg%d" % g) + sfx, [128, 4, 128], RW_DT) for g in range(2)]
        Mt_ = [wsb(("Mtg%d" % g) + sfx, [128, 4, 128], RW_DT) for g in range(2)]
        Pvg = [wsb(("Pvg%d" % g) + sfx, [128, 4, 128], RW_DT) for g in range(2)]
        Pv = [Pvg[h // 4][:, h % 4, :] for h in range(8)]
        VA = wsb("VA" + sfx, [128, 512], F32)
        VK = wsb("VK" + sfx, [128, 4, 128], F32)
        XT = wsb("XT" + sfx, [128, 512], RW_DT)
        UT = wsb("UT" + sfx, [128, 512], RW_DT)
        y1 = wsb("y1" + sfx, [128, 4, 64], F32)
        ugx = [wsb(("ugx%d" % i) + sfx, [128, NTM + 40], F32) for i in range(2)]
        cc_t = [wsb(("cc_t%d" % i) + sfx, [128, NTM], F32) for i in range(2)]
        gl_t = [wsb(("gl_t%d" % i) + sfx, [128, NTM], F32) for i in range(2)]
        actT = wsb("actT" + sfx, [128, 22, NTM], BF16)
        ccar = wsb("ccar" + sfx, [128, 2, 128], F32)

    P.stack.callback(lambda: wstack[0].close())
    alloc_work(NTP, "")
    P.op("pool", lambda e: e.memset(car[:], 0.0), writes=["car_init"])
    P.op("pool", lambda e: e.memset(ccar[:].rearrange("p a f -> p (a f)"), 0.0), writes=["ccar_init"])

    def debug_tap(name, ap_sb, shape, keys):
        if name in dbg:
            d_ = nc.dram_tensor("dbg_" + name, list(shape), ap_sb.dtype, kind="ExternalOutput").ap()
            D["dbg_" + name] = d_
            P.dma("sp", d_, ap_sb, reads=keys, writes=[("dbg", name)])

    def do_tile(kind, seq, ti):
        prompt = kind == "p"
        NT = NTP if prompt else 128
        nb = NT // 128
        NBs, TB = (1, NT) if prompt else (16, 8)
        C = 64 if prompt else 8
        LV = 5 if prompt else 2
        cm, cl = (C_M64, C_L64) if prompt else (C_M8, C_L8)
        first = prompt and ti == 0
        last = prompt and ti == 2048 // NTP - 1
        xrows = xp[seq, ti * NT:(ti + 1) * NT, :] if prompt else xsm[seq * 128:(seq + 1) * 128, :]
        yrows = yp[seq, ti * NT:(ti + 1) * NT, :] if prompt else ys[seq * 128:(seq + 1) * 128, :]

        for b in range(nb):
            P.dma("sp", xt[b][:], xrows[b * 128:(b + 1) * 128, :], writes=[("xt", b)])

        stop_at("D1")

        def norm_T(gname):
            def steps(b):
                so = 16 * b
                st = stat[:, so:so + 16]
                xnb = xn if b == 0 else xn2
                xk_ = "xn" if b == 0 else "xn2"
                sk_ = "stat%d_" % b
                bks = (4, 5) if b == 0 else (6, 7)

                def tr(half):
                    for q4 in range(4):
                        kc = half * 4 + q4
                        P.op("pe", lambda e, kc=kc, q4=q4: e.transpose(BK[bks[half]][:, q4 * 128:(q4 + 1) * 128], xnb[:, kc * 128:(kc + 1) * 128], ident),
                             reads=[xk_, "cst"], writes=[bk(bks[half])])

                def ev(half):
                    for q4 in range(4):
                        kc = half * 4 + q4
                        if half == 0:
                            P.op("act", lambda e, kc=kc, q4=q4: e.activation(hT[:, kc, b * 128:(b + 1) * 128], BK[bks[0]][:, q4 * 128:(q4 + 1) * 128], AF.Copy, scale=gcol[gname][:, kc:kc + 1]),
                                 reads=[bk(bks[0]), gname], writes=[("hT", kc)])
                        else:
                            P.op("dve", lambda e, kc=kc, q4=q4: e.tensor_scalar(hT[:, kc, b * 128:(b + 1) * 128], BK[bks[1]][:, q4 * 128:(q4 + 1) * 128], gcol[gname][:, kc:kc + 1], None, ALU.mult),
                                 reads=[bk(bks[1]), gname], writes=[("hT", kc)])
                return [
                    lambda: P.op("dve", lambda e: e.bn_stats(st[:, 0:6], xt[b][:, 0:512]), reads=[("xt", b)], writes=[sk_ + "0"]),
                    lambda: P.op("dve", lambda e: e.bn_stats(st[:, 6:12], xt[b][:, 512:1024]), reads=[("xt", b)], writes=[sk_ + "1"]),
                    lambda: P.op("dve", lambda e: e.bn_aggr(st[:, 12:14], st[:, 0:12]), reads=[sk_ + "0", sk_ + "1"], writes=[sk_ + "2"]),
                    lambda: P.op("dve", lambda e: e.scalar_tensor_tensor(st[:, 14:15], st[:, 12:13], st[:, 12:13], st[:, 13:14], ALU.mult, ALU.add), reads=[sk_ + "2"], writes=[sk_ + "3"]),
                    lambda: P.op("act", lambda e: e.activation(st[:, 15:16], st[:, 14:15], AF.Sqrt, bias=1e-6), reads=[sk_ + "3"], writes=[sk_ + "4"]),
                    lambda: P.op("dve", lambda e: e.reciprocal(st[:, 15:16], st[:, 15:16]), reads=[sk_ + "4"], writes=[sk_ + "4"]),
                    lambda: P.op("dve", lambda e: e.tensor_scalar(xnb[:], xt[b][:], st[:, 15:16], None, ALU.mult), reads=[("xt", b), sk_ + "4"], writes=[xk_]),
                    lambda: tr(0), lambda: tr(1), lambda: ev(0), lambda: ev(1),
                ]
            for group in zip(*[steps(b) for b in range(nb)]):
                for step in group:
                    step()
        HT_KEYS = [("hT", kc) for kc in range(8)]

        norm_T("g1b")
        stop_at("D")

        def v3(ap2, k):
            return ap2.rearrange("p (b t) -> p b t", t=k)

        ts_ctr = [0]

        def token_shift(bnk, n, dst, dst_key):
            ti_ = ts_ctr[0] % 2
            ts_ctr[0] += 1
            tA = tmpA if ti_ == 0 else pbuf
            tkey, tkey0 = ("tmpA", ti_), ("tmpA0", ti_)
            PS3 = v3(BK[bnk][:, 0:NT], TB)
            tA3 = v3(tA[:, 0:NT], TB)
            P.op("act", lambda e: e.activation(tA3[:, :, 1:TB], PS3[:, :, 0:TB - 1], AF.Copy, scale=mu_c[:, n:n + 1]), reads=[bk(bnk), "mu_c"], writes=[tkey])
            if prompt:
                if first:
                    P.op("pool", lambda e: e.memset(tA[:, 0:1], 0.0), writes=[tkey0])
                else:
                    P.op("pool", lambda e: e.tensor_scalar(tA[:, 0:1], car[:, n:n + 1], mu_c[:, n:n + 1], None, ALU.mult), reads=[("car", n), "mu_c"], writes=[tkey0])
            else:
                P.op("pool", lambda e: e.tensor_scalar(tA3[:, :, 0:1], SB["shcar"][:, n, :].unsqueeze(2), mu_c[:, n:n + 1], None, ALU.mult), reads=[("shcar", n), "mu_c"], writes=[tkey0])
            P.op("dve", lambda e: e.scalar_tensor_tensor(v3(dst, TB), PS3, om_c[:, n:n + 1], tA3, ALU.mult, ALU.add),
                 reads=[bk(bnk), tkey, tkey0, "om_c"], writes=[dst_key])
            if prompt:
                P.op("dve", lambda e: e.tensor_copy(car[:, n:n + 1], BK[bnk][:, NT - 1:NT]), reads=[bk(bnk)], writes=[("car", n)])
            else:
                P.op("dve", lambda e: e.tensor_copy(SB["shout"][:, n, :].unsqueeze(2), PS3[:, :, TB - 1:TB]), reads=[bk(bnk)], writes=[("shout", n)])

        def xs_set(p):
            if p % 2 == 0:
                return (xs3[0], xs3[1], xs3[2]), ("xs0", "xs1", "xs2")
            return (cc_t[0], cc_t[1], gl_t[0]), (("cc_t", 0), ("cc_t", 1), ("gl_t", 0))

        def pair_process(p):
            xbufs, xkeys = xs_set(p)
            xr, xk, xv = xbufs[0][:, 0:NT], xbufs[1][:, 0:NT], xbufs[2][:, 0:NT]
            kx0, kx1, kx2 = xkeys
            R = {n: rw[n][:, 0:NT] for n in rw}
            nch = NT // C
            b6, b7 = 6, 7
            P.op("pool", lambda e: e.tensor_scalar(R["kkn"], xk, kk_c[:, p:p + 1], None, ALU.mult), reads=[kx1, "kk_c"], writes=["r_kkn"])
            P.op("pool", lambda e: e.tensor_tensor(R["t2"], R["kkn"], R["kkn"], ALU.mult), reads=["r_kkn"], writes=["r_t2"])
            P.op("pe", lambda e: e.matmul(BK[b6][:, 0:NT], w2b[0:64, p * 128:(p + 1) * 128], TW[0:64, 0:NT], start=True, stop=True),
                 reads=["w2b", "TW"], writes=[bk(b6)])
            P.op("pe", lambda e: e.matmul(BK[b7][:, 0:NT], w2b[64:128, p * 128:(p + 1) * 128], TW[64:128, 0:NT], start=True, stop=True),
                 reads=["a2b", "TW"], writes=[bk(b7)])
            P.op("act", lambda e: e.activation(R["sg"], BK[b6][:, 0:NT], AF.Sigmoid, bias=w0_c[:, p:p + 1]), reads=[bk(b6), "w0_c"], writes=["r_sg"])
            P.op("act", lambda e: e.activation(R["a"], BK[b7][:, 0:NT], AF.Sigmoid, bias=a0_c[:, p:p + 1]), reads=[bk(b7), "a0_c"], writes=["r_a"])
            P.op("pe", lambda e: e.matmul(BK[b6][:, 0:NT], bo, R["t2"], start=True, stop=True), reads=["cst", "r_t2"], writes=[bk(b6)])
            for b in range(nb):
                tb_ = 4 + b % 2
                P.op("pe", lambda e, b=b, tb_=tb_: e.transpose(BK[tb_][:, 0:128], xbufs[2][:, b * 128:(b + 1) * 128], ident), reads=[kx2, "cst"], writes=[bk(tb_)])
                P.op("act", lambda e, b=b, tb_=tb_: e.copy(Vtok[:, b, p * 128:(p + 1) * 128], BK[tb_][:, 0:128]), reads=[bk(tb_)], writes=[("Vtok", b, p)])
            for c in range(nch):
                P.op("dve", lambda e, c=c: e.tensor_tensor_scan(R["cs"][:, c * C:(c + 1) * C], ones[:, 0:C], R["sg"][:, c * C:(c + 1) * C], 0.0, ALU.mult, ALU.add),
                     reads=["r_sg", "cst"], writes=["r_cs"])
            P.op("act", lambda e: e.activation(R["t2"], BK[b6][:, 0:NT], AF.Sqrt), reads=[bk(b6)], writes=["r_t2"])
            P.op("dve", lambda e: e.tensor_scalar(R["t3"], R["a"], -1.0, ka_c[:, p:p + 1], ALU.add, ALU.mult), reads=["r_a", "ka_c"], writes=["r_t3"])
            P.op("dve", lambda e: e.scalar_tensor_tensor(R["k2"], R["t3"], 1.0, xk, ALU.add, ALU.mult), reads=["r_t3", kx1], writes=["r_k2"])
            P.op("act", lambda e: e.activation(R["ein"], R["cs"], AF.Exp, scale=-KAPPA), reads=["r_cs"], writes=["r_ein"])
            P.op("act", lambda e: e.activation(R["einv"], R["cs"], AF.Exp, scale=KAPPA), reads=["r_cs"], writes=["r_einv"])
            P.op("pool", lambda e: e.tensor_tensor(R["t1"], R["cs"], R["sg"], ALU.subtract), reads=["r_cs", "r_sg"], writes=["r_t1"])
            P.op("pool", lambda e: e.tensor_scalar(R["nk"][:, 0:nch], R["cs"][:, C - 1:NT:C], -KAPPA, None, ALU.mult), reads=["r_cs"], writes=["r_nk"])
            P.op("act", lambda e: e.activation(R["t1"], R["t1"], AF.Exp, scale=-KAPPA), reads=["r_t1"], writes=["r_t1"])
            for c in range(nch):
                P.op("act", lambda e, c=c: e.activation(R["eend"][:, c * C:(c + 1) * C], R["cs"][:, c * C:(c + 1) * C], AF.Exp, scale=KAPPA, bias=R["nk"][:, c:c + 1]),
                     reads=["r_cs", "r_nk"], writes=["r_eend"])
            P.op("dve", lambda e: e.tensor_scalar(R["t2"], R["t2"], 1e-12, None, ALU.max), reads=["r_t2"], writes=["r_t2"])
            P.op("dve", lambda e: e.reciprocal(R["t2"], R["t2"]), reads=["r_t2"], writes=["r_t2"])
            P.op("dve", lambda e: e.tensor_tensor(R["kkn"], R["kkn"], R["t2"], ALU.mult), reads=["r_kkn", "r_t2"], writes=["r_kkn"])
            P.op("pool", lambda e: e.tensor_copy(cCt[:, p, 0:nch], R["ein"][:, C - 1:NT:C]), reads=["r_ein"], writes=[("cCt", p)])
            P.op("pool", lambda e: e.tensor_tensor(art[:, p, 1, 0:NT], xr, R["ein"], ALU.mult), reads=[kx0, "r_ein"], writes=[("art", p)])
            P.op("dve", lambda e: e.tensor_tensor(kt_[:, p, 0:NT], R["k2"], R["einv"], ALU.mult), reads=["r_k2", "r_einv"], writes=[("kt", p)])
            P.op("dve", lambda e: e.scalar_tensor_tensor(art[:, p, 0, 0:NT], R["kkn"], -1.0, R["t1"], ALU.mult, ALU.mult), reads=["r_kkn", "r_t1"], writes=[("art", p)])
            P.op("pool", lambda e: e.tensor_tensor(R["t3"], R["kkn"], R["a"], ALU.mult), reads=["r_kkn", "r_a", "r_k2"], writes=["r_t3"])
            P.op("dve", lambda e: e.tensor_tensor(bt_[:, p, 0:NT], R["t3"], R["einv"], ALU.mult), reads=["r_t3", "r_einv"], writes=[("bt", p)])
            P.op("pool", lambda e: e.tensor_tensor(R["t2"], R["k2"], R["eend"], ALU.mult), reads=["r_k2", "r_eend", "r_kkn"], writes=["r_t2"])
            P.op("dve", lambda e: e.scalar_tensor_tensor(R["t1"], xr, rk_c[:, p:p + 1], R["k2"], ALU.mult, ALU.mult), reads=[kx0, "rk_c", "r_k2", ("art", p)], writes=["r_t1"])
            P.op("pool", lambda e: e.tensor_tensor(R["t3"], R["t3"], R["eend"], ALU.mult), reads=["r_t3", "r_eend", ("bt", p)], writes=["r_t3"])
            for b in range(nb):
                tb_ = 4 + b % 2
                P.op("pe", lambda e, b=b, tb_=tb_: e.transpose(BK[tb_][:, 0:128], R["t2"][:, b * 128:(b + 1) * 128], ident), reads=["r_t2", "cst"], writes=[bk(tb_)])
                P.op("act", lambda e, b=b, tb_=tb_: e.copy(KPtok[:, b, p * 128:(p + 1) * 128], BK[tb_][:, 0:128]), reads=[bk(tb_)], writes=[("KPtok", b, p)])
            P.op("pe", lambda e: e.matmul(BK[b7][:, 0:NT], bo, R["t1"], start=True, stop=True), reads=["cst", "r_t1"], writes=[bk(b7)])
            for b in range(nb):
                tb_ = 4 + b % 2
                P.op("pe", lambda e, b=b, tb_=tb_: e.transpose(BK[tb_][:, 0:128], R["t3"][:, b * 128:(b + 1) * 128], ident), reads=["r_t3", "cst"], writes=[bk(tb_)])
                P.op("act", lambda e, b=b, tb_=tb_: e.copy(BPtok[:, b, p * 128:(p + 1) * 128], BK[tb_][:, 0:128]), reads=[bk(tb_)], writes=[("BPtok", b, p)])
            P.op("dve", lambda e: e.tensor_tensor(bon[:, p, 0:NT], BK[b7][:, 0:NT], xv, ALU.mult), reads=[bk(b7), kx2], writes=[("bon", p)])
            P.op("pe", lambda e: e.matmul(BK[b6][:, 0:NT], g2bf[:, p * 128:(p + 1) * 128], SG[:, 0:NT], start=True, stop=True), reads=["g2bf", "SG"], writes=[bk(b6)])
            P.op("act", lambda e: e.copy(gT[:, p, 0:NT], BK[b6][:, 0:NT]), reads=[bk(b6)], writes=[("gT", p)])

        pend = [None]
        for blkb in range(9):
            slot, skey = ring_get()
            sl3 = slot[:, 0:4096].rearrange("p (k n) -> p k n", n=512)
            for cc in range(4):
                c = blkb * 4 + cc
                bnk = c % 4
                for kc in range(8):
                    P.op("pe", lambda e, kc=kc, cc=cc, bnk=bnk, sl3=sl3: e.matmul(BK[bnk][:, 0:NT], sl3[:, kc, cc * 128:(cc + 1) * 128], hT[:, kc, 0:NT], start=(kc == 0), stop=(kc == 7)),
                         reads=[skey] + HT_KEYS, writes=[bk(bnk)])
                stop_at("E%d" % c)
                if c < 4:
                    P.op("act", lambda e, c=c, bnk=bnk: e.copy(qT[:, c, 0:NT], BK[bnk][:, 0:NT]), reads=[bk(bnk)], writes=[("qT", c)])
                elif c == 4:
                    P.op("act", lambda e, bnk=bnk: e.copy(kTf[:, 0:NT], BK[bnk][:, 0:NT]), reads=[bk(bnk)], writes=["kTf"])
                    P.op("pool", lambda e: e.tensor_copy(kT[:, 128:128 + NT], kTf[:, 0:NT]), reads=["kTf"], writes=["kT"])
                elif c == 5:
                    P.op("act", lambda e, bnk=bnk: e.copy(vTf[:, 0:NT], BK[bnk][:, 0:NT]), reads=[bk(bnk)], writes=["vTf"])
                    for b in range(nb):
                        tb_ = 4 + b % 2
                        P.op("pe", lambda e, b=b, tb_=tb_: e.transpose(BK[tb_][:, 0:128], vTf[:, b * 128:(b + 1) * 128], ident), reads=["vTf", "cst"], writes=[bk(tb_)])
                        P.op("dve", lambda e, b=b, tb_=tb_: e.tensor_copy(vtok[:, 1 + b, :], BK[tb_][:, 0:128]), reads=[bk(tb_)], writes=[("vtok", 1 + b)])
                        if (prompt and last and b == nb - 1) or not prompt:
                            P.op("dve", lambda e, tb_=tb_: e.tensor_copy(kvrow[:, 1, :], BK[tb_][:, 0:128]), reads=[bk(tb_)], writes=[("kvrow", 1)])
                elif c == 6:
                    token_shift(bnk, 12, xs3[0][:, 0:NT], "xs0")
                    P.op("act", lambda e: e.activation(TW[0:64, 0:NT], xs3[0][0:64, 0:NT], AF.Tanh), reads=["xs0"], writes=["TW"])
                    P.op("pool", lambda e: e.tensor_copy(TW[64:128, 0:NT], xs3[0][64:128, 0:NT]), reads=["xs0"], writes=["TW"])
                elif c == 7:
                    token_shift(bnk, 13, xs3[0][:, 0:NT], "xs0")
                    P.op("act", lambda e: e.activation(SG[:, 0:NT], xs3[0][:, 0:NT], AF.Sigmoid), reads=["xs0"], writes=["SG"])
                elif c < 20:
                    p, which = divmod(c - 8, 3)
                    xb_, xk_ = xs_set(p)
                    token_shift(bnk, which * 4 + p, xb_[which][:, 0:NT], xk_[which])
                    if which == 2:
                        if pend[0] is not None:
                            pair_process(pend[0])
                        pend[0] = p
                else:
                    gi = c - 20
                    P.op("act", lambda e, gi=gi, bnk=bnk: e.activation(gates[:, gi, 0:NT], BK[bnk][:, 0:NT], AF.Sigmoid), reads=[bk(bnk)], writes=[("gates", gi)])
                    if c == 23 and pend[0] is not None:
                        pair_process(pend[0])
                        pend[0] = None

        stop_at("E")
        debug_tap("qT", qT[:, :, 0:NT], [128, 4, NT], [("qT", c) for c in range(4)])

        if prompt:
            def att_steps(b, kv):
                gbk = ti * nb + b
                ph = slice(kv * 64, kv * 64 + 64)
                qv = qT[ph, :, b * 128:(b + 1) * 128]
                xs_ = [0] + ([1] if gbk > 0 else [])
                bb = 4 if kv == 0 else 0
                PTk = PT if kv == 0 else PT2

                def sc():
                    for x in xs_:
                        kcols = slice(128 + b * 128, 256 + b * 128) if x == 0 else slice(b * 128, 128 + b * 128)
                        P.op("pe", lambda e, kcols=kcols, x=x: e.matmul(BK[bb + x][:], kT[ph, kcols], qv, start=True, stop=True),
                             reads=["kT"] + [("qT", c) for c in range(4)], writes=[bk(bb + x)])

                def bias():
                    for x in xs_:
                        P.op("dve", lambda e, x=x: e.scalar_tensor_tensor(BK[bb + x][:], BK[bb + x][:], 0.125, biasT[x][kv][:].rearrange("p g q -> p (g q)"), ALU.mult, ALU.add),
                             reads=[bk(bb + x)] + BIAS_KEYS[(x, kv)], writes=[bk(bb + x)])

                def ex():
                    for x in xs_:
                        P.op("act", lambda e, x=x: e.activation(PTk[x][:], BK[bb + x][:], AF.Exp), reads=[bk(bb + x)], writes=[("PT", kv, x)])

                def pv():
                    for i, x in enumerate(xs_):
                        vb = 1 + b if x == 0 else b
                        P.op("pe", lambda e, x=x, vb=vb, i=i: e.matmul(BK[bb + 2][:], vtok[:, vb, :], PTk[x][:], start=(i == 0), stop=(i == len(xs_) - 1)),
                             reads=[("vtok", vb), ("PT", kv, x)], writes=[bk(bb + 2)])
                    for i, x in enumerate(xs_):
                        P.op("pe", lambda e, x=x, i=i: e.matmul(BK[bb + 3][:], ones_bf[:], PTk[x][:], start=(i == 0), stop=(i == len(xs_) - 1)),
                             reads=["ones_bf", ("PT", kv, x)], writes=[bk(bb + 3)])
                return [
                    sc, bias, ex, pv,
                    lambda: P.op("dve", lambda e: e.tensor_tensor(rd_t[ph, :], BK[bb + 3][ph, :], esink[ph, :, :].rearrange("p g q -> p (g q)"), ALU.add),
                                 reads=[bk(bb + 3)] + ESINK_KEYS, writes=[("rd_t", kv)]),
                    lambda: P.op("dve", lambda e: e.reciprocal(rd_t[ph, :], rd_t[ph, :]), reads=[("rd_t", kv)], writes=[("rd_t", kv)]),
                    lambda: P.op("dve", lambda e: e.tensor_tensor(attT[ph, :, b * 128:(b + 1) * 128], BK[bb + 2][ph, :].rearrange("p (g q) -> p g q", q=128), rd_t[ph, :].rearrange("p (g q) -> p g q", q=128), ALU.mult),
                                 reads=[bk(bb + 2), ("rd_t", kv)], writes=[("attT", kv, b)]),
                ]
            for b in range(nb):
                for s_a, s_b in zip(att_steps(b, 0), att_steps(b, 1)):
                    s_a()
                    s_b()
            P.op("pool", lambda e: e.tensor_copy(kT[:, 0:128], kT[:, NT:NT + 128]), reads=["kT"], writes=["kT"])
            P.op("pool", lambda e: e.tensor_copy(vtok[:, 0, :], vtok[:, nb, :]), reads=[("vtok", nb)], writes=[("vtok", 0)])
            if last and 'pk' not in SKIP:
                P.op("pe", lambda e: e.transpose(BK[4][:, 0:128], kTf[:, NT - 128:NT], ident), reads=["kTf", "cst"], writes=[bk(4)])
                P.op("act", lambda e: e.copy(kvrow[:, 0, :], BK[4][:, 0:128]), reads=[bk(4)], writes=[("kvrow", 0)])
                P.dma("sp", pk[seq], kvrow[:, 0, :], reads=[("kvrow", 0)], writes=["pk"])
                P.dma("sp", pv[seq], kvrow[:, 1, :], reads=[("kvrow", 1)], writes=["pv"])
        else:
            sample_attention()
        ATT_KEYS = [("attT", kv, b) for kv in range(2) for b in range(nb)]
        stop_at("F")
        debug_tap("attT", attT[:, :, 0:NT], [128, 4, NT], ATT_KEYS)

        def hcol(h):
            return (h % 2) * 256 + (h // 2) * 64

        def rw_pre(b):
            bc = slice(b * 128, (b + 1) * 128)
            for h in range(8):
                p, h2 = divmod(h, 2)
                ph = slice(h2 * 64, h2 * 64 + 64)
                b0, b1, b2 = (4, 5, 6) if h % 2 == 0 else (0, 1, 2)
                P.op("pe", lambda e, p=p, ph=ph: e.matmul(BK[b0][:, 0:256], kt_[ph, p, bc], art[ph, p, :, bc], start=True, stop=True),
                     reads=[("kt", p), ("art", p)], writes=[bk(b0)])
                P.op("dve", lambda e, h=h: e.tensor_tensor(AK[h][:], BK[b0][:, 0:256], cst[:, cm:cm + 256], ALU.mult), reads=[bk(b0), "cst"], writes=[("AK", h)])
                P.op("pe", lambda e, p=p, ph=ph: e.matmul(BK[b1][:, 0:256], bt_[ph, p, bc], art[ph, p, :, bc], start=True, stop=True),
                     reads=[("bt", p), ("art", p)], writes=[bk(b1)])
                P.op("dve", lambda e, h=h: e.tensor_tensor(AB[h][:], BK[b1][:, 0:256], cst[:, cm:cm + 256], ALU.mult), reads=[bk(b1), "cst"], writes=[("AB", h)])
                P.op("pe", lambda e, p=p, ph=ph: e.matmul(BK[b2][:, 0:128], art[ph, p, 0, bc], bt_[ph, p, bc], start=True, stop=True),
                     reads=[("bt", p), ("art", p)], writes=[bk(b2)])
                P.op("dve", lambda e, h=h: e.tensor_tensor(Mt_[h // 4][:, h % 4, :], BK[b2][:, 0:128], cst[:, cl:cl + 128], ALU.mult), reads=[bk(b2), "cst"], writes=[("Mt", h // 4)])
                P.op("pool", lambda e, h=h: e.tensor_tensor(Pv[h], AB[h][:, 0:128], ident, ALU.add), reads=[("AB", h), "cst"], writes=[("Pv", h)])
            for lv in range(1, LV + 1):
                lastlv = lv == LV
                for g in range(2):
                    bMT, bM, bP = (4, 5, 6) if g == 0 else (0, 1, 2)
                    for j in range(4):
                        h = 4 * g + j
                        Mcur = AB[h][:, 0:128] if lv == 1 else Mm_[g][:, j, :]
                        mk_c = ("AB", h) if lv == 1 else ("Mm", g)
                        Mtcur = Mt_[g][:, j, :]
                        P.op("pe", lambda e, Mcur=Mcur, Mtcur=Mtcur, j=j: e.matmul(BK[bMT][:, j * 128:(j + 1) * 128], Mcur, Mtcur, start=True, stop=True),
                             reads=[mk_c, ("Mt", g)], writes=[bk(bMT)])
                        if not lastlv:
                            P.op("pe", lambda e, Mcur=Mcur, Mtcur=Mtcur, j=j: e.matmul(BK[bM][:, j * 128:(j + 1) * 128], Mtcur, Mcur, start=True, stop=True),
                                 reads=[mk_c, ("Mt", g)], writes=[bk(bM)])
                    P.op("act", lambda e, g=g: e.copy(Mt_[g][:].rearrange("p a t -> p (a t)"), BK[bMT][:]), reads=[bk(bMT)], writes=[("Mt", g)])
                    if not lastlv:
                        P.op("act", lambda e, g=g: e.copy(Mm_[g][:].rearrange("p a t -> p (a t)"), BK[bM][:]), reads=[bk(bM)], writes=[("Mm", g)])
                    for j in range(4):
                        h = 4 * g + j
                        P.op("pe", lambda e, j=j, h=h, g=g: e.matmul(BK[bP][:, j * 128:(j + 1) * 128], Mt_[g][:, j, :], Pv[h], start=True, stop=True),
                             reads=[("Mt", g), ("Pv", h)], writes=[bk(bP)])
                    P.op("dve", lambda e, g=g: e.tensor_tensor(Pvg[g][:].rearrange("p a t -> p (a t)"), BK[bP][:], Pvg[g][:].rearrange("p a t -> p (a t)"), ALU.add),
                         reads=[bk(bP)] + [("Pv", 4 * g + j) for j in range(4)], writes=[("Pv", 4 * g + j) for j in range(4)])
            for h in range(8):
                P.op("pe", lambda e, h=h, b=b: e.matmul(BK[7][:, hcol(h):hcol(h) + 64], AK[h][:, 0:128], Vtok[:, b, h * 64:(h + 1) * 64], start=True, stop=True),
                     reads=[("AK", h), ("Vtok", b, h // 2)], writes=[bk(7)])
            P.op("act", lambda e: e.copy(VA[:], BK[7][:]), reads=[bk(7)], writes=["VA"])
            for h in range(8):
                p, h2 = divmod(h, 2)
                ph = slice(h2 * 64, h2 * 64 + 64)
                P.op("pe", lambda e, h=h, b=b, p=p, ph=ph: e.matmul(BK[4][ph, p * 128:(p + 1) * 128], Vtok[:, b, h * 64:(h + 1) * 64], AK[h][:, 128:256], start=True, stop=True),
                     reads=[("AK", h), ("Vtok", b, p)], writes=[bk(4)])
            P.op("act", lambda e: e.copy(VK[:].rearrange("p a t -> p (a t)"), BK[4][:]), reads=[bk(4)], writes=["VK"])
            stop_at('G2')

        if prompt:
            if first:
                P.op("pool", lambda e: e.memset(S[:], 0.0), writes=["S"])
            for b in range(nb):
                rw_pre(b)
                for c2 in range(2):
                    cr = slice(c2 * 64, c2 * 64 + 64)
                    tc_ = slice(b * 128 + c2 * 64, b * 128 + c2 * 64 + 64)
                    gci = (b * 128 + c2 * 64) // 64
                    for h in range(8):
                        p, h2 = divmod(h, 2)
                        ph = slice(h2 * 64, h2 * 64 + 64)
                        P.op("pe", lambda e, h=h, p=p, ph=ph, h2=h2: e.matmul(BK[5 + h2][cr, p * 64:(p + 1) * 64], art[ph, p, 0, tc_], S[ph, p, :], start=True, stop=True),
                             reads=[("art", p), "S"], writes=[bk(5 + h2)])
                    for h in range(8):
                        p, h2 = divmod(h, 2)
                        ph = slice(h2 * 64, h2 * 64 + 64)
                        P.op("pe", lambda e, p=p, ph=ph, h2=h2: e.matmul(BK[h2][ph, p * 64:(p + 1) * 64], S[ph, p, :], art[ph, p, 1, tc_], start=True, stop=True),
                             reads=[("art", p), "S"], writes=[bk(h2)])
                    for h2 in range(2):
                        ph = slice(h2 * 64, h2 * 64 + 64)
                        P.op("act", lambda e, h2=h2, ph=ph: e.copy(y1[ph, :, :].rearrange("p a t -> p (a t)"), BK[h2][ph, 0:256]), reads=[bk(h2)], writes=[("y1", h2)])
                    for h2 in range(2):
                        P.op("dve", lambda e, h2=h2: e.tensor_tensor(XT[cr, h2 * 256:(h2 + 1) * 256], BK[5 + h2][cr, 0:256], VA[cr, h2 * 256:(h2 + 1) * 256], ALU.add),
                             reads=[bk(5 + h2), "VA"], writes=[("XT", h2)])
                    for h in range(8):
                        P.op("pe", lambda e, h=h: e.matmul(BK[7][cr, hcol(h):hcol(h) + 64], Pv[h][cr, c2 * 64:(c2 + 1) * 64], XT[cr, hcol(h):hcol(h) + 64], start=True, stop=True),
                             reads=[("Pv", h), ("XT", h % 2)], writes=[bk(7)])
                    P.op("act", lambda e: e.copy(UT[cr, :], BK[7][cr, :]), reads=[bk(7)], writes=["UT"])
                    stop_at('G3')
                    for h in range(8):
                        p, h2 = divmod(h, 2)
                        ph = slice(h2 * 64, h2 * 64 + 64)
                        P.op("pe", lambda e, h=h, p=p, ph=ph: e.matmul(BK[5][ph, 256 + p * 64:256 + (p + 1) * 64], BPtok[cr, b, h * 64:(h + 1) * 64], UT[cr, hcol(h):hcol(h) + 64], start=True, stop=False),
                             reads=[("BPtok", b, p), "UT"], writes=[bk(5)])
                        P.op("pe", lambda e, h=h, p=p, ph=ph: e.matmul(BK[5][ph, 256 + p * 64:256 + (p + 1) * 64], KPtok[cr, b, h * 64:(h + 1) * 64], Vtok[cr, b, h * 64:(h + 1) * 64], start=False, stop=True),
                             reads=[("KPtok", b, p), ("Vtok", b, p)], writes=[bk(5)])
                    for p in range(4):
                        P.op("dve", lambda e, p=p: e.scalar_tensor_tensor(S[:, p, :], S[:, p, :], cCt[:, p, gci:gci + 1], BK[5][:, 256 + p * 64:256 + (p + 1) * 64], ALU.mult, ALU.add),
                             reads=["S", ("cCt", p), bk(5)], writes=["S"])
                    for h in range(8):
                        p, h2 = divmod(h, 2)
                        ph = slice(h2 * 64, h2 * 64 + 64)
                        P.op("pe", lambda e, h=h, p=p, ph=ph: e.matmul(BK[6][ph, 256 + p * 64:256 + (p + 1) * 64], UT[cr, hcol(h):hcol(h) + 64], AB[h][cr, 128 + c2 * 64:128 + (c2 + 1) * 64], start=True, stop=True),
                             reads=[("AB", h), "UT"], writes=[bk(6)])
                    P.op("dve", lambda e: e.tensor_tensor(y1[:], BK[6][:, 256:512].rearrange("p (a t) -> p a t", t=64), y1[:], ALU.add), reads=[bk(6), ("y1", 0), ("y1", 1)], writes=[("y1", 0), ("y1", 1)])
                    P.op("pool", lambda e: e.tensor_tensor(Yt[:, :, tc_], y1[:], VK[:, :, c2 * 64:(c2 + 1) * 64], ALU.add), reads=[("y1", 0), ("y1", 1), "VK"], writes=[("Yt", b)])
                    stop_at('G4')
            if last and 'pwkv' not in SKIP:
                for pp in range(2):
                    P.op("pe", lambda e, pp=pp: e.transpose(BK[4][:, pp * 128:(pp + 1) * 128], S[:, 2 * pp:2 * pp + 2, :].rearrange("p a i -> p (a i)"), ident), reads=["S", "cst"], writes=[bk(4)])
                P.op("act", lambda e: e.copy(xn[:, 0:256], BK[4][:, 0:256]), reads=[bk(4)], writes=["xn"])
                for pp in range(2):
                    for pl in range(2):
                        pidx = 2 * pp + pl
                        P.dma("sp", pwkv[seq, 2 * pidx:2 * pidx + 2].rearrange("h i j -> i h j"), xn[pl * 64:(pl + 1) * 64, pp * 128:(pp + 1) * 128].rearrange("p (h j) -> p h j", j=64), reads=["xn"], writes=[("pwkv", pidx)])
                P.op("pe", lambda e: e.transpose(BK[4][:, 0:128], car[:, :], ident), reads=[("car", n) for n in range(14)] + ["cst", "car_init"], writes=[bk(4)])
                P.op("act", lambda e: e.copy(xn[0:14, 0:128], BK[4][0:14, 0:128]), reads=[bk(4)], writes=["xn"])
                P.dma("sp", psh[seq].rearrange("(c p) -> c p", p=128), xn[0:14, 0:128], reads=["xn"], writes=["psh"])
        else:
            rw_pre(0)
            sample_rwkv(hcol)
        YT_KEYS = [("Yt", b) for b in range(nb)]
        stop_at("G")
        debug_tap("Yt", Yt[:, :, 0:NT], [128, 4, NT], YT_KEYS)

        def post_steps(p):
            tn = ("sg", "cs", "t1", "t2", "t3", "a", "kkn", "k2")
            n1, n2 = tn[2 * p], tn[2 * p + 1]
            T1, T2 = rw[n1][:, 0:NT], rw[n2][:, 0:NT]
            k1, k2_ = "r_" + n1, "r_" + n2
            bm, bv = 4 + p, p
            return [
                lambda: P.op("pe", lambda e: e.matmul(BK[bm][:, 0:NT], bo64, Yt[:, p, 0:NT], start=True, stop=True), reads=YT_KEYS + ["cst"], writes=[bk(bm)]),
                lambda: P.op("dve", lambda e: e.tensor_tensor(T1, Yt[:, p, 0:NT], BK[bm][:, 0:NT], ALU.subtract), reads=YT_KEYS + [bk(bm)], writes=[k1]),
                lambda: P.op("pool", lambda e: e.tensor_tensor(T2, T1, T1, ALU.mult), reads=[k1], writes=[k2_]),
                lambda: P.op("pe", lambda e: e.matmul(BK[bv][:, 0:NT], bo64, T2, start=True, stop=True), reads=[k2_, "cst"], writes=[bk(bv)]),
                lambda: P.op("act", lambda e: e.activation(T2, BK[bv][:, 0:NT], AF.Sqrt, bias=64e-5), reads=[bk(bv)], writes=[k2_]),
                lambda: P.op("dve", lambda e: e.reciprocal(T2, T2), reads=[k2_], writes=[k2_]),
                lambda: P.op("dve", lambda e: e.tensor_tensor(T1, T1, T2, ALU.mult), reads=[k1, k2_], writes=[k1]),
                lambda: P.op("dve", lambda e: e.tensor_scalar(T1, T1, lg_c[:, p:p + 1], lb_c[:, p:p + 1], ALU.mult, ALU.add), reads=[k1, "lg_c", "lb_c"], writes=[k1]),
                lambda: P.op("pool", lambda e: e.tensor_tensor(T1, T1, bon[:, p, 0:NT], ALU.add), reads=[k1, ("bon", p)], writes=[k1]),
                lambda: P.op("dve", lambda e: e.tensor_tensor(rwoT[:, p, 0:NT], T1, gT[:, p, 0:NT], ALU.mult), reads=[k1, ("gT", p)], writes=[("rwoT", p)]),
            ]
        for group in zip(*[post_steps(p) for p in range(4)]):
            for step in group:
                step()
        RWO_KEYS = [("rwoT", p) for p in range(4)]
        stop_at("H")
        debug_tap("rwoT", rwoT[:, :, 0:NT], [128, 4, NT], RWO_KEYS)

        for ch in range(2):
            sa, ka_ = ring_get()
            sb_, kb_ = ring_get(hold=1)
            sa3 = sa[:, 0:2048].rearrange("p (k n) -> p k n", n=512)
            sb3 = sb_[:, 0:2048].rearrange("p (k n) -> p k n", n=512)
            for cc in range(4):
                oc = ch * 4 + cc
                ba, bb = (0, 1) if cc % 2 == 0 else (2, 3)
                for kc in range(4):
                    P.op("pe", lambda e, kc=kc, cc=cc, ba=ba, sa3=sa3: e.matmul(BK[ba][:, 0:NT], sa3[:, kc, cc * 128:(cc + 1) * 128], attT[:, kc, 0:NT], start=(kc == 0), stop=(kc == 3)),
                         reads=[ka_] + ATT_KEYS, writes=[bk(ba)])
                for kc in range(4):
                    P.op("pe", lambda e, kc=kc, cc=cc, bb=bb, sb3=sb3: e.matmul(BK[bb][:, 0:NT], sb3[:, kc, cc * 128:(cc + 1) * 128], rwoT[:, kc, 0:NT], start=(kc == 0), stop=(kc == 3)),
                         reads=[kb_] + RWO_KEYS, writes=[bk(bb)])
                tA = cc_t[cc % 2][:, 0:NT]
                tB = gl_t[cc % 2][:, 0:NT]
                P.op("dve", lambda e, oc=oc, ba=ba, tA=tA: e.tensor_tensor(tA, BK[ba][:, 0:NT], gates[:, oc, 0:NT], ALU.mult), reads=[bk(ba), ("gates", oc)], writes=[("cc_t", cc % 2)])
                P.op("dve", lambda e, oc=oc, bb=bb, tB=tB: e.tensor_tensor(tB, BK[bb][:, 0:NT], gates[:, 8 + oc, 0:NT], ALU.mult), reads=[bk(bb), ("gates", 8 + oc)], writes=[("gl_t", cc % 2)])
                P.op("pool", lambda e, oc=oc, tA=tA, tB=tB: e.tensor_tensor(hT[:, oc, 0:NT], tA, tB, ALU.add), reads=[("cc_t", cc % 2), ("gl_t", cc % 2)], writes=[("hT", oc)])
        MIX_KEYS = [("hT", oc) for oc in range(8)]
        debug_tap("mixT", hT[:, :, 0:NT], [128, 8, NT], MIX_KEYS)

        stop_at("I")
        for ch in range(2):
            so, ko = ring_get()
            so3 = so[:, 0:4096].rearrange("p (k n) -> p k n", n=512)
            for b in range(nb):
                bnk = (ch * nb + b) % 4
                for kc in range(8):
                    P.op("pe", lambda e, kc=kc, b=b, bnk=bnk, so3=so3: e.matmul(BK[bnk][:], hT[:, kc, b * 128:(b + 1) * 128], so3[:, kc, :], start=(kc == 0), stop=(kc == 7)),
                         reads=[ko] + MIX_KEYS, writes=[bk(bnk)])
                P.op("dve", lambda e, b=b, bnk=bnk, ch=ch: e.tensor_tensor(xt[b][:, ch * 512:(ch + 1) * 512], xt[b][:, ch * 512:(ch + 1) * 512], BK[bnk][:], ALU.add),
                     reads=[bk(bnk), ("xt", b)], writes=[("xt", b)])
        stop_at("J")
        debug_tap("x1", xt[0][:], [128, 1024], [("xt", 0)])

        norm_T("g2b")
        def ffn_steps(i, f, bo_):
            ug3 = v3(ugx[i][:, 0:NBs * (TB + 2)], TB + 2)
            ugk = ("ugx", i)
            c3 = v3(cc_t[i][:, 0:NT], TB)

            def carry():
                if prompt:
                    if first:
                        P.op("pool", lambda e: e.memset(ugx[i][:, 0:2], 0.0), writes=[("ugx0", i)])
                    else:
                        P.op("pool", lambda e: e.tensor_copy(ugx[i][:, 0:2], ccar[:, :, f]), reads=[("ccar", f)], writes=[("ugx0", i)])
                    P.op("pool", lambda e: e.tensor_copy(ccar[:, :, f], ugx[i][:, NT:NT + 2]), reads=[ugk], writes=[("ccar", f)])
                else:
                    P.op("pool", lambda e: e.tensor_copy(ug3[:, :, 0:2], SB["ccar_s"][:, f, :, :]), reads=[("ccar_s", f)], writes=[("ugx0", i)])
                    P.op("pool", lambda e: e.tensor_copy(SB["cout_s"][:, f, :, :], ug3[:, :, TB:TB + 2]), reads=[ugk], writes=[("cout_s", f)])
            return [
                lambda: P.op("act", lambda e: e.copy(ug3[:, :, 2:TB + 2], v3(BK[bo_ + i][:, 0:NT], TB)), reads=[bk(bo_ + i)], writes=[ugk]),
                carry,
                lambda: P.op("pool", lambda e: e.tensor_scalar(c3, ug3[:, :, 0:TB], cw_c[:, 0, f:f + 1], cb_c[:, f:f + 1], ALU.mult, ALU.add),
                             reads=[ugk, ("ugx0", i), "cb_c"] + CW_KEYS, writes=[("cc_t", i)]),
                lambda: P.op("dve", lambda e: e.scalar_tensor_tensor(c3, ug3[:, :, 1:TB + 1], cw_c[:, 1, f:f + 1], c3, ALU.mult, ALU.add),
                             reads=[ugk, ("ugx0", i), ("cc_t", i)] + CW_KEYS, writes=[("cc_t", i)]),
                lambda: P.op("dve", lambda e: e.scalar_tensor_tensor(c3, ug3[:, :, 2:TB + 2], cw_c[:, 2, f:f + 1], c3, ALU.mult, ALU.add),
                             reads=[ugk, ("cc_t", i)] + CW_KEYS, writes=[("cc_t", i)]),
                lambda: P.op("act", lambda e: e.activation(gl_t[i][:, 0:NT], cc_t[i][:, 0:NT], AF.Gelu_apprx_tanh), reads=[("cc_t", i)], writes=[("gl_t", i)]),
                lambda: P.op("dve", lambda e: e.tensor_tensor(actT[:, f, 0:NT], gl_t[i][:, 0:NT], BK[bo_ + 2 + i][:, 0:NT], ALU.mult), reads=[("gl_t", i), bk(bo_ + 2 + i)], writes=[("actT", f)]),
            ]
        for blkb in range(11):
            slot, skey = ring_get()
            sl3 = slot[:, 0:4096].rearrange("p (k n) -> p k n", n=512)
            bo_ = 4 * (blkb % 2)
            for cc in range(4):
                for kc in range(8):
                    P.op("pe", lambda e, kc=kc, cc=cc, sl3=sl3, bo_=bo_: e.matmul(BK[bo_ + cc][:, 0:NT], sl3[:, kc, cc * 128:(cc + 1) * 128], hT[:, kc, 0:NT], start=(kc == 0), stop=(kc == 7)),
                         reads=[skey] + HT_KEYS, writes=[bk(bo_ + cc)])
            for sa, sb in zip(ffn_steps(0, blkb * 2, bo_), ffn_steps(1, blkb * 2 + 1, bo_)):
                sa()
                sb()
        ACT_KEYS = [("actT", f) for f in range(22)]
        if prompt and last and 'pconv' not in SKIP:
            for j in range(2):
                P.op("pe", lambda e, j=j: e.transpose(BK[4][:, j * 128:(j + 1) * 128], ccar[:, j, :], ident), reads=[("ccar", f) for f in range(22)] + ["cst", "ccar_init"], writes=[bk(4)])
            P.op("act", lambda e: e.copy(xn[0:22, 0:256], BK[4][0:22, 0:256]), reads=[bk(4)], writes=["xn"])
            for j in range(2):
                P.dma("sp", pconv[seq, j].rearrange("(c p) -> c p", p=128), xn[0:22, j * 128:(j + 1) * 128], reads=["xn"], writes=[("pconv", j)])

        stop_at("K")
        for ch in range(2):
            for rg in range(3):
                sd, kd = ring_get()
                nk = DN_NK[rg]
                sd3 = sd[:, 0:nk * 512].rearrange("p (k n) -> p k n", n=512)
                for b in range(nb):
                    bnk = b % 4
                    for kc in range(nk):
                        f = rg * 8 + kc
                        P.op("pe", lambda e, kc=kc, f=f, b=b, bnk=bnk, sd3=sd3: e.matmul(BK[bnk][:], actT[:, f, b * 128:(b + 1) * 128], sd3[:, kc, :], start=(f == 0), stop=(f == 21)),
                             reads=[kd] + ACT_KEYS, writes=[bk(bnk)])
            for b in range(nb):
                bnk = b % 4
                P.op("dve", lambda e, b=b, bnk=bnk, ch=ch: e.tensor_tensor(xt[b][:, ch * 512:(ch + 1) * 512], xt[b][:, ch * 512:(ch + 1) * 512], BK[bnk][:], ALU.add),
                     reads=[bk(bnk), ("xt", b)], writes=[("xt", b)])

        def fin_steps(b):
            so = 16 * b
            st = stat[:, so:so + 16]
            xnb = xn if b == 0 else xn2
            xk_ = "xn" if b == 0 else "xn2"
            sk_ = "stat%d_" % b
            return [
                lambda: P.op("dve", lambda e: e.bn_stats(st[:, 0:6], xt[b][:, 0:512]), reads=[("xt", b)], writes=[sk_ + "0"]),
                lambda: P.op("dve", lambda e: e.bn_stats(st[:, 6:12], xt[b][:, 512:1024]), reads=[("xt", b)], writes=[sk_ + "1"]),
                lambda: P.op("dve", lambda e: e.bn_aggr(st[:, 12:14], st[:, 0:12]), reads=[sk_ + "0", sk_ + "1"], writes=[sk_ + "2"]),
                lambda: P.op("dve", lambda e: e.scalar_tensor_tensor(st[:, 14:15], st[:, 12:13], st[:, 12:13], st[:, 13:14], ALU.mult, ALU.add), reads=[sk_ + "2"], writes=[sk_ + "3"]),
                lambda: P.op("act", lambda e: e.activation(st[:, 15:16], st[:, 14:15], AF.Sqrt, bias=1e-6), reads=[sk_ + "3"], writes=[sk_ + "4"]),
                lambda: P.op("dve", lambda e: e.reciprocal(st[:, 15:16], st[:, 15:16]), reads=[sk_ + "4"], writes=[sk_ + "4"]),
                lambda: P.op("dve", lambda e: e.scalar_tensor_tensor(xnb[:], xt[b][:], st[:, 15:16], gb["gfb"][:], ALU.mult, ALU.mult), reads=[("xt", b), sk_ + "4", "gfb"], writes=[xk_]),
                lambda: P.dma("sp", yrows[b * 128:(b + 1) * 128, :], xnb[:], reads=[xk_], writes=[("y", kind, seq, ti, b)]),
            ]
        for group in zip(*[fin_steps(b) for b in range(nb)]):
            for step in group:
                step()

    SB = {}

    def sample_alloc():
        SB["shcar"] = wsb("sx_shcar", [128, 14, 16], F32)
        SB["shout"] = wsb("sx_shout", [128, 16, 16], F32)
        SB["ccar_s"] = wsb("sx_ccar_s", [128, 22, 16, 2], F32)
        SB["cout_s"] = wsb("sx_cout_s", [128, 24, 16, 2], F32)
        SB["S_s"] = wsb("sx_S_s", [128, 16, 4, 64], F32)
        SB["kcT"] = wsb("sx_kcT", [128, 16, 128], BF16)
        SB["vc"] = wsb("sx_vc", [128, 16, 128], BF16)
        SB["biasC"] = [wsb("sx_biasC%d" % kv, [128, 16, 4, 8], F32) for kv in range(2)]
        SB["biasN"] = [wsb("sx_biasN%d" % kv, [128, 16, 4, 8], F32) for kv in range(2)]
        SB["Apad"] = wsb("sx_Apad", [128, 16, 128], F32)
        SB["BPpad"] = [wsb("sx_BPpad", [128, 512], F32)] * 2
        SB["KPpad"] = [wsb("sx_KPpad", [128, 512], F32)] * 2
        SB["xn"] = xn

    def sample_const_setup():
        P.op("pool", lambda e: e.memset(SB["Apad"][:].rearrange("p b c -> p (b c)"), 0.0), writes=["Apad"])
        P.op("pool", lambda e: e.memset(SB["shout"][:].rearrange("p a b -> p (a b)"), 0.0), writes=["shout_init"])
        P.op("pool", lambda e: e.memset(SB["cout_s"][:].rearrange("p a b c -> p (a b c)"), 0.0), writes=["cout_init"])
        for kv in range(2):
            P.op("dve", lambda e, kv=kv: e.tensor_copy(SB["biasC"][kv][:], biasT[1][kv][:, :, 0:8].unsqueeze(1).broadcast_to([128, 16, 4, 8])),
                 reads=BIAS_KEYS[(1, kv)], writes=[("biasC", kv)])
            P.op("pool", lambda e, kv=kv: e.memset(SB["biasN"][kv][:].rearrange("p a b c -> p (a b c)"), NEGB), writes=[("biasN", kv)])
            for bb in range(16):
                P.dma("sp", SB["biasN"][kv][bb * 8:(bb + 1) * 8, bb, :, :], biasT[0][kv][0:8, :, 0:8],
                      reads=BIAS_KEYS[(0, kv)] + [("biasN", kv)], writes=[("biasNd", kv, bb)])
    BIASN_KEYS = {kv: [("biasN", kv)] + [("biasNd", kv, bb) for bb in range(16)] for kv in range(2)}

    def sample_setup(st):
        b0 = st * 16
        stgA = SB["xn"]
        for n in range(14):
            bnk = 4 + n % 4
            if n % 7 == 0:
                P.dma("sp", stgA[0:16, 0:896], sshift[b0:b0 + 16, n * 128:n * 128 + 896], writes=["xn"])
            P.op("pe", lambda e, n=n, bnk=bnk: e.transpose(BK[bnk][:, 0:16], stgA[0:16, (n % 7) * 128:(n % 7 + 1) * 128], cst[0:16, C_ID:C_ID + 16]), reads=["xn", "cst"], writes=[bk(bnk)])
            P.op("act", lambda e, n=n, bnk=bnk: e.copy(SB["shcar"][:, n, :], BK[bnk][:, 0:16]), reads=[bk(bnk)], writes=[("shcar", n)])
        for piece, npc in enumerate((8, 8, 6)):
            P.dma("sp", stgA[0:32, 0:npc * 128], sconv[st * 32:(st + 1) * 32, piece * 1024:piece * 1024 + npc * 128], writes=["xn"])
            for c in range(npc):
                f = piece * 8 + c
                bnk = 4 + c % 4
                P.op("pe", lambda e, c=c, bnk=bnk: e.transpose(BK[bnk][:, 0:32], stgA[0:32, c * 128:(c + 1) * 128], cst[0:32, C_ID:C_ID + 32]), reads=["xn", "cst"], writes=[bk(bnk)])
                P.op("act", lambda e, f=f, bnk=bnk: e.copy(SB["ccar_s"][:, f, :, :].rearrange("p b j -> p (b j)"), BK[bnk][:, 0:32]), reads=[bk(bnk)], writes=[("ccar_s", f)])
        for bb in range(16):
            hs = bb % 2
            P.dma("sp", stgA[0:64, hs * 512:(hs + 1) * 512].rearrange("p (h j) -> p h j", j=64), swkv[b0 + bb].rearrange("h i j -> i h j"), writes=[("stgAh", hs), "xn"] if bb < 2 else [("stgAh", hs)], reads=["xn"])
            bnk = 4 + bb % 4
            for p in range(4):
                P.op("pe", lambda e, p=p, bnk=bnk, hs=hs: e.transpose(BK[bnk][:, p * 64:(p + 1) * 64], stgA[0:64, hs * 512 + p * 128:hs * 512 + (p + 1) * 128], cst[0:64, C_ID:C_ID + 64]),
                     reads=[("stgAh", hs), "cst"], writes=[bk(bnk)])
            P.op("dve", lambda e, bb=bb, bnk=bnk: e.tensor_copy(SB["S_s"][:, bb, :, :].rearrange("p a i -> p (a i)"), BK[bnk][:, 0:256]), reads=[bk(bnk)], writes=[("S_s", bb)])
        for bb in range(16):
            bnk = 4 + bb % 4
            if bb % 8 == 0:
                P.dma("sp", stgA[:, 0:1024].rearrange("p (b c) -> p b c", c=128), ck[b0 + bb:b0 + bb + 8].rearrange("b k c -> k b c"), reads=[("stgAh", 0), ("stgAh", 1)], writes=["xn", ("stgAh", 0), ("stgAh", 1)])
            P.op("pe", lambda e, bb=bb, bnk=bnk: e.transpose(BK[bnk][:, 0:128], stgA[:, (bb % 8) * 128:(bb % 8 + 1) * 128], ident), reads=["xn", "cst"], writes=[bk(bnk)])
            P.op("act", lambda e, bb=bb, bnk=bnk: e.copy(SB["kcT"][:, bb, :], BK[bnk][:, 0:128]), reads=[bk(bnk)], writes=[("kcT", bb)])
        P.dma("pool", SB["vc"][:], cv[b0:b0 + 16].rearrange("b k c -> k b c"), writes=["vc"])
        P.dma("sp", sk[b0:b0 + 16, 0:120, :], ck[b0:b0 + 16, 8:128, :], writes=[("sk_old", st)])
        P.dma("sp", sv[b0:b0 + 16, 0:120, :], cv[b0:b0 + 16, 8:128, :], writes=[("sv_old", st)])

    def sample_attention():
        NT = 128
        kcT, vc = SB["kcT"], SB["vc"]
        P.op("pe", lambda e: e.transpose(BK[4][:, 0:128], kTf[:, 0:128], ident), reads=["kTf", "cst"], writes=[bk(4)])
        P.op("act", lambda e: e.copy(kvrow[:, 0, :], BK[4][:, 0:128]), reads=[bk(4)], writes=[("kvrow", 0)])
        for kv in range(2):
            ph = slice(kv * 64, kv * 64 + 64)
            for bb in range(16):
                P.op("pe", lambda e, bb=bb: e.matmul(BK[4][:, bb * 32:(bb + 1) * 32], kcT[ph, bb, :], qT[ph, :, bb * 8:(bb + 1) * 8], start=True, stop=True),
                     reads=[("kcT", bb)] + [("qT", c) for c in range(4)], writes=[bk(4)])
            for bb in range(16):
                P.op("pe", lambda e, bb=bb: e.matmul(BK[5][:, bb * 32:(bb + 1) * 32], kT[ph, 128:256], qT[ph, :, bb * 8:(bb + 1) * 8], start=True, stop=True),
                     reads=["kT"] + [("qT", c) for c in range(4)], writes=[bk(5)])
            P.op("dve", lambda e, kv=kv: e.scalar_tensor_tensor(BK[4][:], BK[4][:], 0.125, SB["biasC"][kv][:].rearrange("p a b c -> p (a b c)"), ALU.mult, ALU.add),
                 reads=[bk(4), ("biasC", kv)], writes=[bk(4)])
            P.op("dve", lambda e, kv=kv: e.scalar_tensor_tensor(BK[5][:], BK[5][:], 0.125, SB["biasN"][kv][:].rearrange("p a b c -> p (a b c)"), ALU.mult, ALU.add),
                 reads=[bk(5)] + BIASN_KEYS[kv], writes=[bk(5)])
            P.op("act", lambda e: e.activation(PT[0][:], BK[4][:], AF.Exp), reads=[bk(4)], writes=[("PT", 0)])
            P.op("act", lambda e: e.activation(PT[1][:], BK[5][:], AF.Exp), reads=[bk(5)], writes=[("PT", 1)])
            for bb in range(16):
                cs = slice(bb * 32, (bb + 1) * 32)
                P.op("pe", lambda e, bb=bb, cs=cs: e.matmul(BK[6][:, cs], vc[:, bb, :], PT[0][:, cs], start=True, stop=False), reads=["vc", ("PT", 0)], writes=[bk(6)])
                P.op("pe", lambda e, cs=cs: e.matmul(BK[6][:, cs], vtok[:, 1, :], PT[1][:, cs], start=False, stop=True), reads=[("vtok", 1), ("PT", 1)], writes=[bk(6)])
            for bb in range(16):
                cs = slice(bb * 32, (bb + 1) * 32)
                P.op("pe", lambda e, cs=cs: e.matmul(BK[7][:, cs], ones_bf[:], PT[0][:, cs], start=True, stop=False), reads=["ones_bf", ("PT", 0)], writes=[bk(7)])
                P.op("pe", lambda e, cs=cs: e.matmul(BK[7][:, cs], ones_bf[:], PT[1][:, cs], start=False, stop=True), reads=["ones_bf", ("PT", 1)], writes=[bk(7)])
            P.op("dve", lambda e: e.tensor_tensor(rd_t[ph, :].rearrange("p (b g t) -> p b g t", g=4, t=8), BK[7][ph, :].rearrange("p (b g t) -> p b g t", g=4, t=8), esink[ph, :, 0:8].unsqueeze(1).broadcast_to([64, 16, 4, 8]), ALU.add), reads=[bk(7)] + ESINK_KEYS, writes=["rd_t"])
            P.op("dve", lambda e: e.reciprocal(rd_t[ph, :], rd_t[ph, :]), reads=["rd_t"], writes=["rd_t"])
            P.op("dve", lambda e, kv=kv: e.tensor_tensor(attT[ph, :, 0:128].rearrange("p g (b t) -> p b g t", t=8), BK[6][ph, :].rearrange("p (b g t) -> p b g t", g=4, t=8),
                                                       rd_t[ph, :].rearrange("p (b g t) -> p b g t", g=4, t=8), ALU.mult),
                 reads=[bk(6), "rd_t"], writes=[("attT", kv, 0)])

    def sample_rwkv(hcol):
        S_s, Apad = SB["S_s"], SB["Apad"]
        Rpad = Apad
        y1s = VA[:].rearrange("p (a t) -> p a t", t=128)
        SKEYS = [("S_s", bb) for bb in range(16)]

        def diag(t):
            base = t[:, 0, 0:8]
            return bass.AP(base.tensor, base.offset, [list(base.ap[0]), [136, 16], [1, 8]])
        for p in range(4):
            P.op("pool", lambda e, p=p: e.tensor_copy(diag(Apad), art[:, p, 0, 0:128].rearrange("p (b t) -> p b t", t=8)), reads=[("art", p), "Apad"], writes=["Apad"])
            for h2 in range(2):
                ph = slice(h2 * 64, h2 * 64 + 64)
                for bb in range(16):
                    P.op("pe", lambda e, bb=bb, p=p, ph=ph, h2=h2: e.matmul(BK[5 + h2][:, p * 64:(p + 1) * 64], Apad[ph, bb, :], S_s[ph, bb, p, :], start=(bb == 0), stop=(bb == 15)),
                         reads=["Apad", ("S_s", bb)], writes=[bk(5 + h2)])
            P.op("pool", lambda e, p=p: e.tensor_copy(diag(Apad), art[:, p, 1, 0:128].rearrange("p (b t) -> p b t", t=8)), reads=[("art", p), "Apad"], writes=["Apad"])
            for h2 in range(2):
                ph = slice(h2 * 64, h2 * 64 + 64)
                yb = 4 if h2 == 0 else 7
                for bb in range(16):
                    P.op("pe", lambda e, bb=bb, p=p, ph=ph, yb=yb: e.matmul(BK[yb][ph, p * 128:(p + 1) * 128], S_s[ph, bb, p, :], Apad[ph, bb, :], start=(bb == 0), stop=(bb == 15)),
                         reads=["Apad", ("S_s", bb)], writes=[bk(yb)])
        for h2 in range(2):
            P.op("dve", lambda e, h2=h2: e.tensor_tensor(XT[:, h2 * 256:(h2 + 1) * 256], BK[5 + h2][:, 0:256], VA[:, h2 * 256:(h2 + 1) * 256], ALU.add),
                 reads=[bk(5 + h2), "VA"], writes=[("XT", h2)])
        for h in range(8):
            P.op("pe", lambda e, h=h: e.matmul(BK[5][:, hcol(h):hcol(h) + 64], Pv[h], XT[:, hcol(h):hcol(h) + 64], start=True, stop=True),
                 reads=[("Pv", h), ("XT", h % 2)], writes=[bk(5)])
        P.op("act", lambda e: e.copy(UT[:], BK[5][:]), reads=[bk(5)], writes=["UT"])
        for h in range(8):
            p, h2 = divmod(h, 2)
            ph = slice(h2 * 64, h2 * 64 + 64)
            P.op("pe", lambda e, h=h, p=p, ph=ph: e.matmul(BK[6][ph, p * 128:(p + 1) * 128], UT[:, hcol(h):hcol(h) + 64], AB[h][:, 128:256], start=True, stop=True),
                 reads=[("AB", h), "UT"], writes=[bk(6)])
        P.op("act", lambda e: e.copy(VA[0:64, :], BK[4][0:64, :]), reads=[bk(4)], writes=["VA"])
        P.op("act", lambda e: e.copy(VA[64:128, :], BK[7][64:128, :]), reads=[bk(7)], writes=["VA"])
        P.op("dve", lambda e: e.tensor_tensor(VA[:], BK[6][:], VA[:], ALU.add), reads=[bk(6), "VA", "VA"], writes=["VA", "VA"])
        P.op("pool", lambda e: e.tensor_tensor(Yt[:, :, 0:128], y1s, VK[:], ALU.add), reads=["VA", "VA", "VK"], writes=[("Yt", 0)])
        for bb in range(16):
            i2 = bb % 2
            bnk = 4 + bb % 4
            BPp, KPp = SB["BPpad"][i2], SB["KPpad"][i2]
            P.op("pool", lambda e, bb=bb, BPp=BPp: e.tensor_scalar(BPp[:], BPtok[:, 0, :], cst[:, C_BSEL + bb:C_BSEL + bb + 1], None, ALU.mult),
                 reads=[("BPtok", 0, p) for p in range(4)] + ["cst"], writes=["BPpad"])
            P.op("dve", lambda e, bb=bb, KPp=KPp: e.tensor_scalar(KPp[:], KPtok[:, 0, :], cst[:, C_BSEL + bb:C_BSEL + bb + 1], None, ALU.mult),
                 reads=[("KPtok", 0, p) for p in range(4)] + ["cst"], writes=["KPpad"])
            for h in range(8):
                p, h2 = divmod(h, 2)
                ph = slice(h2 * 64, h2 * 64 + 64)
                P.op("pe", lambda e, h=h, p=p, ph=ph, bnk=bnk, BPp=BPp: e.matmul(BK[bnk][ph, p * 64:(p + 1) * 64], BPp[:, h * 64:(h + 1) * 64], UT[:, hcol(h):hcol(h) + 64], start=True, stop=False),
                     reads=["BPpad", "UT"], writes=[bk(bnk)])
                P.op("pe", lambda e, h=h, p=p, ph=ph, bnk=bnk, KPp=KPp: e.matmul(BK[bnk][ph, p * 64:(p + 1) * 64], KPp[:, h * 64:(h + 1) * 64], Vtok[:, 0, h * 64:(h + 1) * 64], start=False, stop=True),
                     reads=["KPpad", ("Vtok", 0, p)], writes=[bk(bnk)])
            for p in range(4):
                P.op("dve", lambda e, p=p, bb=bb, bnk=bnk: e.scalar_tensor_tensor(S_s[:, bb, p, :], S_s[:, bb, p, :], cCt[:, p, bb:bb + 1], BK[bnk][:, p * 64:(p + 1) * 64], ALU.mult, ALU.add),
                     reads=[("S_s", bb), ("cCt", p), bk(bnk)], writes=[("S_s", bb)])
            for pp in range(2):
                P.op("pe", lambda e, pp=pp, bb=bb, bnk=bnk: e.transpose(BK[bnk][:, 256 + pp * 128:256 + (pp + 1) * 128], S_s[:, bb, 2 * pp:2 * pp + 2, :].rearrange("p a i -> p (a i)"), ident),
                     reads=[("S_s", bb), "cst"], writes=[bk(bnk)])
            stgO = XT[:, i2 * 256:(i2 + 1) * 256]
            P.op("act", lambda e, bnk=bnk, stgO=stgO: e.copy(stgO, BK[bnk][:, 256:512]), reads=[bk(bnk)], writes=[("XT", i2)])
            for pp in range(2):
                for pl in range(2):
                    pidx = 2 * pp + pl
                    P.dma("sp", swkvo[SB["b0"] + bb, 2 * pidx:2 * pidx + 2].rearrange("h i j -> i h j"), stgO[pl * 64:(pl + 1) * 64, pp * 128:(pp + 1) * 128].rearrange("p (h j) -> p h j", j=64),
                          reads=[("XT", i2)], writes=[("swkvo", bb, pidx)])

    def sample_outputs(st):
        b0 = st * 16
        stgA = SB["xn"]
        for bb in range(16):
            P.dma("sp", sk[b0 + bb, 120:128, :], kvrow[bb * 8:(bb + 1) * 8, 0, :], reads=[("kvrow", 0)], writes=[("sk_new", st, bb)])
            P.dma("sp", sv[b0 + bb, 120:128, :], kvrow[bb * 8:(bb + 1) * 8, 1, :], reads=[("kvrow", 1)], writes=[("sv_new", st, bb)])
        for g in range(2):
            P.op("pe", lambda e, g=g: e.transpose(BK[4 + g][:, 0:128], SB["shout"][:, g * 8:(g + 1) * 8, :].rearrange("p a b -> p (a b)"), ident),
                 reads=[("shout", n) for n in range(14)] + ["shout_init", "cst"], writes=[bk(4 + g)])
            P.op("act", lambda e, g=g: e.copy(stgA[:, g * 128:(g + 1) * 128], BK[4 + g][:, 0:128]), reads=[bk(4 + g)], writes=["xn"])
        for n in range(14):
            g, nl = divmod(n, 8)
            P.dma("sp", ssh[b0:b0 + 16, n * 128:(n + 1) * 128], stgA[nl * 16:(nl + 1) * 16, g * 128:(g + 1) * 128], reads=["xn"], writes=[("ssh", st, n)])
        for g in range(6):
            P.op("pe", lambda e, g=g: e.transpose(BK[4 + g % 4][:, 0:128], SB["cout_s"][:, g * 4:(g + 1) * 4, :, :].rearrange("p a b j -> p (a b j)"), ident),
                 reads=[("cout_s", f) for f in range(22)] + ["cout_init", "cst"], writes=[bk(4 + g % 4)])
            P.op("act", lambda e, g=g: e.copy(stgA[:, 256 + g * 128:256 + (g + 1) * 128], BK[4 + g % 4][:, 0:128]), reads=[bk(4 + g % 4)], writes=["xn"])
        for f in range(22):
            g, fl = divmod(f, 4)
            P.dma("sp", sconvo[st * 32:(st + 1) * 32, f * 128:(f + 1) * 128], stgA[fl * 32:(fl + 1) * 32, 256 + g * 128:256 + (g + 1) * 128], reads=["xn"], writes=[("sconvo", st, f)])

    tiles = [("p", t // (2048 // NTP), t % (2048 // NTP)) for t in range(n_ptiles)]
    if os.environ.get('KREP'):
        tiles = tiles * int(os.environ['KREP'])
    try:
        for (kind, seq, ti) in tiles:
            if os.environ.get('KBAR'):
                P.barrier()
            do_tile(kind, seq, ti)
        if do_sample:
            P.barrier()
            wstack[0].close()
            wstack[0] = ExitStack()
            alloc_work(128, "_s")
            sample_alloc()
            P.barrier()
            sample_const_setup()
            stop_at("SA")
            for st in range(n_stiles):
                SB["b0"] = st * 16
                sample_setup(st)
                stop_at("SB")
                do_tile("s", st, 0)
                sample_outputs(st)
    except _Stop:
        pass
    _pe = os.environ.get('PADE'); _pn = int(os.environ.get('PAD', '0'))
    dmy = P.sbuf("dmy", [128, 8], F32)
    for _i in range(_pn):
        if _pe == 'pe':
            P.op("pe", lambda e: e.matmul(BK[0][0:8, 0:8], cst[0:8, 0:8], cst[0:8, 0:8], start=True, stop=True), reads=["cst"], writes=[("ps", 0)])
        else:
            P.op(_pe, lambda e: e.memset(dmy[:], 0.0), writes=["dmy"])
    _pa = int(os.environ.get('KPADALL', '0'))
    for _i in range(_pa):
        P.op("pe", lambda e: e.matmul(BK[0][0:8, 0:8], cst[0:8, 0:8], cst[0:8, 0:8], start=True, stop=True), reads=["cst"], writes=[("ps", 0)])
        P.op("pe", lambda e: e.matmul(BK[1][0:8, 0:8], cst[0:8, 0:8], cst[0:8, 0:8], start=True, stop=True), reads=["cst"], writes=[("ps", 1)])
        P.op("dve", lambda e: e.memset(dmy[:], 0.0), writes=["dmy"])
        if _i % 2 == 0:
            P.op("pool", lambda e: e.memset(dmy[:, 0:4], 0.0), writes=["dmy2"])
    P.finish()
    return D


_STAGE = ""


NCORES = int(os.environ.get("KNCORES", "8"))


def kernel(**inputs):
    global STOP, NSEQ
    inp = {k: np.asarray(v) for k, v in inputs.items()}
    cst, oh = host_consts()
    STOP = ""
    NSEQ = 16 // NCORES
    NSTL = 8 // NCORES
    NB = NSTL * 16
    nc = bass.Bass("TRN2", target_bir_lowering=False)
    D = build(nc, n_ptiles=2048 // NTP * NSEQ, n_stiles=NSTL)
    in_maps = []
    for c in range(NCORES):
        bs = slice(NB * c, NB * (c + 1))
        m = {
            "xp": inp["x_prompt"][NSEQ * c:NSEQ * (c + 1)], "xsm": inp["x_sample"][bs].reshape(NB * 8, 1024),
            "ck": inp["cache_win_k"][0, bs].reshape(NB, 128, 128), "cv": inp["cache_win_v"][0, bs].reshape(NB, 128, 128),
            "sshift": inp["state_shift"][0, bs], "swkv": inp["state_wkv"][0, bs], "sconv": inp["state_conv"][0, bs].reshape(NB * 2, 2816),
            "rel_bias": inp["rel_bias"], "norm1_g": inp["norm1_g"], "w_in": inp["w_in"][0], "sinks": inp["sinks"][0],
            "mu_shift": inp["mu_shift"][0], "w0": inp["w0"][0], "w2": inp["w2"][0], "a0": inp["a0"][0], "a2": inp["a2"][0],
            "g2": inp["g2"][0], "k_k": inp["k_k"][0], "k_a": inp["k_a"][0], "r_k": inp["r_k"][0].reshape(512),
            "lnx_g": inp["lnx_g"][0], "lnx_b": inp["lnx_b"][0], "w_pa": inp["w_pa"][0], "w_pb": inp["w_pb"][0],
            "w_o": inp["w_o"][0], "norm2_g": inp["norm2_g"], "w_up": inp["w_up"][0], "conv_w": inp["conv_w"][0],
            "conv_b": inp["conv_b"][0], "w_down": inp["w_down"][0], "final_g": inp["final_g"].reshape(1, 1024),
            "consts": cst, "oh": oh,
        }
        in_maps.append({k: np.ascontiguousarray(v, dtype=np.float32) for k, v in m.items() if k in D})
    res = run_bass_kernel_spmd(nc, in_maps, core_ids=list(range(NCORES)))
    R = res.results

    def cat(name, shape):
        return np.concatenate([np.asarray(r[name], dtype=np.float32) for r in R], axis=0).reshape(shape)

    return (cat("yp", (16, 2048, 1024)), cat("ys", (128, 8, 1024)),
            cat("pk", (1, 16, 128, 2, 64)), cat("pv", (1, 16, 128, 2, 64)), cat("psh", (1, 16, 1792)),
            cat("pwkv", (1, 16, 8, 64, 64)), cat("pconv", (1, 16, 2, 2816)),
            cat("sk", (1, 128, 128, 2, 64)), cat("sv", (1, 128, 128, 2, 64)), cat("ssh", (1, 128, 1792)),
            cat("swkvo", (1, 128, 8, 64, 64)), cat("sconvo", (1, 128, 2, 2816)))
```

```python
import numpy as np
from contextlib import ExitStack
import concourse.bass as bass
import concourse.mybir as mybir
from concourse.bass_utils import run_bass_kernel_spmd

F32 = mybir.dt.float32
BF16 = mybir.dt.bfloat16
AF = mybir.ActivationFunctionType
ALU = mybir.AluOpType
AX = mybir.AxisListType

import os as _os0
SAME_ENGINE_SYNC = _os0.environ.get("SES", "1") == "1"
EPOCH = 30000
RELAX_FSZ = int(_os0.environ.get('KRELAX', '0'))
N_DMA_SEMS = {"sp": 8, "pool": 4}


class _Op:
    __slots__ = ("eng", "fn", "deps", "isdma", "ms", "dsem", "dval", "needed", "desc", "fsz")


class _Rec:
    def __init__(self):
        self.call = None

    def __getattr__(self, name):
        def f(*a, **k):
            assert self.call is None
            self.call = (name, a, k)
            return self
        return f


class Prog:
    def __init__(self, nc):
        self.nc = nc
        self.ops = []
        self.lastw = {}
        self.rd_c = {}
        self.rd_d = {}
        self.stack = ExitStack()
        self.last_op = {}
        self.pending = {}
        self.bar_from = 0

    def sbuf(self, name, shape, dtype):
        return self.stack.enter_context(self.nc.sbuf_tensor(name, list(shape), dtype))

    def psum(self, name, shape, dtype):
        return self.stack.enter_context(self.nc.psum_tensor(name, list(shape), dtype))

    def op(self, eng, fn, reads=(), writes=(), isdma=False):
        idx = len(self.ops)
        deps = set()
        for r in reads:
            w = self.lastw.get(r)
            if w is not None:
                deps.add(w)
        for r in writes:
            w = self.lastw.get(r)
            if w is not None:
                deps.add(w)
            for i in self.rd_c.get(r, {}).values():
                deps.add(i)
            for i in self.rd_d.get(r, ()):
                deps.add(i)
        for r in writes:
            self.lastw[r] = idx
            self.rd_c[r] = {}
            self.rd_d[r] = []
        ws = set(writes)
        for r in reads:
            if r in ws:
                continue
            if isdma:
                self.rd_d.setdefault(r, []).append(idx)
            else:
                self.rd_c.setdefault(r, {})[eng] = idx
        if eng in self.pending:
            deps.update(self.pending.pop(eng))
        rec = _Rec()
        fn(rec)
        name_, a_, k_ = rec.call
        o = _Op()
        o.desc = name_ + " w=" + str(list(writes))[:60]
        o.fsz = 0
        try:
            out_ap = k_.get("out", a_[0] if a_ else None)
            shp = tuple(out_ap.shape)
            n = 1
            for d_ in shp[1:]:
                n *= int(d_)
            o.fsz = n
        except Exception:
            o.fsz = 0
        o.eng, o.fn, o.deps, o.isdma = eng, (lambda e: getattr(e, name_)(*a_, **k_)), deps, isdma
        o.ms = None
        o.dsem = None
        o.dval = None
        o.needed = False
        self.ops.append(o)
        self.last_op[eng] = idx
        return idx

    def barrier(self):
        prev = set(self.last_op.values())
        prev.update(i for i in range(self.bar_from, len(self.ops)) if self.ops[i].isdma)
        self.bar_from = len(self.ops)
        for eng in ("pe", "act", "dve", "pool", "sp"):
            self.pending.setdefault(eng, set()).update(prev)

    def dma(self, q, out, in_, reads=(), writes=(), **kw):
        return self.op(q, lambda e: e.dma_start(out=out, in_=in_, **kw), reads, writes, isdma=True)

    def finish(self):
        nc = self.nc
        ops = self.ops
        last_dma = [i for i, o in enumerate(ops) if o.isdma]
        for i, o in enumerate(ops):
            best = {}
            nd = set()
            for d in o.deps:
                od = ops[d]
                if od.isdma:
                    nd.add(d)
                else:
                    if od.eng == o.eng and not o.isdma:
                        if od.eng == "pe" or not SAME_ENGINE_SYNC:
                            continue
                        if RELAX_FSZ and od.fsz >= RELAX_FSZ and od.eng in ("dve", "act"):
                            continue
                    if best.get(od.eng, -1) < d:
                        best[od.eng] = d
            nd.update(best.values())
            o.deps = nd
            for d in nd:
                ops[d].needed = True
        for i in last_dma:
            ops[i].needed = True
        tail_ops = []
        for en in ("pe", "act", "dve", "pool"):
            idxs = [i for i, o in enumerate(ops) if o.eng == en and not o.isdma]
            if idxs:
                ops[idxs[-1]].needed = True
                tail_ops.append(idxs[-1])
        cnt = {e: 0 for e in ("pe", "act", "dve", "pool", "sp")}
        dcount = {}
        nd_used = {"sp": 0, "pool": 0}
        for o in ops:
            if o.isdma:
                k = nd_used[o.eng] % N_DMA_SEMS[o.eng]
                nd_used[o.eng] += 1
                key = (o.eng, k)
                dcount[key] = dcount.get(key, 0) + 1
                o.dsem = key
                o.dval = 16 * dcount[key]
            elif o.needed:
                o.ms = cnt[o.eng]
                cnt[o.eng] += 1
        sems = {}
        for e in cnt:
            for ep in range(cnt[e] // EPOCH + 1):
                sems[(e, ep)] = self.stack.enter_context(nc.semaphore("m_%s_%d" % (e, ep)))
        dsems = {}
        for key in dcount:
            dsems[key] = self.stack.enter_context(nc.semaphore("d_%s_%d" % key))
        final_dma = {}
        for i in last_dma:
            final_dma[ops[i].dsem] = max(final_dma.get(ops[i].dsem, 0), ops[i].dval)
        by_eng = {e: [] for e in cnt}
        for o in ops:
            by_eng[o.eng].append(o)
        import os as _os
        dump = _os.environ.get("DUMP")

        def emit(ename, e):
            known = {}
            for o in by_eng[ename]:
                if o.isdma and o.dval > 16:
                    k = ("d",) + o.dsem
                    if known.get(k, 0) < o.dval - 16:
                        e.wait_ge(dsems[o.dsem], o.dval - 16)
                        known[k] = o.dval - 16
                for d in sorted(o.deps):
                    od = ops[d]
                    if od.isdma:
                        k = ("d",) + od.dsem
                        if known.get(k, 0) < od.dval:
                            e.wait_ge(dsems[od.dsem], od.dval)
                            known[k] = od.dval
                            if dump: print("   ", ename, "WAITD", od.dsem, od.dval)
                    else:
                        k = ("m", od.eng)
                        if known.get(k, -1) < od.ms:
                            ep = od.ms // EPOCH
                            e.wait_ge(sems[(od.eng, ep)], od.ms % EPOCH + 1)
                            known[k] = od.ms
                            if dump: print("   ", ename, "WAITM", od.eng, od.ms + 1)
                ins = o.fn(e)
                if dump: print(ename, "OP", o.desc, "ms", o.ms)
                if o.isdma:
                    ins.then_inc(dsems[o.dsem], 16)
                elif o.ms is not None:
                    ins.then_inc(sems[(o.eng, o.ms // EPOCH)], 1)
            if ename == "sp":
                for key, v in final_dma.items():
                    e.wait_ge(dsems[key], v)
                for i in tail_ops:
                    od = ops[i]
                    e.wait_ge(sems[(od.eng, od.ms // EPOCH)], od.ms % EPOCH + 1)

        with nc.Block() as block:
            @block.tensor
            def _(e):
                emit("pe", e)

            @block.scalar
            def _(e):
                emit("act", e)

            @block.vector
            def _(e):
                emit("dve", e)

            @block.gpsimd
            def _(e):
                emit("pool", e)

            @block.sync
            def _(e):
                emit("sp", e)
        self.stack.close()

import os
STOP = os.environ.get('KSTOP', '')
SKIP = os.environ.get('KSKIP', '').split(',')
NTP = 256
NSEQ = int(os.environ.get('KNSEQ', '2'))
KAPPA = 0.6065306597126334
NEGB = -30000.0
RW_DT = F32
C_ID, C_BO, C_BO64, C_ONE, C_M64, C_L64, C_M8, C_L8, C_BSEL, C_END = 0, 128, 256, 384, 512, 768, 896, 1152, 1280, 1296


def host_consts():
    c = np.zeros((128, C_END), np.float32)
    c[:, C_ID:C_ID + 128] = np.eye(128)
    blk = (np.arange(128)[:, None] // 64 == np.arange(128)[None] // 64)
    c[:, C_BO:C_BO + 128] = blk
    c[:, C_BO64:C_BO64 + 128] = blk / 64.0
    c[:, C_ONE:C_ONE + 128] = 1.0
    s = np.arange(128)[:, None]
    t = np.arange(128)[None]
    for (C, cm, cl) in ((64, C_M64, C_L64), (8, C_M8, C_L8)):
        same = (s // C == t // C)
        c[:, cm:cm + 128] = same & (s < t)
        c[:, cm + 128:cm + 256] = same & (s <= t)
        c[:, cl:cl + 128] = same & (s > t)
    c[:, C_BSEL:C_BSEL + 16] = (np.arange(128)[:, None] // 8 == np.arange(16)[None])
    def bucket(d):
        d = np.asarray(d)
        n = np.maximum(d, 0)
        nf = np.maximum(n, 1).astype(np.float32)
        large = 16 + (np.log(nf / np.float32(16)) / np.float32(np.log(128 / 16)) * np.float32(16)).astype(np.int32)
        return np.where(n < 16, n, np.minimum(large, 31))
    oh = np.zeros((33, 2, 384), np.float32)
    m = np.arange(384)
    for x in range(2):
        dist = (m - 128) if x == 0 else m
        valid = (dist >= 0) & (dist < 128) if x == 0 else (m >= 1) & (m < 128)
        b = bucket(np.clip(dist, 0, 255))
        for mm_ in range(384):
            if valid[mm_]:
                oh[b[mm_], x, mm_] = 1.0
            else:
                oh[32, x, mm_] = NEGB
    return c, oh.reshape(33, 768)


class _Stop(Exception):
    pass


def stop_at(tag):
    if STOP == tag:
        raise _Stop()


def build(nc, n_ptiles=16, do_sample=True, dbg=(), n_stiles=None):
    if n_stiles is None:
        n_stiles = 1 if do_sample else 0
    do_sample = n_stiles > 0
    NST = max(n_stiles, 1)
    P = Prog(nc)
    D = {}

    def din(name, shape):
        D[name] = nc.dram_tensor(name, list(shape), F32, kind="ExternalInput").ap()
        return D[name]

    def dout(name, shape):
        D[name] = nc.dram_tensor(name, list(shape), F32, kind="ExternalOutput").ap()
        return D[name]

    xp = din("xp", [NSEQ, 2048, 1024]); xsm = din("xsm", [NST * 128, 1024])
    ck = din("ck", [NST * 16, 128, 128]); cv = din("cv", [NST * 16, 128, 128])
    sshift = din("sshift", [NST * 16, 1792]); swkv = din("swkv", [NST * 16, 8, 64, 64]); sconv = din("sconv", [NST * 32, 2816])
    rel_bias = din("rel_bias", [32, 8]); norm1_g = din("norm1_g", [1, 1024]); w_in = din("w_in", [1024, 4608])
    sinks = din("sinks", [8]); mu_shift = din("mu_shift", [1792]); w0 = din("w0", [512]); w2 = din("w2", [64, 512])
    a0 = din("a0", [512]); a2 = din("a2", [64, 512]); g2 = din("g2", [128, 512]); k_k = din("k_k", [512])
    k_a = din("k_a", [512]); r_k = din("r_k", [512]); lnx_g = din("lnx_g", [512]); lnx_b = din("lnx_b", [512])
    w_pa = din("w_pa", [512, 1024]); w_pb = din("w_pb", [512, 1024]); w_o = din("w_o", [1024, 1024])
    norm2_g = din("norm2_g", [1, 1024]); w_up = din("w_up", [1024, 5632]); conv_w = din("conv_w", [3, 2816])
    conv_b = din("conv_b", [2816]); w_down = din("w_down", [2816, 1024]); final_g = din("final_g", [1, 1024])
    consts_d = din("consts", [128, C_END]); oh_d = din("oh", [33, 768])

    yp = dout("yp", [NSEQ, 2048, 1024]); ys = dout("ys", [NST * 128, 1024])
    pk = dout("pk", [NSEQ, 128, 128]); pv = dout("pv", [NSEQ, 128, 128]); psh = dout("psh", [NSEQ, 1792])
    pwkv = dout("pwkv", [NSEQ, 8, 64, 64]); pconv = dout("pconv", [NSEQ, 2, 2816])
    sk = dout("sk", [NST * 16, 128, 128]); sv = dout("sv", [NST * 16, 128, 128]); ssh = dout("ssh", [NST * 16, 1792])
    swkvo = dout("swkvo", [NST * 16, 8, 64, 64]); sconvo = dout("sconvo", [NST * 32, 2816])
    dbg_out = {}

    def scratch(name, shape, dtype=BF16):
        return nc.dram_tensor(name, list(shape), dtype, kind="Internal").ap()

    wsc_in = scratch("wsc_in", [9, 128, 4096]); wsc_pa = scratch("wsc_pa", [2, 128, 2048])
    wsc_pb = scratch("wsc_pb", [2, 128, 2048]); wsc_o = scratch("wsc_o", [2, 128, 4096])
    wsc_up = scratch("wsc_up", [11, 128, 4096]); wsc_dn = scratch("wsc_dn", [6, 128, 4096])
    E_d = scratch("E_d", [16, 128, 384], F32)

    chunks_src = []
    for j in range(4):
        chunks_src.append([(j * 64, 64), ((4 + j) * 64, 64)])
    chunks_src.append([(512, 128)]); chunks_src.append([(640, 128)])
    rwb = 768
    chunks_src.append([(rwb + 1536, 128)]); chunks_src.append([(rwb + 1664, 128)])
    for p in range(4):
        chunks_src += [[(rwb + p * 128, 128)], [(rwb + 512 + p * 128, 128)], [(rwb + 1024 + p * 128, 128)]]
    for g in range(16):
        chunks_src.append([(2560 + g * 128, 128)])
    for c, srcs in enumerate(chunks_src):
        b, cc = divmod(c, 4)
        dst = wsc_in[b].rearrange("p (k n) -> p k n", n=512)
        off = cc * 128
        for (lo, n) in srcs:
            P.dma("pool", dst[:, :, off:off + n], w_in[:, lo:lo + n].rearrange("(k p) n -> p k n", p=128),
                  writes=[("wsc_in", c, lo)])
            off += n
    for ch in range(2):
        dpa = wsc_pa[ch].rearrange("p (k n) -> p k n", n=512)
        for j in range(4):
            for half in range(2):
                r0 = (half * 4 + j) * 64
                P.dma("pool", dpa[half * 64:(half + 1) * 64, j, :], w_pa[r0:r0 + 64, ch * 512:(ch + 1) * 512],
                      writes=[("wsc_pa", ch, j, half)])
        P.dma("pool", wsc_pb[ch].rearrange("p (k n) -> p k n", n=512),
              w_pb[:, ch * 512:(ch + 1) * 512].rearrange("(k p) n -> p k n", p=128), writes=[("wsc_pb", ch)])
        P.dma("pool", wsc_o[ch].rearrange("p (k n) -> p k n", n=512),
              w_o[:, ch * 512:(ch + 1) * 512].rearrange("(k p) n -> p k n", p=128), writes=[("wsc_o", ch)])
    for b in range(11):
        dst = wsc_up[b].rearrange("p (k n) -> p k n", n=512)
        P.dma("pool", dst[:, :, 0:256], w_up[:, b * 256:(b + 1) * 256].rearrange("(k p) n -> p k n", p=128),
              writes=[("wsc_up", b, 0)])
        P.dma("pool", dst[:, :, 256:512], w_up[:, 2816 + b * 256:2816 + (b + 1) * 256].rearrange("(k p) n -> p k n", p=128),
              writes=[("wsc_up", b, 1)])
    DN_NK = (8, 8, 6)
    for ch in range(2):
        for rg in range(3):
            nk = DN_NK[rg]
            dst = wsc_dn[ch * 3 + rg].rearrange("p (k n) -> p k n", n=512)
            P.dma("pool", dst[:, 0:nk, :],
                  w_down[rg * 1024:rg * 1024 + nk * 128, ch * 512:(ch + 1) * 512].rearrange("(k p) n -> p k n", p=128),
                  writes=[("wsc_dn", ch * 3 + rg)])

    if STOP == 'A':
        P.finish(); return D
    blk_sched = []
    for b in range(9):
        keys = []
        for c in range(4 * b, 4 * b + 4):
            keys += [("wsc_in", c, lo) for (lo, n) in chunks_src[c]]
        blk_sched.append((wsc_in[b], 4096, keys))
    for ch in range(2):
        blk_sched.append((wsc_pa[ch], 2048, [("wsc_pa", ch, j, h) for j in range(4) for h in range(2)]))
        blk_sched.append((wsc_pb[ch], 2048, [("wsc_pb", ch)]))
    for ch in range(2):
        blk_sched.append((wsc_o[ch], 4096, [("wsc_o", ch)]))
    for b in range(11):
        blk_sched.append((wsc_up[b], 4096, [("wsc_up", b, 0), ("wsc_up", b, 1)]))
    for i in range(6):
        blk_sched.append((wsc_dn[i], DN_NK[i % 3] * 512, [("wsc_dn", i)]))
    NBLK_T = len(blk_sched)
    n_tiles_total = n_ptiles + n_stiles
    NSLOT = 3
    ring_tiles = [P.sbuf("ring%d" % i, [128, 4096], BF16) for i in range(NSLOT)]
    ring_state = {"loaded": 0, "consumed": 0}
    total_blocks = NBLK_T * n_tiles_total

    def ring_get(hold=0):
        k = ring_state["consumed"]
        while ring_state["loaded"] < min(k + NSLOT - hold, total_blocks):
            j = ring_state["loaded"]
            src, nel, keys = blk_sched[j % NBLK_T]
            s = j % NSLOT
            P.dma("sp", ring_tiles[s][:, 0:nel], src[:, 0:nel], reads=keys, writes=[("ring", s)])
            ring_state["loaded"] += 1
        ring_state["consumed"] += 1
        s = k % NSLOT
        return ring_tiles[s], ("ring", s)

    cst = P.sbuf("cst", [128, C_END], F32)
    P.dma("sp", cst[:], consts_d, writes=["cst"])
    ident = cst[:, C_ID:C_ID + 128]
    bo = cst[:, C_BO:C_BO + 128]
    bo64 = cst[:, C_BO64:C_BO64 + 128]
    ones = cst[:, C_ONE:C_ONE + 128]
    ones_bf = P.sbuf("ones_bf", [128, 128], BF16)
    P.op("dve", lambda e: e.tensor_copy(ones_bf[:], ones), reads=["cst"], writes=["ones_bf"])
    zeros_t = P.sbuf("zeros_t", [128, 64], F32)
    P.op("pool", lambda e: e.memset(zeros_t[:], 0.0), writes=["zeros_t"])

    def load_cols(name, vec, ncol):
        t = P.sbuf(name, [128, ncol], F32)
        P.dma("sp", t[:], vec.rearrange("(c p) -> p c", p=128), writes=[name], allow_slow_non_contiguous=True)
        return t

    mu_c = load_cols("mu_c", mu_shift, 14)
    om_c = P.sbuf("om_c", [128, 14], F32)
    P.op("dve", lambda e: e.tensor_scalar(om_c[:], mu_c[:], -1.0, 1.0, ALU.mult, ALU.add), reads=["mu_c"], writes=["om_c"])
    w0_c = load_cols("w0_c", w0, 4); a0_c = load_cols("a0_c", a0, 4); kk_c = load_cols("kk_c", k_k, 4)
    ka_c = load_cols("ka_c", k_a, 4); rk_c = load_cols("rk_c", r_k, 4); lg_c = load_cols("lg_c", lnx_g, 4)
    lb_c = load_cols("lb_c", lnx_b, 4); cb_c = load_cols("cb_c", conv_b, 22)
    cw_c = P.sbuf("cw_c", [128, 3, 22], F32)
    for j in range(3):
        P.dma("sp", cw_c[:, j, :], conv_w[j].rearrange("(c p) -> p c", p=128), writes=[("cw_c", j)], allow_slow_non_contiguous=True)
    CW_KEYS = [("cw_c", j) for j in range(3)]
    gb = {}
    g1c = load_cols("g1b", norm1_g.rearrange("a n -> (a n)"), 8)
    g2c = load_cols("g2b", norm2_g.rearrange("a n -> (a n)"), 8)
    gcol = {"g1b": g1c, "g2b": g2c}
    for nm, src in (("gfb", final_g),):
        gb[nm] = P.sbuf(nm, [128, 1024], F32)
        P.dma("sp", gb[nm][:], src.partition_broadcast(128).rearrange("p a n -> p (a n)"), writes=[nm])
    w2b = P.sbuf("w2b", [128, 512], BF16); g2bf = P.sbuf("g2bf", [128, 512], BF16)
    P.dma("pool", w2b[0:64, :], w2, writes=["w2b"])
    P.dma("pool", w2b[64:128, :], a2, writes=["a2b"])
    P.dma("pool", g2bf[:], g2, writes=["g2bf"])
    sk_t = P.sbuf("sk_t", [128, 4], F32)
    P.dma("sp", sk_t[0:64, :], sinks[0:4].partition_broadcast(64), writes=[("sk_t", 0)])
    P.dma("sp", sk_t[64:128, :], sinks[4:8].partition_broadcast(64), writes=[("sk_t", 1)])
    esk = P.sbuf("esk", [128, 4], F32)
    P.op("act", lambda e: e.activation(esk[:], sk_t[:], AF.Exp), reads=[("sk_t", 0), ("sk_t", 1)], writes=["esk"])
    esink = P.sbuf("esink", [128, 4, 128], F32)
    for g in range(4):
        P.op("act", lambda e, g=g: e.activation(esink[:, g, :], ones, AF.Copy, scale=esk[:, g:g + 1]),
             reads=["cst", "esk"], writes=[("esink", g)])
    ESINK_KEYS = [("esink", g) for g in range(4)]

    xn = P.sbuf("xn", [128, 1024], F32)
    if STOP == 'B':
        P.finish(); return D
    for _i in range(int(os.environ.get('KDUMMY', '0'))):
        P.dma('sp', zeros_t[:, 0:32], consts_d[:, 0:32], writes=['zeros_dummy'])
    for _i in range(int(os.environ.get('KBIG', '0'))):
        P.dma('sp', yp[1], xp[0], writes=['yp1_dummy'])
    BK = [P.psum("bank%d" % i, [128, 512], F32) for i in range(8)]

    def bk(i):
        return ("ps", i)

    rb = P.sbuf("rb", [33, 8], F32)
    Yt = P.sbuf("Yt", [128, 4, NTP], F32)
    bon = P.sbuf("bon", [128, 4, NTP], F32)
    Lh = Yt[0:33, :, :].rearrange("p a t -> p (a t)").rearrange("p (h t) -> p h t", t=128)
    oh_t = bon[0:33, :, :].rearrange("p a t -> p (a t)")[:, 0:768]
    P.dma("sp", rb[0:32, :], rel_bias, writes=["rb"])
    P.dma("sp", oh_t, oh_d, writes=["oh_t"])
    P.op("pool", lambda e: e.memset(Lh[32:33, :, :], 1.0), writes=["Lh1"])
    for h in range(8):
        P.op("act", lambda e, h=h: e.activation(Lh[0:32, h, :], cst[0:32, C_ONE:C_ONE + 128], AF.Copy, scale=rb[0:32, h:h + 1]),
             reads=["cst", "rb"], writes=[("Lh", h)])
    biasT = [[P.sbuf("biasT%d%d" % (x, kv), [128, 4, 128], F32) for kv in range(2)] for x in range(2)]
    for x in range(2):
        for h in range(8):
            bnk = 4 + (x * 8 + h) % 4
            P.op("pe", lambda e, h=h, x=x, bnk=bnk: e.matmul(BK[bnk][:, 0:384], Lh[:, h, :], oh_t[:, x * 384:(x + 1) * 384], start=True, stop=True),
                 reads=["Lh1", ("Lh", h), "oh_t"], writes=[bk(bnk)])
            P.op("dve", lambda e, bnk=bnk: e.tensor_copy(xn[:, 0:384], BK[bnk][:, 0:384]), reads=[bk(bnk)], writes=["xn"])
            P.dma("sp", E_d[x * 8 + h], xn[:, 0:384], reads=["xn"], writes=[("E_d", x, h)])
            kv, g = divmod(h, 4)
            skew = bass.AP(E_d.tensor, (x * 8 + h) * 128 * 384 + 128, [[383, 128], [1, 128]])
            P.dma("sp", biasT[x][kv][:, g, :], skew, reads=[("E_d", x, h)], writes=[("biasT", x, kv, g)])
    if STOP == 'C':
        P.finish(); return D
    if os.environ.get('NOBAR') is None:
        P.barrier()
    BIAS_KEYS = {(x, kv): [("biasT", x, kv, g) for g in range(4)] for x in range(2) for kv in range(2)}

    wstack = [ExitStack()]

    def wsb(name, shape, dtype):
        return wstack[0].enter_context(nc.sbuf_tensor(name, list(shape), dtype))

    PT2 = None
    xn2 = None
    Pvg = None
    xt = None
    stat = None
    hT = None
    qT = None
    kT = None
    kTf = None
    vTf = None
    vtok = None
    kvrow = None
    gates = None
    attT = None
    rwoT = None
    PT = None
    rd_t = None
    pbuf = None
    tmpA = None
    xs3 = None
    TW = None
    SG = None
    car = None
    art = None
    kt_ = None
    bt_ = None
    gT = None
    cCt = None
    KPtok = None
    BPtok = None
    Vtok = None
    rw = None
    S = None
    AK = None
    AB = None
    Mm_ = None
    Mt_ = None
    Pv = None
    VA = None
    VK = None
    XT = None
    UT = None
    y1 = None
    ugx = None
    cc_t = None
    gl_t = None
    actT = None
    ccar = None

    def alloc_work(NTM, sfx):
        nonlocal PT2, xn2, Pvg, xt, stat, hT, qT, kT, kTf, vTf, vtok, kvrow, gates, attT, rwoT, PT, rd_t, pbuf, tmpA, xs3, TW, SG, car, art, kt_, bt_, gT, cCt, KPtok, BPtok, Vtok, rw, S, AK, AB, Mm_, Mt_, Pv, VA, VK, XT, UT, y1, ugx, cc_t, gl_t, actT, ccar
        NB_MAX = NTM // 128
        xt = [wsb(("xt%d" % i) + sfx, [128, 1024], F32) for i in range(NB_MAX)]
        stat = wsb("stat" + sfx, [128, 32], F32)
        xn2 = wsb("xn2" + sfx, [128, 1024], F32) if NTM > 128 else None
        hT = wsb("hT" + sfx, [128, 8, NTM], BF16)
        qT = wsb("qT" + sfx, [128, 4, NTM], BF16)
        kT = wsb("kT" + sfx, [128, 128 + NTM], BF16)
        kTf = wsb("kTf" + sfx, [128, NTM], F32)
        vTf = wsb("vTf" + sfx, [128, NTM], F32)
        vtok = wsb("vtok" + sfx, [128, NB_MAX + 1, 128], BF16)
        kvrow = wsb("kvrow" + sfx, [128, 2, 128], F32)
        gates = wsb("gates" + sfx, [128, 16, NTM], BF16)
        attT = wsb("attT" + sfx, [128, 4, NTM], BF16)
        rwoT = wsb("rwoT" + sfx, [128, 4, NTM], BF16)
        PT = [wsb(("PT%d" % i) + sfx, [128, 512], BF16) for i in range(2)]
        PT2 = [wsb(("PTb%d" % i) + sfx, [128, 512], BF16) for i in range(2)] if NTM > 128 else PT
        rd_t = wsb("rd_t" + sfx, [128, 512], F32)
        pbuf = wsb("pbuf" + sfx, [128, NTM + 16], F32)
        tmpA = wsb("tmpA" + sfx, [128, NTM], F32)
        xs3 = [wsb(("xs3_%d" % i) + sfx, [128, NTM], F32) for i in range(3)]
        TW = wsb("TW" + sfx, [128, NTM], BF16)
        SG = wsb("SG" + sfx, [128, NTM], BF16)
        car = wsb("car" + sfx, [128, 128], F32)
        art = wsb("art" + sfx, [128, 4, 2, NTM], RW_DT)
        kt_ = wsb("kt_" + sfx, [128, 4, NTM], RW_DT)
        bt_ = wsb("bt_" + sfx, [128, 4, NTM], RW_DT)
        gT = wsb("gT" + sfx, [128, 4, NTM], F32)
        cCt = wsb("cCt" + sfx, [128, 4, NTM // 8], F32)
        KPtok = wsb("KPtok" + sfx, [128, NB_MAX, 512], RW_DT)
        BPtok = wsb("BPtok" + sfx, [128, NB_MAX, 512], RW_DT)
        Vtok = wsb("Vtok" + sfx, [128, NB_MAX, 512], RW_DT)
        rw = {n: wsb("rw_" + sfx + n, [128, NTM], F32) for n in ("sg", "cs", "t1", "t2", "t3", "a", "kkn", "k2", "ein", "einv", "eend", "nk")}
        S = wsb("S" + sfx, [128, 4, 64], F32)
        AK = [wsb(("AK%d" % h) + sfx, [128, 256], RW_DT) for h in range(8)]
        AB = [wsb(("AB%d" % h) + sfx, [128, 256], RW_DT) for h in range(8)]
        Mm_ = [wsb(("Mmg%d" % g) + sfx, [128, 4, 128], RW_DT) for g in range(2)]
        Mt_ = [wsb(("Mtg%d" % g) + sfx, [128, 4, 128], RW_DT) for g in range(2)]
        Pvg = [wsb(("Pvg%d" % g) + sfx, [128, 4, 128], RW_DT) for g in range(2)]
        Pv = [Pvg[h // 4][:, h % 4, :] for h in range(8)]
        VA = wsb("VA" + sfx, [128, 512], F32)
        VK = wsb("VK" + sfx, [128, 4, 128], F32)
        XT = wsb("XT" + sfx, [128, 512], RW_DT)
        UT = wsb("UT" + sfx, [128, 512], RW_DT)
        y1 = wsb("y1" + sfx, [128, 4, 64], F32)
        ugx = [wsb(("ugx%d" % i) + sfx, [128, NTM + 40], F32) for i in range(2)]
        cc_t = [wsb(("cc_t%d" % i) + sfx, [128, NTM], F32) for i in range(2)]
        gl_t = [wsb(("gl_t%d" % i) + sfx, [128, NTM], F32) for i in range(2)]
        actT = wsb("actT" + sfx, [128, 22, NTM], BF16)
        ccar = wsb("ccar" + sfx, [128, 2, 128], F32)

    P.stack.callback(lambda: wstack[0].close())
    alloc_work(NTP, "")
    P.op("pool", lambda e: e.memset(car[:], 0.0), writes=["car_init"])
    P.op("pool", lambda e: e.memset(ccar[:].rearrange("p a f -> p (a f)"), 0.0), writes=["ccar_init"])

    def debug_tap(name, ap_sb, shape, keys):
        if name in dbg:
            d_ = nc.dram_tensor("dbg_" + name, list(shape), ap_sb.dtype, kind="ExternalOutput").ap()
            D["dbg_" + name] = d_
            P.dma("sp", d_, ap_sb, reads=keys, writes=[("dbg", name)])

    def do_tile(kind, seq, ti):
        prompt = kind == "p"
        NT = NTP if prompt else 128
        nb = NT // 128
        NBs, TB = (1, NT) if prompt else (16, 8)
        C = 64 if prompt else 8
        LV = 5 if prompt else 2
        cm, cl = (C_M64, C_L64) if prompt else (C_M8, C_L8)
        first = prompt and ti == 0
        last = prompt and ti == 2048 // NTP - 1
        xrows = xp[seq, ti * NT:(ti + 1) * NT, :] if prompt else xsm[seq * 128:(seq + 1) * 128, :]
        yrows = yp[seq, ti * NT:(ti + 1) * NT, :] if prompt else ys[seq * 128:(seq + 1) * 128, :]

        for b in range(nb):
            P.dma("sp", xt[b][:], xrows[b * 128:(b + 1) * 128, :], writes=[("xt", b)])

        stop_at("D1")

        def norm_T(gname):
            def steps(b):
                so = 16 * b
                st = stat[:, so:so + 16]
                xnb = xn if b == 0 else xn2
                xk_ = "xn" if b == 0 else "xn2"
                sk_ = "stat%d_" % b
                bks = (4, 5) if b == 0 else (6, 7)

                def tr(half):
                    for q4 in range(4):
                        kc = half * 4 + q4
                        P.op("pe", lambda e, kc=kc, q4=q4: e.transpose(BK[bks[half]][:, q4 * 128:(q4 + 1) * 128], xnb[:, kc * 128:(kc + 1) * 128], ident),
                             reads=[xk_, "cst"], writes=[bk(bks[half])])

                def ev(half):
                    for q4 in range(4):
                        kc = half * 4 + q4
                        if half == 0:
                            P.op("act", lambda e, kc=kc, q4=q4: e.activation(hT[:, kc, b * 128:(b + 1) * 128], BK[bks[0]][:, q4 * 128:(q4 + 1) * 128], AF.Copy, scale=gcol[gname][:, kc:kc + 1]),
                                 reads=[bk(bks[0]), gname], writes=[("hT", kc)])
                        else:
                            P.op("dve", lambda e, kc=kc, q4=q4: e.tensor_scalar(hT[:, kc, b * 128:(b + 1) * 128], BK[bks[1]][:, q4 * 128:(q4 + 1) * 128], gcol[gname][:, kc:kc + 1], None, ALU.mult),
                                 reads=[bk(bks[1]), gname], writes=[("hT", kc)])
                return [
                    lambda: P.op("dve", lambda e: e.bn_stats(st[:, 0:6], xt[b][:, 0:512]), reads=[("xt", b)], writes=[sk_ + "0"]),
                    lambda: P.op("dve", lambda e: e.bn_stats(st[:, 6:12], xt[b][:, 512:1024]), reads=[("xt", b)], writes=[sk_ + "1"]),
                    lambda: P.op("dve", lambda e: e.bn_aggr(st[:, 12:14], st[:, 0:12]), reads=[sk_ + "0", sk_ + "1"], writes=[sk_ + "2"]),
                    lambda: P.op("dve", lambda e: e.scalar_tensor_tensor(st[:, 14:15], st[:, 12:13], st[:, 12:13], st[:, 13:14], ALU.mult, ALU.add), reads=[sk_ + "2"], writes=[sk_ + "3"]),
                    lambda: P.op("act", lambda e: e.activation(st[:, 15:16], st[:, 14:15], AF.Sqrt, bias=1e-6), reads=[sk_ + "3"], writes=[sk_ + "4"]),
                    lambda: P.op("dve", lambda e: e.reciprocal(st[:, 15:16], st[:, 15:16]), reads=[sk_ + "4"], writes=[sk_ + "4"]),
                    lambda: P.op("dve", lambda e: e.tensor_scalar(xnb[:], xt[b][:], st[:, 15:16], None, ALU.mult), reads=[("xt", b), sk_ + "4"], writes=[xk_]),
                    lambda: tr(0), lambda: tr(1), lambda: ev(0), lambda: ev(1),
                ]
            for group in zip(*[steps(b) for b in range(nb)]):
                for step in group:
                    step()
        HT_KEYS = [("hT", kc) for kc in range(8)]

        norm_T("g1b")
        stop_at("D")

        def v3(ap2, k):
            return ap2.rearrange("p (b t) -> p b t", t=k)

        ts_ctr = [0]

        def token_shift(bnk, n, dst, dst_key):
            ti_ = ts_ctr[0] % 2
            ts_ctr[0] += 1
            tA = tmpA if ti_ == 0 else pbuf
            tkey, tkey0 = ("tmpA", ti_), ("tmpA0", ti_)
            PS3 = v3(BK[bnk][:, 0:NT], TB)
            tA3 = v3(tA[:, 0:NT], TB)
            P.op("act", lambda e: e.activation(tA3[:, :, 1:TB], PS3[:, :, 0:TB - 1], AF.Copy, scale=mu_c[:, n:n + 1]), reads=[bk(bnk), "mu_c"], writes=[tkey])
            if prompt:
                if first:
                    P.op("pool", lambda e: e.memset(tA[:, 0:1], 0.0), writes=[tkey0])
                else:
                    P.op("pool", lambda e: e.tensor_scalar(tA[:, 0:1], car[:, n:n + 1], mu_c[:, n:n + 1], None, ALU.mult), reads=[("car", n), "mu_c"], writes=[tkey0])
            else:
                P.op("pool", lambda e: e.tensor_scalar(tA3[:, :, 0:1], SB["shcar"][:, n, :].unsqueeze(2), mu_c[:, n:n + 1], None, ALU.mult), reads=[("shcar", n), "mu_c"], writes=[tkey0])
            P.op("dve", lambda e: e.scalar_tensor_tensor(v3(dst, TB), PS3, om_c[:, n:n + 1], tA3, ALU.mult, ALU.add),
                 reads=[bk(bnk), tkey, tkey0, "om_c"], writes=[dst_key])
            if prompt:
                P.op("dve", lambda e: e.tensor_copy(car[:, n:n + 1], BK[bnk][:, NT - 1:NT]), reads=[bk(bnk)], writes=[("car", n)])
            else:
                P.op("dve", lambda e: e.tensor_copy(SB["shout"][:, n, :].unsqueeze(2), PS3[:, :, TB - 1:TB]), reads=[bk(bnk)], writes=[("shout", n)])

        def xs_set(p):
            if p % 2 == 0:
                return (xs3[0], xs3[1], xs3[2]), ("xs0", "xs1", "xs2")
            return (cc_t[0], cc_t[1], gl_t[0]), (("cc_t", 0), ("cc_t", 1), ("gl_t", 0))

        def pair_process(p):
            xbufs, xkeys = xs_set(p)
            xr, xk, xv = xbufs[0][:, 0:NT], xbufs[1][:, 0:NT], xbufs[2][:, 0:NT]
            kx0, kx1, kx2 = xkeys
            R = {n: rw[n][:, 0:NT] for n in rw}
            nch = NT // C
            b6, b7 = 6, 7
            P.op("pool", lambda e: e.tensor_scalar(R["kkn"], xk, kk_c[:, p:p + 1], None, ALU.mult), reads=[kx1, "kk_c"], writes=["r_kkn"])
            P.op("pool", lambda e: e.tensor_tensor(R["t2"], R["kkn"], R["kkn"], ALU.mult), reads=["r_kkn"], writes=["r_t2"])
            P.op("pe", lambda e: e.matmul(BK[b6][:, 0:NT], w2b[0:64, p * 128:(p + 1) * 128], TW[0:64, 0:NT], start=True, stop=True),
                 reads=["w2b", "TW"], writes=[bk(b6)])
            P.op("pe", lambda e: e.matmul(BK[b7][:, 0:NT], w2b[64:128, p * 128:(p + 1) * 128], TW[64:128, 0:NT], start=True, stop=True),
                 reads=["a2b", "TW"], writes=[bk(b7)])
            P.op("act", lambda e: e.activation(R["sg"], BK[b6][:, 0:NT], AF.Sigmoid, bias=w0_c[:, p:p + 1]), reads=[bk(b6), "w0_c"], writes=["r_sg"])
            P.op("act", lambda e: e.activation(R["a"], BK[b7][:, 0:NT], AF.Sigmoid, bias=a0_c[:, p:p + 1]), reads=[bk(b7), "a0_c"], writes=["r_a"])
            P.op("pe", lambda e: e.matmul(BK[b6][:, 0:NT], bo, R["t2"], start=True, stop=True), reads=["cst", "r_t2"], writes=[bk(b6)])
            for b in range(nb):
                tb_ = 4 + b % 2
                P.op("pe", lambda e, b=b, tb_=tb_: e.transpose(BK[tb_][:, 0:128], xbufs[2][:, b * 128:(b + 1) * 128], ident), reads=[kx2, "cst"], writes=[bk(tb_)])
                P.op("act", lambda e, b=b, tb_=tb_: e.copy(Vtok[:, b, p * 128:(p + 1) * 128], BK[tb_][:, 0:128]), reads=[bk(tb_)], writes=[("Vtok", b, p)])
            for c in range(nch):
                P.op("dve", lambda e, c=c: e.tensor_tensor_scan(R["cs"][:, c * C:(c + 1) * C], ones[:, 0:C], R["sg"][:, c * C:(c + 1) * C], 0.0, ALU.mult, ALU.add),
                     reads=["r_sg", "cst"], writes=["r_cs"])
            P.op("act", lambda e: e.activation(R["t2"], BK[b6][:, 0:NT], AF.Sqrt), reads=[bk(b6)], writes=["r_t2"])
            P.op("dve", lambda e: e.tensor_scalar(R["t3"], R["a"], -1.0, ka_c[:, p:p + 1], ALU.add, ALU.mult), reads=["r_a", "ka_c"], writes=["r_t3"])
            P.op("dve", lambda e: e.scalar_tensor_tensor(R["k2"], R["t3"], 1.0, xk, ALU.add, ALU.mult), reads=["r_t3", kx1], writes=["r_k2"])
            P.op("act", lambda e: e.activation(R["ein"], R["cs"], AF.Exp, scale=-KAPPA), reads=["r_cs"], writes=["r_ein"])
            P.op("act", lambda e: e.activation(R["einv"], R["cs"], AF.Exp, scale=KAPPA), reads=["r_cs"], writes=["r_einv"])
            P.op("dve", lambda e: e.tensor_scalar(R["t2"], R["t2"], 1e-12, None, ALU.max), reads=["r_t2"], writes=["r_t2"])
            P.op("dve", lambda e: e.reciprocal(R["t2"], R["t2"]), reads=["r_t2"], writes=["r_t2"])
            P.op("dve", lambda e: e.tensor_tensor(R["kkn"], R["kkn"], R["t2"], ALU.mult), reads=["r_kkn", "r_t2"], writes=["r_kkn"])
            P.op("pool", lambda e: e.tensor_copy(cCt[:, p, 0:nch], R["ein"][:, C - 1:NT:C]), reads=["r_ein"], writes=[("cCt", p)])
            P.op("pool", lambda e: e.tensor_tensor(art[:, p, 1, 0:NT], xr, R["ein"], ALU.mult), reads=[kx0, "r_ein"], writes=[("art", p)])
            P.op("dve", lambda e: e.tensor_tensor(kt_[:, p, 0:NT], R["k2"], R["einv"], ALU.mult), reads=["r_k2", "r_einv"], writes=[("kt", p)])
            c3_ = lambda ap: ap.rearrange("p (c t) -> p c t", t=C)
            P.op("dve", lambda e: e.scalar_tensor_tensor(c3_(art[:, p, 0, 0:NT])[:, :, 1:C], c3_(R["kkn"])[:, :, 1:C], -1.0, c3_(R["ein"])[:, :, 0:C - 1], ALU.mult, ALU.mult),
                 reads=["r_kkn", "r_ein"], writes=[("art", p)])
            P.op("pool", lambda e: e.tensor_scalar(c3_(art[:, p, 0, 0:NT])[:, :, 0:1], c3_(R["kkn"])[:, :, 0:1], -1.0, None, ALU.mult), reads=["r_kkn"], writes=[("art", p)])
            P.op("pool", lambda e: e.tensor_tensor(R["t3"], R["kkn"], R["a"], ALU.mult), reads=["r_kkn", "r_a", "r_k2"], writes=["r_t3"])
            P.op("dve", lambda e: e.tensor_tensor(bt_[:, p, 0:NT], R["t3"], R["einv"], ALU.mult), reads=["r_t3", "r_einv"], writes=[("bt", p)])
            cCb = cCt[:, p, 0:nch].unsqueeze(2).broadcast_to([128, nch, C])
            P.op("pool", lambda e: e.tensor_tensor(c3_(R["t2"]), c3_(kt_[:, p, 0:NT]), cCb, ALU.mult), reads=[("kt", p), ("cCt", p), "r_kkn"], writes=["r_t2"])
            P.op("dve", lambda e: e.scalar_tensor_tensor(R["t1"], xr, rk_c[:, p:p + 1], R["k2"], ALU.mult, ALU.mult), reads=[kx0, "rk_c", "r_k2", ("art", p)], writes=["r_t1"])
            P.op("pool", lambda e: e.tensor_tensor(c3_(R["t3"]), c3_(bt_[:, p, 0:NT]), cCb, ALU.mult), reads=[("bt", p), ("cCt", p), "r_t3"], writes=["r_t3"])
            for b in range(nb):
                tb_ = 4 + b % 2
                P.op("pe", lambda e, b=b, tb_=tb_: e.transpose(BK[tb_][:, 0:128], R["t2"][:, b * 128:(b + 1) * 128], ident), reads=["r_t2", "cst"], writes=[bk(tb_)])
                P.op("act", lambda e, b=b, tb_=tb_: e.copy(KPtok[:, b, p * 128:(p + 1) * 128], BK[tb_][:, 0:128]), reads=[bk(tb_)], writes=[("KPtok", b, p)])
            P.op("pe", lambda e: e.matmul(BK[b7][:, 0:NT], bo, R["t1"], start=True, stop=True), reads=["cst", "r_t1"], writes=[bk(b7)])
            for b in range(nb):
                tb_ = 4 + b % 2
                P.op("pe", lambda e, b=b, tb_=tb_: e.transpose(BK[tb_][:, 0:128], R["t3"][:, b * 128:(b + 1) * 128], ident), reads=["r_t3", "cst"], writes=[bk(tb_)])
                P.op("act", lambda e, b=b, tb_=tb_: e.copy(BPtok[:, b, p * 128:(p + 1) * 128], BK[tb_][:, 0:128]), reads=[bk(tb_)], writes=[("BPtok", b, p)])
            P.op("dve", lambda e: e.tensor_tensor(bon[:, p, 0:NT], BK[b7][:, 0:NT], xv, ALU.mult), reads=[bk(b7), kx2], writes=[("bon", p)])
            P.op("pe", lambda e: e.matmul(BK[b6][:, 0:NT], g2bf[:, p * 128:(p + 1) * 128], SG[:, 0:NT], start=True, stop=True), reads=["g2bf", "SG"], writes=[bk(b6)])
            P.op("act", lambda e: e.copy(gT[:, p, 0:NT], BK[b6][:, 0:NT]), reads=[bk(b6)], writes=[("gT", p)])

        pend = [None]
        for blkb in range(9):
            slot, skey = ring_get()
            sl3 = slot[:, 0:4096].rearrange("p (k n) -> p k n", n=512)
            for cc in range(4):
                c = blkb * 4 + cc
                bnk = c % 4
                for kc in range(8):
                    P.op("pe", lambda e, kc=kc, cc=cc, bnk=bnk, sl3=sl3: e.matmul(BK[bnk][:, 0:NT], sl3[:, kc, cc * 128:(cc + 1) * 128], hT[:, kc, 0:NT], start=(kc == 0), stop=(kc == 7)),
                         reads=[skey] + HT_KEYS, writes=[bk(bnk)])
                stop_at("E%d" % c)
                if c < 4:
                    P.op("act", lambda e, c=c, bnk=bnk: e.copy(qT[:, c, 0:NT], BK[bnk][:, 0:NT]), reads=[bk(bnk)], writes=[("qT", c)])
                elif c == 4:
                    P.op("act", lambda e, bnk=bnk: e.copy(kTf[:, 0:NT], BK[bnk][:, 0:NT]), reads=[bk(bnk)], writes=["kTf"])
                    P.op("pool", lambda e: e.tensor_copy(kT[:, 128:128 + NT], kTf[:, 0:NT]), reads=["kTf"], writes=["kT"])
                elif c == 5:
                    P.op("act", lambda e, bnk=bnk: e.copy(vTf[:, 0:NT], BK[bnk][:, 0:NT]), reads=[bk(bnk)], writes=["vTf"])
                    for b in range(nb):
                        tb_ = 4 + b % 2
                        P.op("pe", lambda e, b=b, tb_=tb_: e.transpose(BK[tb_][:, 0:128], vTf[:, b * 128:(b + 1) * 128], ident), reads=["vTf", "cst"], writes=[bk(tb_)])
                        P.op("dve", lambda e, b=b, tb_=tb_: e.tensor_copy(vtok[:, 1 + b, :], BK[tb_][:, 0:128]), reads=[bk(tb_)], writes=[("vtok", 1 + b)])
                        if (prompt and last and b == nb - 1) or not prompt:
                            P.op("dve", lambda e, tb_=tb_: e.tensor_copy(kvrow[:, 1, :], BK[tb_][:, 0:128]), reads=[bk(tb_)], writes=[("kvrow", 1)])
                elif c == 6:
                    token_shift(bnk, 12, xs3[0][:, 0:NT], "xs0")
                    P.op("act", lambda e: e.activation(TW[0:64, 0:NT], xs3[0][0:64, 0:NT], AF.Tanh), reads=["xs0"], writes=["TW"])
                    P.op("pool", lambda e: e.tensor_copy(TW[64:128, 0:NT], xs3[0][64:128, 0:NT]), reads=["xs0"], writes=["TW"])
                elif c == 7:
                    token_shift(bnk, 13, xs3[0][:, 0:NT], "xs0")
                    P.op("act", lambda e: e.activation(SG[:, 0:NT], xs3[0][:, 0:NT], AF.Sigmoid), reads=["xs0"], writes=["SG"])
                elif c < 20:
                    p, which = divmod(c - 8, 3)
                    xb_, xk_ = xs_set(p)
                    token_shift(bnk, which * 4 + p, xb_[which][:, 0:NT], xk_[which])
                    if which == 2:
                        if pend[0] is not None:
                            pair_process(pend[0])
                        pend[0] = p
                else:
                    gi = c - 20
                    P.op("act", lambda e, gi=gi, bnk=bnk: e.activation(gates[:, gi, 0:NT], BK[bnk][:, 0:NT], AF.Sigmoid), reads=[bk(bnk)], writes=[("gates", gi)])
                    if c == 23 and pend[0] is not None:
                        pair_process(pend[0])
                        pend[0] = None

        stop_at("E")
        debug_tap("qT", qT[:, :, 0:NT], [128, 4, NT], [("qT", c) for c in range(4)])

        if prompt:
            def att_steps(b, kv):
                gbk = ti * nb + b
                ph = slice(kv * 64, kv * 64 + 64)
                qv = qT[ph, :, b * 128:(b + 1) * 128]
                xs_ = [0] + ([1] if gbk > 0 else [])
                bb = 4 if kv == 0 else 0
                PTk = PT if kv == 0 else PT2

                def sc():
                    for x in xs_:
                        kcols = slice(128 + b * 128, 256 + b * 128) if x == 0 else slice(b * 128, 128 + b * 128)
                        P.op("pe", lambda e, kcols=kcols, x=x: e.matmul(BK[bb + x][:], kT[ph, kcols], qv, start=True, stop=True),
                             reads=["kT"] + [("qT", c) for c in range(4)], writes=[bk(bb + x)])

                def bias():
                    for x in xs_:
                        P.op("dve", lambda e, x=x: e.scalar_tensor_tensor(BK[bb + x][:], BK[bb + x][:], 0.125, biasT[x][kv][:].rearrange("p g q -> p (g q)"), ALU.mult, ALU.add),
                             reads=[bk(bb + x)] + BIAS_KEYS[(x, kv)], writes=[bk(bb + x)])

                def ex():
                    for x in xs_:
                        P.op("act", lambda e, x=x: e.activation(PTk[x][:], BK[bb + x][:], AF.Exp), reads=[bk(bb + x)], writes=[("PT", kv, x)])

                def pv():
                    for i, x in enumerate(xs_):
                        vb = 1 + b if x == 0 else b
                        P.op("pe", lambda e, x=x, vb=vb, i=i: e.matmul(BK[bb + 2][:], vtok[:, vb, :], PTk[x][:], start=(i == 0), stop=(i == len(xs_) - 1)),
                             reads=[("vtok", vb), ("PT", kv, x)], writes=[bk(bb + 2)])
                    for i, x in enumerate(xs_):
                        P.op("pe", lambda e, x=x, i=i: e.matmul(BK[bb + 3][:], ones_bf[:], PTk[x][:], start=(i == 0), stop=(i == len(xs_) - 1)),
                             reads=["ones_bf", ("PT", kv, x)], writes=[bk(bb + 3)])
                return [
                    sc, bias, ex, pv,
                    lambda: P.op("dve", lambda e: e.tensor_tensor(rd_t[ph, :], BK[bb + 3][ph, :], esink[ph, :, :].rearrange("p g q -> p (g q)"), ALU.add),
                                 reads=[bk(bb + 3)] + ESINK_KEYS, writes=[("rd_t", kv)]),
                    lambda: P.op("dve", lambda e: e.reciprocal(rd_t[ph, :], rd_t[ph, :]), reads=[("rd_t", kv)], writes=[("rd_t", kv)]),
                    lambda: P.op("dve", lambda e: e.tensor_tensor(attT[ph, :, b * 128:(b + 1) * 128], BK[bb + 2][ph, :].rearrange("p (g q) -> p g q", q=128), rd_t[ph, :].rearrange("p (g q) -> p g q", q=128), ALU.mult),
                                 reads=[bk(bb + 2), ("rd_t", kv)], writes=[("attT", kv, b)]),
                ]
            for b in range(nb):
                for s_a, s_b in zip(att_steps(b, 0), att_steps(b, 1)):
                    s_a()
                    s_b()
            P.op("pool", lambda e: e.tensor_copy(kT[:, 0:128], kT[:, NT:NT + 128]), reads=["kT"], writes=["kT"])
            P.op("pool", lambda e: e.tensor_copy(vtok[:, 0, :], vtok[:, nb, :]), reads=[("vtok", nb)], writes=[("vtok", 0)])
            if last and 'pk' not in SKIP:
                P.op("pe", lambda e: e.transpose(BK[4][:, 0:128], kTf[:, NT - 128:NT], ident), reads=["kTf", "cst"], writes=[bk(4)])
                P.op("act", lambda e: e.copy(kvrow[:, 0, :], BK[4][:, 0:128]), reads=[bk(4)], writes=[("kvrow", 0)])
                P.dma("sp", pk[seq], kvrow[:, 0, :], reads=[("kvrow", 0)], writes=["pk"])
                P.dma("sp", pv[seq], kvrow[:, 1, :], reads=[("kvrow", 1)], writes=["pv"])
        else:
            sample_attention()
        ATT_KEYS = [("attT", kv, b) for kv in range(2) for b in range(nb)]
        stop_at("F")
        debug_tap("attT", attT[:, :, 0:NT], [128, 4, NT], ATT_KEYS)

        def hcol(h):
            return (h % 2) * 256 + (h // 2) * 64

        def rw_pre(b):
            bc = slice(b * 128, (b + 1) * 128)
            for h in range(8):
                p, h2 = divmod(h, 2)
                ph = slice(h2 * 64, h2 * 64 + 64)
                b0, b1, b2 = (4, 5, 6) if h % 2 == 0 else (0, 1, 2)
                P.op("pe", lambda e, p=p, ph=ph: e.matmul(BK[b0][:, 0:256], kt_[ph, p, bc], art[ph, p, :, bc], start=True, stop=True),
                     reads=[("kt", p), ("art", p)], writes=[bk(b0)])
                P.op("dve", lambda e, h=h: e.tensor_tensor(AK[h][:], BK[b0][:, 0:256], cst[:, cm:cm + 256], ALU.mult), reads=[bk(b0), "cst"], writes=[("AK", h)])
                P.op("pe", lambda e, p=p, ph=ph: e.matmul(BK[b1][:, 0:256], bt_[ph, p, bc], art[ph, p, :, bc], start=True, stop=True),
                     reads=[("bt", p), ("art", p)], writes=[bk(b1)])
                P.op("dve", lambda e, h=h: e.tensor_tensor(AB[h][:], BK[b1][:, 0:256], cst[:, cm:cm + 256], ALU.mult), reads=[bk(b1), "cst"], writes=[("AB", h)])
                P.op("pe", lambda e, p=p, ph=ph: e.matmul(BK[b2][:, 0:128], art[ph, p, 0, bc], bt_[ph, p, bc], start=True, stop=True),
                     reads=[("bt", p), ("art", p)], writes=[bk(b2)])
                P.op("dve", lambda e, h=h: e.tensor_tensor(Mt_[h // 4][:, h % 4, :], BK[b2][:, 0:128], cst[:, cl:cl + 128], ALU.mult), reads=[bk(b2), "cst"], writes=[("Mt", h // 4)])
                P.op("pool", lambda e, h=h: e.tensor_tensor(Pv[h], AB[h][:, 0:128], ident, ALU.add), reads=[("AB", h), "cst"], writes=[("Pv", h)])
            for lv in range(1, LV + 1):
                lastlv = lv == LV
                for g in range(2):
                    bMT, bM, bP = (4, 5, 6) if g == 0 else (0, 1, 2)
                    for j in range(4):
                        h = 4 * g + j
                        Mcur = AB[h][:, 0:128] if lv == 1 else Mm_[g][:, j, :]
                        mk_c = ("AB", h) if lv == 1 else ("Mm", g)
                        Mtcur = Mt_[g][:, j, :]
                        P.op("pe", lambda e, Mcur=Mcur, Mtcur=Mtcur, j=j: e.matmul(BK[bMT][:, j * 128:(j + 1) * 128], Mcur, Mtcur, start=True, stop=True),
                             reads=[mk_c, ("Mt", g)], writes=[bk(bMT)])
                        if not lastlv:
                            P.op("pe", lambda e, Mcur=Mcur, Mtcur=Mtcur, j=j: e.matmul(BK[bM][:, j * 128:(j + 1) * 128], Mtcur, Mcur, start=True, stop=True),
                                 reads=[mk_c, ("Mt", g)], writes=[bk(bM)])
                    P.op("act", lambda e, g=g: e.copy(Mt_[g][:].rearrange("p a t -> p (a t)"), BK[bMT][:]), reads=[bk(bMT)], writes=[("Mt", g)])
                    if not lastlv:
                        P.op("act", lambda e, g=g: e.copy(Mm_[g][:].rearrange("p a t -> p (a t)"), BK[bM][:]), reads=[bk(bM)], writes=[("Mm", g)])
                    for j in range(4):
                        h = 4 * g + j
                        P.op("pe", lambda e, j=j, h=h, g=g: e.matmul(BK[bP][:, j * 128:(j + 1) * 128], Mt_[g][:, j, :], Pv[h], start=True, stop=True),
                             reads=[("Mt", g), ("Pv", h)], writes=[bk(bP)])
                    P.op("dve", lambda e, g=g: e.tensor_tensor(Pvg[g][:].rearrange("p a t -> p (a t)"), BK[bP][:], Pvg[g][:].rearrange("p a t -> p (a t)"), ALU.add),
                         reads=[bk(bP)] + [("Pv", 4 * g + j) for j in range(4)], writes=[("Pv", 4 * g + j) for j in range(4)])
            for h in range(8):
                P.op("pe", lambda e, h=h, b=b: e.matmul(BK[7][:, hcol(h):hcol(h) + 64], AK[h][:, 0:128], Vtok[:, b, h * 64:(h + 1) * 64], start=True, stop=True),
                     reads=[("AK", h), ("Vtok", b, h // 2)], writes=[bk(7)])
            P.op("act", lambda e: e.copy(VA[:], BK[7][:]), reads=[bk(7)], writes=["VA"])
            for h in range(8):
                p, h2 = divmod(h, 2)
                ph = slice(h2 * 64, h2 * 64 + 64)
                P.op("pe", lambda e, h=h, b=b, p=p, ph=ph: e.matmul(BK[4][ph, p * 128:(p + 1) * 128], Vtok[:, b, h * 64:(h + 1) * 64], AK[h][:, 128:256], start=True, stop=True),
                     reads=[("AK", h), ("Vtok", b, p)], writes=[bk(4)])
            P.op("act", lambda e: e.copy(VK[:].rearrange("p a t -> p (a t)"), BK[4][:]), reads=[bk(4)], writes=["VK"])
            stop_at('G2')

        if prompt:
            if first:
                P.op("pool", lambda e: e.memset(S[:], 0.0), writes=["S"])
            for b in range(nb):
                rw_pre(b)
                for c2 in range(2):
                    cr = slice(c2 * 64, c2 * 64 + 64)
                    tc_ = slice(b * 128 + c2 * 64, b * 128 + c2 * 64 + 64)
                    gci = (b * 128 + c2 * 64) // 64
                    for h in range(8):
                        p, h2 = divmod(h, 2)
                        ph = slice(h2 * 64, h2 * 64 + 64)
                        P.op("pe", lambda e, h=h, p=p, ph=ph, h2=h2: e.matmul(BK[5 + h2][cr, p * 64:(p + 1) * 64], art[ph, p, 0, tc_], S[ph, p, :], start=True, stop=True),
                             reads=[("art", p), "S"], writes=[bk(5 + h2)])
                    for h in range(8):
                        p, h2 = divmod(h, 2)
                        ph = slice(h2 * 64, h2 * 64 + 64)
                        P.op("pe", lambda e, p=p, ph=ph, h2=h2: e.matmul(BK[h2][ph, p * 64:(p + 1) * 64], S[ph, p, :], art[ph, p, 1, tc_], start=True, stop=True),
                             reads=[("art", p), "S"], writes=[bk(h2)])
                    for h2 in range(2):
                        ph = slice(h2 * 64, h2 * 64 + 64)
                        P.op("act", lambda e, h2=h2, ph=ph: e.copy(y1[ph, :, :].rearrange("p a t -> p (a t)"), BK[h2][ph, 0:256]), reads=[bk(h2)], writes=[("y1", h2)])
                    for h2 in range(2):
                        P.op("dve", lambda e, h2=h2: e.tensor_tensor(XT[cr, h2 * 256:(h2 + 1) * 256], BK[5 + h2][cr, 0:256], VA[cr, h2 * 256:(h2 + 1) * 256], ALU.add),
                             reads=[bk(5 + h2), "VA"], writes=[("XT", h2)])
                    for h in range(8):
                        P.op("pe", lambda e, h=h: e.matmul(BK[7][cr, hcol(h):hcol(h) + 64], Pv[h][cr, c2 * 64:(c2 + 1) * 64], XT[cr, hcol(h):hcol(h) + 64], start=True, stop=True),
                             reads=[("Pv", h), ("XT", h % 2)], writes=[bk(7)])
                    P.op("act", lambda e: e.copy(UT[cr, :], BK[7][cr, :]), reads=[bk(7)], writes=["UT"])
                    stop_at('G3')
                    for h in range(8):
                        p, h2 = divmod(h, 2)
                        ph = slice(h2 * 64, h2 * 64 + 64)
                        P.op("pe", lambda e, h=h, p=p, ph=ph: e.matmul(BK[5][ph, 256 + p * 64:256 + (p + 1) * 64], BPtok[cr, b, h * 64:(h + 1) * 64], UT[cr, hcol(h):hcol(h) + 64], start=True, stop=False),
                             reads=[("BPtok", b, p), "UT"], writes=[bk(5)])
                        P.op("pe", lambda e, h=h, p=p, ph=ph: e.matmul(BK[5][ph, 256 + p * 64:256 + (p + 1) * 64], KPtok[cr, b, h * 64:(h + 1) * 64], Vtok[cr, b, h * 64:(h + 1) * 64], start=False, stop=True),
                             reads=[("KPtok", b, p), ("Vtok", b, p)], writes=[bk(5)])
                    for p in range(4):
                        P.op("dve", lambda e, p=p: e.scalar_tensor_tensor(S[:, p, :], S[:, p, :], cCt[:, p, gci:gci + 1], BK[5][:, 256 + p * 64:256 + (p + 1) * 64], ALU.mult, ALU.add),
                             reads=["S", ("cCt", p), bk(5)], writes=["S"])
                    for h in range(8):
                        p, h2 = divmod(h, 2)
                        ph = slice(h2 * 64, h2 * 64 + 64)
                        P.op("pe", lambda e, h=h, p=p, ph=ph: e.matmul(BK[6][ph, 256 + p * 64:256 + (p + 1) * 64], UT[cr, hcol(h):hcol(h) + 64], AB[h][cr, 128 + c2 * 64:128 + (c2 + 1) * 64], start=True, stop=True),
                             reads=[("AB", h), "UT"], writes=[bk(6)])
                    P.op("dve", lambda e: e.tensor_tensor(y1[:], BK[6][:, 256:512].rearrange("p (a t) -> p a t", t=64), y1[:], ALU.add), reads=[bk(6), ("y1", 0), ("y1", 1)], writes=[("y1", 0), ("y1", 1)])
                    P.op("pool", lambda e: e.tensor_tensor(Yt[:, :, tc_], y1[:], VK[:, :, c2 * 64:(c2 + 1) * 64], ALU.add), reads=[("y1", 0), ("y1", 1), "VK"], writes=[("Yt", b)])
                    stop_at('G4')
            if last and 'pwkv' not in SKIP:
                for pp in range(2):
                    P.op("pe", lambda e, pp=pp: e.transpose(BK[4][:, pp * 128:(pp + 1) * 128], S[:, 2 * pp:2 * pp + 2, :].rearrange("p a i -> p (a i)"), ident), reads=["S", "cst"], writes=[bk(4)])
                P.op("act", lambda e: e.copy(xn[:, 0:256], BK[4][:, 0:256]), reads=[bk(4)], writes=["xn"])
                for pp in range(2):
                    for pl in range(2):
                        pidx = 2 * pp + pl
                        P.dma("sp", pwkv[seq, 2 * pidx:2 * pidx + 2].rearrange("h i j -> i h j"), xn[pl * 64:(pl + 1) * 64, pp * 128:(pp + 1) * 128].rearrange("p (h j) -> p h j", j=64), reads=["xn"], writes=[("pwkv", pidx)])
                P.op("pe", lambda e: e.transpose(BK[4][:, 0:128], car[:, :], ident), reads=[("car", n) for n in range(14)] + ["cst", "car_init"], writes=[bk(4)])
                P.op("act", lambda e: e.copy(xn[0:14, 0:128], BK[4][0:14, 0:128]), reads=[bk(4)], writes=["xn"])
                P.dma("sp", psh[seq].rearrange("(c p) -> c p", p=128), xn[0:14, 0:128], reads=["xn"], writes=["psh"])
        else:
            rw_pre(0)
            sample_rwkv(hcol)
        YT_KEYS = [("Yt", b) for b in range(nb)]
        stop_at("G")
        debug_tap("Yt", Yt[:, :, 0:NT], [128, 4, NT], YT_KEYS)

        def post_steps(p):
            tn = ("sg", "cs", "t1", "t2", "t3", "a", "kkn", "k2")
            n1, n2 = tn[2 * p], tn[2 * p + 1]
            T1, T2 = rw[n1][:, 0:NT], rw[n2][:, 0:NT]
            k1, k2_ = "r_" + n1, "r_" + n2
            bm, bv = 4 + p, p
            return [
                lambda: P.op("pe", lambda e: e.matmul(BK[bm][:, 0:NT], bo64, Yt[:, p, 0:NT], start=True, stop=True), reads=YT_KEYS + ["cst"], writes=[bk(bm)]),
                lambda: P.op("dve", lambda e: e.tensor_tensor(T1, Yt[:, p, 0:NT], BK[bm][:, 0:NT], ALU.subtract), reads=YT_KEYS + [bk(bm)], writes=[k1]),
                lambda: P.op("pool", lambda e: e.tensor_tensor(T2, T1, T1, ALU.mult), reads=[k1], writes=[k2_]),
                lambda: P.op("pe", lambda e: e.matmul(BK[bv][:, 0:NT], bo64, T2, start=True, stop=True), reads=[k2_, "cst"], writes=[bk(bv)]),
                lambda: P.op("act", lambda e: e.activation(T2, BK[bv][:, 0:NT], AF.Sqrt, bias=64e-5), reads=[bk(bv)], writes=[k2_]),
                lambda: P.op("dve", lambda e: e.reciprocal(T2, T2), reads=[k2_], writes=[k2_]),
                lambda: P.op("dve", lambda e: e.tensor_tensor(T1, T1, T2, ALU.mult), reads=[k1, k2_], writes=[k1]),
                lambda: P.op("dve", lambda e: e.tensor_scalar(T1, T1, lg_c[:, p:p + 1], lb_c[:, p:p + 1], ALU.mult, ALU.add), reads=[k1, "lg_c", "lb_c"], writes=[k1]),
                lambda: P.op("pool", lambda e: e.tensor_tensor(T1, T1, bon[:, p, 0:NT], ALU.add), reads=[k1, ("bon", p)], writes=[k1]),
                lambda: P.op("dve", lambda e: e.tensor_tensor(rwoT[:, p, 0:NT], T1, gT[:, p, 0:NT], ALU.mult), reads=[k1, ("gT", p)], writes=[("rwoT", p)]),
            ]
        for group in zip(*[post_steps(p) for p in range(4)]):
            for step in group:
                step()
        RWO_KEYS = [("rwoT", p) for p in range(4)]
        stop_at("H")
        debug_tap("rwoT", rwoT[:, :, 0:NT], [128, 4, NT], RWO_KEYS)

        for ch in range(2):
            sa, ka_ = ring_get()
            sb_, kb_ = ring_get(hold=1)
            sa3 = sa[:, 0:2048].rearrange("p (k n) -> p k n", n=512)
            sb3 = sb_[:, 0:2048].rearrange("p (k n) -> p k n", n=512)
            for cc in range(4):
                oc = ch * 4 + cc
                ba, bb = (0, 1) if cc % 2 == 0 else (2, 3)
                for kc in range(4):
                    P.op("pe", lambda e, kc=kc, cc=cc, ba=ba, sa3=sa3: e.matmul(BK[ba][:, 0:NT], sa3[:, kc, cc * 128:(cc + 1) * 128], attT[:, kc, 0:NT], start=(kc == 0), stop=(kc == 3)),
                         reads=[ka_] + ATT_KEYS, writes=[bk(ba)])
                for kc in range(4):
                    P.op("pe", lambda e, kc=kc, cc=cc, bb=bb, sb3=sb3: e.matmul(BK[bb][:, 0:NT], sb3[:, kc, cc * 128:(cc + 1) * 128], rwoT[:, kc, 0:NT], start=(kc == 0), stop=(kc == 3)),
                         reads=[kb_] + RWO_KEYS, writes=[bk(bb)])
                tA = cc_t[cc % 2][:, 0:NT]
                tB = gl_t[cc % 2][:, 0:NT]
                P.op("dve", lambda e, oc=oc, ba=ba, tA=tA: e.tensor_tensor(tA, BK[ba][:, 0:NT], gates[:, oc, 0:NT], ALU.mult), reads=[bk(ba), ("gates", oc)], writes=[("cc_t", cc % 2)])
                P.op("dve", lambda e, oc=oc, bb=bb, tB=tB: e.tensor_tensor(tB, BK[bb][:, 0:NT], gates[:, 8 + oc, 0:NT], ALU.mult), reads=[bk(bb), ("gates", 8 + oc)], writes=[("gl_t", cc % 2)])
                P.op("pool", lambda e, oc=oc, tA=tA, tB=tB: e.tensor_tensor(hT[:, oc, 0:NT], tA, tB, ALU.add), reads=[("cc_t", cc % 2), ("gl_t", cc % 2)], writes=[("hT", oc)])
        MIX_KEYS = [("hT", oc) for oc in range(8)]
        debug_tap("mixT", hT[:, :, 0:NT], [128, 8, NT], MIX_KEYS)

        stop_at("I")
        for ch in range(2):
            so, ko = ring_get()
            so3 = so[:, 0:4096].rearrange("p (k n) -> p k n", n=512)
            for b in range(nb):
                bnk = (ch * nb + b) % 4
                for kc in range(8):
                    P.op("pe", lambda e, kc=kc, b=b, bnk=bnk, so3=so3: e.matmul(BK[bnk][:], hT[:, kc, b * 128:(b + 1) * 128], so3[:, kc, :], start=(kc == 0), stop=(kc == 7)),
                         reads=[ko] + MIX_KEYS, writes=[bk(bnk)])
                P.op("dve", lambda e, b=b, bnk=bnk, ch=ch: e.tensor_tensor(xt[b][:, ch * 512:(ch + 1) * 512], xt[b][:, ch * 512:(ch + 1) * 512], BK[bnk][:], ALU.add),
                     reads=[bk(bnk), ("xt", b)], writes=[("xt", b)])
        stop_at("J")
        debug_tap("x1", xt[0][:], [128, 1024], [("xt", 0)])

        norm_T("g2b")
        def ffn_steps(i, f, bo_):
            PS3 = v3(BK[bo_ + i][:, 0:NT], TB)
            c3 = v3(cc_t[i][:, 0:NT], TB)
            ck_ = ("cc_t", i)
            if prompt:
                cprev3 = (zeros_t[:, 0:2] if first else ccar[:, :, f]).unsqueeze(1)
                cprev_keys = ["zeros_t"] if first else [("ccar", f)]
            else:
                cprev3 = SB["ccar_s"][:, f, :, :]
                cprev_keys = [("ccar_s", f)]
            w0_, w1_, w2_, cb_ = cw_c[:, 0, f:f + 1], cw_c[:, 1, f:f + 1], cw_c[:, 2, f:f + 1], cb_c[:, f:f + 1]

            def carry_out():
                if prompt:
                    P.op("dve", lambda e: e.tensor_copy(ccar[:, :, f], BK[bo_ + i][:, NT - 2:NT]), reads=[bk(bo_ + i)], writes=[("ccar", f)])
                else:
                    P.op("dve", lambda e: e.tensor_copy(SB["cout_s"][:, f, :, :], PS3[:, :, TB - 2:TB]), reads=[bk(bo_ + i)], writes=[("cout_s", f)])
            return [
                lambda: P.op("act", lambda e: e.activation(c3[:, :, 2:TB], PS3[:, :, 0:TB - 2], AF.Identity, bias=cb_, scale=w0_), reads=[bk(bo_ + i), "cb_c"] + CW_KEYS, writes=[ck_]),
                lambda: P.op("pool", lambda e: e.tensor_scalar(c3[:, :, 0:2], cprev3, w0_, cb_, ALU.mult, ALU.add), reads=cprev_keys + ["cb_c"] + CW_KEYS, writes=[ck_]),
                lambda: P.op("dve", lambda e: e.scalar_tensor_tensor(c3[:, :, 1:TB], PS3[:, :, 0:TB - 1], w1_, c3[:, :, 1:TB], ALU.mult, ALU.add), reads=[bk(bo_ + i), ck_] + CW_KEYS, writes=[ck_]),
                lambda: P.op("dve", lambda e: e.scalar_tensor_tensor(c3[:, :, 0:1], cprev3[:, :, 1:2], w1_, c3[:, :, 0:1], ALU.mult, ALU.add), reads=cprev_keys + [ck_] + CW_KEYS, writes=[ck_]),
                lambda: P.op("dve", lambda e: e.scalar_tensor_tensor(c3, PS3, w2_, c3, ALU.mult, ALU.add), reads=[bk(bo_ + i), ck_] + CW_KEYS, writes=[ck_]),
                carry_out,
                lambda: P.op("act", lambda e: e.activation(gl_t[i][:, 0:NT], cc_t[i][:, 0:NT], AF.Gelu_apprx_tanh), reads=[ck_], writes=[("gl_t", i)]),
                lambda: P.op("dve", lambda e: e.tensor_tensor(actT[:, f, 0:NT], gl_t[i][:, 0:NT], BK[bo_ + 2 + i][:, 0:NT], ALU.mult), reads=[("gl_t", i), bk(bo_ + 2 + i)], writes=[("actT", f)]),
            ]
        for blkb in range(11):
            slot, skey = ring_get()
            sl3 = slot[:, 0:4096].rearrange("p (k n) -> p k n", n=512)
            bo_ = 4 * (blkb % 2)
            for cc in range(4):
                for kc in range(8):
                    P.op("pe", lambda e, kc=kc, cc=cc, sl3=sl3, bo_=bo_: e.matmul(BK[bo_ + cc][:, 0:NT], sl3[:, kc, cc * 128:(cc + 1) * 128], hT[:, kc, 0:NT], start=(kc == 0), stop=(kc == 7)),
                         reads=[skey] + HT_KEYS, writes=[bk(bo_ + cc)])
            for sa, sb in zip(ffn_steps(0, blkb * 2, bo_), ffn_steps(1, blkb * 2 + 1, bo_)):
                sa()
                sb()
        ACT_KEYS = [("actT", f) for f in range(22)]
        if prompt and last and 'pconv' not in SKIP:
            for j in range(2):
                P.op("pe", lambda e, j=j: e.transpose(BK[4][:, j * 128:(j + 1) * 128], ccar[:, j, :], ident), reads=[("ccar", f) for f in range(22)] + ["cst", "ccar_init"], writes=[bk(4)])
            P.op("act", lambda e: e.copy(xn[0:22, 0:256], BK[4][0:22, 0:256]), reads=[bk(4)], writes=["xn"])
            for j in range(2):
                P.dma("sp", pconv[seq, j].rearrange("(c p) -> c p", p=128), xn[0:22, j * 128:(j + 1) * 128], reads=["xn"], writes=[("pconv", j)])

        stop_at("K")
        for ch in range(2):
            for rg in range(3):
                sd, kd = ring_get()
                nk = DN_NK[rg]
                sd3 = sd[:, 0:nk * 512].rearrange("p (k n) -> p k n", n=512)
                for b in range(nb):
                    bnk = b % 4
                    for kc in range(nk):
                        f = rg * 8 + kc
                        P.op("pe", lambda e, kc=kc, f=f, b=b, bnk=bnk, sd3=sd3: e.matmul(BK[bnk][:], actT[:, f, b * 128:(b + 1) * 128], sd3[:, kc, :], start=(f == 0), stop=(f == 21)),
                             reads=[kd] + ACT_KEYS, writes=[bk(bnk)])
            for b in range(nb):
                bnk = b % 4
                P.op("dve", lambda e, b=b, bnk=bnk, ch=ch: e.tensor_tensor(xt[b][:, ch * 512:(ch + 1) * 512], xt[b][:, ch * 512:(ch + 1) * 512], BK[bnk][:], ALU.add),
                     reads=[bk(bnk), ("xt", b)], writes=[("xt", b)])

        def fin_steps(b):
            so = 16 * b
            st = stat[:, so:so + 16]
            xnb = xn if b == 0 else xn2
            xk_ = "xn" if b == 0 else "xn2"
            sk_ = "stat%d_" % b
            return [
                lambda: P.op("dve", lambda e: e.bn_stats(st[:, 0:6], xt[b][:, 0:512]), reads=[("xt", b)], writes=[sk_ + "0"]),
                lambda: P.op("dve", lambda e: e.bn_stats(st[:, 6:12], xt[b][:, 512:1024]), reads=[("xt", b)], writes=[sk_ + "1"]),
                lambda: P.op("dve", lambda e: e.bn_aggr(st[:, 12:14], st[:, 0:12]), reads=[sk_ + "0", sk_ + "1"], writes=[sk_ + "2"]),
                lambda: P.op("dve", lambda e: e.scalar_tensor_tensor(st[:, 14:15], st[:, 12:13], st[:, 12:13], st[:, 13:14], ALU.mult, ALU.add), reads=[sk_ + "2"], writes=[sk_ + "3"]),
                lambda: P.op("act", lambda e: e.activation(st[:, 15:16], st[:, 14:15], AF.Sqrt, bias=1e-6), reads=[sk_ + "3"], writes=[sk_ + "4"]),
                lambda: P.op("dve", lambda e: e.reciprocal(st[:, 15:16], st[:, 15:16]), reads=[sk_ + "4"], writes=[sk_ + "4"]),
                lambda: P.op("dve", lambda e: e.scalar_tensor_tensor(xnb[:], xt[b][:], st[:, 15:16], gb["gfb"][:], ALU.mult, ALU.mult), reads=[("xt", b), sk_ + "4", "gfb"], writes=[xk_]),
                lambda: P.dma("sp", yrows[b * 128:(b + 1) * 128, :], xnb[:], reads=[xk_], writes=[("y", kind, seq, ti, b)]),
            ]
        for group in zip(*[fin_steps(b) for b in range(nb)]):
            for step in group:
                step()

    SB = {}

    def sample_alloc():
        SB["shcar"] = wsb("sx_shcar", [128, 14, 16], F32)
        SB["shout"] = wsb("sx_shout", [128, 16, 16], F32)
        SB["ccar_s"] = wsb("sx_ccar_s", [128, 22, 16, 2], F32)
        SB["cout_s"] = wsb("sx_cout_s", [128, 24, 16, 2], F32)
        SB["S_s"] = wsb("sx_S_s", [128, 16, 4, 64], F32)
        SB["kcT"] = wsb("sx_kcT", [128, 16, 128], BF16)
        SB["vc"] = wsb("sx_vc", [128, 16, 128], BF16)
        SB["biasC"] = [wsb("sx_biasC%d" % kv, [128, 16, 4, 8], F32) for kv in range(2)]
        SB["biasN"] = [wsb("sx_biasN%d" % kv, [128, 16, 4, 8], F32) for kv in range(2)]
        SB["Apad"] = wsb("sx_Apad", [128, 16, 128], F32)
        SB["BPpad"] = [wsb("sx_BPpad", [128, 512], F32)] * 2
        SB["KPpad"] = [wsb("sx_KPpad", [128, 512], F32)] * 2
        SB["xn"] = xn

    def sample_const_setup():
        P.op("pool", lambda e: e.memset(SB["Apad"][:].rearrange("p b c -> p (b c)"), 0.0), writes=["Apad"])
        P.op("pool", lambda e: e.memset(SB["shout"][:].rearrange("p a b -> p (a b)"), 0.0), writes=["shout_init"])
        P.op("pool", lambda e: e.memset(SB["cout_s"][:].rearrange("p a b c -> p (a b c)"), 0.0), writes=["cout_init"])
        for kv in range(2):
            P.op("dve", lambda e, kv=kv: e.tensor_copy(SB["biasC"][kv][:], biasT[1][kv][:, :, 0:8].unsqueeze(1).broadcast_to([128, 16, 4, 8])),
                 reads=BIAS_KEYS[(1, kv)], writes=[("biasC", kv)])
            P.op("pool", lambda e, kv=kv: e.memset(SB["biasN"][kv][:].rearrange("p a b c -> p (a b c)"), NEGB), writes=[("biasN", kv)])
            for bb in range(16):
                P.dma("sp", SB["biasN"][kv][bb * 8:(bb + 1) * 8, bb, :, :], biasT[0][kv][0:8, :, 0:8],
                      reads=BIAS_KEYS[(0, kv)] + [("biasN", kv)], writes=[("biasNd", kv, bb)])
    BIASN_KEYS = {kv: [("biasN", kv)] + [("biasNd", kv, bb) for bb in range(16)] for kv in range(2)}

    def sample_setup(st):
        b0 = st * 16
        stgA = SB["xn"]
        for n in range(14):
            bnk = 4 + n % 4
            if n % 7 == 0:
                P.dma("sp", stgA[0:16, 0:896], sshift[b0:b0 + 16, n * 128:n * 128 + 896], writes=["xn"])
            P.op("pe", lambda e, n=n, bnk=bnk: e.transpose(BK[bnk][:, 0:16], stgA[0:16, (n % 7) * 128:(n % 7 + 1) * 128], cst[0:16, C_ID:C_ID + 16]), reads=["xn", "cst"], writes=[bk(bnk)])
            P.op("act", lambda e, n=n, bnk=bnk: e.copy(SB["shcar"][:, n, :], BK[bnk][:, 0:16]), reads=[bk(bnk)], writes=[("shcar", n)])
        for piece, npc in enumerate((8, 8, 6)):
            P.dma("sp", stgA[0:32, 0:npc * 128], sconv[st * 32:(st + 1) * 32, piece * 1024:piece * 1024 + npc * 128], writes=["xn"])
            for c in range(npc):
                f = piece * 8 + c
                bnk = 4 + c % 4
                P.op("pe", lambda e, c=c, bnk=bnk: e.transpose(BK[bnk][:, 0:32], stgA[0:32, c * 128:(c + 1) * 128], cst[0:32, C_ID:C_ID + 32]), reads=["xn", "cst"], writes=[bk(bnk)])
                P.op("act", lambda e, f=f, bnk=bnk: e.copy(SB["ccar_s"][:, f, :, :].rearrange("p b j -> p (b j)"), BK[bnk][:, 0:32]), reads=[bk(bnk)], writes=[("ccar_s", f)])
        for bb in range(16):
            hs = bb % 2
            P.dma("sp", stgA[0:64, hs * 512:(hs + 1) * 512].rearrange("p (h j) -> p h j", j=64), swkv[b0 + bb].rearrange("h i j -> i h j"), writes=[("stgAh", hs), "xn"] if bb < 2 else [("stgAh", hs)], reads=["xn"])
            bnk = 4 + bb % 4
            for p in range(4):
                P.op("pe", lambda e, p=p, bnk=bnk, hs=hs: e.transpose(BK[bnk][:, p * 64:(p + 1) * 64], stgA[0:64, hs * 512 + p * 128:hs * 512 + (p + 1) * 128], cst[0:64, C_ID:C_ID + 64]),
                     reads=[("stgAh", hs), "cst"], writes=[bk(bnk)])
            P.op("dve", lambda e, bb=bb, bnk=bnk: e.tensor_copy(SB["S_s"][:, bb, :, :].rearrange("p a i -> p (a i)"), BK[bnk][:, 0:256]), reads=[bk(bnk)], writes=[("S_s", bb)])
        for bb in range(16):
            bnk = 4 + bb % 4
            if bb % 8 == 0:
                P.dma("sp", stgA[:, 0:1024].rearrange("p (b c) -> p b c", c=128), ck[b0 + bb:b0 + bb + 8].rearrange("b k c -> k b c"), reads=[("stgAh", 0), ("stgAh", 1)], writes=["xn", ("stgAh", 0), ("stgAh", 1)])
            P.op("pe", lambda e, bb=bb, bnk=bnk: e.transpose(BK[bnk][:, 0:128], stgA[:, (bb % 8) * 128:(bb % 8 + 1) * 128], ident), reads=["xn", "cst"], writes=[bk(bnk)])
            P.op("act", lambda e, bb=bb, bnk=bnk: e.copy(SB["kcT"][:, bb, :], BK[bnk][:, 0:128]), reads=[bk(bnk)], writes=[("kcT", bb)])
        P.dma("pool", SB["vc"][:], cv[b0:b0 + 16].rearrange("b k c -> k b c"), writes=["vc"])
        P.dma("sp", sk[b0:b0 + 16, 0:120, :], ck[b0:b0 + 16, 8:128, :], writes=[("sk_old", st)])
        P.dma("sp", sv[b0:b0 + 16, 0:120, :], cv[b0:b0 + 16, 8:128, :], writes=[("sv_old", st)])

    def sample_attention():
        NT = 128
        kcT, vc = SB["kcT"], SB["vc"]
        P.op("pe", lambda e: e.transpose(BK[4][:, 0:128], kTf[:, 0:128], ident), reads=["kTf", "cst"], writes=[bk(4)])
        P.op("act", lambda e: e.copy(kvrow[:, 0, :], BK[4][:, 0:128]), reads=[bk(4)], writes=[("kvrow", 0)])
        for kv in range(2):
            ph = slice(kv * 64, kv * 64 + 64)
            for bb in range(16):
                P.op("pe", lambda e, bb=bb: e.matmul(BK[4][:, bb * 32:(bb + 1) * 32], kcT[ph, bb, :], qT[ph, :, bb * 8:(bb + 1) * 8], start=True, stop=True),
                     reads=[("kcT", bb)] + [("qT", c) for c in range(4)], writes=[bk(4)])
            for bb in range(16):
                P.op("pe", lambda e, bb=bb: e.matmul(BK[5][:, bb * 32:(bb + 1) * 32], kT[ph, 128:256], qT[ph, :, bb * 8:(bb + 1) * 8], start=True, stop=True),
                     reads=["kT"] + [("qT", c) for c in range(4)], writes=[bk(5)])
            P.op("dve", lambda e, kv=kv: e.scalar_tensor_tensor(BK[4][:], BK[4][:], 0.125, SB["biasC"][kv][:].rearrange("p a b c -> p (a b c)"), ALU.mult, ALU.add),
                 reads=[bk(4), ("biasC", kv)], writes=[bk(4)])
            P.op("dve", lambda e, kv=kv: e.scalar_tensor_tensor(BK[5][:], BK[5][:], 0.125, SB["biasN"][kv][:].rearrange("p a b c -> p (a b c)"), ALU.mult, ALU.add),
                 reads=[bk(5)] + BIASN_KEYS[kv], writes=[bk(5)])
            P.op("act", lambda e: e.activation(PT[0][:], BK[4][:], AF.Exp), reads=[bk(4)], writes=[("PT", 0)])
            P.op("act", lambda e: e.activation(PT[1][:], BK[5][:], AF.Exp), reads=[bk(5)], writes=[("PT", 1)])
            for bb in range(16):
                cs = slice(bb * 32, (bb + 1) * 32)
                P.op("pe", lambda e, bb=bb, cs=cs: e.matmul(BK[6][:, cs], vc[:, bb, :], PT[0][:, cs], start=True, stop=False), reads=["vc", ("PT", 0)], writes=[bk(6)])
                P.op("pe", lambda e, cs=cs: e.matmul(BK[6][:, cs], vtok[:, 1, :], PT[1][:, cs], start=False, stop=True), reads=[("vtok", 1), ("PT", 1)], writes=[bk(6)])
            for bb in range(16):
                cs = slice(bb * 32, (bb + 1) * 32)
                P.op("pe", lambda e, cs=cs: e.matmul(BK[7][:, cs], ones_bf[:], PT[0][:, cs], start=True, stop=False), reads=["ones_bf", ("PT", 0)], writes=[bk(7)])
                P.op("pe", lambda e, cs=cs: e.matmul(BK[7][:, cs], ones_bf[:], PT[1][:, cs], start=False, stop=True), reads=["ones_bf", ("PT", 1)], writes=[bk(7)])
            P.op("dve", lambda e: e.tensor_tensor(rd_t[ph, :].rearrange("p (b g t) -> p b g t", g=4, t=8), BK[7][ph, :].rearrange("p (b g t) -> p b g t", g=4, t=8), esink[ph, :, 0:8].unsqueeze(1).broadcast_to([64, 16, 4, 8]), ALU.add), reads=[bk(7)] + ESINK_KEYS, writes=["rd_t"])
            P.op("dve", lambda e: e.reciprocal(rd_t[ph, :], rd_t[ph, :]), reads=["rd_t"], writes=["rd_t"])
            P.op("dve", lambda e, kv=kv: e.tensor_tensor(attT[ph, :, 0:128].rearrange("p g (b t) -> p b g t", t=8), BK[6][ph, :].rearrange("p (b g t) -> p b g t", g=4, t=8),
                                                       rd_t[ph, :].rearrange("p (b g t) -> p b g t", g=4, t=8), ALU.mult),
                 reads=[bk(6), "rd_t"], writes=[("attT", kv, 0)])

    def sample_rwkv(hcol):
        S_s, Apad = SB["S_s"], SB["Apad"]
        Rpad = Apad
        y1s = VA[:].rearrange("p (a t) -> p a t", t=128)
        SKEYS = [("S_s", bb) for bb in range(16)]

        def diag(t):
            base = t[:, 0, 0:8]
            return bass.AP(base.tensor, base.offset, [list(base.ap[0]), [136, 16], [1, 8]])
        for p in range(4):
            P.op("pool", lambda e, p=p: e.tensor_copy(diag(Apad), art[:, p, 0, 0:128].rearrange("p (b t) -> p b t", t=8)), reads=[("art", p), "Apad"], writes=["Apad"])
            for h2 in range(2):
                ph = slice(h2 * 64, h2 * 64 + 64)
                for bb in range(16):
                    P.op("pe", lambda e, bb=bb, p=p, ph=ph, h2=h2: e.matmul(BK[5 + h2][:, p * 64:(p + 1) * 64], Apad[ph, bb, :], S_s[ph, bb, p, :], start=(bb == 0), stop=(bb == 15)),
                         reads=["Apad", ("S_s", bb)], writes=[bk(5 + h2)])
            P.op("pool", lambda e, p=p: e.tensor_copy(diag(Apad), art[:, p, 1, 0:128].rearrange("p (b t) -> p b t", t=8)), reads=[("art", p), "Apad"], writes=["Apad"])
            for h2 in range(2):
                ph = slice(h2 * 64, h2 * 64 + 64)
                yb = 4 if h2 == 0 else 7
                for bb in range(16):
                    P.op("pe", lambda e, bb=bb, p=p, ph=ph, yb=yb: e.matmul(BK[yb][ph, p * 128:(p + 1) * 128], S_s[ph, bb, p, :], Apad[ph, bb, :], start=(bb == 0), stop=(bb == 15)),
                         reads=["Apad", ("S_s", bb)], writes=[bk(yb)])
        for h2 in range(2):
            P.op("dve", lambda e, h2=h2: e.tensor_tensor(XT[:, h2 * 256:(h2 + 1) * 256], BK[5 + h2][:, 0:256], VA[:, h2 * 256:(h2 + 1) * 256], ALU.add),
                 reads=[bk(5 + h2), "VA"], writes=[("XT", h2)])
        for h in range(8):
            P.op("pe", lambda e, h=h: e.matmul(BK[5][:, hcol(h):hcol(h) + 64], Pv[h], XT[:, hcol(h):hcol(h) + 64], start=True, stop=True),
                 reads=[("Pv", h), ("XT", h % 2)], writes=[bk(5)])
        P.op("act", lambda e: e.copy(UT[:], BK[5][:]), reads=[bk(5)], writes=["UT"])
        for h in range(8):
            p, h2 = divmod(h, 2)
            ph = slice(h2 * 64, h2 * 64 + 64)
            P.op("pe", lambda e, h=h, p=p, ph=ph: e.matmul(BK[6][ph, p * 128:(p + 1) * 128], UT[:, hcol(h):hcol(h) + 64], AB[h][:, 128:256], start=True, stop=True),
                 reads=[("AB", h), "UT"], writes=[bk(6)])
        P.op("act", lambda e: e.copy(VA[0:64, :], BK[4][0:64, :]), reads=[bk(4)], writes=["VA"])
        P.op("act", lambda e: e.copy(VA[64:128, :], BK[7][64:128, :]), reads=[bk(7)], writes=["VA"])
        P.op("dve", lambda e: e.tensor_tensor(VA[:], BK[6][:], VA[:], ALU.add), reads=[bk(6), "VA", "VA"], writes=["VA", "VA"])
        P.op("pool", lambda e: e.tensor_tensor(Yt[:, :, 0:128], y1s, VK[:], ALU.add), reads=["VA", "VA", "VK"], writes=[("Yt", 0)])
        for bb in range(16):
            i2 = bb % 2
            bnk = 4 + bb % 4
            BPp, KPp = SB["BPpad"][i2], SB["KPpad"][i2]
            P.op("pool", lambda e, bb=bb, BPp=BPp: e.tensor_scalar(BPp[:], BPtok[:, 0, :], cst[:, C_BSEL + bb:C_BSEL + bb + 1], None, ALU.mult),
                 reads=[("BPtok", 0, p) for p in range(4)] + ["cst"], writes=["BPpad"])
            P.op("dve", lambda e, bb=bb, KPp=KPp: e.tensor_scalar(KPp[:], KPtok[:, 0, :], cst[:, C_BSEL + bb:C_BSEL + bb + 1], None, ALU.mult),
                 reads=[("KPtok", 0, p) for p in range(4)] + ["cst"], writes=["KPpad"])
            for h in range(8):
                p, h2 = divmod(h, 2)
                ph = slice(h2 * 64, h2 * 64 + 64)
                P.op("pe", lambda e, h=h, p=p, ph=ph, bnk=bnk, BPp=BPp: e.matmul(BK[bnk][ph, p * 64:(p + 1) * 64], BPp[:, h * 64:(h + 1) * 64], UT[:, hcol(h):hcol(h) + 64], start=True, stop=False),
                     reads=["BPpad", "UT"], writes=[bk(bnk)])
                P.op("pe", lambda e, h=h, p=p, ph=ph, bnk=bnk, KPp=KPp: e.matmul(BK[bnk][ph, p * 64:(p + 1) * 64], KPp[:, h * 64:(h + 1) * 64], Vtok[:, 0, h * 64:(h + 1) * 64], start=False, stop=True),
                     reads=["KPpad", ("Vtok", 0, p)], writes=[bk(bnk)])
            for p in range(4):
                P.op("dve", lambda e, p=p, bb=bb, bnk=bnk: e.scalar_tensor_tensor(S_s[:, bb, p, :], S_s[:, bb, p, :], cCt[:, p, bb:bb + 1], BK[bnk][:, p * 64:(p + 1) * 64], ALU.mult, ALU.add),
                     reads=[("S_s", bb), ("cCt", p), bk(bnk)], writes=[("S_s", bb)])
            for pp in range(2):
                P.op("pe", lambda e, pp=pp, bb=bb, bnk=bnk: e.transpose(BK[bnk][:, 256 + pp * 128:256 + (pp + 1) * 128], S_s[:, bb, 2 * pp:2 * pp + 2, :].rearrange("p a i -> p (a i)"), ident),
                     reads=[("S_s", bb), "cst"], writes=[bk(bnk)])
            stgO = XT[:, i2 * 256:(i2 + 1) * 256]
            P.op("act", lambda e, bnk=bnk, stgO=stgO: e.copy(stgO, BK[bnk][:, 256:512]), reads=[bk(bnk)], writes=[("XT", i2)])
            for pp in range(2):
                for pl in range(2):
                    pidx = 2 * pp + pl
                    P.dma("sp", swkvo[SB["b0"] + bb, 2 * pidx:2 * pidx + 2].rearrange("h i j -> i h j"), stgO[pl * 64:(pl + 1) * 64, pp * 128:(pp + 1) * 128].rearrange("p (h j) -> p h j", j=64),
                          reads=[("XT", i2)], writes=[("swkvo", bb, pidx)])

    def sample_outputs(st):
        b0 = st * 16
        stgA = SB["xn"]
        for bb in range(16):
            P.dma("sp", sk[b0 + bb, 120:128, :], kvrow[bb * 8:(bb + 1) * 8, 0, :], reads=[("kvrow", 0)], writes=[("sk_new", st, bb)])
            P.dma("sp", sv[b0 + bb, 120:128, :], kvrow[bb * 8:(bb + 1) * 8, 1, :], reads=[("kvrow", 1)], writes=[("sv_new", st, bb)])
        for g in range(2):
            P.op("pe", lambda e, g=g: e.transpose(BK[4 + g][:, 0:128], SB["shout"][:, g * 8:(g + 1) * 8, :].rearrange("p a b -> p (a b)"), ident),
                 reads=[("shout", n) for n in range(14)] + ["shout_init", "cst"], writes=[bk(4 + g)])
            P.op("act", lambda e, g=g: e.copy(stgA[:, g * 128:(g + 1) * 128], BK[4 + g][:, 0:128]), reads=[bk(4 + g)], writes=["xn"])
        for n in range(14):
            g, nl = divmod(n, 8)
            P.dma("sp", ssh[b0:b0 + 16, n * 128:(n + 1) * 128], stgA[nl * 16:(nl + 1) * 16, g * 128:(g + 1) * 128], reads=["xn"], writes=[("ssh", st, n)])
        for g in range(6):
            P.op("pe", lambda e, g=g: e.transpose(BK[4 + g % 4][:, 0:128], SB["cout_s"][:, g * 4:(g + 1) * 4, :, :].rearrange("p a b j -> p (a b j)"), ident),
                 reads=[("cout_s", f) for f in range(22)] + ["cout_init", "cst"], writes=[bk(4 + g % 4)])
            P.op("act", lambda e, g=g: e.copy(stgA[:, 256 + g * 128:256 + (g + 1) * 128], BK[4 + g % 4][:, 0:128]), reads=[bk(4 + g % 4)], writes=["xn"])
        for f in range(22):
            g, fl = divmod(f, 4)
            P.dma("sp", sconvo[st * 32:(st + 1) * 32, f * 128:(f + 1) * 128], stgA[fl * 32:(fl + 1) * 32, 256 + g * 128:256 + (g + 1) * 128], reads=["xn"], writes=[("sconvo", st, f)])

    tiles = [("p", t // (2048 // NTP), t % (2048 // NTP)) for t in range(n_ptiles)]
    if os.environ.get('KREP'):
        tiles = tiles * int(os.environ['KREP'])
    try:
        for (kind, seq, ti) in tiles:
            if os.environ.get('KBAR'):
                P.barrier()
            do_tile(kind, seq, ti)
        if do_sample:
            P.barrier()
            wstack[0].close()
            wstack[0] = ExitStack()
            alloc_work(128, "_s")
            sample_alloc()
            P.barrier()
            sample_const_setup()
            stop_at("SA")
            for st in range(n_stiles):
                SB["b0"] = st * 16
                sample_setup(st)
                stop_at("SB")
                do_tile("s", st, 0)
                sample_outputs(st)
    except _Stop:
        pass
    _pe = os.environ.get('PADE'); _pn = int(os.environ.get('PAD', '0'))
    dmy = P.sbuf("dmy", [128, 8], F32)
    for _i in range(_pn):
        if _pe == 'pe':
            P.op("pe", lambda e: e.matmul(BK[0][0:8, 0:8], cst[0:8, 0:8], cst[0:8, 0:8], start=True, stop=True), reads=["cst"], writes=[("ps", 0)])
        else:
            P.op(_pe, lambda e: e.memset(dmy[:], 0.0), writes=["dmy"])
    _pa = int(os.environ.get('KPADALL', '0'))
    for _i in range(_pa):
        P.op("pe", lambda e: e.matmul(BK[0][0:8, 0:8], cst[0:8, 0:8], cst[0:8, 0:8], start=True, stop=True), reads=["cst"], writes=[("ps", 0)])
        P.op("pe", lambda e: e.matmul(BK[1][0:8, 0:8], cst[0:8, 0:8], cst[0:8, 0:8], start=True, stop=True), reads=["cst"], writes=[("ps", 1)])
        P.op("dve", lambda e: e.memset(dmy[:], 0.0), writes=["dmy"])
        if _i % 2 == 0:
            P.op("pool", lambda e: e.memset(dmy[:, 0:4], 0.0), writes=["dmy2"])
    P.finish()
    return D


_STAGE = ""


NCORES = int(os.environ.get("KNCORES", "8"))


def kernel(**inputs):
    global STOP, NSEQ
    inp = {k: np.asarray(v) for k, v in inputs.items()}
    cst, oh = host_consts()
    STOP = ""
    NSEQ = 16 // NCORES
    NSTL = 8 // NCORES
    NB = NSTL * 16
    nc = bass.Bass("TRN2", target_bir_lowering=False)
    D = build(nc, n_ptiles=2048 // NTP * NSEQ, n_stiles=NSTL)
    in_maps = []
    for c in range(NCORES):
        bs = slice(NB * c, NB * (c + 1))
        m = {
            "xp": inp["x_prompt"][NSEQ * c:NSEQ * (c + 1)], "xsm": inp["x_sample"][bs].reshape(NB * 8, 1024),
            "ck": inp["cache_win_k"][0, bs].reshape(NB, 128, 128), "cv": inp["cache_win_v"][0, bs].reshape(NB, 128, 128),
            "sshift": inp["state_shift"][0, bs], "swkv": inp["state_wkv"][0, bs], "sconv": inp["state_conv"][0, bs].reshape(NB * 2, 2816),
            "rel_bias": inp["rel_bias"], "norm1_g": inp["norm1_g"], "w_in": inp["w_in"][0], "sinks": inp["sinks"][0],
            "mu_shift": inp["mu_shift"][0], "w0": inp["w0"][0], "w2": inp["w2"][0], "a0": inp["a0"][0], "a2": inp["a2"][0],
            "g2": inp["g2"][0], "k_k": inp["k_k"][0], "k_a": inp["k_a"][0], "r_k": inp["r_k"][0].reshape(512),
            "lnx_g": inp["lnx_g"][0], "lnx_b": inp["lnx_b"][0], "w_pa": inp["w_pa"][0], "w_pb": inp["w_pb"][0],
            "w_o": inp["w_o"][0], "norm2_g": inp["norm2_g"], "w_up": inp["w_up"][0], "conv_w": inp["conv_w"][0],
            "conv_b": inp["conv_b"][0], "w_down": inp["w_down"][0], "final_g": inp["final_g"].reshape(1, 1024),
            "consts": cst, "oh": oh,
        }
        in_maps.append({k: np.ascontiguousarray(v, dtype=np.float32) for k, v in m.items() if k in D})
    res = run_bass_kernel_spmd(nc, in_maps, core_ids=list(range(NCORES)))
    R = res.results

    def cat(name, shape):
        return np.concatenate([np.asarray(r[name], dtype=np.float32) for r in R], axis=0).reshape(shape)

    return (cat("yp", (16, 2048, 1024)), cat("ys", (128, 8, 1024)),
            cat("pk", (1, 16, 128, 2, 64)), cat("pv", (1, 16, 128, 2, 64)), cat("psh", (1, 16, 1792)),
            cat("pwkv", (1, 16, 8, 64, 64)), cat("pconv", (1, 16, 2, 2816)),
            cat("sk", (1, 128, 128, 2, 64)), cat("sv", (1, 128, 128, 2, 64)), cat("ssh", (1, 128, 1792)),
            cat("swkvo", (1, 128, 8, 64, 64)), cat("sconvo", (1, 128, 2, 2816)))
```
